# Optimizing a Trainium2 kernel written in Bass

```python
import math
import jax, jax.numpy as jnp
from jax import lax
import numpy as np

D_MODEL = 1024
BATCH = 16
SEQ = 2048
DEPTH = 2

A_HEADS = 8
A_HEAD_DIM = 64
A_WIDTH = A_HEADS * A_HEAD_DIM
IDX_HEADS = 8
IDX_DIM = 64
TOPK_MAX = 256
Q_BLOCK = 128
B_GROUPS = 4
B_GROUP_DIM = 128
B_WIDTH = B_GROUPS * B_GROUP_DIM
CHUNK = 128
REL_BUCKETS = 32
REL_MAX_DIST = 128
N_BRANCH = 2
EPS = 1e-6

SPLIT_SIZES = (A_WIDTH, A_HEAD_DIM, A_HEAD_DIM, A_WIDTH,
               IDX_HEADS * IDX_DIM, IDX_DIM, IDX_HEADS,
               B_WIDTH, B_WIDTH, B_WIDTH,
               D_MODEL, D_MODEL)
N_IN = 5320

kernel_name = "hybrid_dsa_gmlp_gated_parallel"


def rms_norm(x, g):
    xf = x.astype(jnp.float32)
    y = xf * lax.rsqrt(jnp.mean(xf * xf, axis=-1, keepdims=True) + EPS)
    return (y * g.astype(jnp.float32)).astype(x.dtype)


def layer_norm(x, g, b):
    xf = x.astype(jnp.float32)
    mu = jnp.mean(xf, axis=-1, keepdims=True)
    var = jnp.mean(jnp.square(xf - mu), axis=-1, keepdims=True)
    y = (xf - mu) * lax.rsqrt(var + EPS)
    return (y * g.astype(jnp.float32) + b.astype(jnp.float32)).astype(x.dtype)


def split_offsets():
    offs, acc = [], 0
    for s in SPLIT_SIZES[:-1]:
        acc += s
        offs.append(acc)
    return offs


def t5_bucket(rel):
    max_exact = REL_BUCKETS // 2
    is_small = rel < max_exact
    nf = jnp.maximum(rel, 1).astype(jnp.float32)
    large = max_exact + (jnp.log(nf / max_exact) / math.log(REL_MAX_DIST / max_exact)
                         * (REL_BUCKETS - max_exact)).astype(jnp.int32)
    large = jnp.minimum(large, REL_BUCKETS - 1)
    return jnp.where(is_small, rel, large)


def dsa_attention(q, k, v, q_idx, k_idx, w_idx, rel_bias):
    B, S = q.shape[0], q.shape[1]
    k_top = min(TOPK_MAX, S // 4)
    n_blk = S // Q_BLOCK
    scale = A_HEAD_DIM ** -0.5
    key_pos = jnp.arange(S, dtype=jnp.int32)
    w_idx = w_idx * (IDX_HEADS ** -0.5 * IDX_DIM ** -0.5)

    def block(i):
        start = i * Q_BLOCK
        qb = lax.dynamic_slice_in_dim(q, start, Q_BLOCK, axis=1)
        qib = lax.dynamic_slice_in_dim(q_idx, start, Q_BLOCK, axis=1)
        wb = lax.dynamic_slice_in_dim(w_idx, start, Q_BLOCK, axis=1)
        qpos = start + jnp.arange(Q_BLOCK, dtype=jnp.int32)
        sc = jnp.einsum('bqhd,bsd->bqhs', qib, k_idx)
        sc = jnp.einsum('bqhs,bqh->bqs', jax.nn.relu(sc), wb).astype(jnp.float32)
        causal = key_pos[None, :] <= qpos[:, None]
        sc = jnp.where(causal[None], sc, -jnp.inf)
        _, idx = lax.top_k(sc, k_top)
        valid = idx <= qpos[None, :, None]
        k_sel = jax.vmap(lambda kb, ib: kb[ib])(k, idx)
        v_sel = jax.vmap(lambda vb, ib: vb[ib])(v, idx)
        logits = jnp.einsum('bqhd,bqkd->bqhk', qb, k_sel).astype(jnp.float32) * scale
        rel = jnp.maximum(qpos[None, :, None] - idx, 0)
        bias = rel_bias[t5_bucket(rel)]
        logits = logits + jnp.moveaxis(bias, -1, 2).astype(jnp.float32)
        logits = jnp.where(valid[:, :, None, :], logits, -jnp.inf)
        p = jax.nn.softmax(logits, axis=-1).astype(v.dtype)
        return jnp.einsum('bqhk,bqkd->bqhd', p, v_sel)

    out = lax.map(block, jnp.arange(n_blk, dtype=jnp.int32))
    return jnp.moveaxis(out, 0, 1).reshape(B, S, A_WIDTH)


def chunked_sgu(u, v, ln_g, ln_b, w_s, b_s):
    B, S = u.shape[0], u.shape[1]
    v = layer_norm(v, ln_g, ln_b)
    vc = v.reshape(B, S // CHUNK, CHUNK, B_GROUPS, B_GROUP_DIM)
    mask = jnp.tril(jnp.ones((CHUNK, CHUNK), dtype=bool))
    w = jnp.where(mask, w_s, 0)
    s = jnp.einsum('gts,bnsgc->bntgc', w, vc) + b_s.T[None, None, :, :, None]
    return u * s.reshape(B, S, B_WIDTH)


def setup_inputs(seed: int = 0) -> dict:
    key = jax.random.key(seed)
    ks = jax.random.split(key, 16)
    f32 = jnp.float32
    x = jax.random.normal(ks[0], (BATCH, SEQ, D_MODEL), f32)
    norm_g = 1.0 + 0.05 * jax.random.normal(ks[1], (DEPTH, D_MODEL), f32)
    w_in = jax.random.normal(ks[2], (DEPTH, D_MODEL, N_IN), f32) * D_MODEL ** -0.5
    q_norm_g = 1.0 + 0.05 * jax.random.normal(ks[3], (DEPTH, A_HEAD_DIM), f32)
    k_norm_g = 1.0 + 0.05 * jax.random.normal(ks[4], (DEPTH, A_HEAD_DIM), f32)
    rel_bias = 0.5 * jax.random.normal(ks[5], (REL_BUCKETS, A_HEADS), f32)
    sgu_ln_g = 1.0 + 0.05 * jax.random.normal(ks[6], (DEPTH, B_WIDTH), f32)
    sgu_ln_b = 0.02 * jax.random.normal(ks[7], (DEPTH, B_WIDTH), f32)
    w_spatial = jax.random.normal(ks[8], (DEPTH, B_GROUPS, CHUNK, CHUNK), f32) * CHUNK ** -0.5
    b_spatial = 1.0 + 0.1 * jax.random.normal(ks[9], (DEPTH, B_GROUPS, CHUNK), f32)
    w_branch = jax.random.normal(ks[10], (DEPTH, N_BRANCH, A_WIDTH, D_MODEL), f32) * A_WIDTH ** -0.5
    w_out = jax.random.normal(ks[11], (DEPTH, D_MODEL, D_MODEL), f32) * D_MODEL ** -0.5
    return {"x": x, "norm_g": norm_g, "w_in": w_in, "q_norm_g": q_norm_g,
            "k_norm_g": k_norm_g, "rel_bias": rel_bias, "sgu_ln_g": sgu_ln_g,
            "sgu_ln_b": sgu_ln_b, "w_spatial": w_spatial, "b_spatial": b_spatial,
            "w_branch": w_branch, "w_out": w_out}


def reference(x, norm_g, w_in, q_norm_g, k_norm_g, rel_bias, sgu_ln_g, sgu_ln_b,
              w_spatial, b_spatial, w_branch, w_out):
    B, S, _ = x.shape
    offs = split_offsets()
    for l in range(DEPTH):
        h = rms_norm(x, norm_g[l])
        z = jnp.einsum('bsd,dn->bsn', h, w_in[l])
        (q, k, v, gate_a, q_idx, k_idx, w_idx,
         u, v_b, gate_b, merge_a, merge_b) = jnp.split(z, offs, axis=-1)
        q = rms_norm(q.reshape(B, S, A_HEADS, A_HEAD_DIM), q_norm_g[l])
        k = rms_norm(k, k_norm_g[l])
        q_idx = q_idx.reshape(B, S, IDX_HEADS, IDX_DIM)
        y_a = dsa_attention(q, k, v, q_idx, k_idx, w_idx, rel_bias) * jax.nn.silu(gate_a)
        y_b = chunked_sgu(jax.nn.gelu(u), jax.nn.gelu(v_b), sgu_ln_g[l], sgu_ln_b[l],
                          w_spatial[l], b_spatial[l]) * jax.nn.silu(gate_b)
        y = jnp.stack([y_a, y_b], axis=2)
        y_d = jnp.einsum('bsnc,ncd->bsnd', y, w_branch[l])
        merged = jax.nn.sigmoid(merge_a) * y_d[:, :, 0] + jax.nn.sigmoid(merge_b) * y_d[:, :, 1]
        x = x + jnp.einsum('bsd,de->bse', merged, w_out[l])
    return x
```

```python
import contextlib
import math
import numpy as np
import concourse.bass as bass
import concourse.mybir as mybir
from concourse.bass_utils import run_bass_kernel_spmd

F32 = mybir.dt.float32
BF16 = mybir.dt.bfloat16
ALU = mybir.AluOpType
AF = mybir.ActivationFunctionType

D = 1024
NIN = 5320
NEG = -30000.0
EPS = 1e-6
NIT = 20

C_Q, C_K, C_V, C_GA, C_QI, C_KI, C_WI, C_U, C_VB, C_GB, C_MA, C_MB = (
    0, 512, 576, 640, 1152, 1664, 1728, 1736, 2248, 2760, 3272, 4296)


class Res:
    __slots__ = ("name", "w", "r")

    def __init__(self, name):
        self.name = name
        self.w = None
        self.r = []


class Sched:
    ENG = ("pe", "act", "dve", "pool", "sp")
    NDS = 24

    def __init__(self, nc, stack):
        self.nc = nc
        self.sems = {}
        for e in self.ENG:
            self.sems[e] = stack.enter_context(nc.semaphore("s_" + e))
        for i in range(self.NDS):
            self.sems[("d", i)] = stack.enter_context(nc.semaphore("d%d" % i))
        self.dval = [0] * self.NDS
        self.dnext = 0
        self.cnt = {e: 0 for e in self.ENG}
        self.ops = {e: [] for e in self.ENG}
        self.waited = {e: {} for e in self.ENG}

    def _deps(self, eng, reads, writes):
        deps = {}

        def add(tok, kind):
            if tok is None:
                return
            k, v = tok
            if k == eng and (eng == "pe" or kind != "raw"):
                return
            if deps.get(k, 0) < v:
                deps[k] = v
        for r in reads:
            add(r.w, "raw")
        for w in writes:
            add(w.w, "waw")
            for t in w.r:
                add(t, "war")
        need = []
        wd = self.waited[eng]
        for k, v in deps.items():
            if wd.get(k, 0) < v:
                wd[k] = v
                need.append((k, v))
        return need

    def _mark(self, tok, reads, writes):
        for r in reads:
            r.r.append(tok)
            if len(r.r) > 64:
                best = {}
                for k, v in r.r:
                    if best.get(k, 0) < v:
                        best[k] = v
                r.r = list(best.items())
        for w in writes:
            w.w = tok
            w.r = []

    def op(self, eng, fn, reads=(), writes=()):
        need = self._deps(eng, reads, writes)
        self.cnt[eng] += 1
        tok = (eng, self.cnt[eng])
        self.ops[eng].append((need, fn, (eng, 1)))
        self._mark(tok, reads, writes)
        return tok

    def dma(self, eng, fn, reads=(), writes=()):
        j = self.dnext
        self.dnext = (self.dnext + 1) % self.NDS
        need = self._deps(eng, reads, writes)
        if self.dval[j] > 0:
            wd = self.waited[eng]
            if wd.get(("d", j), 0) < self.dval[j]:
                wd[("d", j)] = self.dval[j]
                need.append((("d", j), self.dval[j]))
        self.dval[j] += 16
        tok = (("d", j), self.dval[j])
        self.ops[eng].append((need, fn, (("d", j), 16)))
        self._mark(tok, reads, writes)
        return tok

    def barrier(self):
        for e in self.ENG:
            need = []
            wd = self.waited[e]
            for k in self.ENG:
                if k != e and self.cnt[k] > wd.get(k, 0):
                    wd[k] = self.cnt[k]
                    need.append((k, self.cnt[k]))
            for j in range(self.NDS):
                if self.dval[j] > wd.get(("d", j), 0):
                    wd[("d", j)] = self.dval[j]
                    need.append((("d", j), self.dval[j]))
            self.ops[e].append((need, None, None))

    def emit(self):
        nc = self.nc
        sems = self.sems
        with nc.Block() as block:
            def run(e_name):
                def body(engine):
                    for need, fn, inc in self.ops[e_name]:
                        for k, v in need:
                            engine.wait_ge(sems[k], v)
                        if fn is not None:
                            fn(engine).then_inc(sems[inc[0]], inc[1])
                return body
            block.tensor(run("pe"))
            block.scalar(run("act"))
            block.vector(run("dve"))
            block.gpsimd(run("pool"))
            block.sync(run("sp"))


class Arena:
    def __init__(self, t16):
        self.t16 = t16
        self.t32 = t16.bitcast(F32)
        self.off = 0
        self.cap = t16.shape[1] * 2
        self.peak = 0

    def reset(self):
        self.off = 0

    def alloc(self, free_shape, dt, parts=128):
        es = 4 if dt == F32 else 2
        n = int(np.prod(free_shape))
        self.off = (self.off + 63) // 64 * 64
        o = self.off
        self.off += n * es
        self.peak = max(self.peak, self.off)
        assert self.off <= self.cap, ("arena overflow", self.off, self.cap)
        base = self.t32 if dt == F32 else self.t16
        ap = base[0:parts, o // es:o // es + n]
        if len(free_shape) == 2:
            ap = ap.rearrange("p (a b) -> p a b", a=free_shape[0])
        elif len(free_shape) == 3:
            ap = ap.rearrange("p (a b c) -> p a b c", a=free_shape[0], b=free_shape[1])
        return ap


def _bucket_table():
    oh = np.zeros((33, 384), np.float32)
    for m in range(384):
        rel = m - 127
        if rel < 0:
            oh[32, m] = 1.0
            continue
        if rel < 16:
            b = rel
        else:
            nf = np.float32(max(rel, 1))
            v = np.log(nf / np.float32(16.0)) / np.float32(math.log(128 / 16)) * np.float32(16.0)
            b = min(16 + int(np.float32(v).astype(np.int32)), 31)
        oh[b, m] = 1.0
    return oh


def host_consts():
    t = np.arange(128)
    return {
        "c_ident": np.eye(128, dtype=np.float32),
        "c_tril": (t[:, None] >= t[None, :]).astype(np.float32),
        "c_cneg": np.where(t[None, :] <= t[:, None], 0.0, -1e30).astype(np.float32),
        "c_oh": _bucket_table(),
    }


def build(S, NSEQ, DEPTH, dbg=None):
    nc = bass.Bass("TRN2", target_bir_lowering=False)
    NTT = S // 512
    KTOP = min(256, S // 4)
    dt_in = lambda name, shape: nc.dram_tensor(name, shape, F32, kind="ExternalInput")
    x_in = dt_in("x", [NSEQ, S, D])
    norm_g = dt_in("norm_g", [DEPTH, D])
    w_in = dt_in("w_in", [DEPTH, D, NIN])
    q_norm_g = dt_in("q_norm_g", [DEPTH, 64])
    k_norm_g = dt_in("k_norm_g", [DEPTH, 64])
    rel_bias = dt_in("rel_bias", [32, 8])
    sgu_ln_g = dt_in("sgu_ln_g", [DEPTH, 512])
    sgu_ln_b = dt_in("sgu_ln_b", [DEPTH, 512])
    w_spatial = dt_in("w_spatial", [DEPTH, 4, 128, 128])
    b_spatial = dt_in("b_spatial", [DEPTH, 4, 128])
    w_branch = dt_in("w_branch", [DEPTH, 2, 512, D])
    w_out = dt_in("w_out", [DEPTH, D, D])
    c_ident = dt_in("c_ident", [128, 128])
    c_tril = dt_in("c_tril", [128, 128])
    c_cneg = dt_in("c_cneg", [128, 128])
    c_oh = dt_in("c_oh", [33, 384])
    out = nc.dram_tensor("out", [NSEQ, S, D], F32, kind="ExternalOutput")
    xmid = nc.dram_tensor("xmid", [NSEQ, S, D], F32, kind="Internal")
    hT_scr = nc.dram_tensor("hT_scr", [NSEQ, NTT, 128, 8 * 512], BF16, kind="Internal")
    ya_scr = nc.dram_tensor("ya_scr", [NSEQ, NTT, 512, 512], BF16, kind="Internal")
    yb_scr = nc.dram_tensor("yb_scr", [NSEQ, NTT, 512, 512], BF16, kind="Internal")
    bias_scr = nc.dram_tensor("bias_scr", [128, 8 * 384], BF16, kind="Internal")
    R_xmid, R_hT_scr, R_ya_scr, R_yb_scr, R_bias_scr = (Res(n) for n in ("xmid", "hTs", "yas", "ybs", "bs"))
    R_ya = {}
    R_yb = {}
    R_hs = {}

    with contextlib.ExitStack() as st:
        S_ = Sched(nc, st)
        sb = lambda name, shape, dt: st.enter_context(nc.sbuf_tensor(name, shape, dt))
        def MM(out_, lhsT, rhs, start, stop, R, W):
            S_.op("pe", lambda e: e.matmul(out_, lhsT=lhsT, rhs=rhs, start=start, stop=stop), R, W)

        def TR(out_, in_, R, W):
            S_.op("pe", lambda e: e.transpose(out=out_, in_=in_, identity=idb[:]), list(R) + [R_const], W)

        def ACT(out_, in_, func, R, W, bias=None, scale=None, accum=None):
            kw = {}
            if bias is not None:
                kw["bias"] = bias
            if scale is not None:
                kw["scale"] = scale
            if accum is not None:
                kw["accum_out"] = accum
            S_.op("act", lambda e: e.activation(out=out_, in_=in_, func=func, **kw), R, W)

        def TS(eng, out_, in0, s1, s2, op0, op1, R, W, accum=None):
            kw = {}
            if accum is not None:
                kw["accum_out"] = accum
            if op1 is None:
                S_.op(eng, lambda e: e.tensor_scalar(out=out_, in0=in0, scalar1=s1, scalar2=None, op0=op0, **kw), R, W)
            else:
                S_.op(eng, lambda e: e.tensor_scalar(out=out_, in0=in0, scalar1=s1, scalar2=s2, op0=op0, op1=op1, **kw), R, W)

        def TT(eng, out_, in0, in1, op, R, W):
            S_.op(eng, lambda e: e.tensor_tensor(out=out_, in0=in0, in1=in1, op=op), R, W)

        def STT(out_, in0, scalar, in1, op0, op1, R, W):
            S_.op("dve", lambda e: e.scalar_tensor_tensor(out=out_, in0=in0, scalar=scalar, in1=in1, op0=op0, op1=op1), R, W)

        def CP(eng, out_, in_, R, W):
            if eng == "act":
                S_.op("act", lambda e: e.copy(out=out_, in_=in_), R, W)
            else:
                S_.op(eng, lambda e: e.tensor_copy(out=out_, in_=in_), R, W)

        def MS(eng, ap, val, W):
            S_.op(eng, lambda e: e.memset(ap, val), (), W)

        def DMA(q, out_, in_, R, W):
            return S_.dma(q, lambda e: e.dma_start(out=out_, in_=in_), R, W)

        R_const = Res("const")
        idb = sb("idb", [128, 128], BF16)
        tril = sb("tril", [128, 128], F32)
        cneg = sb("cneg", [128, 128], F32)
        ones_bf = sb("ones_bf", [128, 128], BF16)
        o64 = sb("o64", [128, 128], BF16)
        smallc = sb("smallc", [128, 8], F32)
        biasT = sb("biasT", [128, 8, 2, 128], BF16)
        cfar = sb("cfar", [128, 8], F32)
        gbc = sb("gbc", [128, D], F32)
        lngb = sb("lngb", [128, 512], F32)
        lnbb = sb("lnbb", [128, 512], F32)
        gqk = sb("gqk", [64, 4], F32)
        wT_sp = sb("wT_sp", [128, 4, 128], BF16)
        bsp = sb("bsp", [1, 4, 128], BF16)
        hT = sb("hT", [128, 8, 512], BF16)
        R_hT = Res("hT")
        R_lay = Res("layerconst")
        psum = [st.enter_context(nc.psum_tensor("ps%d" % i, [128, 512], F32)) for i in range(8)]
        psum16 = [p.bitcast(BF16) for p in psum]
        R_ps = [Res("ps%d" % i) for i in range(8)]

        class PPool:
            def __init__(self, idxs):
                self.idxs = idxs
                self.n = 0

            def next(self):
                i = self.idxs[self.n % len(self.idxs)]
                self.n += 1
                return psum[i][:, :], psum16[i][:, :], R_ps[i]

        ARENA_BYTES = 132 * 1024
        arena_t = sb("arena", [128, ARENA_BYTES // 2], BF16)
        AR = Arena(arena_t)

        DMA("pool", idb[:], c_ident[:, :], (), [R_const])
        DMA("sp", tril[:], c_tril[:, :], (), [R_const])
        DMA("sp", cneg[:], c_cneg[:, :], (), [R_const])
        MS("dve", ones_bf[:], 1.0, [R_const])
        MS("dve", o64[:], 1.0 / 64, [R_const])
        MS("dve", smallc[:, 0:1], -0.5, [R_const])
        MS("dve", smallc[:, 1:2], EPS, [R_const])
        MS("dve", smallc[:, 2:3], 4 * EPS, [R_const])
        mhalf = smallc[:, 0:1]
        eps_t = smallc[:, 1:2]
        eps4_t = smallc[:, 2:3]

        AR.reset()
        oh_f = AR.alloc([384], F32)
        rb_ext = AR.alloc([8], F32)
        ones33 = AR.alloc([128], F32)
        lh = AR.alloc([8, 128], F32)
        Bsb = AR.alloc([8, 384], BF16)
        R_su = Res("setup")
        DMA("sp", oh_f[0:33, :], c_oh[:, :], (), [R_su])
        MS("dve", rb_ext[32:33, :], NEG, [R_su])
        DMA("sp", rb_ext[0:32, :], rel_bias[:, :], (), [R_su])
        MS("dve", ones33[0:33, :], 1.0, [R_su])
        for h in range(8):
            TS("dve", lh[0:33, h, :], ones33[0:33, :], rb_ext[0:33, h:h + 1], None, ALU.mult, None, [R_su], [R_su])
        for h in range(8):
            pB, _, rB = psum[h % 2], None, R_ps[h % 2]
            MM(pB[:, 0:384], lh[0:33, h, :], oh_f[0:33, :], True, True, [R_su], [rB])
            CP("dve", cfar[:, h:h + 1], pB[:, 300:301], [rB], [R_const])
            TS("dve", Bsb[:, h, :], pB[:, 0:384], cfar[:, h:h + 1], None, ALU.subtract, None, [rB, R_const], [R_su])
        DMA("sp", bias_scr[:, :], Bsb[:].rearrange("p h m -> p (h m)"), [R_su], [R_bias_scr])
        for h in range(8):
            for dl in range(2):
                src = bass.AP(bias_scr, h * 384 + 127 + 128 * dl, [[8 * 384 - 1, 128], [1, 128]])
                DMA("sp", biasT[:, h, dl, :], src, [R_bias_scr], [R_const])
        S_.barrier()

        out_toks = []
        for l in range(DEPTH):
            src_t = x_in if l == 0 else xmid
            dst_t = out if l == DEPTH - 1 else xmid
            R_src = Res("src") if l == 0 else R_xmid
            assert DEPTH <= 2
            AR.reset()
            DMA("sp", gbc[:], bass.AP(norm_g, l * D, [[0, 128], [1, D]]), (), [R_lay])
            DMA("sp", lngb[:], bass.AP(sgu_ln_g, l * 512, [[0, 128], [1, 512]]), (), [R_lay])
            DMA("sp", lnbb[:], bass.AP(sgu_ln_b, l * 512, [[0, 128], [1, 512]]), (), [R_lay])
            DMA("sp", gqk[:, 0:1], bass.AP(q_norm_g, l * 64, [[1, 64], [1, 1]]), (), [R_lay])
            DMA("sp", gqk[:, 1:2], bass.AP(k_norm_g, l * 64, [[1, 64], [1, 1]]), (), [R_lay])
            TS("dve", gqk[:, 2:3], gqk[:, 1:2], 0.125, None, ALU.mult, None, [R_lay], [R_lay])
            DMA("pool", bsp[:].rearrange("o g t -> o (g t)"), bass.AP(b_spatial, l * 512, [[0, 1], [1, 512]]), (), [R_lay])
            wsp_f = AR.alloc([4, 128], F32)
            wsp_m = AR.alloc([4, 128], BF16)
            R_w = Res("wsp")
            DMA("sp", wsp_f, w_spatial[l].rearrange("g t s -> t g s"), (), [R_w])
            for g in range(4):
                TT("dve", wsp_m[:, g, :], wsp_f[:, g, :], tril[:], ALU.mult, [R_w, R_const], [R_w])
            pT, pT16, rT = psum[0][:, :], psum16[0][:, :], R_ps[0]
            for g in range(4):
                TR(pT16[:, g * 128:(g + 1) * 128], wsp_m[:, g, :], [R_w], [rT])
            CP("dve", wT_sp[:].rearrange("p g t -> p (g t)"), pT16[:, 0:512], [rT], [R_lay])
            S_.barrier()

            AR.reset()
            W1 = AR.alloc([8, 1736], BF16)
            R_W1 = Res("W1")
            for k in range(8):
                DMA("pool", W1[:, k, :], w_in[l, k * 128:(k + 1) * 128, 0:1736], (), [R_W1])
            kT = AR.alloc([S], BF16)
            kidxT = AR.alloc([S], BF16)
            vaug = AR.alloc([S // 128, 66], BF16)
            xt = [AR.alloc([D], F32) for _ in range(2)]
            R_xt = [Res("xt0"), Res("xt1")]
            hb = AR.alloc([D], BF16)
            R_hb = Res("hb")
            st_ = AR.alloc([16], F32)
            R_st = Res("st")
            qidxT = AR.alloc([8, 512], BF16)
            R_qi = Res("qidxT")
            widx = AR.alloc([4, 8], F32)
            R_wi = Res("widx")
            Dg = [AR.alloc([8, 128], BF16) for _ in range(2)]
            R_Dg = [Res("Dg0"), Res("Dg1")]
            Rb = [AR.alloc([512], BF16) for _ in range(4)]
            R_Rb = [Res("Rb%d" % i) for i in range(4)]
            Isb = [AR.alloc([S], F32) for _ in range(2)]
            R_I = [Res("I0"), Res("I1")]
            junkI = AR.alloc([S], BF16)
            R_junk = Res("junkI")
            bis = [AR.alloc([8], F32) for _ in range(2)]
            R_bis = [Res("bis0"), Res("bis1")]
            mk = [AR.alloc([S], BF16) for _ in range(2)]
            R_mk = [Res("mk0"), Res("mk1")]
            maskT = AR.alloc([S // 128, 512], BF16)
            R_mT = Res("maskT")
            qT = [AR.alloc([512], BF16) for _ in range(2)]
            R_qT = [Res("qT0"), Res("qT1")]
            sga = [AR.alloc([512], BF16) for _ in range(2)]
            R_sga = [Res("sga0"), Res("sga1")]
            scr = [AR.alloc([512], F32) for _ in range(4)]
            R_scr = [Res("scr%d" % i) for i in range(4)]
            scrn = [0]
            Eb = [AR.alloc([512], BF16) for _ in range(4)]
            R_E = [Res("E%d" % i) for i in range(4)]
            rdb = [AR.alloc([512], BF16) for _ in range(2)]
            R_rd = [Res("rd0"), Res("rd1")]
            yah = [AR.alloc([512], BF16) for _ in range(2)]
            R_yah = [Res("yah0"), Res("yah1")]
            R_kv = Res("kv")
            pA = PPool([0, 1, 2])
            pS = PPool([3, 4])
            pI = PPool([5])
            pO = PPool([6, 7])

            def next_scr():
                i = scrn[0] % 4
                scrn[0] += 1
                return scr[i], R_scr[i]

            MS("dve", vaug[:, :, 64:65], 2.0, [R_kv])

            def norm_tile(seq, tt, xtl, R_xtl, hbl, R_hbl, stl, R_stl, pool_):
                for blk in range(4):
                    r0 = tt * 512 + blk * 128
                    xb, R_xb = xtl[blk % 2], R_xtl[blk % 2]
                    DMA("sp", xb, src_t[seq, r0:r0 + 128, :], [R_src], [R_xb])
                    c = blk * 3
                    ACT(hbl, xb, AF.Square, [R_xb], [R_hbl, R_stl], accum=stl[:, c:c + 1])
                    TS("dve", stl[:, c + 1:c + 2], stl[:, c:c + 1], 1.0 / D, EPS, ALU.mult, ALU.add, [R_stl], [R_stl])
                    TT("pool", stl[:, c + 2:c + 3], stl[:, c + 1:c + 2], mhalf, ALU.pow, [R_stl, R_const], [R_stl])
                    STT(hbl, xb, stl[:, c + 2:c + 3], gbc[:], ALU.mult, ALU.mult, [R_xb, R_stl, R_lay], [R_hbl])
                    _, p16, rp = pool_.next()
                    for k in range(8):
                        TR(p16[:, k * 128:(k + 1) * 128], hbl[:, k * 128:(k + 1) * 128], [R_hbl], [rp])
                    CP("act", hT[:, :, blk * 128:(blk + 1) * 128],
                       p16[:, 0:1024].rearrange("p (k t) -> p k t", k=8), [rp], [R_hT])

            def proj_fm(pool_, Wt, R_Wt, c0, M):
                p, _, rp = pool_.next()
                for k in range(8):
                    MM(p[0:M, :], Wt[:, k, c0:c0 + M], hT[:, k, :], k == 0, k == 7, [R_Wt, R_hT], [rp])
                return p, rp

            def rms64(p, rp, gcol, out_ap, R_out):
                sq, R_sq = next_scr()
                s1, R_s1 = next_scr()
                ACT(sqb[0:64, :], p[0:64, :], AF.Square, [rp], [R_sqb])
                pm, _, rpm = pA.next()
                MM(pm[0:64, :], o64[0:64, 0:64], sqb[0:64, :], True, True, [R_sqb, R_const], [rpm])
                ACT(sq[0:64, :], pm[0:64, :], AF.Ln, [rpm, R_const], [R_sq], bias=eps_t[0:64, :])
                ACT(s1[0:64, :], sq[0:64, :], AF.Exp, [R_sq], [R_s1], scale=-0.5)
                STT(out_ap, p[0:64, :], gcol, s1[0:64, :], ALU.mult, ALU.mult, [rp, R_s1, R_lay], [R_out])

            sqb = AR.alloc([512], BF16)
            R_sqb = Res("sqb")

            for seq in range(NSEQ):
                for tt in range(NTT):
                    T0 = tt * 512
                    R_hs[(l, seq, tt)] = Res("hs")
                    norm_tile(seq, tt, xt, R_xt, hb, R_hb, st_, R_st, pA)
                    DMA("sp", hT_scr[seq, tt], hT[:].rearrange("p k t -> p (k t)"), [R_hT], [R_hs[(l, seq, tt)]])
                    p, rp = proj_fm(pA, W1, R_W1, C_K, 64)
                    rms64(p, rp, gqk[:, 2:3], kT[0:64, T0:T0 + 512], R_kv)
                    p, rp = proj_fm(pA, W1, R_W1, C_KI, 64)
                    CP("dve", kidxT[0:64, T0:T0 + 512], p[0:64, :], [rp], [R_kv])
                    for blk in range(4):
                        p, _, rp = pA.next()
                        for k in range(8):
                            MM(p[:, 0:64], hT[:, k, blk * 128:(blk + 1) * 128], W1[:, k, C_V:C_V + 64], k == 0, k == 7,
                               [R_W1, R_hT], [rp])
                        for k in range(8):
                            MM(p[:, 64:72], hT[:, k, blk * 128:(blk + 1) * 128], W1[:, k, C_WI:C_WI + 8], k == 0, k == 7,
                               [R_W1, R_hT], [rp])
                        CP("dve", vaug[:, tt * 4 + blk, 0:64], p[:, 0:64], [rp], [R_kv])
                        TS("dve", widx[:, blk, :], p[:, 64:72], (8 ** -0.5) * (64 ** -0.5), None, ALU.mult, None, [rp], [R_wi])
                    for h in range(8):
                        p, rp = proj_fm(pA, W1, R_W1, C_QI + 64 * h, 64)
                        CP("act", qidxT[0:64, h, :], p[0:64, :], [rp], [R_qi])
                    for blk in range(4):
                        tb = tt * 4 + blk
                        Wd = (tb + 1) * 128
                        Ib, R_Ib = Isb[tb % 2], R_I[tb % 2]
                        Dgb, R_Dgb = Dg[tb % 2], R_Dg[tb % 2]
                        for h in range(8):
                            TS("dve", Dgb[:, h, :], idb[:], widx[:, blk, h:h + 1], None, ALU.mult, None,
                               [R_const, R_wi], [R_Dgb])
                        for j in range(tt + 1):
                            wdt = 512 if j < tt else (blk + 1) * 128
                            s0 = j * 512
                            pi, _, rpi = pI.next()
                            pend = None
                            for h in range(9):
                                if h < 8:
                                    psc, _, rps = pS.next()
                                    MM(psc[:, 0:wdt], qidxT[0:64, h, blk * 128:(blk + 1) * 128], kidxT[0:64, s0:s0 + wdt],
                                       True, True, [R_qi, R_kv], [rps])
                                    rb_, R_rb_ = Rb[h % 4], R_Rb[h % 4]
                                    ACT(rb_[:, 0:wdt], psc[:, 0:wdt], AF.Relu, [rps], [R_rb_])
                                if pend is not None:
                                    hh, rbp, R_rbp = pend
                                    MM(pi[:, 0:wdt], Dgb[:, hh, :], rbp[:, 0:wdt], hh == 0, hh == 7, [R_Dgb, R_rbp], [rpi])
                                if h < 8:
                                    pend = (h, rb_, R_rb_)
                            if j < tt:
                                CP("dve", Ib[:, s0:s0 + 512], pi[:, 0:512], [rpi], [R_Ib])
                            else:
                                if blk > 0:
                                    CP("dve", Ib[:, s0:s0 + blk * 128], pi[:, 0:blk * 128], [rpi], [R_Ib])
                                TT("dve", Ib[:, s0 + blk * 128:s0 + wdt], pi[:, blk * 128:wdt], cneg[:], ALU.add,
                                   [rpi, R_const], [R_Ib])
                        bs, R_bs = bis[tb % 2], R_bis[tb % 2]
                        lo, W0, mid, cnt, stp, hi = (bs[:, i:i + 1] for i in range(6))
                        if Wd <= KTOP:
                            MS("dve", lo, -1e29, [R_bs])
                        else:
                            S_.op("dve", lambda e, o=lo, i_=Ib[:, 0:tb * 128]: e.tensor_reduce(
                                out=o, in_=i_, axis=mybir.AxisListType.X, op=ALU.min), [R_Ib], [R_bs])
                            S_.op("dve", lambda e, o=hi, i_=Ib[:, 0:Wd]: e.tensor_reduce(
                                out=o, in_=i_, axis=mybir.AxisListType.X, op=ALU.max), [R_Ib], [R_bs])
                            STT(W0, hi, 1.0, lo, ALU.add, ALU.subtract, [R_bs], [R_bs])
                            for it in range(NIT):
                                c = 2.0 ** -(it + 1)
                                STT(mid, W0, c, lo, ALU.mult, ALU.add, [R_bs], [R_bs])
                                TS("dve", junkI[:, 0:Wd], Ib[:, 0:Wd], mid, None, ALU.is_ge, ALU.add, [R_Ib, R_bs],
                                   [R_junk, R_bs], accum=cnt)
                                STT(stp, cnt, KTOP - 0.5, W0, ALU.is_ge, ALU.mult, [R_bs], [R_bs])
                                STT(lo, stp, c, lo, ALU.mult, ALU.add, [R_bs], [R_bs])
                        mkb, R_mkb = mk[tb % 2], R_mk[tb % 2]
                        TS("dve", mkb[:, 0:Wd], Ib[:, 0:Wd], lo, NEG, ALU.is_lt, ALU.mult, [R_Ib, R_bs], [R_mkb])
                        sb0 = 0
                        while sb0 <= tb:
                            n = min(8, tb + 1 - sb0)
                            _, p16, rp = pA.next()
                            for i in range(n):
                                TR(p16[:, i * 128:(i + 1) * 128], mkb[:, (sb0 + i) * 128:(sb0 + i + 1) * 128], [R_mkb], [rp])
                            CP("act", maskT[:, sb0:sb0 + n, blk * 128:(blk + 1) * 128],
                               p16[:, 0:n * 128].rearrange("p (k t) -> p k t", k=n), [rp], [R_mT])
                            sb0 += n
                    NSB = 4 * (tt + 1)

                    def proj_head(h):
                        p, rp = proj_fm(pA, W1, R_W1, C_Q + 64 * h, 64)
                        rms64(p, rp, gqk[:, 0:1], qT[h % 2][0:64, :], R_qT[h % 2])
                        p, rp = proj_fm(pA, W1, R_W1, C_GA + 64 * h, 64)
                        th, R_th = next_scr()
                        ACT(th[0:64, :], p[0:64, :], AF.Tanh, [rp], [R_th], scale=0.5)
                        STT(sga[h % 2][0:64, :], th[0:64, :], 1.0, p[0:64, :], ALU.add, ALU.mult, [R_th, rp], [R_sga[h % 2]])

                    def attn_head(h):
                        q_, R_q_ = qT[h % 2], R_qT[h % 2]
                        po, _, rpo = pO.next()
                        pend = None
                        for sbi in range(NSB + 1):
                            if sbi < NSB:
                                j = sbi - 4 * tt
                                c0 = max(j, 0) * 128
                                plt, _, rpl = pS.next()
                                MM(plt[:, c0:512], kT[0:64, sbi * 128:(sbi + 1) * 128], q_[0:64, c0:512], True, False,
                                   [R_kv, R_q_], [rpl])
                                near = []
                                for blk in range(4):
                                    tb = 4 * tt + blk
                                    if sbi == tb:
                                        near.append((blk, 0))
                                    elif sbi == tb - 1:
                                        near.append((blk, 1))
                                MM(plt[:, c0:512], idb[:], maskT[:, sbi, c0:512], False, len(near) == 0,
                                   [R_const, R_mT], [rpl])
                                for ni, (blk, dl) in enumerate(near):
                                    MM(plt[:, blk * 128:(blk + 1) * 128], idb[:], biasT[:, h, dl, :], False,
                                       ni == len(near) - 1, [R_const], [rpl])
                                e_, R_e_ = Eb[sbi % 4], R_E[sbi % 4]
                                ACT(e_[:, c0:512], plt[:, c0:512], AF.Exp, [rpl, R_const], [R_e_], bias=cfar[:, h:h + 1])
                            if pend is not None:
                                ps_, pc0, pe_, R_pe_ = pend
                                MM(po[0:65, pc0:512], vaug[:, ps_, 0:65], pe_[:, pc0:512], ps_ == 0, ps_ == NSB - 1,
                                   [R_kv, R_pe_], [rpo])
                            if sbi < NSB:
                                pend = (sbi, c0, e_, R_e_)
                        ln_, R_ln = next_scr()
                        ACT(ln_[64:65, :], po[64:65, :], AF.Ln, [rpo], [R_ln])
                        rd, R_rd_ = rdb[h % 2], R_rd[h % 2]
                        ACT(rd[64:65, :], ln_[64:65, :], AF.Exp, [R_ln], [R_rd_], scale=-1.0)
                        pb, _, rpb = pA.next()
                        MM(pb[0:64, :], ones_bf[64:65, 0:64], rd[64:65, :], True, True, [R_const, R_rd_], [rpb])
                        tmp, R_tmp = next_scr()
                        TT("dve", tmp[0:64, :], sga[h % 2][0:64, :], po[0:64, :], ALU.mult, [R_sga[h % 2], rpo], [R_tmp])
                        y_, R_y_ = yah[h % 2], R_yah[h % 2]
                        TT("dve", y_[0:64, :], tmp[0:64, :], pb[0:64, :], ALU.mult, [R_tmp, rpb], [R_y_])
                        key = (l, seq, tt)
                        if key not in R_ya:
                            R_ya[key] = Res("ya")
                        DMA("sp", ya_scr[seq, tt, h * 64:(h + 1) * 64, :], y_[0:64, :], [R_y_], [R_ya[key]])

                    proj_head(0)
                    for h in range(8):
                        if h + 1 < 8:
                            proj_head(h + 1)
                        attn_head(h)
            S_.barrier()

            AR.reset()
            W2 = AR.alloc([8, 1536], BF16)
            R_W2 = Res("W2")
            for k in range(8):
                DMA("pool", W2[:, k, :], w_in[l, k * 128:(k + 1) * 128, C_U:C_U + 1536], (), [R_W2])
            vln = AR.alloc([4, 512], BF16)
            R_vln = Res("vln")
            gu2 = AR.alloc([4, 512], BF16)
            R_gu = Res("gu2")
            sgb = AR.alloc([4, 512], BF16)
            R_sgb = Res("sgb")
            ybT = AR.alloc([4, 512], BF16)
            R_ybT = Res("ybT")
            scr2 = [AR.alloc([512], F32) for _ in range(8)]
            R_scr2 = [Res("s2_%d" % i) for i in range(8)]
            sc2n = [0]
            bnst = AR.alloc([4, 8], F32)
            R_bn = Res("bn")
            pA = PPool([0, 1, 2, 3, 4, 5])
            pS = PPool([6, 7])

            def nscr2():
                i = sc2n[0] % 8
                sc2n[0] += 1
                return scr2[i], R_scr2[i]

            def gelu2(p, rp, out_ap, R_out):
                a, R_a = nscr2()
                ACT(a, p, AF.Square, [rp], [R_a])
                TS("pool", a, a, 0.044715, 1.0, ALU.mult, ALU.add, [R_a], [R_a])
                b_, R_b = nscr2()
                TT("dve", b_, a, p, ALU.mult, [R_a, rp], [R_b])
                ACT(b_, b_, AF.Tanh, [R_b], [R_b], scale=0.7978845608028654)
                STT(out_ap, b_, 1.0, p, ALU.add, ALU.mult, [R_b, rp], [R_out])

            for seq in range(NSEQ):
                for tt in range(NTT):
                    DMA("sp", hT[:].rearrange("p k t -> p (k t)"), hT_scr[seq, tt], [R_hs[(l, seq, tt)]], [R_hT])
                    for blk in range(4):
                        p, _, rp = pA.next()
                        for k in range(8):
                            MM(p[:, :], hT[:, k, blk * 128:(blk + 1) * 128], W2[:, k, 512:1024], k == 0, k == 7,
                               [R_W2, R_hT], [rp])
                        g2, R_g2 = nscr2()
                        gelu2(p, rp, g2, R_g2)
                        S_.op("dve", lambda e, o=bnst[:, blk, 0:6], i_=g2: e.bn_stats(out=o, in_=i_), [R_g2], [R_bn])
                        S_.op("dve", lambda e, o=bnst[:, blk, 6:8], i_=bnst[:, blk, 0:6]: e.bn_aggr(out=o, in_=i_), [R_bn], [R_bn])
                        TS("dve", bnst[:, blk, 7:8], bnst[:, blk, 7:8], 4 * EPS, None, ALU.add, None, [R_bn], [R_bn])
                        TT("pool", bnst[:, blk, 7:8], bnst[:, blk, 7:8], mhalf, ALU.pow, [R_bn, R_const], [R_bn])
                        TS("dve", g2, g2, bnst[:, blk, 6:7], bnst[:, blk, 7:8], ALU.subtract, ALU.mult, [R_g2, R_bn], [R_g2])
                        TT("pool", g2, g2, lngb[:], ALU.mult, [R_g2, R_lay], [R_g2])
                        TT("pool", vln[:, blk, :], g2, lnbb[:], ALU.add, [R_g2, R_lay], [R_vln])
                    for c in range(4):
                        p, rp = proj_fm(pA, W2, R_W2, c * 128, 128)
                        gelu2(p, rp, gu2[:, c, :], R_gu)
                        p, rp = proj_fm(pA, W2, R_W2, 1024 + c * 128, 128)
                        th, R_th = nscr2()
                        ACT(th, p, AF.Tanh, [rp], [R_th], scale=0.5)
                        STT(sgb[:, c, :], th, 1.0, p, ALU.add, ALU.mult, [R_th, rp], [R_sgb])
                    for g in range(4):
                        p, _, rp = pS.next()
                        for blk in range(4):
                            MM(p[:, blk * 128:(blk + 1) * 128], vln[:, blk, g * 128:(g + 1) * 128], wT_sp[:, g, :], True, False,
                               [R_vln, R_lay], [rp])
                            MM(p[:, blk * 128:(blk + 1) * 128], ones_bf[0:1, :], bsp[0:1, g, :], False, True,
                               [R_const, R_lay], [rp])
                        t_, R_t = nscr2()
                        STT(t_, gu2[:, g, :], 0.25, p, ALU.mult, ALU.mult, [R_gu, rp], [R_t])
                        TT("pool", ybT[:, g, :], t_, sgb[:, g, :], ALU.mult, [R_t, R_sgb], [R_ybT])
                    key = (l, seq, tt)
                    R_yb[key] = Res("yb")
                    DMA("sp", yb_scr[seq, tt].rearrange("(g p) t -> p g t", p=128), ybT, [R_ybT], [R_yb[key]])
            S_.barrier()

            AR.reset()
            Wm = AR.alloc([8, 2048], BF16)
            wbr = AR.alloc([2, 4, D], BF16)
            wo = AR.alloc([8, D], BF16)
            R_W3 = Res("W3")
            for k in range(8):
                DMA("pool", Wm[:, k, :], w_in[l, k * 128:(k + 1) * 128, C_MA:C_MA + 2048], (), [R_W3])
            for n_ in range(2):
                for k in range(4):
                    DMA("pool", wbr[:, n_, k, :], w_branch[l, n_, k * 128:(k + 1) * 128, :], (), [R_W3])
            for k in range(8):
                DMA("pool", wo[:, k, :], w_out[l, k * 128:(k + 1) * 128, :], (), [R_W3])
            yaT = AR.alloc([4, 512], BF16)
            R_yaT = Res("yaT")
            ybT3 = AR.alloc([4, 512], BF16)
            R_ybT3 = Res("ybT3")
            mg = AR.alloc([8, 512], BF16)
            R_mg = Res("mg")
            scr3 = [AR.alloc([512], F32) for _ in range(8)]
            R_scr3 = [Res("s3_%d" % i) for i in range(8)]
            sc3n = [0]
            xres = [AR.alloc([D], F32) for _ in range(2)]
            R_xr = [Res("xr0"), Res("xr1")]
            pA = PPool([0, 1, 2, 3, 4, 5])
            pS = PPool([6, 7])

            def nscr3():
                i = sc3n[0] % 8
                sc3n[0] += 1
                return scr3[i], R_scr3[i]

            for seq in range(NSEQ):
                for tt in range(NTT):
                    key = (l, seq, tt)
                    DMA("sp", hT[:].rearrange("p k t -> p (k t)"), hT_scr[seq, tt], [R_hs[key]], [R_hT])
                    DMA("sp", yaT, ya_scr[seq, tt].rearrange("(k p) t -> p k t", p=128), [R_ya[key]], [R_yaT])
                    DMA("sp", ybT3, yb_scr[seq, tt].rearrange("(k p) t -> p k t", p=128), [R_yb[key]], [R_ybT3])
                    for ec in range(8):
                        es = slice(ec * 128, (ec + 1) * 128)
                        pa, _, rpa = pA.next()
                        for k in range(4):
                            MM(pa, wbr[:, 0, k, es], yaT[:, k, :], k == 0, k == 3, [R_W3, R_yaT], [rpa])
                        pb, _, rpb = pA.next()
                        for k in range(4):
                            MM(pb, wbr[:, 1, k, es], ybT3[:, k, :], k == 0, k == 3, [R_W3, R_ybT3], [rpb])
                        pc, _, rpc = pA.next()
                        for k in range(8):
                            MM(pc, Wm[:, k, es], hT[:, k, :], k == 0, k == 7, [R_W3, R_hT], [rpc])
                        pd, _, rpd = pA.next()
                        for k in range(8):
                            MM(pd, Wm[:, k, 1024 + ec * 128:1024 + (ec + 1) * 128], hT[:, k, :], k == 0, k == 7,
                               [R_W3, R_hT], [rpd])
                        ta, R_ta = nscr3()
                        ACT(ta, pc, AF.Tanh, [rpc], [R_ta], scale=0.5)
                        tb_, R_tb = nscr3()
                        ACT(tb_, pd, AF.Tanh, [rpd], [R_tb], scale=0.5)
                        STT(ta, ta, 1.0, pa, ALU.add, ALU.mult, [R_ta, rpa], [R_ta])
                        STT(tb_, tb_, 1.0, pb, ALU.add, ALU.mult, [R_tb, rpb], [R_tb])
                        TT("pool", mg[:, ec, :], ta, tb_, ALU.add, [R_ta, R_tb], [R_mg])
                    for blk in range(4):
                        r0 = tt * 512 + blk * 128
                        xr, R_x = xres[blk % 2], R_xr[blk % 2]
                        DMA("sp", xr, src_t[seq, r0:r0 + 128, :], [R_src], [R_x])
                        for ch in range(2):
                            po, _, rpo = pS.next()
                            for ec in range(8):
                                MM(po, mg[:, ec, blk * 128:(blk + 1) * 128], wo[:, ec, ch * 512:(ch + 1) * 512], ec == 0, ec == 7,
                                   [R_mg, R_W3], [rpo])
                            STT(xr[:, ch * 512:(ch + 1) * 512], po, 0.5, xr[:, ch * 512:(ch + 1) * 512], ALU.mult, ALU.add,
                                [rpo, R_x], [R_x])
                        R_dst = [R_xmid] if l < DEPTH - 1 else []
                        tok = DMA("sp", dst_t[seq, r0:r0 + 128, :], xr, [R_x], R_dst)
                        if l == DEPTH - 1:
                            out_toks.append(tok)
            S_.barrier()
        S_.emit()
    global LAST_SCHED
    LAST_SCHED = S_
    return nc


_CACHE = {}
LAST_SCHED = None


def kernel(**inputs):
    NC = 8
    x = np.ascontiguousarray(inputs["x"], dtype=np.float32)
    B, S, _ = x.shape
    NSEQ = B // NC
    DEPTH = inputs["w_in"].shape[0]
    key = (S, NSEQ, DEPTH)
    if key not in _CACHE:
        _CACHE[key] = build(S, NSEQ, DEPTH)
    nc = _CACHE[key]
    consts = host_consts()
    shared = {k: np.ascontiguousarray(v, dtype=np.float32) for k, v in inputs.items() if k != "x"}
    shared.update(consts)
    in_maps = []
    for c in range(NC):
        m = dict(shared)
        m["x"] = np.ascontiguousarray(x[c * NSEQ:(c + 1) * NSEQ])
        in_maps.append(m)
    res = run_bass_kernel_spmd(nc, in_maps, core_ids=list(range(NC)))
    return np.concatenate([np.asarray(r["out"], dtype=np.float32) for r in res.results], axis=0)
```

```python
import contextlib
import math
import numpy as np
import concourse.bass as bass
import concourse.mybir as mybir
from concourse.bass_utils import run_bass_kernel_spmd

F32 = mybir.dt.float32
BF16 = mybir.dt.bfloat16
ALU = mybir.AluOpType
AF = mybir.ActivationFunctionType

D = 1024
NIN = 5320
NEG = -30000.0
EPS = 1e-6
import os
INTERLEAVE = int(os.environ.get('K_IL', '1'))
POOL_MASK = int(os.environ.get('K_PM', '1'))
POOL_DG = int(os.environ.get('K_PD', '1'))
PE_CAUSAL = int(os.environ.get('K_PC', '1'))
K_E1 = int(os.environ.get('K_E1', '1'))
K_STOP = int(os.environ.get('K_STOP', '0'))
MASK_MOD = int(os.environ.get('K_MM', '1000000'))
BCH = int(os.environ.get('K_BCH', '1'))
NIT = int(os.environ.get('K_NIT', '16'))

C_Q, C_K, C_V, C_GA, C_QI, C_KI, C_WI, C_U, C_VB, C_GB, C_MA, C_MB = (
    0, 512, 576, 640, 1152, 1664, 1728, 1736, 2248, 2760, 3272, 4296)


class Res:
    __slots__ = ("name", "w", "r")

    def __init__(self, name):
        self.name = name
        self.w = None
        self.r = []


class Sched:
    ENG = ("pe", "act", "dve", "pool", "sp")
    NDS = 24

    def __init__(self, nc, stack):
        self.nc = nc
        self.sems = {}
        for e in self.ENG:
            self.sems[e] = stack.enter_context(nc.semaphore("s_" + e))
        for i in range(self.NDS):
            self.sems[("d", i)] = stack.enter_context(nc.semaphore("d%d" % i))
        self.dval = [0] * self.NDS
        self.dnext = {"sp": 0, "pool": 0, "act": 0}
        self.drange = {"sp": (0, 16), "act": (0, 16), "pool": (16, self.NDS)}
        self.cnt = {e: 0 for e in self.ENG}
        self.ops = {e: [] for e in self.ENG}
        self.waited = {e: {} for e in self.ENG}

    def _deps(self, eng, reads, writes):
        deps = {}

        def add(tok, kind):
            if tok is None:
                return
            k, v = tok
            if k == eng and (eng == "pe" or kind != "raw"):
                return
            if deps.get(k, 0) < v:
                deps[k] = v
        for r in reads:
            add(r.w, "raw")
        for w in writes:
            add(w.w, "waw")
            for t in w.r:
                add(t, "war")
        need = []
        wd = self.waited[eng]
        for k, v in deps.items():
            if wd.get(k, 0) < v:
                wd[k] = v
                need.append((k, v))
        return need

    def _mark(self, tok, reads, writes):
        for r in reads:
            r.r.append(tok)
            if len(r.r) > 64:
                best = {}
                for k, v in r.r:
                    if best.get(k, 0) < v:
                        best[k] = v
                r.r = list(best.items())
        for w in writes:
            w.w = tok
            w.r = []

    def op(self, eng, fn, reads=(), writes=()):
        need = self._deps(eng, reads, writes)
        self.cnt[eng] += 1
        tok = (eng, self.cnt[eng])
        self.ops[eng].append((need, fn, (eng, 1)))
        self._mark(tok, reads, writes)
        return tok

    def dma(self, eng, fn, reads=(), writes=()):
        lo_, hi_ = self.drange[eng]
        j = lo_ + self.dnext[eng] % (hi_ - lo_)
        self.dnext[eng] += 1
        need = self._deps(eng, reads, writes)
        if self.dval[j] > 0:
            wd = self.waited[eng]
            if wd.get(("d", j), 0) < self.dval[j]:
                wd[("d", j)] = self.dval[j]
                need.append((("d", j), self.dval[j]))
        self.dval[j] += 16
        tok = (("d", j), self.dval[j])
        self.ops[eng].append((need, fn, (("d", j), 16)))
        self._mark(tok, reads, writes)
        return tok

    def barrier(self):
        for e in self.ENG:
            need = []
            wd = self.waited[e]
            for k in self.ENG:
                if k != e and self.cnt[k] > wd.get(k, 0):
                    wd[k] = self.cnt[k]
                    need.append((k, self.cnt[k]))
            for j in range(self.NDS):
                if self.dval[j] > wd.get(("d", j), 0):
                    wd[("d", j)] = self.dval[j]
                    need.append((("d", j), self.dval[j]))
            self.ops[e].append((need, None, None))

    def emit(self):
        nc = self.nc
        sems = self.sems
        with nc.Block() as block:
            def run(e_name):
                def body(engine):
                    for need, fn, inc in self.ops[e_name]:
                        for k, v in need:
                            engine.wait_ge(sems[k], v)
                        if fn is not None:
                            fn(engine).then_inc(sems[inc[0]], inc[1])
                return body
            block.tensor(run("pe"))
            block.scalar(run("act"))
            block.vector(run("dve"))
            block.gpsimd(run("pool"))
            block.sync(run("sp"))


class Arena:
    def __init__(self, t16):
        self.t16 = t16
        self.t32 = t16.bitcast(F32)
        self.off = 0
        self.cap = t16.shape[1] * 2
        self.peak = 0

    def reset(self):
        self.off = 0

    def alloc_dual(self, n32):
        self.off = (self.off + 63) // 64 * 64
        o = self.off
        self.off += n32 * 4
        self.peak = max(self.peak, self.off)
        assert self.off <= self.cap, ("arena overflow", self.off, self.cap)
        return self.t32[:, o // 4:o // 4 + n32], self.t16[:, o // 2:o // 2 + 2 * n32]

    def alloc(self, free_shape, dt, parts=128):
        es = 4 if dt == F32 else 2
        n = int(np.prod(free_shape))
        self.off = (self.off + 63) // 64 * 64
        o = self.off
        self.off += n * es
        self.peak = max(self.peak, self.off)
        assert self.off <= self.cap, ("arena overflow", self.off, self.cap)
        base = self.t32 if dt == F32 else self.t16
        ap = base[0:parts, o // es:o // es + n]
        if len(free_shape) == 2:
            ap = ap.rearrange("p (a b) -> p a b", a=free_shape[0])
        elif len(free_shape) == 3:
            ap = ap.rearrange("p (a b c) -> p a b c", a=free_shape[0], b=free_shape[1])
        return ap


def _bucket_table():
    oh = np.zeros((33, 384), np.float32)
    for m in range(384):
        rel = m - 127
        if rel < 0:
            oh[32, m] = 1.0
            continue
        if rel < 16:
            b = rel
        else:
            nf = np.float32(max(rel, 1))
            v = np.log(nf / np.float32(16.0)) / np.float32(math.log(128 / 16)) * np.float32(16.0)
            b = min(16 + int(np.float32(v).astype(np.int32)), 31)
        oh[b, m] = 1.0
    return oh


def host_consts():
    t = np.arange(128)
    return {
        "c_ident": np.eye(128, dtype=np.float32),
        "c_tril": (t[:, None] >= t[None, :]).astype(np.float32),
        "c_cneg": np.where(t[None, :] <= t[:, None], 0.0, -1e30).astype(np.float32),
        "c_oh": _bucket_table(),
    }


def build(S, NSEQ, DEPTH, dbg=None):
    nc = bass.Bass("TRN2", target_bir_lowering=False)
    NTT = S // 512
    KTOP = min(256, S // 4)
    dt_in = lambda name, shape: nc.dram_tensor(name, shape, F32, kind="ExternalInput")
    x_in = dt_in("x", [NSEQ, S, D])
    norm_g = dt_in("norm_g", [DEPTH, D])
    w_in = dt_in("w_in", [DEPTH, D, NIN])
    q_norm_g = dt_in("q_norm_g", [DEPTH, 64])
    k_norm_g = dt_in("k_norm_g", [DEPTH, 64])
    rel_bias = dt_in("rel_bias", [32, 8])
    sgu_ln_g = dt_in("sgu_ln_g", [DEPTH, 512])
    sgu_ln_b = dt_in("sgu_ln_b", [DEPTH, 512])
    w_spatial = dt_in("w_spatial", [DEPTH, 4, 128, 128])
    b_spatial = dt_in("b_spatial", [DEPTH, 4, 128])
    w_branch = dt_in("w_branch", [DEPTH, 2, 512, D])
    w_out = dt_in("w_out", [DEPTH, D, D])
    c_ident = dt_in("c_ident", [128, 128])
    c_tril = dt_in("c_tril", [128, 128])
    c_cneg = dt_in("c_cneg", [128, 128])
    c_oh = dt_in("c_oh", [33, 384])
    out = nc.dram_tensor("out", [NSEQ, S, D], F32, kind="ExternalOutput")
    xmid = nc.dram_tensor("xmid", [NSEQ, S, D], F32, kind="Internal")
    hT_scr = nc.dram_tensor("hT_scr", [NSEQ, NTT, 128, 8 * 512], BF16, kind="Internal")
    ya_scr = nc.dram_tensor("ya_scr", [NSEQ, NTT, 512, 512], BF16, kind="Internal")
    yb_scr = nc.dram_tensor("yb_scr", [NSEQ, NTT, 512, 512], BF16, kind="Internal")
    bias_scr = nc.dram_tensor("bias_scr", [128, 8 * 384], BF16, kind="Internal")
    R_xmid, R_hT_scr, R_ya_scr, R_yb_scr, R_bias_scr = (Res(n) for n in ("xmid", "hTs", "yas", "ybs", "bs"))
    R_ya = {}
    R_yb = {}
    R_hs = {}

    with contextlib.ExitStack() as st:
        S_ = Sched(nc, st)
        sb = lambda name, shape, dt: st.enter_context(nc.sbuf_tensor(name, shape, dt))
        def MM(out_, lhsT, rhs, start, stop, R, W):
            S_.op("pe", lambda e: e.matmul(out_, lhsT=lhsT, rhs=rhs, start=start, stop=stop), R, W)

        def TR(out_, in_, R, W):
            S_.op("pe", lambda e: e.transpose(out=out_, in_=in_, identity=idb[:]), list(R) + [R_const], W)

        def ACT(out_, in_, func, R, W, bias=None, scale=None, accum=None):
            kw = {}
            if bias is not None:
                kw["bias"] = bias
            if scale is not None:
                kw["scale"] = scale
            if accum is not None:
                kw["accum_out"] = accum
            S_.op("act", lambda e: e.activation(out=out_, in_=in_, func=func, **kw), R, W)

        def TS(eng, out_, in0, s1, s2, op0, op1, R, W, accum=None):
            kw = {}
            if accum is not None:
                kw["accum_out"] = accum
            if op1 is None:
                S_.op(eng, lambda e: e.tensor_scalar(out=out_, in0=in0, scalar1=s1, scalar2=None, op0=op0, **kw), R, W)
            else:
                S_.op(eng, lambda e: e.tensor_scalar(out=out_, in0=in0, scalar1=s1, scalar2=s2, op0=op0, op1=op1, **kw), R, W)

        def TT(eng, out_, in0, in1, op, R, W):
            S_.op(eng, lambda e: e.tensor_tensor(out=out_, in0=in0, in1=in1, op=op), R, W)

        def STT(out_, in0, scalar, in1, op0, op1, R, W):
            S_.op("dve", lambda e: e.scalar_tensor_tensor(out=out_, in0=in0, scalar=scalar, in1=in1, op0=op0, op1=op1), R, W)

        def CP(eng, out_, in_, R, W):
            if eng == "act":
                S_.op("act", lambda e: e.copy(out=out_, in_=in_), R, W)
            else:
                S_.op(eng, lambda e: e.tensor_copy(out=out_, in_=in_), R, W)

        def MS(eng, ap, val, W):
            S_.op(eng, lambda e: e.memset(ap, val), (), W)

        def DMA(q, out_, in_, R, W):
            return S_.dma(q, lambda e: e.dma_start(out=out_, in_=in_), R, W)

        R_const = Res("const")
        idb = sb("idb", [128, 128], BF16)
        tril = sb("tril", [128, 128], F32)
        idf = sb("idf", [128, 128], F32)
        cneg = sb("cneg", [128, 128], F32)
        ones_bf = sb("ones_bf", [128, 128], BF16)
        o64 = sb("o64", [128, 128], BF16)
        smallc = sb("smallc", [128, 8], F32)
        biasT = sb("biasT", [128, 8, 2, 128], BF16)
        cfar = sb("cfar", [128, 8], F32)
        gbc = sb("gbc", [128, D], F32)
        lngb = sb("lngb", [128, 512], F32)
        lnbb = sb("lnbb", [128, 512], F32)
        gqk = sb("gqk", [64, 4], F32)
        wT_sp = sb("wT_sp", [128, 4, 128], BF16)
        bsp = sb("bsp", [1, 4, 128], BF16)
        hT = sb("hT", [128, 8, 512], BF16)
        R_hT = Res("hT")
        R_lay = Res("layerconst")
        psum = [st.enter_context(nc.psum_tensor("ps%d" % i, [128, 512], F32)) for i in range(8)]
        psum16 = [p.bitcast(BF16) for p in psum]
        R_ps = [Res("ps%d" % i) for i in range(8)]

        class PPool:
            def __init__(self, idxs):
                self.idxs = idxs
                self.n = 0

            def next(self):
                i = self.idxs[self.n % len(self.idxs)]
                self.n += 1
                return psum[i][:, :], psum16[i][:, :], R_ps[i]

        ARENA_BYTES = int(os.environ.get("K_AR", "176")) * 1024
        arena_t = sb("arena", [128, ARENA_BYTES // 2], BF16)
        AR = Arena(arena_t)

        DMA("pool", idb[:], c_ident[:, :], (), [R_const])
        DMA("sp", tril[:], c_tril[:, :], (), [R_const])
        DMA("sp", idf[:], c_ident[:, :], (), [R_const])
        DMA("sp", cneg[:], c_cneg[:, :], (), [R_const])
        MS("dve", ones_bf[:], 1.0, [R_const])
        MS("dve", o64[:], 1.0 / 64, [R_const])
        MS("dve", smallc[:, 0:1], -0.5, [R_const])
        MS("dve", smallc[:, 1:2], EPS, [R_const])
        MS("dve", smallc[:, 2:3], 4 * EPS, [R_const])
        MS("dve", smallc[:, 3:4], 1.0, [R_const])
        one_t = smallc[:, 3:4]
        mhalf = smallc[:, 0:1]
        eps_t = smallc[:, 1:2]
        eps4_t = smallc[:, 2:3]

        AR.reset()
        oh_f = AR.alloc([384], F32)
        rb_ext = AR.alloc([8], F32)
        ones33 = AR.alloc([128], F32)
        lh = AR.alloc([8, 128], F32)
        Bsb = AR.alloc([8, 384], BF16)
        R_su = Res("setup")
        DMA("sp", oh_f[0:33, :], c_oh[:, :], (), [R_su])
        MS("dve", rb_ext[32:33, :], NEG, [R_su])
        DMA("sp", rb_ext[0:32, :], rel_bias[:, :], (), [R_su])
        MS("dve", ones33[0:33, :], 1.0, [R_su])
        for h in range(8):
            TS("dve", lh[0:33, h, :], ones33[0:33, :], rb_ext[0:33, h:h + 1], None, ALU.mult, None, [R_su], [R_su])
        for h in range(8):
            pB, _, rB = psum[h % 2], None, R_ps[h % 2]
            MM(pB[:, 0:384], lh[0:33, h, :], oh_f[0:33, :], True, True, [R_su], [rB])
            CP("dve", cfar[:, h:h + 1], pB[:, 300:301], [rB], [R_const])
            TS("dve", Bsb[:, h, :], pB[:, 0:384], cfar[:, h:h + 1], None, ALU.subtract, None, [rB, R_const], [R_su])
        DMA("sp", bias_scr[:, :], Bsb[:].rearrange("p h m -> p (h m)"), [R_su], [R_bias_scr])
        for h in range(8):
            for dl in range(2):
                src = bass.AP(bias_scr, h * 384 + 127 + 128 * dl, [[8 * 384 - 1, 128], [1, 128]])
                DMA("sp", biasT[:, h, dl, :], src, [R_bias_scr], [R_const])
        S_.barrier()

        out_toks = []
        for l in range(DEPTH):
            src_t = x_in if l == 0 else xmid
            dst_t = out if l == DEPTH - 1 else xmid
            R_src = Res("src") if l == 0 else R_xmid
            assert DEPTH <= 2
            AR.reset()
            DMA("sp", gbc[:], bass.AP(norm_g, l * D, [[0, 128], [1, D]]), (), [R_lay])
            DMA("sp", lngb[:], bass.AP(sgu_ln_g, l * 512, [[0, 128], [1, 512]]), (), [R_lay])
            DMA("sp", lnbb[:], bass.AP(sgu_ln_b, l * 512, [[0, 128], [1, 512]]), (), [R_lay])
            DMA("sp", gqk[:, 0:1], bass.AP(q_norm_g, l * 64, [[1, 64], [1, 1]]), (), [R_lay])
            DMA("sp", gqk[:, 1:2], bass.AP(k_norm_g, l * 64, [[1, 64], [1, 1]]), (), [R_lay])
            TS("dve", gqk[:, 2:3], gqk[:, 1:2], 0.125, None, ALU.mult, None, [R_lay], [R_lay])
            DMA("pool", bsp[:].rearrange("o g t -> o (g t)"), bass.AP(b_spatial, l * 512, [[0, 1], [1, 512]]), (), [R_lay])
            wsp_f = AR.alloc([4, 128], F32)
            wsp_m = AR.alloc([4, 128], BF16)
            R_w = Res("wsp")
            DMA("sp", wsp_f, w_spatial[l].rearrange("g t s -> t g s"), (), [R_w])
            for g in range(4):
                TT("dve", wsp_m[:, g, :], wsp_f[:, g, :], tril[:], ALU.mult, [R_w, R_const], [R_w])
            pT, pT16, rT = psum[0][:, :], psum16[0][:, :], R_ps[0]
            for g in range(4):
                TR(pT16[:, g * 128:(g + 1) * 128], wsp_m[:, g, :], [R_w], [rT])
            CP("dve", wT_sp[:].rearrange("p g t -> p (g t)"), pT16[:, 0:512], [rT], [R_lay])
            S_.barrier()

            AR.reset()
            W1 = AR.alloc([8, 1736], BF16)
            R_W1 = Res("W1")
            for k in range(8):
                DMA("pool", W1[:, k, :], w_in[l, k * 128:(k + 1) * 128, 0:1736], (), [R_W1])
            hT2 = AR.alloc([8, 512], BF16)
            hTs = [hT[:, :, :], hT2]
            R_hTs = [R_hT, Res("hT2")]
            kT = [AR.alloc([S], BF16) for _ in range(2)]
            kidxT = [AR.alloc([S], BF16) for _ in range(2)]
            vaug = [AR.alloc([S // 128, 66], BF16) for _ in range(2)]
            R_kv = {}
            xt = [AR.alloc([D], F32) for _ in range(2)]
            R_xt = [Res("xt0"), Res("xt1")]
            hb = [AR.alloc([D], BF16) for _ in range(2)]
            R_hb = [Res("hb0"), Res("hb1")]
            st_ = AR.alloc([16], F32)
            R_st = Res("st")
            qidxT = AR.alloc([8, 512], BF16)
            R_qi = Res("qidxT")
            widx = AR.alloc([4, 8], F32)
            R_wi = Res("widx")
            Dg = AR.alloc([4, 8, 128], BF16)
            R_Dg = [Res("Dg%d" % i) for i in range(4)]
            Rb = [AR.alloc([512], BF16) for _ in range(4)]
            R_Rb = [Res("Rb%d" % i) for i in range(4)]
            Isb = [AR.alloc([S], F32) for _ in range(3)]
            R_I = [Res("I0"), Res("I1"), Res("I2")]
            junkI = AR.alloc([S], BF16)
            R_junk = Res("junkI")
            bis = [AR.alloc([8], F32) for _ in range(4)]
            R_bis = [Res("bis%d" % i) for i in range(4)]
            maskT = [AR.alloc([max(4, S // 128 - 4), 512], BF16), AR.alloc([S // 128, 512], BF16)]
            R_mT = [Res("maskT0"), Res("maskT1")]
            cnegb = AR.alloc([128], BF16)
            qT = [AR.alloc([512], BF16) for _ in range(2)]
            R_qT = [Res("qT0"), Res("qT1")]
            sga = [AR.alloc([512], BF16) for _ in range(3)]
            R_sga = [Res("sga0"), Res("sga1"), Res("sga2")]
            scr = [AR.alloc([512], F32) for _ in range(6)]
            R_scr = [Res("scr%d" % i) for i in range(6)]
            scrn = [0]
            Eb = [AR.alloc([512], BF16) for _ in range(4)]
            R_E = [Res("E%d" % i) for i in range(4)]
            rdb = [AR.alloc([512], BF16) for _ in range(2)]
            R_rd = [Res("rd0"), Res("rd1")]
            yah = [AR.alloc([512], BF16) for _ in range(2)]
            R_yah = [Res("yah0"), Res("yah1")]
            sqb = [AR.alloc([512], BF16) for _ in range(2)]
            R_sqb = [Res("sqb0"), Res("sqb1")]
            sqn = [0]
            pA = PPool([0, 1, 2])
            pS = PPool([3, 4])
            pI = PPool([5])
            pO = PPool([6, 7])
            mcnt = [0]

            def next_scr():
                i = scrn[0] % 6
                scrn[0] += 1
                return scr[i], R_scr[i]

            for i in range(2):
                MS("dve", vaug[i][:, :, 64:65], 1.0, [R_const])
            CP("dve", cnegb, cneg[:], [R_const], [R_const])

            def proj_fm(hTb, R_hTb, Wt, R_Wt, c0, M):
                p, _, rp = pA.next()
                for k in range(8):
                    MM(p[0:M, :], Wt[:, k, c0:c0 + M], hTb[:, k, :], k == 0, k == 7, [R_Wt, R_hTb], [rp])
                return p, rp

            def rms64(p, rp, gcol, out_ap, R_out):
                sq, R_sq = next_scr()
                s1, R_s1 = next_scr()
                sb_, R_sb_ = sqb[sqn[0] % 2], R_sqb[sqn[0] % 2]
                sqn[0] += 1
                ACT(sb_[0:64, :], p[0:64, :], AF.Square, [rp], [R_sb_])
                pm, _, rpm = pA.next()
                MM(pm[0:64, :], o64[0:64, 0:64], sb_[0:64, :], True, True, [R_sb_, R_const], [rpm])
                ACT(sq[0:64, :], pm[0:64, :], AF.Ln, [rpm, R_const], [R_sq], bias=eps_t[0:64, :])
                ACT(s1[0:64, :], sq[0:64, :], AF.Exp, [R_sq], [R_s1], scale=-0.5)
                STT(out_ap, p[0:64, :], gcol, s1[0:64, :], ALU.mult, ALU.mult, [rp, R_s1, R_lay], [R_out])

            def X_chunks(seq, tt, par):
                ch = []
                T0 = tt * 512
                hTb, R_hTb = hTs[par], R_hTs[par]
                kTs, kis, vas = kT[seq % 2], kidxT[seq % 2], vaug[seq % 2]
                rkv = Res("kv")
                R_kv[(seq, tt)] = rkv
                mT, R_mTb = maskT[par], R_mT[par]

                def c_norm(blk):
                    r0 = T0 + blk * 128
                    xb, R_xb = xt[blk % 2], R_xt[blk % 2]
                    hbl, R_hbl = hb[blk % 2], R_hb[blk % 2]
                    DMA("sp", xb, src_t[seq, r0:r0 + 128, :], [R_src], [R_xb])
                    c = blk * 3
                    ACT(hbl, xb, AF.Square, [R_xb], [R_hbl, R_st], accum=st_[:, c:c + 1])
                    TS("dve", st_[:, c + 1:c + 2], st_[:, c:c + 1], 1.0 / D, EPS, ALU.mult, ALU.add, [R_st], [R_st])
                    TT("pool", st_[:, c + 2:c + 3], st_[:, c + 1:c + 2], mhalf, ALU.pow, [R_st, R_const], [R_st])
                    STT(hbl, xb, st_[:, c + 2:c + 3], gbc[:], ALU.mult, ALU.mult, [R_xb, R_st, R_lay], [R_hbl])
                    _, p16, rp = pA.next()
                    for k in range(8):
                        TR(p16[:, k * 128:(k + 1) * 128], hbl[:, k * 128:(k + 1) * 128], [R_hbl], [rp])
                    CP("act", hTb[:, :, blk * 128:(blk + 1) * 128],
                       p16[:, 0:1024].rearrange("p (k t) -> p k t", k=8), [rp], [R_hTb])
                for blk in range(4):
                    ch.append(lambda blk=blk: c_norm(blk))

                def c_store():
                    R_hs[(l, seq, tt)] = Res("hs")
                    DMA("sp", hT_scr[seq, tt], hTb.rearrange("p k t -> p (k t)"), [R_hTb], [R_hs[(l, seq, tt)]])
                ch.append(c_store)

                def c_k():
                    p, rp = proj_fm(hTb, R_hTb, W1, R_W1, C_K, 64)
                    rms64(p, rp, gqk[:, 2:3], kTs[0:64, T0:T0 + 512], rkv)
                    p, rp = proj_fm(hTb, R_hTb, W1, R_W1, C_KI, 64)
                    CP("dve" if K_E1 else "act", kis[0:64, T0:T0 + 512], p[0:64, :], [rp], [rkv])
                ch.append(c_k)

                def c_v(blk):
                    p, _, rp = pA.next()
                    for k in range(8):
                        MM(p[:, 0:64], hTb[:, k, blk * 128:(blk + 1) * 128], W1[:, k, C_V:C_V + 64], k == 0, k == 7,
                           [R_W1, R_hTb], [rp])
                    for k in range(8):
                        MM(p[:, 64:72], hTb[:, k, blk * 128:(blk + 1) * 128], W1[:, k, C_WI:C_WI + 8], k == 0, k == 7,
                           [R_W1, R_hTb], [rp])
                    CP("dve" if K_E1 else "act", vas[:, tt * 4 + blk, 0:64], p[:, 0:64], [rp], [rkv])
                    TS("dve", widx[:, blk, :], p[:, 64:72], (8 ** -0.5) * (64 ** -0.5), None, ALU.mult, None, [rp], [R_wi])
                    for h in range(8):
                        if POOL_DG:
                            TS("pool", Dg[:, blk, h, :], idb[:], widx[:, blk, h:h + 1], 1.0, ALU.mult, ALU.mult,
                               [R_const, R_wi], [R_Dg[blk]])
                        else:
                            TS("dve", Dg[:, blk, h, :], idb[:], widx[:, blk, h:h + 1], None, ALU.mult, None,
                               [R_const, R_wi], [R_Dg[blk]])
                for blk in range(4):
                    ch.append(lambda blk=blk: c_v(blk))

                def c_qi(h):
                    p, rp = proj_fm(hTb, R_hTb, W1, R_W1, C_QI + 64 * h, 64)
                    CP("act", qidxT[0:64, h, :], p[0:64, :], [rp], [R_qi])
                for h in range(8):
                    ch.append(lambda h=h: c_qi(h))

                def c_idx(blk, j):
                    tb = tt * 4 + blk
                    Ib, R_Ib = Isb[tb % 3], R_I[tb % 3]
                    wdt = 512 if j < tt else (blk + 1) * 128
                    s0 = j * 512
                    pi, _, rpi = pI.next()
                    pend = None
                    for h in range(9):
                        if h < 8:
                            psc, _, rps = pS.next()
                            MM(psc[:, 0:wdt], qidxT[0:64, h, blk * 128:(blk + 1) * 128], kis[0:64, s0:s0 + wdt],
                               True, True, [R_qi] + [R_kv[(seq, jj)] for jj in ([j])], [rps])
                            rb_, R_rb_ = Rb[h % 4], R_Rb[h % 4]
                            ACT(rb_[:, 0:wdt], psc[:, 0:wdt], AF.Relu, [rps], [R_rb_])
                        if pend is not None:
                            hh, rbp, R_rbp = pend
                            last = (hh == 7) and (j < tt or not PE_CAUSAL)
                            MM(pi[:, 0:wdt], Dg[:, blk, hh, :], rbp[:, 0:wdt], hh == 0, last, [R_Dg[blk], R_rbp], [rpi])
                        if h < 8:
                            pend = (h, rb_, R_rb_)
                    if j == tt and PE_CAUSAL:
                        MM(pi[:, blk * 128:wdt], idb[:], cnegb, False, True, [R_const], [rpi])
                    if j == tt and not PE_CAUSAL:
                        if blk > 0:
                            CP("act", Ib[:, s0:s0 + blk * 128], pi[:, 0:blk * 128], [rpi], [R_Ib])
                        TT("dve", Ib[:, s0 + blk * 128:s0 + wdt], pi[:, blk * 128:wdt], cneg[:], ALU.add,
                           [rpi, R_const], [R_Ib])
                    else:
                        CP("act", Ib[:, s0:s0 + wdt], pi[:, 0:wdt], [rpi], [R_Ib])

                def c_bis0(blk):
                    tb = tt * 4 + blk
                    Wd = (tb + 1) * 128
                    Ib, R_Ib = Isb[tb % 3], R_I[tb % 3]
                    bs, R_bs = bis[tb % 4], R_bis[tb % 4]
                    lo, W0, mid, cnt, stp, hi = (bs[:, i:i + 1] for i in range(6))
                    if Wd <= KTOP:
                        MS("dve", lo, -1e29, [R_bs])
                    else:
                        S_.op("dve", lambda e, o=lo, i_=Ib[:, 0:tb * 128]: e.tensor_reduce(
                            out=o, in_=i_, axis=mybir.AxisListType.X, op=ALU.min), [R_Ib], [R_bs])
                        S_.op("dve", lambda e, o=hi, i_=Ib[:, 0:Wd]: e.tensor_reduce(
                            out=o, in_=i_, axis=mybir.AxisListType.X, op=ALU.max), [R_Ib], [R_bs])
                        STT(W0, hi, 1.0, lo, ALU.add, ALU.subtract, [R_bs], [R_bs])

                def c_bis(blk, it0, it1):
                    tb = tt * 4 + blk
                    Wd = (tb + 1) * 128
                    if Wd <= KTOP:
                        return
                    Ib, R_Ib = Isb[tb % 3], R_I[tb % 3]
                    bs, R_bs = bis[tb % 4], R_bis[tb % 4]
                    lo, W0, mid, cnt, stp, hi = (bs[:, i:i + 1] for i in range(6))
                    for it in range(it0, it1):
                        c = 2.0 ** -(it + 1)
                        STT(mid, W0, c, lo, ALU.mult, ALU.add, [R_bs], [R_bs])
                        TS("dve", junkI[:, 0:Wd], Ib[:, 0:Wd], mid, None, ALU.is_ge, ALU.add, [R_Ib, R_bs],
                           [R_junk, R_bs], accum=cnt)
                        STT(stp, cnt, KTOP - 0.5, W0, ALU.is_ge, ALU.mult, [R_bs], [R_bs])
                        STT(lo, stp, c, lo, ALU.mult, ALU.add, [R_bs], [R_bs])

                def c_maskD(blk):
                    tb = tt * 4 + blk
                    Wd = (tb + 1) * 128
                    Ib, R_Ib = Isb[tb % 3], R_I[tb % 3]
                    bs, R_bs = bis[tb % 4], R_bis[tb % 4]
                    lo = bs[:, 0:1]
                    TS("dve", Ib[:, 0:Wd], Ib[:, 0:Wd], lo, None, ALU.is_ge, None, [R_Ib, R_bs], [R_Ib])

                def c_maskP(blk):
                    tb = tt * 4 + blk
                    mkb, R_mkb = Isb[tb % 3], R_I[tb % 3]
                    sb0 = 0
                    while sb0 <= tb:
                        n = min(4, tb + 1 - sb0)
                        p32, _, rp = pA.next()
                        for i in range(n):
                            S_.op("pe", lambda e, o=p32[:, i * 128:(i + 1) * 128], i_=mkb[:, (sb0 + i) * 128:(sb0 + i + 1) * 128]:
                                  e.transpose(out=o, in_=i_, identity=idf[:]), [R_mkb, R_const], [rp])
                        CP("act", mT[:, sb0:sb0 + n, blk * 128:(blk + 1) * 128],
                           p32[:, 0:n * 128].rearrange("p (k t) -> p k t", k=n), [rp], [R_mTb])
                        sb0 += n

                def add_I(blk):
                    for j in range(tt + 1):
                        ch.append(lambda blk=blk, j=j: c_idx(blk, j))

                def add_B(blk):
                    ch.append(lambda blk=blk: c_bis0(blk))
                    for it0 in range(0, NIT, BCH):
                        ch.append(lambda blk=blk, it0=it0: c_bis(blk, it0, min(NIT, it0 + BCH)))
                    ch.append(lambda blk=blk: c_maskD(blk))

                add_I(0); add_I(1); add_I(2)
                add_B(0); add_B(1)
                ch.append(lambda: c_maskP(0))
                add_I(3)
                add_B(2)
                ch.append(lambda: c_maskP(1))
                add_B(3)
                ch.append(lambda: c_maskP(2))
                ch.append(lambda: c_maskP(3))
                return ch

            def Y_stage(seq, tt, par, xch):
                hTb, R_hTb = hTs[par], R_hTs[par]
                kTs, vas = kT[seq % 2], vaug[seq % 2]
                mT, R_mTb = maskT[par], R_mT[par]
                NSB = 4 * (tt + 1)
                nsteps = 8 * (NSB + 2)
                state = {"done": 0, "step": 0}

                def pump():
                    state["step"] += 1
                    target = (len(xch) * state["step"]) // nsteps
                    while state["done"] < min(target, len(xch)):
                        xch[state["done"]]()
                        state["done"] += 1

                def proj_head(h):
                    p, rp = proj_fm(hTb, R_hTb, W1, R_W1, C_Q + 64 * h, 64)
                    pg, rpg = proj_fm(hTb, R_hTb, W1, R_W1, C_GA + 64 * h, 64)
                    rms64(p, rp, gqk[:, 0:1], qT[h % 2][0:64, :], R_qT[h % 2])
                    th, R_th = next_scr()
                    ACT(th[0:64, :], pg[0:64, :], AF.Exp, [rpg], [R_th], scale=-1.0)
                    ACT(th[0:64, :], th[0:64, :], AF.Ln, [R_th, R_const], [R_th], bias=one_t[0:64, :])
                    ACT(th[0:64, :], th[0:64, :], AF.Exp, [R_th], [R_th], scale=-1.0)
                    TT("dve", sga[h % 3][0:64, :], th[0:64, :], pg[0:64, :], ALU.mult, [R_th, rpg], [R_sga[h % 3]])

                def attn_head(h, fin_prev):
                    q_, R_q_ = qT[h % 2], R_qT[h % 2]
                    po, _, rpo = pO.next()
                    pend = []
                    for sbi in range(NSB + 2):
                        if sbi < NSB:
                            j = sbi - 4 * tt
                            c0 = max(j, 0) * 128
                            rk = R_kv[(seq, sbi // 4)]
                            plt, _, rpl = pS.next()
                            near = []
                            for blk in range(4):
                                tb = 4 * tt + blk
                                if sbi == tb:
                                    near.append((blk, 0))
                                elif sbi == tb - 1:
                                    near.append((blk, 1))
                            MM(plt[:, c0:512], kTs[0:64, sbi * 128:(sbi + 1) * 128], q_[0:64, c0:512], True, len(near) == 0,
                               [rk, R_q_], [rpl])
                            for ni, (blk, dl) in enumerate(near):
                                MM(plt[:, blk * 128:(blk + 1) * 128], idb[:], biasT[:, h, dl, :], False,
                                   ni == len(near) - 1, [R_const], [rpl])
                            e_, R_e_ = Eb[sbi % 4], R_E[sbi % 4]
                            ACT(e_[:, c0:512], plt[:, c0:512], AF.Exp, [rpl, R_const], [R_e_], bias=cfar[:, h:h + 1])
                            mcnt[0] += 1
                            TT("pool" if (POOL_MASK and mcnt[0] % MASK_MOD != 0) else "dve", e_[:, c0:512], e_[:, c0:512],
                               mT[:, sbi, c0:512], ALU.mult, [R_e_, R_mTb], [R_e_])
                            pend.append((sbi, c0, e_, R_e_, rk))
                        if sbi >= 2:
                            ps_, pc0, pe_, R_pe_, prk = pend.pop(0)
                            MM(po[0:65, pc0:512], vas[:, ps_, 0:65], pe_[:, pc0:512], ps_ == 0, ps_ == NSB - 1,
                               [prk, R_pe_, R_const], [rpo])
                        if sbi == 2 and fin_prev is not None:
                            fin_prev()
                        pump()
                    def finish(h=h, po=po, rpo=rpo):
                        ln_, R_ln = next_scr()
                        ACT(ln_[64:65, :], po[64:65, :], AF.Ln, [rpo], [R_ln])
                        rd, R_rd_ = rdb[h % 2], R_rd[h % 2]
                        ACT(rd[64:65, :], ln_[64:65, :], AF.Exp, [R_ln], [R_rd_], scale=-1.0)
                        pb, _, rpb = pA.next()
                        MM(pb[0:64, :], ones_bf[64:65, 0:64], rd[64:65, :], True, True, [R_const, R_rd_], [rpb])
                        tmp, R_tmp = next_scr()
                        TT("dve", tmp[0:64, :], sga[h % 3][0:64, :], po[0:64, :], ALU.mult, [R_sga[h % 3], rpo], [R_tmp])
                        y_, R_y_ = yah[h % 2], R_yah[h % 2]
                        TT("dve", y_[0:64, :], tmp[0:64, :], pb[0:64, :], ALU.mult, [R_tmp, rpb], [R_y_])
                        key = (l, seq, tt)
                        if key not in R_ya:
                            R_ya[key] = Res("ya")
                        DMA("sp", ya_scr[seq, tt, h * 64:(h + 1) * 64, :], y_[0:64, :], [R_y_], [R_ya[key]])
                    return finish

                proj_head(0)
                fin = None
                for h in range(8):
                    fin_prev = fin
                    if h + 1 < 8:
                        proj_head(h + 1)
                    fin = attn_head(h, fin_prev)
                fin()
                while state["done"] < len(xch):
                    xch[state["done"]]()
                    state["done"] += 1

            tiles = [(seq, tt) for seq in range(NSEQ) for tt in range(NTT)]
            for c_ in X_chunks(tiles[0][0], tiles[0][1], 0):
                c_()
            for n_, (seq, tt) in enumerate(tiles):
                if K_STOP == 1:
                    R_ya[(l, seq, tt)] = Res("ya")
                    continue
                nxt = X_chunks(tiles[n_ + 1][0], tiles[n_ + 1][1], (n_ + 1) % 2) if n_ + 1 < len(tiles) else []
                if INTERLEAVE:
                    Y_stage(seq, tt, n_ % 2, nxt)
                else:
                    Y_stage(seq, tt, n_ % 2, [])
                    for c_ in nxt:
                        c_()
            S_.barrier()

            AR.reset()
            W2 = AR.alloc([8, 1536], BF16)
            R_W2 = Res("W2")
            for k in range(8):
                DMA("pool", W2[:, k, :], w_in[l, k * 128:(k + 1) * 128, C_U:C_U + 1536], (), [R_W2])
            vln = AR.alloc([4, 512], BF16)
            R_vln = Res("vln")
            gu2 = AR.alloc([4, 512], BF16)
            R_gu = Res("gu2")
            sgb = AR.alloc([4, 512], BF16)
            R_sgb = Res("sgb")
            ybT = AR.alloc([4, 512], BF16)
            R_ybT = Res("ybT")
            scr2 = [AR.alloc([512], F32) for _ in range(8)]
            R_scr2 = [Res("s2_%d" % i) for i in range(8)]
            sc2n = [0]
            bnst = AR.alloc([4, 8], F32)
            R_bn = Res("bn")
            pA = PPool([0, 1, 2, 3, 4, 5])
            pS = PPool([6, 7])

            def nscr2():
                i = sc2n[0] % 8
                sc2n[0] += 1
                return scr2[i], R_scr2[i]

            def gelu2(p, rp, out_ap, R_out):
                a, R_a = nscr2()
                ACT(a, p, AF.Square, [rp], [R_a])
                TS("pool", a, a, 0.044715, 1.0, ALU.mult, ALU.add, [R_a], [R_a])
                b_, R_b = nscr2()
                TT("dve", b_, a, p, ALU.mult, [R_a, rp], [R_b])
                ACT(b_, b_, AF.Tanh, [R_b], [R_b], scale=0.7978845608028654)
                STT(out_ap, b_, 1.0, p, ALU.add, ALU.mult, [R_b, rp], [R_out])

            for seq in range(NSEQ):
                for tt in range(NTT):
                    DMA("sp", hT[:].rearrange("p k t -> p (k t)"), hT_scr[seq, tt], [R_hs[(l, seq, tt)]], [R_hT])
                    for blk in range(4):
                        p, _, rp = pA.next()
                        for k in range(8):
                            MM(p[:, :], hT[:, k, blk * 128:(blk + 1) * 128], W2[:, k, 512:1024], k == 0, k == 7,
                               [R_W2, R_hT], [rp])
                        g2, R_g2 = nscr2()
                        gelu2(p, rp, g2, R_g2)
                        S_.op("dve", lambda e, o=bnst[:, blk, 0:6], i_=g2: e.bn_stats(out=o, in_=i_), [R_g2], [R_bn])
                        S_.op("dve", lambda e, o=bnst[:, blk, 6:8], i_=bnst[:, blk, 0:6]: e.bn_aggr(out=o, in_=i_), [R_bn], [R_bn])
                        TS("dve", bnst[:, blk, 7:8], bnst[:, blk, 7:8], 4 * EPS, None, ALU.add, None, [R_bn], [R_bn])
                        TT("pool", bnst[:, blk, 7:8], bnst[:, blk, 7:8], mhalf, ALU.pow, [R_bn, R_const], [R_bn])
                        TS("dve", g2, g2, bnst[:, blk, 6:7], bnst[:, blk, 7:8], ALU.subtract, ALU.mult, [R_g2, R_bn], [R_g2])
                        TT("pool", g2, g2, lngb[:], ALU.mult, [R_g2, R_lay], [R_g2])
                        TT("pool", vln[:, blk, :], g2, lnbb[:], ALU.add, [R_g2, R_lay], [R_vln])
                    for c in range(4):
                        p, rp = proj_fm(hT, R_hT, W2, R_W2, c * 128, 128)
                        gelu2(p, rp, gu2[:, c, :], R_gu)
                        p, rp = proj_fm(hT, R_hT, W2, R_W2, 1024 + c * 128, 128)
                        th, R_th = nscr2()
                        ACT(th, p, AF.Tanh, [rp], [R_th], scale=0.5)
                        STT(sgb[:, c, :], th, 1.0, p, ALU.add, ALU.mult, [R_th, rp], [R_sgb])
                    for g in range(4):
                        p, _, rp = pS.next()
                        for blk in range(4):
                            MM(p[:, blk * 128:(blk + 1) * 128], vln[:, blk, g * 128:(g + 1) * 128], wT_sp[:, g, :], True, False,
                               [R_vln, R_lay], [rp])
                            MM(p[:, blk * 128:(blk + 1) * 128], ones_bf[0:1, :], bsp[0:1, g, :], False, True,
                               [R_const, R_lay], [rp])
                        t_, R_t = nscr2()
                        STT(t_, gu2[:, g, :], 0.25, p, ALU.mult, ALU.mult, [R_gu, rp], [R_t])
                        TT("pool", ybT[:, g, :], t_, sgb[:, g, :], ALU.mult, [R_t, R_sgb], [R_ybT])
                    key = (l, seq, tt)
                    R_yb[key] = Res("yb")
                    DMA("sp", yb_scr[seq, tt].rearrange("(g p) t -> p g t", p=128), ybT, [R_ybT], [R_yb[key]])
            S_.barrier()

            AR.reset()
            Wm = AR.alloc([8, 2048], BF16)
            wbr = AR.alloc([2, 4, D], BF16)
            wo = AR.alloc([8, D], BF16)
            R_W3 = Res("W3")
            for k in range(8):
                DMA("pool", Wm[:, k, :], w_in[l, k * 128:(k + 1) * 128, C_MA:C_MA + 2048], (), [R_W3])
            for n_ in range(2):
                for k in range(4):
                    DMA("pool", wbr[:, n_, k, :], w_branch[l, n_, k * 128:(k + 1) * 128, :], (), [R_W3])
            for k in range(8):
                DMA("pool", wo[:, k, :], w_out[l, k * 128:(k + 1) * 128, :], (), [R_W3])
            yaT = AR.alloc([4, 512], BF16)
            R_yaT = Res("yaT")
            ybT3 = AR.alloc([4, 512], BF16)
            R_ybT3 = Res("ybT3")
            mg = AR.alloc([8, 512], BF16)
            R_mg = Res("mg")
            scr3 = [AR.alloc([512], F32) for _ in range(8)]
            R_scr3 = [Res("s3_%d" % i) for i in range(8)]
            sc3n = [0]
            xres = [AR.alloc([D], F32) for _ in range(2)]
            R_xr = [Res("xr0"), Res("xr1")]
            pA = PPool([0, 1, 2, 3, 4, 5])
            pS = PPool([6, 7])

            def nscr3():
                i = sc3n[0] % 8
                sc3n[0] += 1
                return scr3[i], R_scr3[i]

            for seq in range(NSEQ):
                for tt in range(NTT):
                    key = (l, seq, tt)
                    DMA("sp", hT[:].rearrange("p k t -> p (k t)"), hT_scr[seq, tt], [R_hs[key]], [R_hT])
                    DMA("sp", yaT, ya_scr[seq, tt].rearrange("(k p) t -> p k t", p=128), [R_ya[key]], [R_yaT])
                    DMA("sp", ybT3, yb_scr[seq, tt].rearrange("(k p) t -> p k t", p=128), [R_yb[key]], [R_ybT3])
                    for ec in range(8):
                        es = slice(ec * 128, (ec + 1) * 128)
                        pa, _, rpa = pA.next()
                        for k in range(4):
                            MM(pa, wbr[:, 0, k, es], yaT[:, k, :], k == 0, k == 3, [R_W3, R_yaT], [rpa])
                        pb, _, rpb = pA.next()
                        for k in range(4):
                            MM(pb, wbr[:, 1, k, es], ybT3[:, k, :], k == 0, k == 3, [R_W3, R_ybT3], [rpb])
                        pc, _, rpc = pA.next()
                        for k in range(8):
                            MM(pc, Wm[:, k, es], hT[:, k, :], k == 0, k == 7, [R_W3, R_hT], [rpc])
                        pd, _, rpd = pA.next()
                        for k in range(8):
                            MM(pd, Wm[:, k, 1024 + ec * 128:1024 + (ec + 1) * 128], hT[:, k, :], k == 0, k == 7,
                               [R_W3, R_hT], [rpd])
                        ta, R_ta = nscr3()
                        ACT(ta, pc, AF.Tanh, [rpc], [R_ta], scale=0.5)
                        tb_, R_tb = nscr3()
                        ACT(tb_, pd, AF.Tanh, [rpd], [R_tb], scale=0.5)
                        STT(ta, ta, 1.0, pa, ALU.add, ALU.mult, [R_ta, rpa], [R_ta])
                        STT(tb_, tb_, 1.0, pb, ALU.add, ALU.mult, [R_tb, rpb], [R_tb])
                        TT("pool", mg[:, ec, :], ta, tb_, ALU.add, [R_ta, R_tb], [R_mg])
                    for blk in range(4):
                        r0 = tt * 512 + blk * 128
                        xr, R_x = xres[blk % 2], R_xr[blk % 2]
                        DMA("sp", xr, src_t[seq, r0:r0 + 128, :], [R_src], [R_x])
                        for ch in range(2):
                            po, _, rpo = pS.next()
                            for ec in range(8):
                                MM(po, mg[:, ec, blk * 128:(blk + 1) * 128], wo[:, ec, ch * 512:(ch + 1) * 512], ec == 0, ec == 7,
                                   [R_mg, R_W3], [rpo])
                            STT(xr[:, ch * 512:(ch + 1) * 512], po, 0.5, xr[:, ch * 512:(ch + 1) * 512], ALU.mult, ALU.add,
                                [rpo, R_x], [R_x])
                        R_dst = [R_xmid] if l < DEPTH - 1 else []
                        tok = DMA("sp", dst_t[seq, r0:r0 + 128, :], xr, [R_x], R_dst)
                        if l == DEPTH - 1:
                            out_toks.append(tok)
            S_.barrier()
        S_.emit()
    global LAST_SCHED
    LAST_SCHED = S_
    print('arena peak', AR.peak)
    return nc


_CACHE = {}
LAST_SCHED = None


def kernel(**inputs):
    NC = 8
    x = np.ascontiguousarray(inputs["x"], dtype=np.float32)
    B, S, _ = x.shape
    NSEQ = B // NC
    DEPTH = inputs["w_in"].shape[0]
    key = (S, NSEQ, DEPTH)
    if key not in _CACHE:
        _CACHE[key] = build(S, NSEQ, DEPTH)
    nc = _CACHE[key]
    consts = host_consts()
    shared = {k: np.ascontiguousarray(v, dtype=np.float32) for k, v in inputs.items() if k != "x"}
    shared.update(consts)
    in_maps = []
    for c in range(NC):
        m = dict(shared)
        m["x"] = np.ascontiguousarray(x[c * NSEQ:(c + 1) * NSEQ])
        in_maps.append(m)
    res = run_bass_kernel_spmd(nc, in_maps, core_ids=list(range(NC)))
    return np.concatenate([np.asarray(r["out"], dtype=np.float32) for r in res.results], axis=0)
```

```python
import contextlib
import math
import numpy as np
import concourse.bass as bass
import concourse.mybir as mybir
from concourse.bass_utils import run_bass_kernel_spmd

F32 = mybir.dt.float32
BF16 = mybir.dt.bfloat16
ALU = mybir.AluOpType
AF = mybir.ActivationFunctionType

D = 1024
NIN = 5320
NEG = -30000.0
EPS = 1e-6
import os
INTERLEAVE = int(os.environ.get('K_IL', '1'))
POOL_MASK = int(os.environ.get('K_PM', '1'))
POOL_DG = int(os.environ.get('K_PD', '1'))
PE_CAUSAL = int(os.environ.get('K_PC', '1'))
K_E1 = int(os.environ.get('K_E1', '1'))
K_STOP = int(os.environ.get('K_STOP', '0'))
MASK_MOD = int(os.environ.get('K_MM', '1000000'))
BCH = int(os.environ.get('K_BCH', '1'))
NIT = int(os.environ.get('K_NIT', '16'))

C_Q, C_K, C_V, C_GA, C_QI, C_KI, C_WI, C_U, C_VB, C_GB, C_MA, C_MB = (
    0, 512, 576, 640, 1152, 1664, 1728, 1736, 2248, 2760, 3272, 4296)


class Res:
    __slots__ = ("name", "w", "r")

    def __init__(self, name):
        self.name = name
        self.w = None
        self.r = []


class Sched:
    ENG = ("pe", "act", "dve", "pool", "sp")
    NDS = 56

    def __init__(self, nc, stack):
        self.nc = nc
        self.sems = {}
        for e in self.ENG:
            self.sems[e] = stack.enter_context(nc.semaphore("s_" + e))
        for i in range(self.NDS):
            self.sems[("d", i)] = stack.enter_context(nc.semaphore("d%d" % i))
        self.dval = [0] * self.NDS
        self.dnext = {"sp": 0, "pool": 0, "act": 0}
        self.drange = {"sp": (0, 16), "act": (0, 16), "pool": (16, self.NDS)}
        self.cnt = {e: 0 for e in self.ENG}
        self.ops = {e: [] for e in self.ENG}
        self.waited = {e: {} for e in self.ENG}

    def _deps(self, eng, reads, writes):
        deps = {}

        def add(tok, kind):
            if tok is None:
                return
            k, v = tok
            if k == eng and (eng == "pe" or kind != "raw"):
                return
            if deps.get(k, 0) < v:
                deps[k] = v
        def addw(wt, kind):
            if isinstance(wt, list):
                for t_ in wt:
                    add(t_, kind)
            else:
                add(wt, kind)
        for r in reads:
            addw(r.w, "raw")
        for w in writes:
            addw(w.w, "waw")
            for t in w.r:
                add(t, "war")
        need = []
        wd = self.waited[eng]
        for k, v in deps.items():
            if wd.get(k, 0) < v:
                wd[k] = v
                need.append((k, v))
        return need

    def _mark(self, tok, reads, writes, append=False):
        for r in reads:
            r.r.append(tok)
            if len(r.r) > 64:
                best = {}
                for k, v in r.r:
                    if best.get(k, 0) < v:
                        best[k] = v
                r.r = list(best.items())
        for w in writes:
            if append and isinstance(w.w, list):
                w.w.append(tok)
            elif append:
                w.w = [tok]
            else:
                w.w = tok
            w.r = []

    def op(self, eng, fn, reads=(), writes=()):
        need = self._deps(eng, reads, writes)
        self.cnt[eng] += 1
        tok = (eng, self.cnt[eng])
        self.ops[eng].append((need, fn, (eng, 1)))
        self._mark(tok, reads, writes)
        return tok

    def dma(self, eng, fn, reads=(), writes=(), append=False):
        lo_, hi_ = self.drange[eng]
        j = lo_ + self.dnext[eng] % (hi_ - lo_)
        self.dnext[eng] += 1
        need = self._deps(eng, reads, writes)
        if self.dval[j] > 0:
            wd = self.waited[eng]
            if wd.get(("d", j), 0) < self.dval[j]:
                wd[("d", j)] = self.dval[j]
                need.append((("d", j), self.dval[j]))
        self.dval[j] += 16
        tok = (("d", j), self.dval[j])
        self.ops[eng].append((need, fn, (("d", j), 16)))
        self._mark(tok, reads, writes, append)
        return tok

    def barrier(self):
        for e in self.ENG:
            need = []
            wd = self.waited[e]
            for k in self.ENG:
                if k != e and self.cnt[k] > wd.get(k, 0):
                    wd[k] = self.cnt[k]
                    need.append((k, self.cnt[k]))
            for j in range(self.NDS):
                if self.dval[j] > wd.get(("d", j), 0):
                    wd[("d", j)] = self.dval[j]
                    need.append((("d", j), self.dval[j]))
            self.ops[e].append((need, None, None))

    def emit(self):
        nc = self.nc
        sems = self.sems
        with nc.Block() as block:
            def run(e_name):
                def body(engine):
                    for need, fn, inc in self.ops[e_name]:
                        for k, v in need:
                            engine.wait_ge(sems[k], v)
                        if fn is not None:
                            fn(engine).then_inc(sems[inc[0]], inc[1])
                return body
            block.tensor(run("pe"))
            block.scalar(run("act"))
            block.vector(run("dve"))
            block.gpsimd(run("pool"))
            block.sync(run("sp"))


class Arena:
    def __init__(self, t16):
        self.t16 = t16
        self.t32 = t16.bitcast(F32)
        self.off = 0
        self.cap = t16.shape[1] * 2
        self.peak = 0

    def reset(self):
        self.off = 0

    def alloc_at(self, off, free_shape, dt):
        save = self.off
        self.off = off
        ap = self.alloc(free_shape, dt)
        self.off = save
        return ap

    def alloc_dual(self, n32):
        self.off = (self.off + 63) // 64 * 64
        o = self.off
        self.off += n32 * 4
        self.peak = max(self.peak, self.off)
        assert self.off <= self.cap, ("arena overflow", self.off, self.cap)
        return self.t32[:, o // 4:o // 4 + n32], self.t16[:, o // 2:o // 2 + 2 * n32]

    def alloc(self, free_shape, dt, parts=128):
        es = 4 if dt == F32 else 2
        n = int(np.prod(free_shape))
        self.off = (self.off + 63) // 64 * 64
        o = self.off
        self.off += n * es
        self.peak = max(self.peak, self.off)
        assert self.off <= self.cap, ("arena overflow", self.off, self.cap)
        base = self.t32 if dt == F32 else self.t16
        ap = base[0:parts, o // es:o // es + n]
        if len(free_shape) == 2:
            ap = ap.rearrange("p (a b) -> p a b", a=free_shape[0])
        elif len(free_shape) == 3:
            ap = ap.rearrange("p (a b c) -> p a b c", a=free_shape[0], b=free_shape[1])
        return ap


def _bucket_table():
    oh = np.zeros((33, 384), np.float32)
    for m in range(384):
        rel = m - 127
        if rel < 0:
            oh[32, m] = 1.0
            continue
        if rel < 16:
            b = rel
        else:
            nf = np.float32(max(rel, 1))
            v = np.log(nf / np.float32(16.0)) / np.float32(math.log(128 / 16)) * np.float32(16.0)
            b = min(16 + int(np.float32(v).astype(np.int32)), 31)
        oh[b, m] = 1.0
    return oh


def host_consts():
    t = np.arange(128)
    return {
        "c_ident": np.eye(128, dtype=np.float32),
        "c_tril": (t[:, None] >= t[None, :]).astype(np.float32),
        "c_cneg": np.where(t[None, :] <= t[:, None], 0.0, -1e30).astype(np.float32),
        "c_oh": _bucket_table(),
    }


def build(S, NSEQ, DEPTH, dbg=None):
    nc = bass.Bass("TRN2", target_bir_lowering=False)
    NTT = S // 512
    KTOP = min(256, S // 4)
    dt_in = lambda name, shape: nc.dram_tensor(name, shape, F32, kind="ExternalInput")
    x_in = dt_in("x", [NSEQ, S, D])
    norm_g = dt_in("norm_g", [DEPTH, D])
    w_in = dt_in("w_in", [DEPTH, D, NIN])
    q_norm_g = dt_in("q_norm_g", [DEPTH, 64])
    k_norm_g = dt_in("k_norm_g", [DEPTH, 64])
    rel_bias = dt_in("rel_bias", [32, 8])
    sgu_ln_g = dt_in("sgu_ln_g", [DEPTH, 512])
    sgu_ln_b = dt_in("sgu_ln_b", [DEPTH, 512])
    w_spatial = dt_in("w_spatial", [DEPTH, 4, 128, 128])
    b_spatial = dt_in("b_spatial", [DEPTH, 4, 128])
    w_branch = dt_in("w_branch", [DEPTH, 2, 512, D])
    w_out = dt_in("w_out", [DEPTH, D, D])
    c_ident = dt_in("c_ident", [128, 128])
    c_tril = dt_in("c_tril", [128, 128])
    c_cneg = dt_in("c_cneg", [128, 128])
    c_oh = dt_in("c_oh", [33, 384])
    out = nc.dram_tensor("out", [NSEQ, S, D], F32, kind="ExternalOutput")
    xmid = nc.dram_tensor("xmid", [NSEQ, S, D], F32, kind="Internal")
    hT_scr = nc.dram_tensor("hT_scr", [NSEQ, NTT, 128, 8 * 512], BF16, kind="Internal")
    ya_scr = nc.dram_tensor("ya_scr", [NSEQ, NTT, 512, 512], BF16, kind="Internal")
    yb_scr = nc.dram_tensor("yb_scr", [NSEQ, NTT, 512, 512], BF16, kind="Internal")
    bias_scr = nc.dram_tensor("bias_scr", [128, 8 * 384], BF16, kind="Internal")
    R_xmid, R_hT_scr, R_ya_scr, R_yb_scr, R_bias_scr = (Res(n) for n in ("xmid", "hTs", "yas", "ybs", "bs"))
    R_ya = {}
    R_yb = {}
    R_hs = {}

    with contextlib.ExitStack() as st:
        S_ = Sched(nc, st)
        sb = lambda name, shape, dt: st.enter_context(nc.sbuf_tensor(name, shape, dt))
        def MM(out_, lhsT, rhs, start, stop, R, W):
            S_.op("pe", lambda e: e.matmul(out_, lhsT=lhsT, rhs=rhs, start=start, stop=stop), R, W)

        def TR(out_, in_, R, W):
            S_.op("pe", lambda e: e.transpose(out=out_, in_=in_, identity=idb[:]), list(R) + [R_const], W)

        def ACT(out_, in_, func, R, W, bias=None, scale=None, accum=None):
            kw = {}
            if bias is not None:
                kw["bias"] = bias
            if scale is not None:
                kw["scale"] = scale
            if accum is not None:
                kw["accum_out"] = accum
            S_.op("act", lambda e: e.activation(out=out_, in_=in_, func=func, **kw), R, W)

        def TS(eng, out_, in0, s1, s2, op0, op1, R, W, accum=None):
            kw = {}
            if accum is not None:
                kw["accum_out"] = accum
            if op1 is None:
                S_.op(eng, lambda e: e.tensor_scalar(out=out_, in0=in0, scalar1=s1, scalar2=None, op0=op0, **kw), R, W)
            else:
                S_.op(eng, lambda e: e.tensor_scalar(out=out_, in0=in0, scalar1=s1, scalar2=s2, op0=op0, op1=op1, **kw), R, W)

        def TT(eng, out_, in0, in1, op, R, W):
            S_.op(eng, lambda e: e.tensor_tensor(out=out_, in0=in0, in1=in1, op=op), R, W)

        def STT(out_, in0, scalar, in1, op0, op1, R, W):
            S_.op("dve", lambda e: e.scalar_tensor_tensor(out=out_, in0=in0, scalar=scalar, in1=in1, op0=op0, op1=op1), R, W)

        def CP(eng, out_, in_, R, W):
            if eng == "act":
                S_.op("act", lambda e: e.copy(out=out_, in_=in_), R, W)
            else:
                S_.op(eng, lambda e: e.tensor_copy(out=out_, in_=in_), R, W)

        def MS(eng, ap, val, W):
            S_.op(eng, lambda e: e.memset(ap, val), (), W)

        def DMA(q, out_, in_, R, W, append=False):
            return S_.dma(q, lambda e: e.dma_start(out=out_, in_=in_), R, W, append)

        R_const = Res("const")
        idb = sb("idb", [128, 128], BF16)
        tril = sb("tril", [128, 128], F32)
        idf = sb("idf", [128, 128], F32)
        cneg = sb("cneg", [128, 128], F32)
        ones_bf = sb("ones_bf", [128, 128], BF16)
        o64 = sb("o64", [128, 128], BF16)
        smallc = sb("smallc", [128, 8], F32)
        biasT = sb("biasT", [128, 8, 2, 128], BF16)
        cfar = sb("cfar", [128, 8], F32)
        gbc = sb("gbc", [128, D], F32)
        lngb = sb("lngb", [128, 512], F32)
        lnbb = sb("lnbb", [128, 512], F32)
        gqk = sb("gqk", [64, 4], F32)
        wT_sp = sb("wT_sp", [128, 4, 128], BF16)
        bsp = sb("bsp", [1, 4, 128], BF16)
        hT = sb("hT", [128, 8, 512], BF16)
        R_hT = Res("hT")
        R_lay = Res("layerconst")
        psum = [st.enter_context(nc.psum_tensor("ps%d" % i, [128, 512], F32)) for i in range(8)]
        psum16 = [p.bitcast(BF16) for p in psum]
        R_ps = [Res("ps%d" % i) for i in range(8)]

        class PPool:
            def __init__(self, idxs):
                self.idxs = idxs
                self.n = 0

            def next(self):
                i = self.idxs[self.n % len(self.idxs)]
                self.n += 1
                return psum[i][:, :], psum16[i][:, :], R_ps[i]

        ARENA_BYTES = int(os.environ.get("K_AR", "176")) * 1024
        arena_t = sb("arena", [128, ARENA_BYTES // 2], BF16)
        AR = Arena(arena_t)

        DMA("pool", idb[:], c_ident[:, :], (), [R_const])
        DMA("sp", tril[:], c_tril[:, :], (), [R_const])
        DMA("sp", idf[:], c_ident[:, :], (), [R_const])
        DMA("sp", cneg[:], c_cneg[:, :], (), [R_const])
        MS("dve", ones_bf[:], 1.0, [R_const])
        MS("dve", o64[:], 1.0 / 64, [R_const])
        MS("dve", smallc[:, 0:1], -0.5, [R_const])
        MS("dve", smallc[:, 1:2], EPS, [R_const])
        MS("dve", smallc[:, 2:3], 4 * EPS, [R_const])
        MS("dve", smallc[:, 3:4], 1.0, [R_const])
        one_t = smallc[:, 3:4]
        mhalf = smallc[:, 0:1]
        eps_t = smallc[:, 1:2]
        eps4_t = smallc[:, 2:3]

        AR.reset()
        oh_f = AR.alloc([384], F32)
        rb_ext = AR.alloc([8], F32)
        ones33 = AR.alloc([128], F32)
        lh = AR.alloc([8, 128], F32)
        Bsb = AR.alloc([8, 384], BF16)
        R_su = Res("setup")
        DMA("sp", oh_f[0:33, :], c_oh[:, :], (), [R_su])
        MS("dve", rb_ext[32:33, :], NEG, [R_su])
        DMA("sp", rb_ext[0:32, :], rel_bias[:, :], (), [R_su])
        MS("dve", ones33[0:33, :], 1.0, [R_su])
        for h in range(8):
            TS("dve", lh[0:33, h, :], ones33[0:33, :], rb_ext[0:33, h:h + 1], None, ALU.mult, None, [R_su], [R_su])
        for h in range(8):
            pB, _, rB = psum[h % 2], None, R_ps[h % 2]
            MM(pB[:, 0:384], lh[0:33, h, :], oh_f[0:33, :], True, True, [R_su], [rB])
            CP("dve", cfar[:, h:h + 1], pB[:, 300:301], [rB], [R_const])
            TS("dve", Bsb[:, h, :], pB[:, 0:384], cfar[:, h:h + 1], None, ALU.subtract, None, [rB, R_const], [R_su])
        DMA("sp", bias_scr[:, :], Bsb[:].rearrange("p h m -> p (h m)"), [R_su], [R_bias_scr])
        for h in range(8):
            for dl in range(2):
                src = bass.AP(bias_scr, h * 384 + 127 + 128 * dl, [[8 * 384 - 1, 128], [1, 128]])
                DMA("sp", biasT[:, h, dl, :], src, [R_bias_scr], [R_const])
        S_.barrier()

        W1_OFF = ARENA_BYTES - 8 * 1736 * 2 - 64
        W1_OFF -= W1_OFF % 64
        W1 = AR.alloc_at(W1_OFF, [8, 1736], BF16)
        R_W1 = Res("W1")
        lay = {}

        def load_W1(l_, first):
            for k in range(8):
                DMA("pool", W1[:, k, :], w_in[l_, k * 128:(k + 1) * 128, 0:1736], (), [R_W1], append=(k > 0))

        def load_W2(l_, W2_, R_W2_, extraW):
            for k in range(8):
                DMA("pool", W2_[:, k, :], w_in[l_, k * 128:(k + 1) * 128, C_U:C_U + 1536], (),
                    [R_W2_] + (list(extraW) if k == 0 else []), append=(k > 0))

        def load_W3(l_, Wm_, wbr_, wo_, R_W3_):
            th = []
            for k in range(8):
                th.append(lambda k=k: DMA("pool", Wm_[:, k, :], w_in[l_, k * 128:(k + 1) * 128, C_MA:C_MA + 2048], (),
                                          [R_W3_], append=(k > 0)))
            for n_ in range(2):
                for k in range(4):
                    th.append(lambda n_=n_, k=k: DMA("pool", wbr_[:, n_, k, :], w_branch[l_, n_, k * 128:(k + 1) * 128, :], (),
                                                     [R_W3_], append=True))
            for k in range(8):
                th.append(lambda k=k: DMA("pool", wo_[:, k, :], w_out[l_, k * 128:(k + 1) * 128, :], (), [R_W3_], append=True))
            return th

        def load_W1_th(l_):
            return [lambda k=k: DMA("pool", W1[:, k, :], w_in[l_, k * 128:(k + 1) * 128, 0:1736], (), [R_W1], append=(k > 0))
                    for k in range(8)]

        load_W1(0, True)
        out_toks = []
        for l in range(DEPTH):
            src_t = x_in if l == 0 else xmid
            dst_t = out if l == DEPTH - 1 else xmid
            R_src = Res("src") if l == 0 else R_xmid
            assert DEPTH <= 2
            AR.reset()
            DMA("sp", gbc[:], bass.AP(norm_g, l * D, [[0, 128], [1, D]]), (), [R_lay])
            DMA("sp", lngb[:], bass.AP(sgu_ln_g, l * 512, [[0, 128], [1, 512]]), (), [R_lay])
            DMA("sp", lnbb[:], bass.AP(sgu_ln_b, l * 512, [[0, 128], [1, 512]]), (), [R_lay])
            DMA("sp", gqk[:, 0:1], bass.AP(q_norm_g, l * 64, [[1, 64], [1, 1]]), (), [R_lay])
            DMA("sp", gqk[:, 1:2], bass.AP(k_norm_g, l * 64, [[1, 64], [1, 1]]), (), [R_lay])
            TS("dve", gqk[:, 2:3], gqk[:, 1:2], 0.125, None, ALU.mult, None, [R_lay], [R_lay])
            DMA("pool", bsp[:].rearrange("o g t -> o (g t)"), bass.AP(b_spatial, l * 512, [[0, 1], [1, 512]]), (), [R_lay])
            wsp_f = AR.alloc([4, 128], F32)
            wsp_m = AR.alloc([4, 128], BF16)
            R_w = Res("wsp")
            DMA("sp", wsp_f, w_spatial[l].rearrange("g t s -> t g s"), (), [R_w])
            for g in range(4):
                TT("dve", wsp_m[:, g, :], wsp_f[:, g, :], tril[:], ALU.mult, [R_w, R_const], [R_w])
            pT, pT16, rT = psum[0][:, :], psum16[0][:, :], R_ps[0]
            for g in range(4):
                TR(pT16[:, g * 128:(g + 1) * 128], wsp_m[:, g, :], [R_w], [rT])
            CP("dve", wT_sp[:].rearrange("p g t -> p (g t)"), pT16[:, 0:512], [rT], [R_lay])
            S_.barrier()

            AR.reset()
            hT2 = AR.alloc([8, 512], BF16)
            hTs = [hT[:, :, :], hT2]
            R_hTs = [R_hT, Res("hT2")]
            kT = [AR.alloc([S], BF16) for _ in range(2)]
            kidxT = [AR.alloc([S], BF16) for _ in range(2)]
            vaug = [AR.alloc([S // 128, 66], BF16) for _ in range(2)]
            R_kv = {}
            qT = [AR.alloc([512], BF16) for _ in range(2)]
            R_qT = [Res("qT0"), Res("qT1")]
            sga = [AR.alloc([512], BF16) for _ in range(3)]
            R_sga = [Res("sga0"), Res("sga1"), Res("sga2")]
            rdb = [AR.alloc([512], BF16) for _ in range(2)]
            R_rd = [Res("rd0"), Res("rd1")]
            yah = [AR.alloc([512], BF16) for _ in range(2)]
            R_yah = [Res("yah0"), Res("yah1")]
            maskT0 = AR.alloc([max(4, S // 128 - 4), 512], BF16)
            AR.off = max((AR.off + 63) // 64 * 64, 48 * 1024)
            X0 = AR.off
            xt = [AR.alloc([D], F32) for _ in range(2)]
            R_xt = [Res("xt0"), Res("xt1")]
            hb = [AR.alloc([D], BF16) for _ in range(2)]
            R_hb = [Res("hb0"), Res("hb1")]
            qidxT = AR.alloc([8, 512], BF16)
            R_qi = Res("qidxT")
            Dg = AR.alloc([4, 8, 128], BF16)
            R_Dg = [Res("Dg%d" % i) for i in range(4)]
            Rb = [AR.alloc([512], BF16) for _ in range(4)]
            R_Rb = [Res("Rb%d" % i) for i in range(4)]
            Isb = [AR.alloc([S], F32) for _ in range(3)]
            R_I = [Res("I0"), Res("I1"), Res("I2")]
            junkI = AR.alloc([S], BF16)
            R_junk = Res("junkI")
            X1 = AR.off
            assert X1 - X0 >= 8 * 1536 * 2, (X0, X1)
            xonly_res = R_xt + R_hb + [R_qi] + R_Dg + R_Rb + R_I + [R_junk]
            st_ = AR.alloc([16], F32)
            R_st = Res("st")
            widx = AR.alloc([4, 8], F32)
            R_wi = Res("widx")
            bis = [AR.alloc([8], F32) for _ in range(4)]
            R_bis = [Res("bis%d" % i) for i in range(4)]
            maskT = [maskT0, AR.alloc([S // 128, 512], BF16)]
            R_mT = [Res("maskT0"), Res("maskT1")]
            cnegb = AR.alloc([128], BF16)
            scr = [AR.alloc([512], F32) for _ in range(6)]
            R_scr = [Res("scr%d" % i) for i in range(6)]
            scrn = [0]
            Eb = [AR.alloc([512], BF16) for _ in range(4)]
            R_E = [Res("E%d" % i) for i in range(4)]
            sqb = [AR.alloc([512], BF16) for _ in range(2)]
            R_sqb = [Res("sqb0"), Res("sqb1")]
            sqn = [0]
            assert AR.off <= W1_OFF, (AR.off, W1_OFF)
            W2_OFF = X0
            W3_OFF = (X0 + 8 * 1536 * 2 + 63) // 64 * 64
            W2 = AR.alloc_at(W2_OFF, [8, 1536], BF16)
            R_W2 = Res("W2")
            assert W3_OFF + (8 * 2048 + 8 * D + 8 * D) * 2 + 256 <= W1_OFF
            pA = PPool([0, 1, 2])
            pS = PPool([3, 4])
            pI = PPool([5])
            pO = PPool([6, 7])
            mcnt = [0]

            def next_scr():
                i = scrn[0] % 6
                scrn[0] += 1
                return scr[i], R_scr[i]

            for i in range(2):
                MS("dve", vaug[i][:, :, 64:65], 1.0, [R_const])
            CP("dve", cnegb, cneg[:], [R_const], [R_const])

            def proj_fm(hTb, R_hTb, Wt, R_Wt, c0, M):
                p, _, rp = pA.next()
                for k in range(8):
                    MM(p[0:M, :], Wt[:, k, c0:c0 + M], hTb[:, k, :], k == 0, k == 7, [R_Wt, R_hTb], [rp])
                return p, rp

            def rms64(p, rp, gcol, out_ap, R_out):
                sq, R_sq = next_scr()
                s1, R_s1 = next_scr()
                sb_, R_sb_ = sqb[sqn[0] % 2], R_sqb[sqn[0] % 2]
                sqn[0] += 1
                ACT(sb_[0:64, :], p[0:64, :], AF.Square, [rp], [R_sb_])
                pm, _, rpm = pA.next()
                MM(pm[0:64, :], o64[0:64, 0:64], sb_[0:64, :], True, True, [R_sb_, R_const], [rpm])
                ACT(sq[0:64, :], pm[0:64, :], AF.Ln, [rpm, R_const], [R_sq], bias=eps_t[0:64, :])
                ACT(s1[0:64, :], sq[0:64, :], AF.Exp, [R_sq], [R_s1], scale=-0.5)
                STT(out_ap, p[0:64, :], gcol, s1[0:64, :], ALU.mult, ALU.mult, [rp, R_s1, R_lay], [R_out])

            def X_chunks(seq, tt, par):
                ch = []
                T0 = tt * 512
                hTb, R_hTb = hTs[par], R_hTs[par]
                kTs, kis, vas = kT[seq % 2], kidxT[seq % 2], vaug[seq % 2]
                rkv = Res("kv")
                R_kv[(seq, tt)] = rkv
                mT, R_mTb = maskT[par], R_mT[par]

                def c_norm(blk):
                    r0 = T0 + blk * 128
                    xb, R_xb = xt[blk % 2], R_xt[blk % 2]
                    hbl, R_hbl = hb[blk % 2], R_hb[blk % 2]
                    DMA("sp", xb, src_t[seq, r0:r0 + 128, :], [R_src], [R_xb])
                    c = blk * 3
                    ACT(hbl, xb, AF.Square, [R_xb], [R_hbl, R_st], accum=st_[:, c:c + 1])
                    TS("dve", st_[:, c + 1:c + 2], st_[:, c:c + 1], 1.0 / D, EPS, ALU.mult, ALU.add, [R_st], [R_st])
                    TT("pool", st_[:, c + 2:c + 3], st_[:, c + 1:c + 2], mhalf, ALU.pow, [R_st, R_const], [R_st])
                    STT(hbl, xb, st_[:, c + 2:c + 3], gbc[:], ALU.mult, ALU.mult, [R_xb, R_st, R_lay], [R_hbl])
                    _, p16, rp = pA.next()
                    for k in range(8):
                        TR(p16[:, k * 128:(k + 1) * 128], hbl[:, k * 128:(k + 1) * 128], [R_hbl], [rp])
                    CP("act", hTb[:, :, blk * 128:(blk + 1) * 128],
                       p16[:, 0:1024].rearrange("p (k t) -> p k t", k=8), [rp], [R_hTb])
                for blk in range(4):
                    ch.append(lambda blk=blk: c_norm(blk))

                def c_store():
                    R_hs[(l, seq, tt)] = Res("hs")
                    DMA("sp", hT_scr[seq, tt], hTb.rearrange("p k t -> p (k t)"), [R_hTb], [R_hs[(l, seq, tt)]])
                ch.append(c_store)

                def c_k():
                    p, rp = proj_fm(hTb, R_hTb, W1, R_W1, C_K, 64)
                    rms64(p, rp, gqk[:, 2:3], kTs[0:64, T0:T0 + 512], rkv)
                    p, rp = proj_fm(hTb, R_hTb, W1, R_W1, C_KI, 64)
                    CP("dve" if K_E1 else "act", kis[0:64, T0:T0 + 512], p[0:64, :], [rp], [rkv])
                ch.append(c_k)

                def c_v(blk):
                    p, _, rp = pA.next()
                    for k in range(8):
                        MM(p[:, 0:64], hTb[:, k, blk * 128:(blk + 1) * 128], W1[:, k, C_V:C_V + 64], k == 0, k == 7,
                           [R_W1, R_hTb], [rp])
                    for k in range(8):
                        MM(p[:, 64:72], hTb[:, k, blk * 128:(blk + 1) * 128], W1[:, k, C_WI:C_WI + 8], k == 0, k == 7,
                           [R_W1, R_hTb], [rp])
                    CP("dve" if K_E1 else "act", vas[:, tt * 4 + blk, 0:64], p[:, 0:64], [rp], [rkv])
                    TS("dve", widx[:, blk, :], p[:, 64:72], (8 ** -0.5) * (64 ** -0.5), None, ALU.mult, None, [rp], [R_wi])
                    for h in range(8):
                        if POOL_DG:
                            TS("pool", Dg[:, blk, h, :], idb[:], widx[:, blk, h:h + 1], 1.0, ALU.mult, ALU.mult,
                               [R_const, R_wi], [R_Dg[blk]])
                        else:
                            TS("dve", Dg[:, blk, h, :], idb[:], widx[:, blk, h:h + 1], None, ALU.mult, None,
                               [R_const, R_wi], [R_Dg[blk]])
                for blk in range(4):
                    ch.append(lambda blk=blk: c_v(blk))

                def c_qi(h):
                    p, rp = proj_fm(hTb, R_hTb, W1, R_W1, C_QI + 64 * h, 64)
                    CP("act", qidxT[0:64, h, :], p[0:64, :], [rp], [R_qi])
                for h in range(8):
                    ch.append(lambda h=h: c_qi(h))

                def c_idx(blk, j):
                    tb = tt * 4 + blk
                    Ib, R_Ib = Isb[tb % 3], R_I[tb % 3]
                    wdt = 512 if j < tt else (blk + 1) * 128
                    s0 = j * 512
                    pi, _, rpi = pI.next()
                    pend = None
                    for h in range(9):
                        if h < 8:
                            psc, _, rps = pS.next()
                            MM(psc[:, 0:wdt], qidxT[0:64, h, blk * 128:(blk + 1) * 128], kis[0:64, s0:s0 + wdt],
                               True, True, [R_qi] + [R_kv[(seq, jj)] for jj in ([j])], [rps])
                            rb_, R_rb_ = Rb[h % 4], R_Rb[h % 4]
                            ACT(rb_[:, 0:wdt], psc[:, 0:wdt], AF.Relu, [rps], [R_rb_])
                        if pend is not None:
                            hh, rbp, R_rbp = pend
                            last = (hh == 7) and (j < tt or not PE_CAUSAL)
                            MM(pi[:, 0:wdt], Dg[:, blk, hh, :], rbp[:, 0:wdt], hh == 0, last, [R_Dg[blk], R_rbp], [rpi])
                        if h < 8:
                            pend = (h, rb_, R_rb_)
                    if j == tt and PE_CAUSAL:
                        MM(pi[:, blk * 128:wdt], idb[:], cnegb, False, True, [R_const], [rpi])
                    if j == tt and not PE_CAUSAL:
                        if blk > 0:
                            CP("act", Ib[:, s0:s0 + blk * 128], pi[:, 0:blk * 128], [rpi], [R_Ib])
                        TT("dve", Ib[:, s0 + blk * 128:s0 + wdt], pi[:, blk * 128:wdt], cneg[:], ALU.add,
                           [rpi, R_const], [R_Ib])
                    else:
                        CP("act", Ib[:, s0:s0 + wdt], pi[:, 0:wdt], [rpi], [R_Ib])

                def c_bis0(blk):
                    tb = tt * 4 + blk
                    Wd = (tb + 1) * 128
                    Ib, R_Ib = Isb[tb % 3], R_I[tb % 3]
                    bs, R_bs = bis[tb % 4], R_bis[tb % 4]
                    lo, W0, mid, cnt, stp, hi = (bs[:, i:i + 1] for i in range(6))
                    if Wd <= KTOP:
                        MS("dve", lo, -1e29, [R_bs])
                    else:
                        S_.op("dve", lambda e, o=lo, i_=Ib[:, 0:tb * 128]: e.tensor_reduce(
                            out=o, in_=i_, axis=mybir.AxisListType.X, op=ALU.min), [R_Ib], [R_bs])
                        S_.op("dve", lambda e, o=hi, i_=Ib[:, 0:Wd]: e.tensor_reduce(
                            out=o, in_=i_, axis=mybir.AxisListType.X, op=ALU.max), [R_Ib], [R_bs])
                        STT(W0, hi, 1.0, lo, ALU.add, ALU.subtract, [R_bs], [R_bs])

                def c_bis(blk, it0, it1):
                    tb = tt * 4 + blk
                    Wd = (tb + 1) * 128
                    if Wd <= KTOP:
                        return
                    Ib, R_Ib = Isb[tb % 3], R_I[tb % 3]
                    bs, R_bs = bis[tb % 4], R_bis[tb % 4]
                    lo, W0, mid, cnt, stp, hi = (bs[:, i:i + 1] for i in range(6))
                    for it in range(it0, it1):
                        c = 2.0 ** -(it + 1)
                        STT(mid, W0, c, lo, ALU.mult, ALU.add, [R_bs], [R_bs])
                        TS("dve", junkI[:, 0:Wd], Ib[:, 0:Wd], mid, None, ALU.is_ge, ALU.add, [R_Ib, R_bs],
                           [R_junk, R_bs], accum=cnt)
                        STT(stp, cnt, KTOP - 0.5, W0, ALU.is_ge, ALU.mult, [R_bs], [R_bs])
                        STT(lo, stp, c, lo, ALU.mult, ALU.add, [R_bs], [R_bs])

                def c_maskD(blk):
                    tb = tt * 4 + blk
                    Wd = (tb + 1) * 128
                    Ib, R_Ib = Isb[tb % 3], R_I[tb % 3]
                    bs, R_bs = bis[tb % 4], R_bis[tb % 4]
                    lo = bs[:, 0:1]
                    TS("dve", Ib[:, 0:Wd], Ib[:, 0:Wd], lo, None, ALU.is_ge, None, [R_Ib, R_bs], [R_Ib])

                def c_maskP(blk):
                    tb = tt * 4 + blk
                    mkb, R_mkb = Isb[tb % 3], R_I[tb % 3]
                    sb0 = 0
                    while sb0 <= tb:
                        n = min(4, tb + 1 - sb0)
                        p32, _, rp = pA.next()
                        for i in range(n):
                            S_.op("pe", lambda e, o=p32[:, i * 128:(i + 1) * 128], i_=mkb[:, (sb0 + i) * 128:(sb0 + i + 1) * 128]:
                                  e.transpose(out=o, in_=i_, identity=idf[:]), [R_mkb, R_const], [rp])
                        CP("act", mT[:, sb0:sb0 + n, blk * 128:(blk + 1) * 128],
                           p32[:, 0:n * 128].rearrange("p (k t) -> p k t", k=n), [rp], [R_mTb])
                        sb0 += n

                def add_I(blk):
                    for j in range(tt + 1):
                        ch.append(lambda blk=blk, j=j: c_idx(blk, j))

                def add_B(blk):
                    ch.append(lambda blk=blk: c_bis0(blk))
                    for it0 in range(0, NIT, BCH):
                        ch.append(lambda blk=blk, it0=it0: c_bis(blk, it0, min(NIT, it0 + BCH)))
                    ch.append(lambda blk=blk: c_maskD(blk))

                add_I(0); add_I(1); add_I(2)
                add_B(0); add_B(1)
                ch.append(lambda: c_maskP(0))
                add_I(3)
                add_B(2)
                ch.append(lambda: c_maskP(1))
                add_B(3)
                ch.append(lambda: c_maskP(2))
                ch.append(lambda: c_maskP(3))
                return ch

            def Y_stage(seq, tt, par, xch):
                hTb, R_hTb = hTs[par], R_hTs[par]
                kTs, vas = kT[seq % 2], vaug[seq % 2]
                mT, R_mTb = maskT[par], R_mT[par]
                NSB = 4 * (tt + 1)
                nsteps = 8 * (NSB + 2)
                state = {"done": 0, "step": 0}

                def pump():
                    state["step"] += 1
                    target = (len(xch) * state["step"]) // nsteps
                    while state["done"] < min(target, len(xch)):
                        xch[state["done"]]()
                        state["done"] += 1

                def proj_head(h):
                    p, rp = proj_fm(hTb, R_hTb, W1, R_W1, C_Q + 64 * h, 64)
                    pg, rpg = proj_fm(hTb, R_hTb, W1, R_W1, C_GA + 64 * h, 64)
                    rms64(p, rp, gqk[:, 0:1], qT[h % 2][0:64, :], R_qT[h % 2])
                    th, R_th = next_scr()
                    ACT(th[0:64, :], pg[0:64, :], AF.Exp, [rpg], [R_th], scale=-1.0)
                    ACT(th[0:64, :], th[0:64, :], AF.Ln, [R_th, R_const], [R_th], bias=one_t[0:64, :])
                    ACT(th[0:64, :], th[0:64, :], AF.Exp, [R_th], [R_th], scale=-1.0)
                    TT("dve", sga[h % 3][0:64, :], th[0:64, :], pg[0:64, :], ALU.mult, [R_th, rpg], [R_sga[h % 3]])

                def attn_head(h, fin_prev):
                    q_, R_q_ = qT[h % 2], R_qT[h % 2]
                    po, _, rpo = pO.next()
                    pend = []
                    for sbi in range(NSB + 2):
                        if sbi < NSB:
                            j = sbi - 4 * tt
                            c0 = max(j, 0) * 128
                            rk = R_kv[(seq, sbi // 4)]
                            plt, _, rpl = pS.next()
                            near = []
                            for blk in range(4):
                                tb = 4 * tt + blk
                                if sbi == tb:
                                    near.append((blk, 0))
                                elif sbi == tb - 1:
                                    near.append((blk, 1))
                            MM(plt[:, c0:512], kTs[0:64, sbi * 128:(sbi + 1) * 128], q_[0:64, c0:512], True, len(near) == 0,
                               [rk, R_q_], [rpl])
                            for ni, (blk, dl) in enumerate(near):
                                MM(plt[:, blk * 128:(blk + 1) * 128], idb[:], biasT[:, h, dl, :], False,
                                   ni == len(near) - 1, [R_const], [rpl])
                            e_, R_e_ = Eb[sbi % 4], R_E[sbi % 4]
                            ACT(e_[:, c0:512], plt[:, c0:512], AF.Exp, [rpl, R_const], [R_e_], bias=cfar[:, h:h + 1])
                            mcnt[0] += 1
                            TT("pool" if (POOL_MASK and mcnt[0] % MASK_MOD != 0) else "dve", e_[:, c0:512], e_[:, c0:512],
                               mT[:, sbi, c0:512], ALU.mult, [R_e_, R_mTb], [R_e_])
                            pend.append((sbi, c0, e_, R_e_, rk))
                        if sbi >= 2:
                            ps_, pc0, pe_, R_pe_, prk = pend.pop(0)
                            MM(po[0:65, pc0:512], vas[:, ps_, 0:65], pe_[:, pc0:512], ps_ == 0, ps_ == NSB - 1,
                               [prk, R_pe_, R_const], [rpo])
                        if sbi == 2 and fin_prev is not None:
                            fin_prev()
                        pump()
                    def finish(h=h, po=po, rpo=rpo):
                        ln_, R_ln = next_scr()
                        ACT(ln_[64:65, :], po[64:65, :], AF.Ln, [rpo], [R_ln])
                        rd, R_rd_ = rdb[h % 2], R_rd[h % 2]
                        ACT(rd[64:65, :], ln_[64:65, :], AF.Exp, [R_ln], [R_rd_], scale=-1.0)
                        pb, _, rpb = pA.next()
                        MM(pb[0:64, :], ones_bf[64:65, 0:64], rd[64:65, :], True, True, [R_const, R_rd_], [rpb])
                        tmp, R_tmp = next_scr()
                        TT("dve", tmp[0:64, :], sga[h % 3][0:64, :], po[0:64, :], ALU.mult, [R_sga[h % 3], rpo], [R_tmp])
                        y_, R_y_ = yah[h % 2], R_yah[h % 2]
                        TT("dve", y_[0:64, :], tmp[0:64, :], pb[0:64, :], ALU.mult, [R_tmp, rpb], [R_y_])
                        key = (l, seq, tt)
                        if key not in R_ya:
                            R_ya[key] = Res("ya")
                        DMA("sp", ya_scr[seq, tt, h * 64:(h + 1) * 64, :], y_[0:64, :], [R_y_], [R_ya[key]])
                    return finish

                proj_head(0)
                fin = None
                for h in range(8):
                    fin_prev = fin
                    if h + 1 < 8:
                        proj_head(h + 1)
                    fin = attn_head(h, fin_prev)
                fin()
                while state["done"] < len(xch):
                    xch[state["done"]]()
                    state["done"] += 1

            tiles = [(seq, tt) for seq in range(NSEQ) for tt in range(NTT)]
            for c_ in X_chunks(tiles[0][0], tiles[0][1], 0):
                c_()
            for n_, (seq, tt) in enumerate(tiles):
                if n_ == len(tiles) - 1:
                    load_W2(l, W2, R_W2, xonly_res)
                if K_STOP == 1:
                    R_ya[(l, seq, tt)] = Res("ya")
                    continue
                nxt = X_chunks(tiles[n_ + 1][0], tiles[n_ + 1][1], (n_ + 1) % 2) if n_ + 1 < len(tiles) else []
                if INTERLEAVE:
                    Y_stage(seq, tt, n_ % 2, nxt)
                else:
                    Y_stage(seq, tt, n_ % 2, [])
                    for c_ in nxt:
                        c_()
            S_.barrier()

            AR.reset()
            Wm = AR.alloc_at(W3_OFF, [8, 2048], BF16)
            wbr = AR.alloc_at(W3_OFF + 8 * 2048 * 2, [2, 4, D], BF16)
            wo = AR.alloc_at(W3_OFF + 8 * 2048 * 2 + 8 * D * 2, [8, D], BF16)
            R_W3 = Res("W3")
            w3th = load_W3(l, Wm, wbr, wo, R_W3)
            hT2b = AR.alloc([8, 512], BF16)
            hTp = [hT[:, :, :], hT2b]
            R_hTp = [R_hT, Res("hT2b")]
            vln = AR.alloc([4, 512], BF16)
            R_vln = Res("vln")
            gu2 = AR.alloc([4, 512], BF16)
            R_gu = Res("gu2")
            sgb = AR.alloc([4, 512], BF16)
            R_sgb = Res("sgb")
            ybT = AR.alloc([4, 512], BF16)
            R_ybT = Res("ybT")
            scr2 = [AR.alloc([512], F32) for _ in range(8)]
            R_scr2 = [Res("s2_%d" % i) for i in range(8)]
            sc2n = [0]
            bnst = AR.alloc([4, 8], F32)
            R_bn = Res("bn")
            assert AR.off <= W2_OFF, (AR.off, W2_OFF)
            pA = PPool([0, 1, 2, 3, 4, 5])
            pS = PPool([6, 7])

            def nscr2():
                i = sc2n[0] % 8
                sc2n[0] += 1
                return scr2[i], R_scr2[i]

            def gelu2(p, rp, out_ap, R_out):
                a, R_a = nscr2()
                ACT(a, p, AF.Square, [rp], [R_a])
                TS("pool", a, a, 0.044715, 1.0, ALU.mult, ALU.add, [R_a], [R_a])
                b_, R_b = nscr2()
                TT("dve", b_, a, p, ALU.mult, [R_a, rp], [R_b])
                ACT(b_, b_, AF.Tanh, [R_b], [R_b], scale=0.7978845608028654)
                STT(out_ap, b_, 1.0, p, ALU.add, ALU.mult, [R_b, rp], [R_out])

            tiles2 = [(seq, tt) for seq in range(NSEQ) for tt in range(NTT)]

            def ld2(n_):
                sq_, t_ = tiles2[n_]
                DMA("sp", hTp[n_ % 2].rearrange("p k t -> p (k t)"), hT_scr[sq_, t_], [R_hs[(l, sq_, t_)]], [R_hTp[n_ % 2]])
            ld2(0)
            if True:
                for n2, (seq, tt) in enumerate(tiles2):
                    if n2 + 1 < len(tiles2):
                        ld2(n2 + 1)
                    per = -(-len(w3th) // len(tiles2))
                    for th_ in w3th[n2 * per:(n2 + 1) * per]:
                        th_()
                    hTc, R_hTc = hTp[n2 % 2], R_hTp[n2 % 2]
                    for blk in range(4):
                        p, _, rp = pA.next()
                        for k in range(8):
                            MM(p[:, :], hTc[:, k, blk * 128:(blk + 1) * 128], W2[:, k, 512:1024], k == 0, k == 7,
                               [R_W2, R_hTc], [rp])
                        g2, R_g2 = nscr2()
                        gelu2(p, rp, g2, R_g2)
                        S_.op("dve", lambda e, o=bnst[:, blk, 0:6], i_=g2: e.bn_stats(out=o, in_=i_), [R_g2], [R_bn])
                        S_.op("dve", lambda e, o=bnst[:, blk, 6:8], i_=bnst[:, blk, 0:6]: e.bn_aggr(out=o, in_=i_), [R_bn], [R_bn])
                        TS("dve", bnst[:, blk, 7:8], bnst[:, blk, 7:8], 4 * EPS, None, ALU.add, None, [R_bn], [R_bn])
                        TT("pool", bnst[:, blk, 7:8], bnst[:, blk, 7:8], mhalf, ALU.pow, [R_bn, R_const], [R_bn])
                        TS("dve", g2, g2, bnst[:, blk, 6:7], bnst[:, blk, 7:8], ALU.subtract, ALU.mult, [R_g2, R_bn], [R_g2])
                        TT("pool", g2, g2, lngb[:], ALU.mult, [R_g2, R_lay], [R_g2])
                        TT("pool", vln[:, blk, :], g2, lnbb[:], ALU.add, [R_g2, R_lay], [R_vln])
                    for c in range(4):
                        p, rp = proj_fm(hTc, R_hTc, W2, R_W2, c * 128, 128)
                        gelu2(p, rp, gu2[:, c, :], R_gu)
                        p, rp = proj_fm(hTc, R_hTc, W2, R_W2, 1024 + c * 128, 128)
                        th, R_th = nscr2()
                        ACT(th, p, AF.Tanh, [rp], [R_th], scale=0.5)
                        STT(sgb[:, c, :], th, 1.0, p, ALU.add, ALU.mult, [R_th, rp], [R_sgb])
                    for g in range(4):
                        p, _, rp = pS.next()
                        for blk in range(4):
                            MM(p[:, blk * 128:(blk + 1) * 128], vln[:, blk, g * 128:(g + 1) * 128], wT_sp[:, g, :], True, False,
                               [R_vln, R_lay], [rp])
                            MM(p[:, blk * 128:(blk + 1) * 128], ones_bf[0:1, :], bsp[0:1, g, :], False, True,
                               [R_const, R_lay], [rp])
                        t_, R_t = nscr2()
                        STT(t_, gu2[:, g, :], 0.25, p, ALU.mult, ALU.mult, [R_gu, rp], [R_t])
                        TT("pool", ybT[:, g, :], t_, sgb[:, g, :], ALU.mult, [R_t, R_sgb], [R_ybT])
                    key = (l, seq, tt)
                    R_yb[key] = Res("yb")
                    DMA("sp", yb_scr[seq, tt].rearrange("(g p) t -> p g t", p=128), ybT, [R_ybT], [R_yb[key]])
            S_.barrier()

            AR.reset()
            w1th = load_W1_th(l + 1) if l + 1 < DEPTH else []
            hT2c = AR.alloc([8, 512], BF16)
            hTp = [hT[:, :, :], hT2c]
            R_hTp = [R_hT, Res("hT2c")]
            yaTs = [AR.alloc([4, 512], BF16) for _ in range(2)]
            R_yaTs = [Res("yaT0"), Res("yaT1")]
            ybT3s = [AR.alloc([4, 512], BF16) for _ in range(2)]
            R_ybT3s = [Res("ybT30"), Res("ybT31")]
            mg = AR.alloc([8, 512], BF16)
            R_mg = Res("mg")
            scr3 = [AR.alloc([512], F32) for _ in range(8)]
            R_scr3 = [Res("s3_%d" % i) for i in range(8)]
            sc3n = [0]
            xres = [AR.alloc([D], F32) for _ in range(2)]
            R_xr = [Res("xr0"), Res("xr1")]
            assert AR.off <= W3_OFF, (AR.off, W3_OFF)
            pA = PPool([0, 1, 2, 3, 4, 5])
            pS = PPool([6, 7])

            def nscr3():
                i = sc3n[0] % 8
                sc3n[0] += 1
                return scr3[i], R_scr3[i]

            tiles3 = [(seq, tt) for seq in range(NSEQ) for tt in range(NTT)]

            def ld3(n_):
                sq_, t_ = tiles3[n_]
                key_ = (l, sq_, t_)
                DMA("sp", hTp[n_ % 2].rearrange("p k t -> p (k t)"), hT_scr[sq_, t_], [R_hs[key_]], [R_hTp[n_ % 2]])
                DMA("sp", yaTs[n_ % 2], ya_scr[sq_, t_].rearrange("(k p) t -> p k t", p=128), [R_ya[key_]], [R_yaTs[n_ % 2]])
                DMA("sp", ybT3s[n_ % 2], yb_scr[sq_, t_].rearrange("(k p) t -> p k t", p=128), [R_yb[key_]], [R_ybT3s[n_ % 2]])
            ld3(0)
            if True:
                for n3, (seq, tt) in enumerate(tiles3):
                    if n3 + 1 < len(tiles3):
                        ld3(n3 + 1)
                    per = -(-len(w1th) // len(tiles3)) if w1th else 0
                    for th_ in w1th[n3 * per:(n3 + 1) * per]:
                        th_()
                    hTc, R_hTc = hTp[n3 % 2], R_hTp[n3 % 2]
                    yaT, R_yaT = yaTs[n3 % 2], R_yaTs[n3 % 2]
                    ybT3, R_ybT3 = ybT3s[n3 % 2], R_ybT3s[n3 % 2]
                    for ec in range(8):
                        es = slice(ec * 128, (ec + 1) * 128)
                        pa, _, rpa = pA.next()
                        for k in range(4):
                            MM(pa, wbr[:, 0, k, es], yaT[:, k, :], k == 0, k == 3, [R_W3, R_yaT], [rpa])
                        pb, _, rpb = pA.next()
                        for k in range(4):
                            MM(pb, wbr[:, 1, k, es], ybT3[:, k, :], k == 0, k == 3, [R_W3, R_ybT3], [rpb])
                        pc, _, rpc = pA.next()
                        for k in range(8):
                            MM(pc, Wm[:, k, es], hTc[:, k, :], k == 0, k == 7, [R_W3, R_hTc], [rpc])
                        pd, _, rpd = pA.next()
                        for k in range(8):
                            MM(pd, Wm[:, k, 1024 + ec * 128:1024 + (ec + 1) * 128], hTc[:, k, :], k == 0, k == 7,
                               [R_W3, R_hTc], [rpd])
                        ta, R_ta = nscr3()
                        ACT(ta, pc, AF.Tanh, [rpc], [R_ta], scale=0.5)
                        tb_, R_tb = nscr3()
                        ACT(tb_, pd, AF.Tanh, [rpd], [R_tb], scale=0.5)
                        STT(ta, ta, 1.0, pa, ALU.add, ALU.mult, [R_ta, rpa], [R_ta])
                        STT(tb_, tb_, 1.0, pb, ALU.add, ALU.mult, [R_tb, rpb], [R_tb])
                        TT("pool", mg[:, ec, :], ta, tb_, ALU.add, [R_ta, R_tb], [R_mg])
                    for blk in range(4):
                        r0 = tt * 512 + blk * 128
                        xr, R_x = xres[blk % 2], R_xr[blk % 2]
                        DMA("sp", xr, src_t[seq, r0:r0 + 128, :], [R_src], [R_x])
                        for ch in range(2):
                            po, _, rpo = pS.next()
                            for ec in range(8):
                                MM(po, mg[:, ec, blk * 128:(blk + 1) * 128], wo[:, ec, ch * 512:(ch + 1) * 512], ec == 0, ec == 7,
                                   [R_mg, R_W3], [rpo])
                            STT(xr[:, ch * 512:(ch + 1) * 512], po, 0.5, xr[:, ch * 512:(ch + 1) * 512], ALU.mult, ALU.add,
                                [rpo, R_x], [R_x])
                        R_dst = [R_xmid] if l < DEPTH - 1 else []
                        tok = DMA("sp", dst_t[seq, r0:r0 + 128, :], xr, [R_x], R_dst)
                        if l == DEPTH - 1:
                            out_toks.append(tok)
            S_.barrier()
        S_.emit()
    global LAST_SCHED
    LAST_SCHED = S_
    print('arena peak', AR.peak)
    return nc


_CACHE = {}
LAST_SCHED = None


def kernel(**inputs):
    NC = 8
    x = np.ascontiguousarray(inputs["x"], dtype=np.float32)
    B, S, _ = x.shape
    NSEQ = B // NC
    DEPTH = inputs["w_in"].shape[0]
    key = (S, NSEQ, DEPTH)
    if key not in _CACHE:
        _CACHE[key] = build(S, NSEQ, DEPTH)
    nc = _CACHE[key]
    consts = host_consts()
    shared = {k: np.ascontiguousarray(v, dtype=np.float32) for k, v in inputs.items() if k != "x"}
    shared.update(consts)
    in_maps = []
    for c in range(NC):
        m = dict(shared)
        m["x"] = np.ascontiguousarray(x[c * NSEQ:(c + 1) * NSEQ])
        in_maps.append(m)
    res = run_bass_kernel_spmd(nc, in_maps, core_ids=list(range(NC)))
    return np.concatenate([np.asarray(r["out"], dtype=np.float32) for r in res.results], axis=0)
```

```python
import contextlib
import math
import numpy as np
import concourse.bass as bass
import concourse.mybir as mybir
from concourse.bass_utils import run_bass_kernel_spmd

F32 = mybir.dt.float32
BF16 = mybir.dt.bfloat16
ALU = mybir.AluOpType
AF = mybir.ActivationFunctionType

D = 1024
NIN = 5320
NEG = -30000.0
EPS = 1e-6
import os
INTERLEAVE = int(os.environ.get('K_IL', '1'))
POOL_MASK = int(os.environ.get('K_PM', '1'))
POOL_DG = int(os.environ.get('K_PD', '1'))
PE_CAUSAL = int(os.environ.get('K_PC', '1'))
K_E1 = int(os.environ.get('K_E1', '1'))
K_STOP = int(os.environ.get('K_STOP', '0'))
MASK_MOD = int(os.environ.get('K_MM', '3'))
BCH = int(os.environ.get('K_BCH', '1'))
NIT = int(os.environ.get('K_NIT', '16'))

C_Q, C_K, C_V, C_GA, C_QI, C_KI, C_WI, C_U, C_VB, C_GB, C_MA, C_MB = (
    0, 512, 576, 640, 1152, 1664, 1728, 1736, 2248, 2760, 3272, 4296)


class Res:
    __slots__ = ("name", "w", "r")

    def __init__(self, name):
        self.name = name
        self.w = None
        self.r = []


class Sched:
    ENG = ("pe", "act", "dve", "pool", "sp")
    NDS = 56

    def __init__(self, nc, stack):
        self.nc = nc
        self.sems = {}
        for e in self.ENG:
            self.sems[e] = stack.enter_context(nc.semaphore("s_" + e))
        for i in range(self.NDS):
            self.sems[("d", i)] = stack.enter_context(nc.semaphore("d%d" % i))
        self.dval = [0] * self.NDS
        self.dnext = {"sp": 0, "pool": 0, "act": 0}
        self.drange = {"sp": (0, 16), "act": (0, 16), "pool": (16, self.NDS)}
        self.cnt = {e: 0 for e in self.ENG}
        self.ops = {e: [] for e in self.ENG}
        self.waited = {e: {} for e in self.ENG}

    def _deps(self, eng, reads, writes):
        deps = {}

        def add(tok, kind):
            if tok is None:
                return
            k, v = tok
            if k == eng and (eng == "pe" or kind != "raw"):
                return
            if deps.get(k, 0) < v:
                deps[k] = v
        def addw(wt, kind):
            if isinstance(wt, list):
                for t_ in wt:
                    add(t_, kind)
            else:
                add(wt, kind)
        for r in reads:
            addw(r.w, "raw")
        for w in writes:
            addw(w.w, "waw")
            for t in w.r:
                add(t, "war")
        need = []
        wd = self.waited[eng]
        for k, v in deps.items():
            if wd.get(k, 0) < v:
                wd[k] = v
                need.append((k, v))
        return need

    def _mark(self, tok, reads, writes, append=False):
        for r in reads:
            r.r.append(tok)
            if len(r.r) > 64:
                best = {}
                for k, v in r.r:
                    if best.get(k, 0) < v:
                        best[k] = v
                r.r = list(best.items())
        for w in writes:
            if append and isinstance(w.w, list):
                w.w.append(tok)
            elif append:
                w.w = [tok]
            else:
                w.w = tok
            w.r = []

    def op(self, eng, fn, reads=(), writes=()):
        need = self._deps(eng, reads, writes)
        self.cnt[eng] += 1
        tok = (eng, self.cnt[eng])
        self.ops[eng].append((need, fn, (eng, 1)))
        self._mark(tok, reads, writes)
        return tok

    def dma(self, eng, fn, reads=(), writes=(), append=False):
        lo_, hi_ = self.drange[eng]
        j = lo_ + self.dnext[eng] % (hi_ - lo_)
        self.dnext[eng] += 1
        need = self._deps(eng, reads, writes)
        if self.dval[j] > 0:
            wd = self.waited[eng]
            if wd.get(("d", j), 0) < self.dval[j]:
                wd[("d", j)] = self.dval[j]
                need.append((("d", j), self.dval[j]))
        self.dval[j] += 16
        tok = (("d", j), self.dval[j])
        self.ops[eng].append((need, fn, (("d", j), 16)))
        self._mark(tok, reads, writes, append)
        return tok

    def barrier(self):
        for e in self.ENG:
            need = []
            wd = self.waited[e]
            for k in self.ENG:
                if k != e and self.cnt[k] > wd.get(k, 0):
                    wd[k] = self.cnt[k]
                    need.append((k, self.cnt[k]))
            for j in range(self.NDS):
                if self.dval[j] > wd.get(("d", j), 0):
                    wd[("d", j)] = self.dval[j]
                    need.append((("d", j), self.dval[j]))
            self.ops[e].append((need, None, None))

    def emit(self):
        nc = self.nc
        sems = self.sems
        with nc.Block() as block:
            def run(e_name):
                def body(engine):
                    for need, fn, inc in self.ops[e_name]:
                        for k, v in need:
                            engine.wait_ge(sems[k], v)
                        if fn is not None:
                            fn(engine).then_inc(sems[inc[0]], inc[1])
                return body
            block.tensor(run("pe"))
            block.scalar(run("act"))
            block.vector(run("dve"))
            block.gpsimd(run("pool"))
            block.sync(run("sp"))


class Arena:
    def __init__(self, t16):
        self.t16 = t16
        self.t32 = t16.bitcast(F32)
        self.off = 0
        self.cap = t16.shape[1] * 2
        self.peak = 0

    def reset(self):
        self.off = 0

    def alloc_at(self, off, free_shape, dt):
        save = self.off
        self.off = off
        ap = self.alloc(free_shape, dt)
        self.off = save
        return ap

    def alloc_dual(self, n32):
        self.off = (self.off + 63) // 64 * 64
        o = self.off
        self.off += n32 * 4
        self.peak = max(self.peak, self.off)
        assert self.off <= self.cap, ("arena overflow", self.off, self.cap)
        return self.t32[:, o // 4:o // 4 + n32], self.t16[:, o // 2:o // 2 + 2 * n32]

    def alloc(self, free_shape, dt, parts=128):
        es = 4 if dt == F32 else 2
        n = int(np.prod(free_shape))
        self.off = (self.off + 63) // 64 * 64
        o = self.off
        self.off += n * es
        self.peak = max(self.peak, self.off)
        assert self.off <= self.cap, ("arena overflow", self.off, self.cap)
        base = self.t32 if dt == F32 else self.t16
        ap = base[0:parts, o // es:o // es + n]
        if len(free_shape) == 2:
            ap = ap.rearrange("p (a b) -> p a b", a=free_shape[0])
        elif len(free_shape) == 3:
            ap = ap.rearrange("p (a b c) -> p a b c", a=free_shape[0], b=free_shape[1])
        return ap


def _bucket_table():
    oh = np.zeros((33, 384), np.float32)
    for m in range(384):
        rel = m - 127
        if rel < 0:
            oh[32, m] = 1.0
            continue
        if rel < 16:
            b = rel
        else:
            nf = np.float32(max(rel, 1))
            v = np.log(nf / np.float32(16.0)) / np.float32(math.log(128 / 16)) * np.float32(16.0)
            b = min(16 + int(np.float32(v).astype(np.int32)), 31)
        oh[b, m] = 1.0
    return oh


def host_consts():
    t = np.arange(128)
    return {
        "c_ident": np.eye(128, dtype=np.float32),
        "c_tril": (t[:, None] >= t[None, :]).astype(np.float32),
        "c_cneg": np.where(t[None, :] <= t[:, None], 0.0, -1e30).astype(np.float32),
        "c_oh": _bucket_table(),
    }


def build(S, NSEQ, DEPTH, dbg=None):
    nc = bass.Bass("TRN2", target_bir_lowering=False)
    NTT = S // 512
    KTOP = min(256, S // 4)
    dt_in = lambda name, shape: nc.dram_tensor(name, shape, F32, kind="ExternalInput")
    x_in = dt_in("x", [NSEQ, S, D])
    norm_g = dt_in("norm_g", [DEPTH, D])
    w_in = dt_in("w_in", [DEPTH, D, NIN])
    q_norm_g = dt_in("q_norm_g", [DEPTH, 64])
    k_norm_g = dt_in("k_norm_g", [DEPTH, 64])
    rel_bias = dt_in("rel_bias", [32, 8])
    sgu_ln_g = dt_in("sgu_ln_g", [DEPTH, 512])
    sgu_ln_b = dt_in("sgu_ln_b", [DEPTH, 512])
    w_spatial = dt_in("w_spatial", [DEPTH, 4, 128, 128])
    b_spatial = dt_in("b_spatial", [DEPTH, 4, 128])
    w_branch = dt_in("w_branch", [DEPTH, 2, 512, D])
    w_out = dt_in("w_out", [DEPTH, D, D])
    c_ident = dt_in("c_ident", [128, 128])
    c_tril = dt_in("c_tril", [128, 128])
    c_cneg = dt_in("c_cneg", [128, 128])
    c_oh = dt_in("c_oh", [33, 384])
    out = nc.dram_tensor("out", [NSEQ, S, D], F32, kind="ExternalOutput")
    xmid = nc.dram_tensor("xmid", [NSEQ, S, D], F32, kind="Internal")
    hT_scr = nc.dram_tensor("hT_scr", [NSEQ, NTT, 128, 8 * 512], BF16, kind="Internal")
    ya_scr = nc.dram_tensor("ya_scr", [NSEQ, NTT, 512, 512], BF16, kind="Internal")
    yb_scr = nc.dram_tensor("yb_scr", [NSEQ, NTT, 512, 512], BF16, kind="Internal")
    bias_scr = nc.dram_tensor("bias_scr", [128, 8 * 384], BF16, kind="Internal")
    R_xmid, R_hT_scr, R_ya_scr, R_yb_scr, R_bias_scr = (Res(n) for n in ("xmid", "hTs", "yas", "ybs", "bs"))
    R_ya = {}
    R_yb = {}
    R_hs = {}

    with contextlib.ExitStack() as st:
        S_ = Sched(nc, st)
        sb = lambda name, shape, dt: st.enter_context(nc.sbuf_tensor(name, shape, dt))
        def MM(out_, lhsT, rhs, start, stop, R, W):
            S_.op("pe", lambda e: e.matmul(out_, lhsT=lhsT, rhs=rhs, start=start, stop=stop), R, W)

        def TR(out_, in_, R, W):
            S_.op("pe", lambda e: e.transpose(out=out_, in_=in_, identity=idb[:]), list(R) + [R_const], W)

        def ACT(out_, in_, func, R, W, bias=None, scale=None, accum=None):
            kw = {}
            if bias is not None:
                kw["bias"] = bias
            if scale is not None:
                kw["scale"] = scale
            if accum is not None:
                kw["accum_out"] = accum
            S_.op("act", lambda e: e.activation(out=out_, in_=in_, func=func, **kw), R, W)

        def TS(eng, out_, in0, s1, s2, op0, op1, R, W, accum=None):
            kw = {}
            if accum is not None:
                kw["accum_out"] = accum
            if op1 is None:
                S_.op(eng, lambda e: e.tensor_scalar(out=out_, in0=in0, scalar1=s1, scalar2=None, op0=op0, **kw), R, W)
            else:
                S_.op(eng, lambda e: e.tensor_scalar(out=out_, in0=in0, scalar1=s1, scalar2=s2, op0=op0, op1=op1, **kw), R, W)

        def TT(eng, out_, in0, in1, op, R, W):
            S_.op(eng, lambda e: e.tensor_tensor(out=out_, in0=in0, in1=in1, op=op), R, W)

        def STT(out_, in0, scalar, in1, op0, op1, R, W):
            S_.op("dve", lambda e: e.scalar_tensor_tensor(out=out_, in0=in0, scalar=scalar, in1=in1, op0=op0, op1=op1), R, W)

        def CP(eng, out_, in_, R, W):
            if eng == "act":
                S_.op("act", lambda e: e.copy(out=out_, in_=in_), R, W)
            else:
                S_.op(eng, lambda e: e.tensor_copy(out=out_, in_=in_), R, W)

        def MS(eng, ap, val, W):
            S_.op(eng, lambda e: e.memset(ap, val), (), W)

        def DMA(q, out_, in_, R, W, append=False):
            return S_.dma(q, lambda e: e.dma_start(out=out_, in_=in_), R, W, append)

        R_const = Res("const")
        idb = sb("idb", [128, 128], BF16)
        tril = sb("tril", [128, 128], F32)
        idf = sb("idf", [128, 128], F32)
        cneg = sb("cneg", [128, 128], F32)
        ones_bf = sb("ones_bf", [128, 128], BF16)
        o64 = sb("o64", [128, 128], BF16)
        smallc = sb("smallc", [128, 8], F32)
        biasT = sb("biasT", [128, 8, 2, 128], BF16)
        cfar = sb("cfar", [128, 8], F32)
        gbc = sb("gbc", [128, D], F32)
        lngb = sb("lngb", [128, 512], F32)
        lnbb = sb("lnbb", [128, 512], F32)
        gqk = sb("gqk", [64, 4], F32)
        wT_sp = sb("wT_sp", [128, 4, 128], BF16)
        bsp = sb("bsp", [1, 4, 128], BF16)
        hT = sb("hT", [128, 8, 512], BF16)
        R_hT = Res("hT")
        R_lay = Res("layerconst")
        psum = [st.enter_context(nc.psum_tensor("ps%d" % i, [128, 512], F32)) for i in range(8)]
        psum16 = [p.bitcast(BF16) for p in psum]
        R_ps = [Res("ps%d" % i) for i in range(8)]

        class PPool:
            def __init__(self, idxs):
                self.idxs = idxs
                self.n = 0

            def next(self):
                i = self.idxs[self.n % len(self.idxs)]
                self.n += 1
                return psum[i][:, :], psum16[i][:, :], R_ps[i]

        ARENA_BYTES = int(os.environ.get("K_AR", "176")) * 1024
        arena_t = sb("arena", [128, ARENA_BYTES // 2], BF16)
        AR = Arena(arena_t)

        DMA("pool", idb[:], c_ident[:, :], (), [R_const])
        DMA("sp", tril[:], c_tril[:, :], (), [R_const])
        DMA("sp", idf[:], c_ident[:, :], (), [R_const])
        DMA("sp", cneg[:], c_cneg[:, :], (), [R_const])
        MS("dve", ones_bf[:], 1.0, [R_const])
        MS("dve", o64[:], 1.0 / 64, [R_const])
        MS("dve", smallc[:, 0:1], -0.5, [R_const])
        MS("dve", smallc[:, 1:2], EPS, [R_const])
        MS("dve", smallc[:, 2:3], 4 * EPS, [R_const])
        MS("dve", smallc[:, 3:4], 1.0, [R_const])
        one_t = smallc[:, 3:4]
        mhalf = smallc[:, 0:1]
        eps_t = smallc[:, 1:2]
        eps4_t = smallc[:, 2:3]

        AR.reset()
        oh_f = AR.alloc([384], F32)
        rb_ext = AR.alloc([8], F32)
        ones33 = AR.alloc([128], F32)
        lh = AR.alloc([8, 128], F32)
        Bsb = AR.alloc([8, 384], BF16)
        R_su = Res("setup")
        DMA("sp", oh_f[0:33, :], c_oh[:, :], (), [R_su])
        MS("dve", rb_ext[32:33, :], NEG, [R_su])
        DMA("sp", rb_ext[0:32, :], rel_bias[:, :], (), [R_su])
        MS("dve", ones33[0:33, :], 1.0, [R_su])
        for h in range(8):
            TS("dve", lh[0:33, h, :], ones33[0:33, :], rb_ext[0:33, h:h + 1], None, ALU.mult, None, [R_su], [R_su])
        for h in range(8):
            pB, _, rB = psum[h % 2], None, R_ps[h % 2]
            MM(pB[:, 0:384], lh[0:33, h, :], oh_f[0:33, :], True, True, [R_su], [rB])
            CP("dve", cfar[:, h:h + 1], pB[:, 300:301], [rB], [R_const])
            TS("dve", Bsb[:, h, :], pB[:, 0:384], cfar[:, h:h + 1], None, ALU.subtract, None, [rB, R_const], [R_su])
        DMA("sp", bias_scr[:, :], Bsb[:].rearrange("p h m -> p (h m)"), [R_su], [R_bias_scr])
        for h in range(8):
            for dl in range(2):
                src = bass.AP(bias_scr, h * 384 + 127 + 128 * dl, [[8 * 384 - 1, 128], [1, 128]])
                DMA("sp", biasT[:, h, dl, :], src, [R_bias_scr], [R_const])
        S_.barrier()

        W1_OFF = ARENA_BYTES - 8 * 1736 * 2 - 64
        W1_OFF -= W1_OFF % 64
        W1 = AR.alloc_at(W1_OFF, [8, 1736], BF16)
        R_W1 = Res("W1")
        lay = {}

        def load_W1(l_, first):
            for k in range(8):
                DMA("pool", W1[:, k, :], w_in[l_, k * 128:(k + 1) * 128, 0:1736], (), [R_W1], append=(k > 0))

        def load_W2(l_, W2_, R_W2_, extraW):
            for k in range(8):
                DMA("pool", W2_[:, k, :], w_in[l_, k * 128:(k + 1) * 128, C_U:C_U + 1536], (),
                    [R_W2_] + (list(extraW) if k == 0 else []), append=(k > 0))

        def load_W3(l_, Wm_, wbr_, wo_, R_W3_):
            th = []
            for k in range(8):
                th.append(lambda k=k: DMA("pool", Wm_[:, k, :], w_in[l_, k * 128:(k + 1) * 128, C_MA:C_MA + 2048], (),
                                          [R_W3_], append=(k > 0)))
            for n_ in range(2):
                for k in range(4):
                    th.append(lambda n_=n_, k=k: DMA("pool", wbr_[:, n_, k, :], w_branch[l_, n_, k * 128:(k + 1) * 128, :], (),
                                                     [R_W3_], append=True))
            for k in range(8):
                th.append(lambda k=k: DMA("pool", wo_[:, k, :], w_out[l_, k * 128:(k + 1) * 128, :], (), [R_W3_], append=True))
            return th

        def load_W1_th(l_):
            return [lambda k=k: DMA("pool", W1[:, k, :], w_in[l_, k * 128:(k + 1) * 128, 0:1736], (), [R_W1], append=(k > 0))
                    for k in range(8)]

        load_W1(0, True)
        out_toks = []
        for l in range(DEPTH):
            src_t = x_in if l == 0 else xmid
            dst_t = out if l == DEPTH - 1 else xmid
            R_src = Res("src") if l == 0 else R_xmid
            assert DEPTH <= 2
            AR.reset()
            DMA("sp", gbc[:], bass.AP(norm_g, l * D, [[0, 128], [1, D]]), (), [R_lay])
            DMA("sp", lngb[:], bass.AP(sgu_ln_g, l * 512, [[0, 128], [1, 512]]), (), [R_lay])
            DMA("sp", lnbb[:], bass.AP(sgu_ln_b, l * 512, [[0, 128], [1, 512]]), (), [R_lay])
            DMA("sp", gqk[:, 0:1], bass.AP(q_norm_g, l * 64, [[1, 64], [1, 1]]), (), [R_lay])
            DMA("sp", gqk[:, 1:2], bass.AP(k_norm_g, l * 64, [[1, 64], [1, 1]]), (), [R_lay])
            TS("dve", gqk[:, 2:3], gqk[:, 1:2], 0.125, None, ALU.mult, None, [R_lay], [R_lay])
            DMA("pool", bsp[:].rearrange("o g t -> o (g t)"), bass.AP(b_spatial, l * 512, [[0, 1], [1, 512]]), (), [R_lay])
            wsp_f = AR.alloc([4, 128], F32)
            wsp_m = AR.alloc([4, 128], BF16)
            R_w = Res("wsp")
            DMA("sp", wsp_f, w_spatial[l].rearrange("g t s -> t g s"), (), [R_w])
            for g in range(4):
                TT("dve", wsp_m[:, g, :], wsp_f[:, g, :], tril[:], ALU.mult, [R_w, R_const], [R_w])
            pT, pT16, rT = psum[0][:, :], psum16[0][:, :], R_ps[0]
            for g in range(4):
                TR(pT16[:, g * 128:(g + 1) * 128], wsp_m[:, g, :], [R_w], [rT])
            CP("dve", wT_sp[:].rearrange("p g t -> p (g t)"), pT16[:, 0:512], [rT], [R_lay])
            S_.barrier()

            AR.reset()
            hT2 = AR.alloc([8, 512], BF16)
            hTs = [hT[:, :, :], hT2]
            R_hTs = [R_hT, Res("hT2")]
            kT = [AR.alloc([S], BF16) for _ in range(2)]
            kidxT = [AR.alloc([S], BF16) for _ in range(2)]
            vaug = [AR.alloc([S // 128, 66], BF16) for _ in range(2)]
            R_kv = {}
            qT = [AR.alloc([512], BF16) for _ in range(2)]
            R_qT = [Res("qT0"), Res("qT1")]
            sga = [AR.alloc([512], BF16) for _ in range(3)]
            R_sga = [Res("sga0"), Res("sga1"), Res("sga2")]
            rdb = [AR.alloc([512], BF16) for _ in range(2)]
            R_rd = [Res("rd0"), Res("rd1")]
            yah = [AR.alloc([512], BF16) for _ in range(2)]
            R_yah = [Res("yah0"), Res("yah1")]
            maskT0 = AR.alloc([max(4, S // 128 - 4), 512], BF16)
            AR.off = max((AR.off + 63) // 64 * 64, 48 * 1024)
            X0 = AR.off
            xt = [AR.alloc([D], F32) for _ in range(2)]
            R_xt = [Res("xt0"), Res("xt1")]
            hb = [AR.alloc([D], BF16) for _ in range(2)]
            R_hb = [Res("hb0"), Res("hb1")]
            qidxT = AR.alloc([8, 512], BF16)
            R_qi = Res("qidxT")
            Dg = AR.alloc([4, 8, 128], BF16)
            R_Dg = [Res("Dg%d" % i) for i in range(4)]
            Rb = [AR.alloc([512], BF16) for _ in range(4)]
            R_Rb = [Res("Rb%d" % i) for i in range(4)]
            Isb = [AR.alloc([S], F32) for _ in range(3)]
            R_I = [Res("I0"), Res("I1"), Res("I2")]
            junkI = AR.alloc([S], BF16)
            R_junk = Res("junkI")
            X1 = AR.off
            assert X1 - X0 >= 8 * 1536 * 2, (X0, X1)
            xonly_res = R_xt + R_hb + [R_qi] + R_Dg + R_Rb + R_I + [R_junk]
            st_ = AR.alloc([16], F32)
            R_st = Res("st")
            widx = AR.alloc([4, 8], F32)
            R_wi = Res("widx")
            bis = [AR.alloc([8], F32) for _ in range(4)]
            R_bis = [Res("bis%d" % i) for i in range(4)]
            maskT = [maskT0, AR.alloc([S // 128, 512], BF16)]
            R_mT = [Res("maskT0"), Res("maskT1")]
            cnegb = AR.alloc([128], BF16)
            scr = [AR.alloc([512], F32) for _ in range(4)]
            R_scr = [Res("scr%d" % i) for i in range(4)]
            junkA = AR.alloc([S], BF16)
            R_junkA = Res("junkA")
            scrn = [0]
            Eb = [AR.alloc([512], BF16) for _ in range(4)]
            R_E = [Res("E%d" % i) for i in range(4)]
            sqb = [AR.alloc([512], BF16) for _ in range(2)]
            R_sqb = [Res("sqb0"), Res("sqb1")]
            sqn = [0]
            assert AR.off <= W1_OFF, (AR.off, W1_OFF)
            W2_OFF = X0
            W3_OFF = (X0 + 8 * 1536 * 2 + 63) // 64 * 64
            W2 = AR.alloc_at(W2_OFF, [8, 1536], BF16)
            R_W2 = Res("W2")
            assert W3_OFF + (8 * 2048 + 8 * D + 8 * D) * 2 + 256 <= W1_OFF
            pA = PPool([0, 1, 2])
            pS = PPool([3, 4])
            pI = PPool([5])
            pO = PPool([6, 7])
            mcnt = [0]

            def next_scr():
                i = scrn[0] % 4
                scrn[0] += 1
                return scr[i], R_scr[i]

            for i in range(2):
                MS("dve", vaug[i][:, :, 64:65], 1.0, [R_const])
            CP("dve", cnegb, cneg[:], [R_const], [R_const])

            def proj_fm(hTb, R_hTb, Wt, R_Wt, c0, M):
                p, _, rp = pA.next()
                for k in range(8):
                    MM(p[0:M, :], Wt[:, k, c0:c0 + M], hTb[:, k, :], k == 0, k == 7, [R_Wt, R_hTb], [rp])
                return p, rp

            def rms64(p, rp, gcol, out_ap, R_out):
                sq, R_sq = next_scr()
                s1, R_s1 = next_scr()
                sb_, R_sb_ = sqb[sqn[0] % 2], R_sqb[sqn[0] % 2]
                sqn[0] += 1
                ACT(sb_[0:64, :], p[0:64, :], AF.Square, [rp], [R_sb_])
                pm, _, rpm = pA.next()
                MM(pm[0:64, :], o64[0:64, 0:64], sb_[0:64, :], True, True, [R_sb_, R_const], [rpm])
                ACT(sq[0:64, :], pm[0:64, :], AF.Ln, [rpm, R_const], [R_sq], bias=eps_t[0:64, :])
                ACT(s1[0:64, :], sq[0:64, :], AF.Exp, [R_sq], [R_s1], scale=-0.5)
                STT(out_ap, p[0:64, :], gcol, s1[0:64, :], ALU.mult, ALU.mult, [rp, R_s1, R_lay], [R_out])

            def X_chunks(seq, tt, par):
                ch = []
                T0 = tt * 512
                hTb, R_hTb = hTs[par], R_hTs[par]
                kTs, kis, vas = kT[seq % 2], kidxT[seq % 2], vaug[seq % 2]
                rkv = Res("kv")
                R_kv[(seq, tt)] = rkv
                mT, R_mTb = maskT[par], R_mT[par]

                def c_norm(blk):
                    r0 = T0 + blk * 128
                    xb, R_xb = xt[blk % 2], R_xt[blk % 2]
                    hbl, R_hbl = hb[blk % 2], R_hb[blk % 2]
                    DMA("sp", xb, src_t[seq, r0:r0 + 128, :], [R_src], [R_xb])
                    c = blk * 3
                    ACT(hbl, xb, AF.Square, [R_xb], [R_hbl, R_st], accum=st_[:, c:c + 1])
                    TS("dve", st_[:, c + 1:c + 2], st_[:, c:c + 1], 1.0 / D, EPS, ALU.mult, ALU.add, [R_st], [R_st])
                    TT("pool", st_[:, c + 2:c + 3], st_[:, c + 1:c + 2], mhalf, ALU.pow, [R_st, R_const], [R_st])
                    STT(hbl, xb, st_[:, c + 2:c + 3], gbc[:], ALU.mult, ALU.mult, [R_xb, R_st, R_lay], [R_hbl])
                    _, p16, rp = pA.next()
                    for k in range(8):
                        TR(p16[:, k * 128:(k + 1) * 128], hbl[:, k * 128:(k + 1) * 128], [R_hbl], [rp])
                    CP("act", hTb[:, :, blk * 128:(blk + 1) * 128],
                       p16[:, 0:1024].rearrange("p (k t) -> p k t", k=8), [rp], [R_hTb])
                for blk in range(4):
                    ch.append(lambda blk=blk: c_norm(blk))

                def c_store():
                    R_hs[(l, seq, tt)] = Res("hs")
                    DMA("sp", hT_scr[seq, tt], hTb.rearrange("p k t -> p (k t)"), [R_hTb], [R_hs[(l, seq, tt)]])
                ch.append(c_store)

                def c_k():
                    p, rp = proj_fm(hTb, R_hTb, W1, R_W1, C_K, 64)
                    rms64(p, rp, gqk[:, 2:3], kTs[0:64, T0:T0 + 512], rkv)
                    p, rp = proj_fm(hTb, R_hTb, W1, R_W1, C_KI, 64)
                    CP("dve" if K_E1 else "act", kis[0:64, T0:T0 + 512], p[0:64, :], [rp], [rkv])
                ch.append(c_k)

                def c_v(blk):
                    p, _, rp = pA.next()
                    for k in range(8):
                        MM(p[:, 0:64], hTb[:, k, blk * 128:(blk + 1) * 128], W1[:, k, C_V:C_V + 64], k == 0, k == 7,
                           [R_W1, R_hTb], [rp])
                    for k in range(8):
                        MM(p[:, 64:72], hTb[:, k, blk * 128:(blk + 1) * 128], W1[:, k, C_WI:C_WI + 8], k == 0, k == 7,
                           [R_W1, R_hTb], [rp])
                    CP("dve" if K_E1 else "act", vas[:, tt * 4 + blk, 0:64], p[:, 0:64], [rp], [rkv])
                    TS("dve", widx[:, blk, :], p[:, 64:72], (8 ** -0.5) * (64 ** -0.5), None, ALU.mult, None, [rp], [R_wi])
                    for h in range(8):
                        if POOL_DG:
                            TS("pool", Dg[:, blk, h, :], idb[:], widx[:, blk, h:h + 1], 1.0, ALU.mult, ALU.mult,
                               [R_const, R_wi], [R_Dg[blk]])
                        else:
                            TS("dve", Dg[:, blk, h, :], idb[:], widx[:, blk, h:h + 1], None, ALU.mult, None,
                               [R_const, R_wi], [R_Dg[blk]])
                for blk in range(4):
                    ch.append(lambda blk=blk: c_v(blk))

                def c_qi(h):
                    p, rp = proj_fm(hTb, R_hTb, W1, R_W1, C_QI + 64 * h, 64)
                    CP("act", qidxT[0:64, h, :], p[0:64, :], [rp], [R_qi])
                for h in range(8):
                    ch.append(lambda h=h: c_qi(h))

                def c_idx(blk, j):
                    tb = tt * 4 + blk
                    Ib, R_Ib = Isb[tb % 3], R_I[tb % 3]
                    wdt = 512 if j < tt else (blk + 1) * 128
                    s0 = j * 512
                    pi, _, rpi = pI.next()
                    pend = None
                    for h in range(9):
                        if h < 8:
                            psc, _, rps = pS.next()
                            MM(psc[:, 0:wdt], qidxT[0:64, h, blk * 128:(blk + 1) * 128], kis[0:64, s0:s0 + wdt],
                               True, True, [R_qi] + [R_kv[(seq, jj)] for jj in ([j])], [rps])
                            rb_, R_rb_ = Rb[h % 4], R_Rb[h % 4]
                            ACT(rb_[:, 0:wdt], psc[:, 0:wdt], AF.Relu, [rps], [R_rb_])
                        if pend is not None:
                            hh, rbp, R_rbp = pend
                            last = (hh == 7) and (j < tt or not PE_CAUSAL)
                            MM(pi[:, 0:wdt], Dg[:, blk, hh, :], rbp[:, 0:wdt], hh == 0, last, [R_Dg[blk], R_rbp], [rpi])
                        if h < 8:
                            pend = (h, rb_, R_rb_)
                    if j == tt and PE_CAUSAL:
                        MM(pi[:, blk * 128:wdt], idb[:], cnegb, False, True, [R_const], [rpi])
                    if j == tt and not PE_CAUSAL:
                        if blk > 0:
                            CP("act", Ib[:, s0:s0 + blk * 128], pi[:, 0:blk * 128], [rpi], [R_Ib])
                        TT("dve", Ib[:, s0 + blk * 128:s0 + wdt], pi[:, blk * 128:wdt], cneg[:], ALU.add,
                           [rpi, R_const], [R_Ib])
                    else:
                        CP("act", Ib[:, s0:s0 + wdt], pi[:, 0:wdt], [rpi], [R_Ib])

                def c_bis0(blk):
                    tb = tt * 4 + blk
                    Wd = (tb + 1) * 128
                    Ib, R_Ib = Isb[tb % 3], R_I[tb % 3]
                    bs, R_bs = bis[tb % 4], R_bis[tb % 4]
                    lo, W0, mid, cnt, stp, hi = (bs[:, i:i + 1] for i in range(6))
                    if Wd <= KTOP:
                        MS("dve", lo, -1e29, [R_bs])
                    else:
                        S_.op("dve", lambda e, o=lo, i_=Ib[:, 0:tb * 128]: e.tensor_reduce(
                            out=o, in_=i_, axis=mybir.AxisListType.X, op=ALU.min), [R_Ib], [R_bs])
                        S_.op("dve", lambda e, o=hi, i_=Ib[:, 0:Wd]: e.tensor_reduce(
                            out=o, in_=i_, axis=mybir.AxisListType.X, op=ALU.max), [R_Ib], [R_bs])
                        STT(W0, hi, 1.0, lo, ALU.add, ALU.subtract, [R_bs], [R_bs])

                def c_bis(blk, it0, it1):
                    tb = tt * 4 + blk
                    Wd = (tb + 1) * 128
                    if Wd <= KTOP:
                        return
                    Ib, R_Ib = Isb[tb % 3], R_I[tb % 3]
                    bs, R_bs = bis[tb % 4], R_bis[tb % 4]
                    lo, W0, mid, cnt, stp, hi = (bs[:, i:i + 1] for i in range(6))
                    for it in range(it0, it1):
                        c = 2.0 ** -(it + 1)
                        STT(mid, W0, c, lo, ALU.mult, ALU.add, [R_bs], [R_bs])
                        TS("dve", junkI[:, 0:Wd], Ib[:, 0:Wd], mid, None, ALU.is_ge, ALU.add, [R_Ib, R_bs],
                           [R_junk, R_bs], accum=cnt)
                        STT(stp, cnt, KTOP - 0.5, W0, ALU.is_ge, ALU.mult, [R_bs], [R_bs])
                        STT(lo, stp, c, lo, ALU.mult, ALU.add, [R_bs], [R_bs])

                def c_maskD(blk):
                    tb = tt * 4 + blk
                    Wd = (tb + 1) * 128
                    Ib, R_Ib = Isb[tb % 3], R_I[tb % 3]
                    bs, R_bs = bis[tb % 4], R_bis[tb % 4]
                    lo = bs[:, 0:1]
                    TS("dve", Ib[:, 0:Wd], Ib[:, 0:Wd], lo, None, ALU.is_ge, None, [R_Ib, R_bs], [R_Ib])

                def c_maskP(blk):
                    tb = tt * 4 + blk
                    mkb, R_mkb = Isb[tb % 3], R_I[tb % 3]
                    sb0 = 0
                    while sb0 <= tb:
                        n = min(4, tb + 1 - sb0)
                        p32, _, rp = pA.next()
                        for i in range(n):
                            S_.op("pe", lambda e, o=p32[:, i * 128:(i + 1) * 128], i_=mkb[:, (sb0 + i) * 128:(sb0 + i + 1) * 128]:
                                  e.transpose(out=o, in_=i_, identity=idf[:]), [R_mkb, R_const], [rp])
                        CP("act", mT[:, sb0:sb0 + n, blk * 128:(blk + 1) * 128],
                           p32[:, 0:n * 128].rearrange("p (k t) -> p k t", k=n), [rp], [R_mTb])
                        sb0 += n

                def add_I(blk):
                    for j in range(tt + 1):
                        ch.append(lambda blk=blk, j=j: c_idx(blk, j))

                def c_bis_pair(bA, bB, it):
                    c = 2.0 ** -(it + 1)
                    info = []
                    for b_ in (bA, bB):
                        tb = tt * 4 + b_
                        Wd = (tb + 1) * 128
                        act = Wd > KTOP
                        Ib, R_Ib = Isb[tb % 3], R_I[tb % 3]
                        bs, R_bs = bis[tb % 4], R_bis[tb % 4]
                        info.append((act, Wd, Ib, R_Ib, bs, R_bs))
                    actA, WdA, IA, R_IA, bsA, R_bsA = info[0]
                    actB, WdB, IB, R_IB, bsB, R_bsB = info[1]
                    if actB:
                        loB, W0B, midB, cntB, stpB = (bsB[:, i:i + 1] for i in range(5))
                        STT(midB, W0B, -c, loB, ALU.mult, ALU.subtract, [R_bsB], [R_bsB])
                        ACT(junkA[:, 0:WdB], IB[:, 0:WdB], AF.Sign, [R_IB, R_bsB], [R_junkA, R_bsB], bias=midB, accum=cntB)
                    if actA:
                        loA, W0A, midA, cntA, stpA = (bsA[:, i:i + 1] for i in range(5))
                        STT(midA, W0A, c, loA, ALU.mult, ALU.add, [R_bsA], [R_bsA])
                        TS("dve", junkI[:, 0:WdA], IA[:, 0:WdA], midA, None, ALU.is_ge, ALU.add, [R_IA, R_bsA],
                           [R_junk, R_bsA], accum=cntA)
                        STT(stpA, cntA, KTOP - 0.5, W0A, ALU.is_ge, ALU.mult, [R_bsA], [R_bsA])
                        STT(loA, stpA, c, loA, ALU.mult, ALU.add, [R_bsA], [R_bsA])
                    if actB:
                        STT(stpB, cntB, 2 * KTOP - WdB - 0.5, W0B, ALU.is_ge, ALU.mult, [R_bsB], [R_bsB])
                        STT(loB, stpB, c, loB, ALU.mult, ALU.add, [R_bsB], [R_bsB])

                def add_Bpair(bA, bB):
                    ch.append(lambda: c_bis0(bA))
                    ch.append(lambda: c_bis0(bB))
                    if (tt * 4 + bB + 1) * 128 > KTOP:
                        for it in range(NIT):
                            ch.append(lambda it=it: c_bis_pair(bA, bB, it))
                    ch.append(lambda: c_maskD(bA))
                    ch.append(lambda: c_maskD(bB))

                add_I(0); add_I(1); add_I(2)
                add_Bpair(0, 1)
                ch.append(lambda: c_maskP(0))
                add_I(3)
                ch.append(lambda: c_maskP(1))
                add_Bpair(2, 3)
                ch.append(lambda: c_maskP(2))
                ch.append(lambda: c_maskP(3))
                return ch

            def Y_stage(seq, tt, par, xch):
                hTb, R_hTb = hTs[par], R_hTs[par]
                kTs, vas = kT[seq % 2], vaug[seq % 2]
                mT, R_mTb = maskT[par], R_mT[par]
                NSB = 4 * (tt + 1)
                nsteps = 8 * (NSB + 2)
                state = {"done": 0, "step": 0}

                def pump():
                    state["step"] += 1
                    target = (len(xch) * state["step"]) // nsteps
                    while state["done"] < min(target, len(xch)):
                        xch[state["done"]]()
                        state["done"] += 1

                def proj_head(h):
                    p, rp = proj_fm(hTb, R_hTb, W1, R_W1, C_Q + 64 * h, 64)
                    pg, rpg = proj_fm(hTb, R_hTb, W1, R_W1, C_GA + 64 * h, 64)
                    rms64(p, rp, gqk[:, 0:1], qT[h % 2][0:64, :], R_qT[h % 2])
                    th, R_th = next_scr()
                    ACT(th[0:64, :], pg[0:64, :], AF.Exp, [rpg], [R_th], scale=-1.0)
                    ACT(th[0:64, :], th[0:64, :], AF.Ln, [R_th, R_const], [R_th], bias=one_t[0:64, :])
                    ACT(th[0:64, :], th[0:64, :], AF.Exp, [R_th], [R_th], scale=-1.0)
                    TT("dve", sga[h % 3][0:64, :], th[0:64, :], pg[0:64, :], ALU.mult, [R_th, rpg], [R_sga[h % 3]])

                def attn_head(h, fin_prev):
                    q_, R_q_ = qT[h % 2], R_qT[h % 2]
                    po, _, rpo = pO.next()
                    pend = []
                    for sbi in range(NSB + 2):
                        if sbi < NSB:
                            j = sbi - 4 * tt
                            c0 = max(j, 0) * 128
                            rk = R_kv[(seq, sbi // 4)]
                            plt, _, rpl = pS.next()
                            near = []
                            for blk in range(4):
                                tb = 4 * tt + blk
                                if sbi == tb:
                                    near.append((blk, 0))
                                elif sbi == tb - 1:
                                    near.append((blk, 1))
                            MM(plt[:, c0:512], kTs[0:64, sbi * 128:(sbi + 1) * 128], q_[0:64, c0:512], True, len(near) == 0,
                               [rk, R_q_], [rpl])
                            for ni, (blk, dl) in enumerate(near):
                                MM(plt[:, blk * 128:(blk + 1) * 128], idb[:], biasT[:, h, dl, :], False,
                                   ni == len(near) - 1, [R_const], [rpl])
                            e_, R_e_ = Eb[sbi % 4], R_E[sbi % 4]
                            ACT(e_[:, c0:512], plt[:, c0:512], AF.Exp, [rpl, R_const], [R_e_], bias=cfar[:, h:h + 1])
                            mcnt[0] += 1
                            TT("pool" if (POOL_MASK and mcnt[0] % MASK_MOD != 0) else "dve", e_[:, c0:512], e_[:, c0:512],
                               mT[:, sbi, c0:512], ALU.mult, [R_e_, R_mTb], [R_e_])
                            pend.append((sbi, c0, e_, R_e_, rk))
                        if sbi >= 2:
                            ps_, pc0, pe_, R_pe_, prk = pend.pop(0)
                            MM(po[0:65, pc0:512], vas[:, ps_, 0:65], pe_[:, pc0:512], ps_ == 0, ps_ == NSB - 1,
                               [prk, R_pe_, R_const], [rpo])
                        if sbi == 0 and fin_prev is not None:
                            fin_prev(0)
                        if sbi == 3 and fin_prev is not None:
                            fin_prev(1)
                        pump()
                    def finish(part, h=h, po=po, rpo=rpo):
                        rd, R_rd_ = rdb[h % 2], R_rd[h % 2]
                        if part == 0:
                            ln_, R_ln = next_scr()
                            ACT(ln_[64:65, :], po[64:65, :], AF.Ln, [rpo], [R_ln])
                            ACT(rd[64:65, :], ln_[64:65, :], AF.Exp, [R_ln], [R_rd_], scale=-1.0)
                            return
                        pb, _, rpb = pA.next()
                        MM(pb[0:64, :], ones_bf[64:65, 0:64], rd[64:65, :], True, True, [R_const, R_rd_], [rpb])
                        tmp, R_tmp = next_scr()
                        TT("dve", tmp[0:64, :], sga[h % 3][0:64, :], po[0:64, :], ALU.mult, [R_sga[h % 3], rpo], [R_tmp])
                        y_, R_y_ = yah[h % 2], R_yah[h % 2]
                        TT("dve", y_[0:64, :], tmp[0:64, :], pb[0:64, :], ALU.mult, [R_tmp, rpb], [R_y_])
                        key = (l, seq, tt)
                        if key not in R_ya:
                            R_ya[key] = Res("ya")
                        DMA("sp", ya_scr[seq, tt, h * 64:(h + 1) * 64, :], y_[0:64, :], [R_y_], [R_ya[key]])
                    return finish

                proj_head(0)
                fin = None
                for h in range(8):
                    fin_prev = fin
                    if h + 1 < 8:
                        proj_head(h + 1)
                    fin = attn_head(h, fin_prev)
                fin(0)
                fin(1)
                while state["done"] < len(xch):
                    xch[state["done"]]()
                    state["done"] += 1

            tiles = [(seq, tt) for seq in range(NSEQ) for tt in range(NTT)]
            for c_ in X_chunks(tiles[0][0], tiles[0][1], 0):
                c_()
            for n_, (seq, tt) in enumerate(tiles):
                if n_ == len(tiles) - 1:
                    load_W2(l, W2, R_W2, xonly_res)
                if K_STOP == 1:
                    R_ya[(l, seq, tt)] = Res("ya")
                    continue
                nxt = X_chunks(tiles[n_ + 1][0], tiles[n_ + 1][1], (n_ + 1) % 2) if n_ + 1 < len(tiles) else []
                if INTERLEAVE:
                    Y_stage(seq, tt, n_ % 2, nxt)
                else:
                    Y_stage(seq, tt, n_ % 2, [])
                    for c_ in nxt:
                        c_()
            S_.barrier()

            AR.reset()
            Wm = AR.alloc_at(W3_OFF, [8, 2048], BF16)
            wbr = AR.alloc_at(W3_OFF + 8 * 2048 * 2, [2, 4, D], BF16)
            wo = AR.alloc_at(W3_OFF + 8 * 2048 * 2 + 8 * D * 2, [8, D], BF16)
            R_W3 = Res("W3")
            w3th = load_W3(l, Wm, wbr, wo, R_W3)
            hT2b = AR.alloc([8, 512], BF16)
            hTp = [hT[:, :, :], hT2b]
            R_hTp = [R_hT, Res("hT2b")]
            vln = AR.alloc([4, 512], BF16)
            R_vln = Res("vln")
            gu2 = AR.alloc([4, 512], BF16)
            R_gu = Res("gu2")
            sgb = AR.alloc([4, 512], BF16)
            R_sgb = Res("sgb")
            ybT = AR.alloc([4, 512], BF16)
            R_ybT = Res("ybT")
            scr2 = [AR.alloc([512], F32) for _ in range(8)]
            R_scr2 = [Res("s2_%d" % i) for i in range(8)]
            sc2n = [0]
            bnst = AR.alloc([4, 8], F32)
            R_bn = Res("bn")
            assert AR.off <= W2_OFF, (AR.off, W2_OFF)
            pA = PPool([0, 1, 2, 3, 4, 5])
            pS = PPool([6, 7])

            def nscr2():
                i = sc2n[0] % 8
                sc2n[0] += 1
                return scr2[i], R_scr2[i]

            def gelu2(p, rp, out_ap, R_out):
                a, R_a = nscr2()
                ACT(a, p, AF.Square, [rp], [R_a])
                TS("pool", a, a, 0.044715, 1.0, ALU.mult, ALU.add, [R_a], [R_a])
                b_, R_b = nscr2()
                TT("dve", b_, a, p, ALU.mult, [R_a, rp], [R_b])
                ACT(b_, b_, AF.Tanh, [R_b], [R_b], scale=0.7978845608028654)
                STT(out_ap, b_, 1.0, p, ALU.add, ALU.mult, [R_b, rp], [R_out])

            tiles2 = [(seq, tt) for seq in range(NSEQ) for tt in range(NTT)]

            def ld2(n_):
                sq_, t_ = tiles2[n_]
                DMA("sp", hTp[n_ % 2].rearrange("p k t -> p (k t)"), hT_scr[sq_, t_], [R_hs[(l, sq_, t_)]], [R_hTp[n_ % 2]])
            ld2(0)
            if True:
                for n2, (seq, tt) in enumerate(tiles2):
                    if n2 + 1 < len(tiles2):
                        ld2(n2 + 1)
                    per = -(-len(w3th) // len(tiles2))
                    for th_ in w3th[n2 * per:(n2 + 1) * per]:
                        th_()
                    hTc, R_hTc = hTp[n2 % 2], R_hTp[n2 % 2]
                    for blk in range(4):
                        p, _, rp = pA.next()
                        for k in range(8):
                            MM(p[:, :], hTc[:, k, blk * 128:(blk + 1) * 128], W2[:, k, 512:1024], k == 0, k == 7,
                               [R_W2, R_hTc], [rp])
                        g2, R_g2 = nscr2()
                        gelu2(p, rp, g2, R_g2)
                        S_.op("dve", lambda e, o=bnst[:, blk, 0:6], i_=g2: e.bn_stats(out=o, in_=i_), [R_g2], [R_bn])
                        S_.op("dve", lambda e, o=bnst[:, blk, 6:8], i_=bnst[:, blk, 0:6]: e.bn_aggr(out=o, in_=i_), [R_bn], [R_bn])
                        TS("dve", bnst[:, blk, 7:8], bnst[:, blk, 7:8], 4 * EPS, None, ALU.add, None, [R_bn], [R_bn])
                        TT("pool", bnst[:, blk, 7:8], bnst[:, blk, 7:8], mhalf, ALU.pow, [R_bn, R_const], [R_bn])
                        TS("dve", g2, g2, bnst[:, blk, 6:7], bnst[:, blk, 7:8], ALU.subtract, ALU.mult, [R_g2, R_bn], [R_g2])
                        TT("pool", g2, g2, lngb[:], ALU.mult, [R_g2, R_lay], [R_g2])
                        TT("pool", vln[:, blk, :], g2, lnbb[:], ALU.add, [R_g2, R_lay], [R_vln])
                    for c in range(4):
                        p, rp = proj_fm(hTc, R_hTc, W2, R_W2, c * 128, 128)
                        gelu2(p, rp, gu2[:, c, :], R_gu)
                        p, rp = proj_fm(hTc, R_hTc, W2, R_W2, 1024 + c * 128, 128)
                        th, R_th = nscr2()
                        ACT(th, p, AF.Tanh, [rp], [R_th], scale=0.5)
                        STT(sgb[:, c, :], th, 1.0, p, ALU.add, ALU.mult, [R_th, rp], [R_sgb])
                    for g in range(4):
                        p, _, rp = pS.next()
                        for blk in range(4):
                            MM(p[:, blk * 128:(blk + 1) * 128], vln[:, blk, g * 128:(g + 1) * 128], wT_sp[:, g, :], True, False,
                               [R_vln, R_lay], [rp])
                            MM(p[:, blk * 128:(blk + 1) * 128], ones_bf[0:1, :], bsp[0:1, g, :], False, True,
                               [R_const, R_lay], [rp])
                        t_, R_t = nscr2()
                        STT(t_, gu2[:, g, :], 0.25, p, ALU.mult, ALU.mult, [R_gu, rp], [R_t])
                        TT("pool", ybT[:, g, :], t_, sgb[:, g, :], ALU.mult, [R_t, R_sgb], [R_ybT])
                    key = (l, seq, tt)
                    R_yb[key] = Res("yb")
                    DMA("sp", yb_scr[seq, tt].rearrange("(g p) t -> p g t", p=128), ybT, [R_ybT], [R_yb[key]])
            S_.barrier()

            AR.reset()
            w1th = load_W1_th(l + 1) if l + 1 < DEPTH else []
            hT2c = AR.alloc([8, 512], BF16)
            hTp = [hT[:, :, :], hT2c]
            R_hTp = [R_hT, Res("hT2c")]
            yaTs = [AR.alloc([4, 512], BF16) for _ in range(2)]
            R_yaTs = [Res("yaT0"), Res("yaT1")]
            ybT3s = [AR.alloc([4, 512], BF16) for _ in range(2)]
            R_ybT3s = [Res("ybT30"), Res("ybT31")]
            mg = AR.alloc([8, 512], BF16)
            R_mg = Res("mg")
            scr3 = [AR.alloc([512], F32) for _ in range(8)]
            R_scr3 = [Res("s3_%d" % i) for i in range(8)]
            sc3n = [0]
            xres = [AR.alloc([D], F32) for _ in range(2)]
            R_xr = [Res("xr0"), Res("xr1")]
            assert AR.off <= W3_OFF, (AR.off, W3_OFF)
            pA = PPool([0, 1, 2, 3, 4, 5])
            pS = PPool([6, 7])

            def nscr3():
                i = sc3n[0] % 8
                sc3n[0] += 1
                return scr3[i], R_scr3[i]

            tiles3 = [(seq, tt) for seq in range(NSEQ) for tt in range(NTT)]

            def ld3(n_):
                sq_, t_ = tiles3[n_]
                key_ = (l, sq_, t_)
                DMA("sp", hTp[n_ % 2].rearrange("p k t -> p (k t)"), hT_scr[sq_, t_], [R_hs[key_]], [R_hTp[n_ % 2]])
                DMA("sp", yaTs[n_ % 2], ya_scr[sq_, t_].rearrange("(k p) t -> p k t", p=128), [R_ya[key_]], [R_yaTs[n_ % 2]])
                DMA("sp", ybT3s[n_ % 2], yb_scr[sq_, t_].rearrange("(k p) t -> p k t", p=128), [R_yb[key_]], [R_ybT3s[n_ % 2]])
            ld3(0)
            if True:
                for n3, (seq, tt) in enumerate(tiles3):
                    if n3 + 1 < len(tiles3):
                        ld3(n3 + 1)
                    per = -(-len(w1th) // len(tiles3)) if w1th else 0
                    for th_ in w1th[n3 * per:(n3 + 1) * per]:
                        th_()
                    hTc, R_hTc = hTp[n3 % 2], R_hTp[n3 % 2]
                    yaT, R_yaT = yaTs[n3 % 2], R_yaTs[n3 % 2]
                    ybT3, R_ybT3 = ybT3s[n3 % 2], R_ybT3s[n3 % 2]
                    for ec in range(8):
                        es = slice(ec * 128, (ec + 1) * 128)
                        pa, _, rpa = pA.next()
                        for k in range(4):
                            MM(pa, wbr[:, 0, k, es], yaT[:, k, :], k == 0, k == 3, [R_W3, R_yaT], [rpa])
                        pb, _, rpb = pA.next()
                        for k in range(4):
                            MM(pb, wbr[:, 1, k, es], ybT3[:, k, :], k == 0, k == 3, [R_W3, R_ybT3], [rpb])
                        pc, _, rpc = pA.next()
                        for k in range(8):
                            MM(pc, Wm[:, k, es], hTc[:, k, :], k == 0, k == 7, [R_W3, R_hTc], [rpc])
                        pd, _, rpd = pA.next()
                        for k in range(8):
                            MM(pd, Wm[:, k, 1024 + ec * 128:1024 + (ec + 1) * 128], hTc[:, k, :], k == 0, k == 7,
                               [R_W3, R_hTc], [rpd])
                        ta, R_ta = nscr3()
                        ACT(ta, pc, AF.Tanh, [rpc], [R_ta], scale=0.5)
                        tb_, R_tb = nscr3()
                        ACT(tb_, pd, AF.Tanh, [rpd], [R_tb], scale=0.5)
                        STT(ta, ta, 1.0, pa, ALU.add, ALU.mult, [R_ta, rpa], [R_ta])
                        STT(tb_, tb_, 1.0, pb, ALU.add, ALU.mult, [R_tb, rpb], [R_tb])
                        TT("pool", mg[:, ec, :], ta, tb_, ALU.add, [R_ta, R_tb], [R_mg])
                    for blk in range(4):
                        r0 = tt * 512 + blk * 128
                        xr, R_x = xres[blk % 2], R_xr[blk % 2]
                        DMA("sp", xr, src_t[seq, r0:r0 + 128, :], [R_src], [R_x])
                        for ch in range(2):
                            po, _, rpo = pS.next()
                            for ec in range(8):
                                MM(po, mg[:, ec, blk * 128:(blk + 1) * 128], wo[:, ec, ch * 512:(ch + 1) * 512], ec == 0, ec == 7,
                                   [R_mg, R_W3], [rpo])
                            STT(xr[:, ch * 512:(ch + 1) * 512], po, 0.5, xr[:, ch * 512:(ch + 1) * 512], ALU.mult, ALU.add,
                                [rpo, R_x], [R_x])
                        R_dst = [R_xmid] if l < DEPTH - 1 else []
                        tok = DMA("sp", dst_t[seq, r0:r0 + 128, :], xr, [R_x], R_dst)
                        if l == DEPTH - 1:
                            out_toks.append(tok)
            S_.barrier()
        S_.emit()
    global LAST_SCHED
    LAST_SCHED = S_
    print('arena peak', AR.peak)
    return nc


_CACHE = {}
LAST_SCHED = None


def kernel(**inputs):
    NC = 8
    x = np.ascontiguousarray(inputs["x"], dtype=np.float32)
    B, S, _ = x.shape
    NSEQ = B // NC
    DEPTH = inputs["w_in"].shape[0]
    key = (S, NSEQ, DEPTH)
    if key not in _CACHE:
        _CACHE[key] = build(S, NSEQ, DEPTH)
    nc = _CACHE[key]
    consts = host_consts()
    shared = {k: np.ascontiguousarray(v, dtype=np.float32) for k, v in inputs.items() if k != "x"}
    shared.update(consts)
    in_maps = []
    for c in range(NC):
        m = dict(shared)
        m["x"] = np.ascontiguousarray(x[c * NSEQ:(c + 1) * NSEQ])
        in_maps.append(m)
    res = run_bass_kernel_spmd(nc, in_maps, core_ids=list(range(NC)))
    return np.concatenate([np.asarray(r["out"], dtype=np.float32) for r in res.results], axis=0)
```

```python
import contextlib
import math
import numpy as np
import concourse.bass as bass
import concourse.mybir as mybir
from concourse.bass_utils import run_bass_kernel_spmd

F32 = mybir.dt.float32
BF16 = mybir.dt.bfloat16
ALU = mybir.AluOpType
AF = mybir.ActivationFunctionType

D = 1024
NIN = 5320
NEG = -30000.0
EPS = 1e-6
import os
INTERLEAVE = int(os.environ.get('K_IL', '1'))
POOL_MASK = int(os.environ.get('K_PM', '1'))
POOL_DG = int(os.environ.get('K_PD', '1'))
PE_CAUSAL = int(os.environ.get('K_PC', '1'))
K_E1 = int(os.environ.get('K_E1', '1'))
K_STOP = int(os.environ.get('K_STOP', '0'))
MASK_MOD = int(os.environ.get('K_MM', '3'))
BCH = int(os.environ.get('K_BCH', '1'))
NIT = int(os.environ.get('K_NIT', '16'))

C_Q, C_K, C_V, C_GA, C_QI, C_KI, C_WI, C_U, C_VB, C_GB, C_MA, C_MB = (
    0, 512, 576, 640, 1152, 1664, 1728, 1736, 2248, 2760, 3272, 4296)


class Res:
    __slots__ = ("name", "w", "r")

    def __init__(self, name):
        self.name = name
        self.w = None
        self.r = []


class Sched:
    ENG = ("pe", "act", "dve", "pool", "sp")
    NDS = 56

    def __init__(self, nc, stack):
        self.nc = nc
        self.sems = {}
        for e in self.ENG:
            self.sems[e] = stack.enter_context(nc.semaphore("s_" + e))
        for i in range(self.NDS):
            self.sems[("d", i)] = stack.enter_context(nc.semaphore("d%d" % i))
        self.dval = [0] * self.NDS
        self.dnext = {"sp": 0, "pool": 0, "act": 0}
        self.drange = {"sp": (0, 16), "act": (0, 16), "pool": (16, self.NDS)}
        self.cnt = {e: 0 for e in self.ENG}
        self.ops = {e: [] for e in self.ENG}
        self.waited = {e: {} for e in self.ENG}

    def _deps(self, eng, reads, writes):
        deps = {}

        def add(tok, kind):
            if tok is None:
                return
            k, v = tok
            if k == eng and (eng == "pe" or kind != "raw"):
                return
            if deps.get(k, 0) < v:
                deps[k] = v
        def addw(wt, kind):
            if isinstance(wt, list):
                for t_ in wt:
                    add(t_, kind)
            else:
                add(wt, kind)
        for r in reads:
            addw(r.w, "raw")
        for w in writes:
            addw(w.w, "waw")
            for t in w.r:
                add(t, "war")
        need = []
        wd = self.waited[eng]
        for k, v in deps.items():
            if wd.get(k, 0) < v:
                wd[k] = v
                need.append((k, v))
        return need

    def _mark(self, tok, reads, writes, append=False):
        for r in reads:
            r.r.append(tok)
            if len(r.r) > 64:
                best = {}
                for k, v in r.r:
                    if best.get(k, 0) < v:
                        best[k] = v
                r.r = list(best.items())
        for w in writes:
            if append and isinstance(w.w, list):
                w.w.append(tok)
            elif append:
                w.w = [tok]
            else:
                w.w = tok
            w.r = []

    def op(self, eng, fn, reads=(), writes=()):
        need = self._deps(eng, reads, writes)
        self.cnt[eng] += 1
        tok = (eng, self.cnt[eng])
        self.ops[eng].append((need, fn, (eng, 1)))
        self._mark(tok, reads, writes)
        return tok

    def dma(self, eng, fn, reads=(), writes=(), append=False):
        lo_, hi_ = self.drange[eng]
        j = lo_ + self.dnext[eng] % (hi_ - lo_)
        self.dnext[eng] += 1
        need = self._deps(eng, reads, writes)
        if self.dval[j] > 0:
            wd = self.waited[eng]
            if wd.get(("d", j), 0) < self.dval[j]:
                wd[("d", j)] = self.dval[j]
                need.append((("d", j), self.dval[j]))
        self.dval[j] += 16
        tok = (("d", j), self.dval[j])
        self.ops[eng].append((need, fn, (("d", j), 16)))
        self._mark(tok, reads, writes, append)
        return tok

    def barrier(self):
        for e in self.ENG:
            need = []
            wd = self.waited[e]
            for k in self.ENG:
                if k != e and self.cnt[k] > wd.get(k, 0):
                    wd[k] = self.cnt[k]
                    need.append((k, self.cnt[k]))
            for j in range(self.NDS):
                if self.dval[j] > wd.get(("d", j), 0):
                    wd[("d", j)] = self.dval[j]
                    need.append((("d", j), self.dval[j]))
            self.ops[e].append((need, None, None))

    def emit(self):
        nc = self.nc
        sems = self.sems
        with nc.Block() as block:
            def run(e_name):
                def body(engine):
                    for need, fn, inc in self.ops[e_name]:
                        for k, v in need:
                            engine.wait_ge(sems[k], v)
                        if fn is not None:
                            fn(engine).then_inc(sems[inc[0]], inc[1])
                return body
            block.tensor(run("pe"))
            block.scalar(run("act"))
            block.vector(run("dve"))
            block.gpsimd(run("pool"))
            block.sync(run("sp"))


class Arena:
    def __init__(self, t16):
        self.t16 = t16
        self.t32 = t16.bitcast(F32)
        self.off = 0
        self.cap = t16.shape[1] * 2
        self.peak = 0

    def reset(self):
        self.off = 0

    def alloc_at(self, off, free_shape, dt):
        save = self.off
        self.off = off
        ap = self.alloc(free_shape, dt)
        self.off = save
        return ap

    def alloc_dual(self, n32):
        self.off = (self.off + 63) // 64 * 64
        o = self.off
        self.off += n32 * 4
        self.peak = max(self.peak, self.off)
        assert self.off <= self.cap, ("arena overflow", self.off, self.cap)
        return self.t32[:, o // 4:o // 4 + n32], self.t16[:, o // 2:o // 2 + 2 * n32]

    def alloc(self, free_shape, dt, parts=128):
        es = 4 if dt == F32 else 2
        n = int(np.prod(free_shape))
        self.off = (self.off + 63) // 64 * 64
        o = self.off
        self.off += n * es
        self.peak = max(self.peak, self.off)
        assert self.off <= self.cap, ("arena overflow", self.off, self.cap)
        base = self.t32 if dt == F32 else self.t16
        ap = base[0:parts, o // es:o // es + n]
        if len(free_shape) == 2:
            ap = ap.rearrange("p (a b) -> p a b", a=free_shape[0])
        elif len(free_shape) == 3:
            ap = ap.rearrange("p (a b c) -> p a b c", a=free_shape[0], b=free_shape[1])
        return ap


def _bucket_table():
    oh = np.zeros((33, 384), np.float32)
    for m in range(384):
        rel = m - 127
        if rel < 0:
            oh[32, m] = 1.0
            continue
        if rel < 16:
            b = rel
        else:
            nf = np.float32(max(rel, 1))
            v = np.log(nf / np.float32(16.0)) / np.float32(math.log(128 / 16)) * np.float32(16.0)
            b = min(16 + int(np.float32(v).astype(np.int32)), 31)
        oh[b, m] = 1.0
    return oh


def host_consts():
    t = np.arange(128)
    return {
        "c_ident": np.eye(128, dtype=np.float32),
        "c_tril": (t[:, None] >= t[None, :]).astype(np.float32),
        "c_cneg": np.where(t[None, :] <= t[:, None], 0.0, -1e30).astype(np.float32),
        "c_oh": _bucket_table(),
    }


def build(S, NSEQ, DEPTH, dbg=None):
    nc = bass.Bass("TRN2", target_bir_lowering=False)
    NTT = S // 512
    KTOP = min(256, S // 4)
    dt_in = lambda name, shape: nc.dram_tensor(name, shape, F32, kind="ExternalInput")
    x_in = dt_in("x", [NSEQ, S, D])
    norm_g = dt_in("norm_g", [DEPTH, D])
    w_in = dt_in("w_in", [DEPTH, D, NIN])
    q_norm_g = dt_in("q_norm_g", [DEPTH, 64])
    k_norm_g = dt_in("k_norm_g", [DEPTH, 64])
    rel_bias = dt_in("rel_bias", [32, 8])
    sgu_ln_g = dt_in("sgu_ln_g", [DEPTH, 512])
    sgu_ln_b = dt_in("sgu_ln_b", [DEPTH, 512])
    w_spatial = dt_in("w_spatial", [DEPTH, 4, 128, 128])
    b_spatial = dt_in("b_spatial", [DEPTH, 4, 128])
    w_branch = dt_in("w_branch", [DEPTH, 2, 512, D])
    w_out = dt_in("w_out", [DEPTH, D, D])
    c_ident = dt_in("c_ident", [128, 128])
    c_tril = dt_in("c_tril", [128, 128])
    c_cneg = dt_in("c_cneg", [128, 128])
    c_oh = dt_in("c_oh", [33, 384])
    out = nc.dram_tensor("out", [NSEQ, S, D], F32, kind="ExternalOutput")
    xmid = nc.dram_tensor("xmid", [NSEQ, S, D], F32, kind="Internal")
    hT_scr = nc.dram_tensor("hT_scr", [NSEQ, NTT, 128, 8 * 512], BF16, kind="Internal")
    ya_scr = nc.dram_tensor("ya_scr", [NSEQ, NTT, 512, 512], BF16, kind="Internal")
    yb_scr = nc.dram_tensor("yb_scr", [NSEQ, NTT, 512, 512], BF16, kind="Internal")
    bias_scr = nc.dram_tensor("bias_scr", [128, 8 * 384], BF16, kind="Internal")
    R_xmid, R_hT_scr, R_ya_scr, R_yb_scr, R_bias_scr = (Res(n) for n in ("xmid", "hTs", "yas", "ybs", "bs"))
    R_ya = {}
    R_yb = {}
    R_hs = {}

    with contextlib.ExitStack() as st:
        S_ = Sched(nc, st)
        sb = lambda name, shape, dt: st.enter_context(nc.sbuf_tensor(name, shape, dt))
        def MM(out_, lhsT, rhs, start, stop, R, W):
            S_.op("pe", lambda e: e.matmul(out_, lhsT=lhsT, rhs=rhs, start=start, stop=stop), R, W)

        def TR(out_, in_, R, W):
            S_.op("pe", lambda e: e.transpose(out=out_, in_=in_, identity=idb[:]), list(R) + [R_const], W)

        def ACT(out_, in_, func, R, W, bias=None, scale=None, accum=None):
            kw = {}
            if bias is not None:
                kw["bias"] = bias
            if scale is not None:
                kw["scale"] = scale
            if accum is not None:
                kw["accum_out"] = accum
            S_.op("act", lambda e: e.activation(out=out_, in_=in_, func=func, **kw), R, W)

        def TS(eng, out_, in0, s1, s2, op0, op1, R, W, accum=None):
            kw = {}
            if accum is not None:
                kw["accum_out"] = accum
            if op1 is None:
                S_.op(eng, lambda e: e.tensor_scalar(out=out_, in0=in0, scalar1=s1, scalar2=None, op0=op0, **kw), R, W)
            else:
                S_.op(eng, lambda e: e.tensor_scalar(out=out_, in0=in0, scalar1=s1, scalar2=s2, op0=op0, op1=op1, **kw), R, W)

        def TT(eng, out_, in0, in1, op, R, W):
            S_.op(eng, lambda e: e.tensor_tensor(out=out_, in0=in0, in1=in1, op=op), R, W)

        def STT(out_, in0, scalar, in1, op0, op1, R, W):
            S_.op("dve", lambda e: e.scalar_tensor_tensor(out=out_, in0=in0, scalar=scalar, in1=in1, op0=op0, op1=op1), R, W)

        def CP(eng, out_, in_, R, W):
            if eng == "act":
                S_.op("act", lambda e: e.copy(out=out_, in_=in_), R, W)
            else:
                S_.op(eng, lambda e: e.tensor_copy(out=out_, in_=in_), R, W)

        def MS(eng, ap, val, W):
            S_.op(eng, lambda e: e.memset(ap, val), (), W)

        def DMA(q, out_, in_, R, W, append=False):
            return S_.dma(q, lambda e: e.dma_start(out=out_, in_=in_), R, W, append)

        R_const = Res("const")
        idb = sb("idb", [128, 128], BF16)
        tril = sb("tril", [128, 128], F32)
        idf = sb("idf", [128, 128], F32)
        cneg = sb("cneg", [128, 128], F32)
        ones_bf = sb("ones_bf", [128, 128], BF16)
        o64 = sb("o64", [128, 128], BF16)
        smallc = sb("smallc", [128, 8], F32)
        biasT = sb("biasT", [128, 8, 2, 128], BF16)
        cfar = sb("cfar", [128, 8], F32)
        gbc = sb("gbc", [128, D], F32)
        lngb = sb("lngb", [128, 512], F32)
        lnbb = sb("lnbb", [128, 512], F32)
        gqk = sb("gqk", [64, 4], F32)
        wT_sp = sb("wT_sp", [128, 4, 128], BF16)
        bsp = sb("bsp", [1, 4, 128], BF16)
        hT = sb("hT", [128, 8, 512], BF16)
        R_hT = Res("hT")
        R_lay = Res("layerconst")
        psum = [st.enter_context(nc.psum_tensor("ps%d" % i, [128, 512], F32)) for i in range(8)]
        psum16 = [p.bitcast(BF16) for p in psum]
        R_ps = [Res("ps%d" % i) for i in range(8)]

        class PPool:
            def __init__(self, idxs):
                self.idxs = idxs
                self.n = 0

            def next(self):
                i = self.idxs[self.n % len(self.idxs)]
                self.n += 1
                return psum[i][:, :], psum16[i][:, :], R_ps[i]

        ARENA_BYTES = int(os.environ.get("K_AR", "176")) * 1024
        arena_t = sb("arena", [128, ARENA_BYTES // 2], BF16)
        AR = Arena(arena_t)

        DMA("pool", idb[:], c_ident[:, :], (), [R_const])
        DMA("sp", tril[:], c_tril[:, :], (), [R_const])
        DMA("sp", idf[:], c_ident[:, :], (), [R_const])
        DMA("sp", cneg[:], c_cneg[:, :], (), [R_const])
        MS("dve", ones_bf[:], 1.0, [R_const])
        MS("dve", o64[:], 1.0 / 64, [R_const])
        MS("dve", smallc[:, 0:1], -0.5, [R_const])
        MS("dve", smallc[:, 1:2], EPS, [R_const])
        MS("dve", smallc[:, 2:3], 4 * EPS, [R_const])
        MS("dve", smallc[:, 3:4], 1.0, [R_const])
        one_t = smallc[:, 3:4]
        mhalf = smallc[:, 0:1]
        eps_t = smallc[:, 1:2]
        eps4_t = smallc[:, 2:3]

        AR.reset()
        oh_f = AR.alloc([384], F32)
        rb_ext = AR.alloc([8], F32)
        ones33 = AR.alloc([128], F32)
        lh = AR.alloc([8, 128], F32)
        Bsb = AR.alloc([8, 384], BF16)
        R_su = Res("setup")
        DMA("sp", oh_f[0:33, :], c_oh[:, :], (), [R_su])
        MS("dve", rb_ext[32:33, :], NEG, [R_su])
        DMA("sp", rb_ext[0:32, :], rel_bias[:, :], (), [R_su])
        MS("dve", ones33[0:33, :], 1.0, [R_su])
        for h in range(8):
            TS("dve", lh[0:33, h, :], ones33[0:33, :], rb_ext[0:33, h:h + 1], None, ALU.mult, None, [R_su], [R_su])
        for h in range(8):
            pB, _, rB = psum[h % 2], None, R_ps[h % 2]
            MM(pB[:, 0:384], lh[0:33, h, :], oh_f[0:33, :], True, True, [R_su], [rB])
            CP("dve", cfar[:, h:h + 1], pB[:, 300:301], [rB], [R_const])
            TS("dve", Bsb[:, h, :], pB[:, 0:384], cfar[:, h:h + 1], None, ALU.subtract, None, [rB, R_const], [R_su])
        DMA("sp", bias_scr[:, :], Bsb[:].rearrange("p h m -> p (h m)"), [R_su], [R_bias_scr])
        for h in range(8):
            for dl in range(2):
                src = bass.AP(bias_scr, h * 384 + 127 + 128 * dl, [[8 * 384 - 1, 128], [1, 128]])
                DMA("sp", biasT[:, h, dl, :], src, [R_bias_scr], [R_const])
        S_.barrier()

        W1_OFF = ARENA_BYTES - 8 * 1736 * 2 - 64
        W1_OFF -= W1_OFF % 64
        W1 = AR.alloc_at(W1_OFF, [8, 1736], BF16)
        R_W1 = Res("W1")
        lay = {}

        def load_W1(l_, first):
            for k in range(8):
                DMA("pool", W1[:, k, :], w_in[l_, k * 128:(k + 1) * 128, 0:1736], (), [R_W1], append=(k > 0))

        def load_W2(l_, W2_, R_W2_, extraW):
            for k in range(8):
                DMA("pool", W2_[:, k, :], w_in[l_, k * 128:(k + 1) * 128, C_U:C_U + 1536], (),
                    [R_W2_] + (list(extraW) if k == 0 else []), append=(k > 0))

        def load_W3(l_, Wm_, wbr_, wo_, R_W3_):
            th = []
            for k in range(8):
                th.append(lambda k=k: DMA("pool", Wm_[:, k, :], w_in[l_, k * 128:(k + 1) * 128, C_MA:C_MA + 2048], (),
                                          [R_W3_], append=(k > 0)))
            for n_ in range(2):
                for k in range(4):
                    th.append(lambda n_=n_, k=k: DMA("pool", wbr_[:, n_, k, :], w_branch[l_, n_, k * 128:(k + 1) * 128, :], (),
                                                     [R_W3_], append=True))
            for k in range(8):
                th.append(lambda k=k: DMA("pool", wo_[:, k, :], w_out[l_, k * 128:(k + 1) * 128, :], (), [R_W3_], append=True))
            return th

        def load_W1_th(l_):
            return [lambda k=k: DMA("pool", W1[:, k, :], w_in[l_, k * 128:(k + 1) * 128, 0:1736], (), [R_W1], append=(k > 0))
                    for k in range(8)]

        load_W1(0, True)
        out_toks = []
        for l in range(DEPTH):
            src_t = x_in if l == 0 else xmid
            dst_t = out if l == DEPTH - 1 else xmid
            R_src = Res("src") if l == 0 else R_xmid
            assert DEPTH <= 2
            AR.reset()
            DMA("sp", gbc[:], bass.AP(norm_g, l * D, [[0, 128], [1, D]]), (), [R_lay])
            DMA("sp", lngb[:], bass.AP(sgu_ln_g, l * 512, [[0, 128], [1, 512]]), (), [R_lay])
            DMA("sp", lnbb[:], bass.AP(sgu_ln_b, l * 512, [[0, 128], [1, 512]]), (), [R_lay])
            DMA("sp", gqk[:, 0:1], bass.AP(q_norm_g, l * 64, [[1, 64], [1, 1]]), (), [R_lay])
            DMA("sp", gqk[:, 1:2], bass.AP(k_norm_g, l * 64, [[1, 64], [1, 1]]), (), [R_lay])
            TS("dve", gqk[:, 2:3], gqk[:, 1:2], 0.125, None, ALU.mult, None, [R_lay], [R_lay])
            DMA("pool", bsp[:].rearrange("o g t -> o (g t)"), bass.AP(b_spatial, l * 512, [[0, 1], [1, 512]]), (), [R_lay])
            wsp_f = AR.alloc([4, 128], F32)
            wsp_m = AR.alloc([4, 128], BF16)
            R_w = Res("wsp")
            DMA("sp", wsp_f, w_spatial[l].rearrange("g t s -> t g s"), (), [R_w])
            for g in range(4):
                TT("dve", wsp_m[:, g, :], wsp_f[:, g, :], tril[:], ALU.mult, [R_w, R_const], [R_w])
            pT, pT16, rT = psum[0][:, :], psum16[0][:, :], R_ps[0]
            for g in range(4):
                TR(pT16[:, g * 128:(g + 1) * 128], wsp_m[:, g, :], [R_w], [rT])
            CP("dve", wT_sp[:].rearrange("p g t -> p (g t)"), pT16[:, 0:512], [rT], [R_lay])
            S_.barrier()

            AR.reset()
            hT2 = AR.alloc([8, 512], BF16)
            hTs = [hT[:, :, :], hT2]
            R_hTs = [R_hT, Res("hT2")]
            kT = [AR.alloc([S], BF16) for _ in range(2)]
            kidxT = [AR.alloc([S], BF16) for _ in range(2)]
            vaug = [AR.alloc([S // 128, 66], BF16) for _ in range(2)]
            R_kv = {}
            qT = [AR.alloc([512], BF16) for _ in range(2)]
            R_qT = [Res("qT0"), Res("qT1")]
            sga = [AR.alloc([512], BF16) for _ in range(3)]
            R_sga = [Res("sga0"), Res("sga1"), Res("sga2")]
            rdb = [AR.alloc([512], BF16) for _ in range(2)]
            R_rd = [Res("rd0"), Res("rd1")]
            yah = [AR.alloc([512], BF16) for _ in range(2)]
            R_yah = [Res("yah0"), Res("yah1")]
            maskT0 = AR.alloc([max(4, S // 128 - 4), 512], BF16)
            AR.off = max((AR.off + 63) // 64 * 64, 48 * 1024)
            X0 = AR.off
            xt = [AR.alloc([D], F32) for _ in range(2)]
            R_xt = [Res("xt0"), Res("xt1")]
            hb = [AR.alloc([D], BF16) for _ in range(2)]
            R_hb = [Res("hb0"), Res("hb1")]
            qidxT = AR.alloc([4, 512], BF16)
            Wki2 = AR.alloc([8, 128], BF16)
            pad_ = AR.alloc([1024], BF16)
            R_Wki2 = Res('Wki2')
            R_qi = Res("qidxT")
            Dg = AR.alloc([4, 8, 128], BF16)
            R_Dg = [Res("Dg%d" % i) for i in range(4)]
            Rb = [AR.alloc([512], BF16) for _ in range(4)]
            R_Rb = [Res("Rb%d" % i) for i in range(4)]
            Isb = [AR.alloc([S], F32) for _ in range(3)]
            R_I = [Res("I0"), Res("I1"), Res("I2")]
            junkI = AR.alloc([S], BF16)
            R_junk = Res("junkI")
            X1 = AR.off
            assert X1 - X0 >= 8 * 1536 * 2, (X0, X1)
            xonly_res = R_xt + R_hb + [R_qi] + R_Dg + R_Rb + R_I + [R_junk, R_Wki2]
            st_ = AR.alloc([16], F32)
            R_st = Res("st")
            widx = AR.alloc([4, 8], F32)
            R_wi = Res("widx")
            bis = [AR.alloc([8], F32) for _ in range(4)]
            R_bis = [Res("bis%d" % i) for i in range(4)]
            maskT = [maskT0, AR.alloc([S // 128, 512], BF16)]
            R_mT = [Res("maskT0"), Res("maskT1")]
            cnegb = AR.alloc([128], BF16)
            scr = [AR.alloc([512], F32) for _ in range(4)]
            R_scr = [Res("scr%d" % i) for i in range(4)]
            junkA = AR.alloc([S], BF16)
            R_junkA = Res("junkA")
            scrn = [0]
            Eb = [AR.alloc([512], BF16) for _ in range(4)]
            R_E = [Res("E%d" % i) for i in range(4)]
            sqb = [AR.alloc([512], BF16) for _ in range(2)]
            R_sqb = [Res("sqb0"), Res("sqb1")]
            sqn = [0]
            assert AR.off <= W1_OFF, (AR.off, W1_OFF)
            W2_OFF = X0
            W3_OFF = (X0 + 8 * 1536 * 2 + 63) // 64 * 64
            W2 = AR.alloc_at(W2_OFF, [8, 1536], BF16)
            R_W2 = Res("W2")
            assert W3_OFF + (8 * 2048 + 8 * D + 8 * D) * 2 + 256 <= W1_OFF
            pA = PPool([0, 1, 2])
            pS = PPool([3, 4])
            pI = PPool([5])
            pO = PPool([6, 7])
            mcnt = [0]

            def next_scr():
                i = scrn[0] % 4
                scrn[0] += 1
                return scr[i], R_scr[i]

            for i in range(2):
                MS("dve", vaug[i][:, :, 64:65], 1.0, [R_const])
            CP("dve", cnegb, cneg[:], [R_const], [R_const])

            def proj_fm(hTb, R_hTb, Wt, R_Wt, c0, M):
                p, _, rp = pA.next()
                for k in range(8):
                    MM(p[0:M, :], Wt[:, k, c0:c0 + M], hTb[:, k, :], k == 0, k == 7, [R_Wt, R_hTb], [rp])
                return p, rp

            def rms64(p, rp, gcol, out_ap, R_out):
                sq, R_sq = next_scr()
                s1, R_s1 = next_scr()
                sb_, R_sb_ = sqb[sqn[0] % 2], R_sqb[sqn[0] % 2]
                sqn[0] += 1
                ACT(sb_[0:64, :], p[0:64, :], AF.Square, [rp], [R_sb_])
                pm, _, rpm = pA.next()
                MM(pm[0:64, :], o64[0:64, 0:64], sb_[0:64, :], True, True, [R_sb_, R_const], [rpm])
                ACT(sq[0:64, :], pm[0:64, :], AF.Ln, [rpm, R_const], [R_sq], bias=eps_t[0:64, :])
                ACT(s1[0:64, :], sq[0:64, :], AF.Exp, [R_sq], [R_s1], scale=-0.5)
                STT(out_ap, p[0:64, :], gcol, s1[0:64, :], ALU.mult, ALU.mult, [rp, R_s1, R_lay], [R_out])

            def X_chunks(seq, tt, par):
                ch = []
                T0 = tt * 512
                hTb, R_hTb = hTs[par], R_hTs[par]
                kTs, kis, vas = kT[seq % 2], kidxT[seq % 2], vaug[seq % 2]
                rkv = Res("kv")
                R_kv[(seq, tt)] = rkv
                mT, R_mTb = maskT[par], R_mT[par]

                def c_norm(blk):
                    r0 = T0 + blk * 128
                    xb, R_xb = xt[blk % 2], R_xt[blk % 2]
                    hbl, R_hbl = hb[blk % 2], R_hb[blk % 2]
                    DMA("sp", xb, src_t[seq, r0:r0 + 128, :], [R_src], [R_xb])
                    c = blk * 3
                    ACT(hbl, xb, AF.Square, [R_xb], [R_hbl, R_st], accum=st_[:, c:c + 1])
                    TS("dve", st_[:, c + 1:c + 2], st_[:, c:c + 1], 1.0 / D, EPS, ALU.mult, ALU.add, [R_st], [R_st])
                    TT("pool", st_[:, c + 2:c + 3], st_[:, c + 1:c + 2], mhalf, ALU.pow, [R_st, R_const], [R_st])
                    STT(hbl, xb, st_[:, c + 2:c + 3], gbc[:], ALU.mult, ALU.mult, [R_xb, R_st, R_lay], [R_hbl])
                    _, p16, rp = pA.next()
                    for k in range(8):
                        TR(p16[:, k * 128:(k + 1) * 128], hbl[:, k * 128:(k + 1) * 128], [R_hbl], [rp])
                    CP("act", hTb[:, :, blk * 128:(blk + 1) * 128],
                       p16[:, 0:1024].rearrange("p (k t) -> p k t", k=8), [rp], [R_hTb])
                for blk in range(4):
                    ch.append(lambda blk=blk: c_norm(blk))

                def c_store():
                    R_hs[(l, seq, tt)] = Res("hs")
                    DMA("sp", hT_scr[seq, tt], hTb.rearrange("p k t -> p (k t)"), [R_hTb], [R_hs[(l, seq, tt)]])
                ch.append(c_store)

                def c_k():
                    p, rp = proj_fm(hTb, R_hTb, W1, R_W1, C_K, 64)
                    rms64(p, rp, gqk[:, 2:3], kTs[0:64, T0:T0 + 512], rkv)
                    p, _, rp = pA.next()
                    for k in range(8):
                        MM(p[:, :], Wki2[:, k, :], hTb[:, k, :], k == 0, k == 7, [R_Wki2, R_hTb], [rp])
                    CP("dve" if K_E1 else "act", kis[:, T0:T0 + 512], p[:, :], [rp], [rkv])
                ch.append(c_k)

                def c_v(blk):
                    p, _, rp = pA.next()
                    for k in range(8):
                        MM(p[:, 0:64], hTb[:, k, blk * 128:(blk + 1) * 128], W1[:, k, C_V:C_V + 64], k == 0, k == 7,
                           [R_W1, R_hTb], [rp])
                    for k in range(8):
                        MM(p[:, 64:72], hTb[:, k, blk * 128:(blk + 1) * 128], W1[:, k, C_WI:C_WI + 8], k == 0, k == 7,
                           [R_W1, R_hTb], [rp])
                    CP("dve" if K_E1 else "act", vas[:, tt * 4 + blk, 0:64], p[:, 0:64], [rp], [rkv])
                    TS("dve", widx[:, blk, :], p[:, 64:72], (8 ** -0.5) * (64 ** -0.5), None, ALU.mult, None, [rp], [R_wi])
                    for h in range(8):
                        if POOL_DG:
                            TS("pool", Dg[:, blk, h, :], idb[:], widx[:, blk, h:h + 1], 1.0, ALU.mult, ALU.mult,
                               [R_const, R_wi], [R_Dg[blk]])
                        else:
                            TS("dve", Dg[:, blk, h, :], idb[:], widx[:, blk, h:h + 1], None, ALU.mult, None,
                               [R_const, R_wi], [R_Dg[blk]])
                for blk in range(4):
                    ch.append(lambda blk=blk: c_v(blk))

                def c_qi(j):
                    p, rp = proj_fm(hTb, R_hTb, W1, R_W1, C_QI + 128 * j, 128)
                    CP("act", qidxT[:, j, :], p[:, :], [rp], [R_qi])
                for j in range(4):
                    ch.append(lambda j=j: c_qi(j))

                def c_idx(blk, j):
                    tb = tt * 4 + blk
                    Ib, R_Ib = Isb[tb % 3], R_I[tb % 3]
                    wdt = 512 if j < tt else (blk + 1) * 128
                    s0 = j * 512
                    pi, _, rpi = pI.next()
                    pend = []
                    for jp in range(5):
                        cur = []
                        if jp < 4:
                            for h in (2 * jp, 2 * jp + 1):
                                lo_ = (h % 2) * 64
                                psc, _, rps = pS.next()
                                MM(psc[:, 0:wdt], qidxT[lo_:lo_ + 64, h // 2, blk * 128:(blk + 1) * 128],
                                   kis[lo_:lo_ + 64, s0:s0 + wdt], True, True, [R_qi, R_kv[(seq, j)]], [rps])
                                cur.append((h, psc, rps))
                            for h, psc, rps in cur:
                                rb_, R_rb_ = Rb[h % 4], R_Rb[h % 4]
                                ACT(rb_[:, 0:wdt], psc[:, 0:wdt], AF.Relu, [rps], [R_rb_])
                        for hh in pend:
                            last = (hh == 7) and (j < tt or not PE_CAUSAL)
                            MM(pi[:, 0:wdt], Dg[:, blk, hh, :], Rb[hh % 4][:, 0:wdt], hh == 0, last, [R_Dg[blk], R_Rb[hh % 4]], [rpi])
                        pend = [h for h, _, _ in cur]
                    if j == tt and PE_CAUSAL:
                        MM(pi[:, blk * 128:wdt], idb[:], cnegb, False, True, [R_const], [rpi])
                    if j == tt and not PE_CAUSAL:
                        if blk > 0:
                            CP("act", Ib[:, s0:s0 + blk * 128], pi[:, 0:blk * 128], [rpi], [R_Ib])
                        TT("dve", Ib[:, s0 + blk * 128:s0 + wdt], pi[:, blk * 128:wdt], cneg[:], ALU.add,
                           [rpi, R_const], [R_Ib])
                    else:
                        CP("act", Ib[:, s0:s0 + wdt], pi[:, 0:wdt], [rpi], [R_Ib])

                def c_bis0(blk):
                    tb = tt * 4 + blk
                    Wd = (tb + 1) * 128
                    Ib, R_Ib = Isb[tb % 3], R_I[tb % 3]
                    bs, R_bs = bis[tb % 4], R_bis[tb % 4]
                    lo, W0, mid, cnt, stp, hi = (bs[:, i:i + 1] for i in range(6))
                    if Wd <= KTOP:
                        MS("dve", lo, -1e29, [R_bs])
                    else:
                        S_.op("dve", lambda e, o=lo, i_=Ib[:, 0:tb * 128]: e.tensor_reduce(
                            out=o, in_=i_, axis=mybir.AxisListType.X, op=ALU.min), [R_Ib], [R_bs])
                        S_.op("dve", lambda e, o=hi, i_=Ib[:, 0:Wd]: e.tensor_reduce(
                            out=o, in_=i_, axis=mybir.AxisListType.X, op=ALU.max), [R_Ib], [R_bs])
                        STT(W0, hi, 1.0, lo, ALU.add, ALU.subtract, [R_bs], [R_bs])

                def c_bis(blk, it0, it1):
                    tb = tt * 4 + blk
                    Wd = (tb + 1) * 128
                    if Wd <= KTOP:
                        return
                    Ib, R_Ib = Isb[tb % 3], R_I[tb % 3]
                    bs, R_bs = bis[tb % 4], R_bis[tb % 4]
                    lo, W0, mid, cnt, stp, hi = (bs[:, i:i + 1] for i in range(6))
                    for it in range(it0, it1):
                        c = 2.0 ** -(it + 1)
                        STT(mid, W0, c, lo, ALU.mult, ALU.add, [R_bs], [R_bs])
                        TS("dve", junkI[:, 0:Wd], Ib[:, 0:Wd], mid, None, ALU.is_ge, ALU.add, [R_Ib, R_bs],
                           [R_junk, R_bs], accum=cnt)
                        STT(stp, cnt, KTOP - 0.5, W0, ALU.is_ge, ALU.mult, [R_bs], [R_bs])
                        STT(lo, stp, c, lo, ALU.mult, ALU.add, [R_bs], [R_bs])

                def c_maskD(blk):
                    tb = tt * 4 + blk
                    Wd = (tb + 1) * 128
                    Ib, R_Ib = Isb[tb % 3], R_I[tb % 3]
                    bs, R_bs = bis[tb % 4], R_bis[tb % 4]
                    lo = bs[:, 0:1]
                    TS("dve", Ib[:, 0:Wd], Ib[:, 0:Wd], lo, None, ALU.is_ge, None, [R_Ib, R_bs], [R_Ib])

                def c_maskP(blk):
                    tb = tt * 4 + blk
                    mkb, R_mkb = Isb[tb % 3], R_I[tb % 3]
                    sb0 = 0
                    while sb0 <= tb:
                        n = min(4, tb + 1 - sb0)
                        p32, _, rp = pA.next()
                        for i in range(n):
                            S_.op("pe", lambda e, o=p32[:, i * 128:(i + 1) * 128], i_=mkb[:, (sb0 + i) * 128:(sb0 + i + 1) * 128]:
                                  e.transpose(out=o, in_=i_, identity=idf[:]), [R_mkb, R_const], [rp])
                        CP("act", mT[:, sb0:sb0 + n, blk * 128:(blk + 1) * 128],
                           p32[:, 0:n * 128].rearrange("p (k t) -> p k t", k=n), [rp], [R_mTb])
                        sb0 += n

                def add_I(blk):
                    for j in range(tt + 1):
                        ch.append(lambda blk=blk, j=j: c_idx(blk, j))

                def c_bis_pair(bA, bB, it):
                    c = 2.0 ** -(it + 1)
                    info = []
                    for b_ in (bA, bB):
                        tb = tt * 4 + b_
                        Wd = (tb + 1) * 128
                        act = Wd > KTOP
                        Ib, R_Ib = Isb[tb % 3], R_I[tb % 3]
                        bs, R_bs = bis[tb % 4], R_bis[tb % 4]
                        info.append((act, Wd, Ib, R_Ib, bs, R_bs))
                    actA, WdA, IA, R_IA, bsA, R_bsA = info[0]
                    actB, WdB, IB, R_IB, bsB, R_bsB = info[1]
                    if actB:
                        loB, W0B, midB, cntB, stpB = (bsB[:, i:i + 1] for i in range(5))
                        STT(midB, W0B, -c, loB, ALU.mult, ALU.subtract, [R_bsB], [R_bsB])
                        ACT(junkA[:, 0:WdB], IB[:, 0:WdB], AF.Sign, [R_IB, R_bsB], [R_junkA, R_bsB], bias=midB, accum=cntB)
                    if actA:
                        loA, W0A, midA, cntA, stpA = (bsA[:, i:i + 1] for i in range(5))
                        STT(midA, W0A, c, loA, ALU.mult, ALU.add, [R_bsA], [R_bsA])
                        TS("dve", junkI[:, 0:WdA], IA[:, 0:WdA], midA, None, ALU.is_ge, ALU.add, [R_IA, R_bsA],
                           [R_junk, R_bsA], accum=cntA)
                        STT(stpA, cntA, KTOP - 0.5, W0A, ALU.is_ge, ALU.mult, [R_bsA], [R_bsA])
                        STT(loA, stpA, c, loA, ALU.mult, ALU.add, [R_bsA], [R_bsA])
                    if actB:
                        STT(stpB, cntB, 2 * KTOP - WdB - 0.5, W0B, ALU.is_ge, ALU.mult, [R_bsB], [R_bsB])
                        STT(loB, stpB, c, loB, ALU.mult, ALU.add, [R_bsB], [R_bsB])

                def add_Bpair(bA, bB):
                    ch.append(lambda: c_bis0(bA))
                    ch.append(lambda: c_bis0(bB))
                    if (tt * 4 + bB + 1) * 128 > KTOP:
                        for it in range(NIT):
                            ch.append(lambda it=it: c_bis_pair(bA, bB, it))
                    ch.append(lambda: c_maskD(bA))
                    ch.append(lambda: c_maskD(bB))

                add_I(0); add_I(1); add_I(2)
                add_Bpair(0, 1)
                ch.append(lambda: c_maskP(0))
                add_I(3)
                ch.append(lambda: c_maskP(1))
                add_Bpair(2, 3)
                ch.append(lambda: c_maskP(2))
                ch.append(lambda: c_maskP(3))
                return ch

            def Y_stage(seq, tt, par, xch):
                hTb, R_hTb = hTs[par], R_hTs[par]
                kTs, vas = kT[seq % 2], vaug[seq % 2]
                mT, R_mTb = maskT[par], R_mT[par]
                NSB = 4 * (tt + 1)
                nsteps = 8 * (NSB + 2)
                state = {"done": 0, "step": 0}

                def pump():
                    state["step"] += 1
                    target = (len(xch) * state["step"]) // nsteps
                    while state["done"] < min(target, len(xch)):
                        xch[state["done"]]()
                        state["done"] += 1

                def proj_head(h):
                    p, rp = proj_fm(hTb, R_hTb, W1, R_W1, C_Q + 64 * h, 64)
                    pg, rpg = proj_fm(hTb, R_hTb, W1, R_W1, C_GA + 64 * h, 64)
                    rms64(p, rp, gqk[:, 0:1], qT[h % 2][0:64, :], R_qT[h % 2])
                    th, R_th = next_scr()
                    ACT(th[0:64, :], pg[0:64, :], AF.Exp, [rpg], [R_th], scale=-1.0)
                    ACT(th[0:64, :], th[0:64, :], AF.Ln, [R_th, R_const], [R_th], bias=one_t[0:64, :])
                    ACT(th[0:64, :], th[0:64, :], AF.Exp, [R_th], [R_th], scale=-1.0)
                    TT("dve", sga[h % 3][0:64, :], th[0:64, :], pg[0:64, :], ALU.mult, [R_th, rpg], [R_sga[h % 3]])

                def attn_head(h, fin_prev):
                    q_, R_q_ = qT[h % 2], R_qT[h % 2]
                    po, _, rpo = pO.next()
                    pend = []
                    for sbi in range(NSB + 2):
                        if sbi < NSB:
                            j = sbi - 4 * tt
                            c0 = max(j, 0) * 128
                            rk = R_kv[(seq, sbi // 4)]
                            plt, _, rpl = pS.next()
                            near = []
                            for blk in range(4):
                                tb = 4 * tt + blk
                                if sbi == tb:
                                    near.append((blk, 0))
                                elif sbi == tb - 1:
                                    near.append((blk, 1))
                            MM(plt[:, c0:512], kTs[0:64, sbi * 128:(sbi + 1) * 128], q_[0:64, c0:512], True, len(near) == 0,
                               [rk, R_q_], [rpl])
                            for ni, (blk, dl) in enumerate(near):
                                MM(plt[:, blk * 128:(blk + 1) * 128], idb[:], biasT[:, h, dl, :], False,
                                   ni == len(near) - 1, [R_const], [rpl])
                            e_, R_e_ = Eb[sbi % 4], R_E[sbi % 4]
                            ACT(e_[:, c0:512], plt[:, c0:512], AF.Exp, [rpl, R_const], [R_e_], bias=cfar[:, h:h + 1])
                            mcnt[0] += 1
                            TT("pool" if (POOL_MASK and mcnt[0] % MASK_MOD != 0) else "dve", e_[:, c0:512], e_[:, c0:512],
                               mT[:, sbi, c0:512], ALU.mult, [R_e_, R_mTb], [R_e_])
                            pend.append((sbi, c0, e_, R_e_, rk))
                        if sbi >= 2:
                            ps_, pc0, pe_, R_pe_, prk = pend.pop(0)
                            MM(po[0:65, pc0:512], vas[:, ps_, 0:65], pe_[:, pc0:512], ps_ == 0, ps_ == NSB - 1,
                               [prk, R_pe_, R_const], [rpo])
                        if sbi == 0 and fin_prev is not None:
                            fin_prev(0)
                        if sbi == 3 and fin_prev is not None:
                            fin_prev(1)
                        pump()
                    def finish(part, h=h, po=po, rpo=rpo):
                        rd, R_rd_ = rdb[h % 2], R_rd[h % 2]
                        if part == 0:
                            ln_, R_ln = next_scr()
                            ACT(ln_[64:65, :], po[64:65, :], AF.Ln, [rpo], [R_ln])
                            ACT(rd[64:65, :], ln_[64:65, :], AF.Exp, [R_ln], [R_rd_], scale=-1.0)
                            return
                        pb, _, rpb = pA.next()
                        MM(pb[0:64, :], ones_bf[64:65, 0:64], rd[64:65, :], True, True, [R_const, R_rd_], [rpb])
                        tmp, R_tmp = next_scr()
                        TT("dve", tmp[0:64, :], sga[h % 3][0:64, :], po[0:64, :], ALU.mult, [R_sga[h % 3], rpo], [R_tmp])
                        y_, R_y_ = yah[h % 2], R_yah[h % 2]
                        TT("dve", y_[0:64, :], tmp[0:64, :], pb[0:64, :], ALU.mult, [R_tmp, rpb], [R_y_])
                        key = (l, seq, tt)
                        if key not in R_ya:
                            R_ya[key] = Res("ya")
                        DMA("sp", ya_scr[seq, tt, h * 64:(h + 1) * 64, :], y_[0:64, :], [R_y_], [R_ya[key]])
                    return finish

                proj_head(0)
                fin = None
                for h in range(8):
                    fin_prev = fin
                    if h + 1 < 8:
                        proj_head(h + 1)
                    fin = attn_head(h, fin_prev)
                fin(0)
                fin(1)
                while state["done"] < len(xch):
                    xch[state["done"]]()
                    state["done"] += 1

            for half in range(2):
                CP("dve", Wki2[:, :, half * 64:(half + 1) * 64], W1[:, :, C_KI:C_KI + 64], [R_W1], [R_Wki2])
            tiles = [(seq, tt) for seq in range(NSEQ) for tt in range(NTT)]
            for c_ in X_chunks(tiles[0][0], tiles[0][1], 0):
                c_()
            for n_, (seq, tt) in enumerate(tiles):
                if n_ == len(tiles) - 1:
                    load_W2(l, W2, R_W2, xonly_res)
                if K_STOP == 1:
                    R_ya[(l, seq, tt)] = Res("ya")
                    continue
                nxt = X_chunks(tiles[n_ + 1][0], tiles[n_ + 1][1], (n_ + 1) % 2) if n_ + 1 < len(tiles) else []
                if INTERLEAVE:
                    Y_stage(seq, tt, n_ % 2, nxt)
                else:
                    Y_stage(seq, tt, n_ % 2, [])
                    for c_ in nxt:
                        c_()
            S_.barrier()

            AR.reset()
            Wm = AR.alloc_at(W3_OFF, [8, 2048], BF16)
            wbr = AR.alloc_at(W3_OFF + 8 * 2048 * 2, [2, 4, D], BF16)
            wo = AR.alloc_at(W3_OFF + 8 * 2048 * 2 + 8 * D * 2, [8, D], BF16)
            R_W3 = Res("W3")
            w3th = load_W3(l, Wm, wbr, wo, R_W3)
            hT2b = AR.alloc([8, 512], BF16)
            hTp = [hT[:, :, :], hT2b]
            R_hTp = [R_hT, Res("hT2b")]
            vln = AR.alloc([4, 512], BF16)
            R_vln = Res("vln")
            gu2 = AR.alloc([4, 512], BF16)
            R_gu = Res("gu2")
            sgb = AR.alloc([4, 512], BF16)
            R_sgb = Res("sgb")
            ybT = AR.alloc([4, 512], BF16)
            R_ybT = Res("ybT")
            scr2 = [AR.alloc([512], F32) for _ in range(8)]
            R_scr2 = [Res("s2_%d" % i) for i in range(8)]
            sc2n = [0]
            bnst = AR.alloc([4, 8], F32)
            R_bn = Res("bn")
            assert AR.off <= W2_OFF, (AR.off, W2_OFF)
            pA = PPool([0, 1, 2, 3, 4, 5])
            pS = PPool([6, 7])

            def nscr2():
                i = sc2n[0] % 8
                sc2n[0] += 1
                return scr2[i], R_scr2[i]

            def gelu2(p, rp, out_ap, R_out):
                a, R_a = nscr2()
                ACT(a, p, AF.Square, [rp], [R_a])
                TS("pool", a, a, 0.044715, 1.0, ALU.mult, ALU.add, [R_a], [R_a])
                b_, R_b = nscr2()
                TT("dve", b_, a, p, ALU.mult, [R_a, rp], [R_b])
                ACT(b_, b_, AF.Tanh, [R_b], [R_b], scale=0.7978845608028654)
                STT(out_ap, b_, 1.0, p, ALU.add, ALU.mult, [R_b, rp], [R_out])

            tiles2 = [(seq, tt) for seq in range(NSEQ) for tt in range(NTT)]

            def ld2(n_):
                sq_, t_ = tiles2[n_]
                DMA("sp", hTp[n_ % 2].rearrange("p k t -> p (k t)"), hT_scr[sq_, t_], [R_hs[(l, sq_, t_)]], [R_hTp[n_ % 2]])
            ld2(0)
            if True:
                for n2, (seq, tt) in enumerate(tiles2):
                    if n2 + 1 < len(tiles2):
                        ld2(n2 + 1)
                    per = -(-len(w3th) // len(tiles2))
                    for th_ in w3th[n2 * per:(n2 + 1) * per]:
                        th_()
                    hTc, R_hTc = hTp[n2 % 2], R_hTp[n2 % 2]
                    for blk in range(4):
                        p, _, rp = pA.next()
                        for k in range(8):
                            MM(p[:, :], hTc[:, k, blk * 128:(blk + 1) * 128], W2[:, k, 512:1024], k == 0, k == 7,
                               [R_W2, R_hTc], [rp])
                        g2, R_g2 = nscr2()
                        gelu2(p, rp, g2, R_g2)
                        S_.op("dve", lambda e, o=bnst[:, blk, 0:6], i_=g2: e.bn_stats(out=o, in_=i_), [R_g2], [R_bn])
                        S_.op("dve", lambda e, o=bnst[:, blk, 6:8], i_=bnst[:, blk, 0:6]: e.bn_aggr(out=o, in_=i_), [R_bn], [R_bn])
                        TS("dve", bnst[:, blk, 7:8], bnst[:, blk, 7:8], 4 * EPS, None, ALU.add, None, [R_bn], [R_bn])
                        TT("pool", bnst[:, blk, 7:8], bnst[:, blk, 7:8], mhalf, ALU.pow, [R_bn, R_const], [R_bn])
                        TS("dve", g2, g2, bnst[:, blk, 6:7], bnst[:, blk, 7:8], ALU.subtract, ALU.mult, [R_g2, R_bn], [R_g2])
                        TT("pool", g2, g2, lngb[:], ALU.mult, [R_g2, R_lay], [R_g2])
                        TT("pool", vln[:, blk, :], g2, lnbb[:], ALU.add, [R_g2, R_lay], [R_vln])
                    for c in range(4):
                        p, rp = proj_fm(hTc, R_hTc, W2, R_W2, c * 128, 128)
                        gelu2(p, rp, gu2[:, c, :], R_gu)
                        p, rp = proj_fm(hTc, R_hTc, W2, R_W2, 1024 + c * 128, 128)
                        th, R_th = nscr2()
                        ACT(th, p, AF.Tanh, [rp], [R_th], scale=0.5)
                        STT(sgb[:, c, :], th, 1.0, p, ALU.add, ALU.mult, [R_th, rp], [R_sgb])
                    for g in range(4):
                        p, _, rp = pS.next()
                        for blk in range(4):
                            MM(p[:, blk * 128:(blk + 1) * 128], vln[:, blk, g * 128:(g + 1) * 128], wT_sp[:, g, :], True, False,
                               [R_vln, R_lay], [rp])
                            MM(p[:, blk * 128:(blk + 1) * 128], ones_bf[0:1, :], bsp[0:1, g, :], False, True,
                               [R_const, R_lay], [rp])
                        t_, R_t = nscr2()
                        STT(t_, gu2[:, g, :], 0.25, p, ALU.mult, ALU.mult, [R_gu, rp], [R_t])
                        TT("pool", ybT[:, g, :], t_, sgb[:, g, :], ALU.mult, [R_t, R_sgb], [R_ybT])
                    key = (l, seq, tt)
                    R_yb[key] = Res("yb")
                    DMA("sp", yb_scr[seq, tt].rearrange("(g p) t -> p g t", p=128), ybT, [R_ybT], [R_yb[key]])
            S_.barrier()

            AR.reset()
            w1th = load_W1_th(l + 1) if l + 1 < DEPTH else []
            hT2c = AR.alloc([8, 512], BF16)
            hTp = [hT[:, :, :], hT2c]
            R_hTp = [R_hT, Res("hT2c")]
            yaTs = [AR.alloc([4, 512], BF16) for _ in range(2)]
            R_yaTs = [Res("yaT0"), Res("yaT1")]
            ybT3s = [AR.alloc([4, 512], BF16) for _ in range(2)]
            R_ybT3s = [Res("ybT30"), Res("ybT31")]
            mg = AR.alloc([8, 512], BF16)
            R_mg = Res("mg")
            scr3 = [AR.alloc([512], F32) for _ in range(8)]
            R_scr3 = [Res("s3_%d" % i) for i in range(8)]
            sc3n = [0]
            xres = [AR.alloc([D], F32) for _ in range(2)]
            R_xr = [Res("xr0"), Res("xr1")]
            assert AR.off <= W3_OFF, (AR.off, W3_OFF)
            pA = PPool([0, 1, 2, 3, 4, 5])
            pS = PPool([6, 7])

            def nscr3():
                i = sc3n[0] % 8
                sc3n[0] += 1
                return scr3[i], R_scr3[i]

            tiles3 = [(seq, tt) for seq in range(NSEQ) for tt in range(NTT)]

            def ld3(n_):
                sq_, t_ = tiles3[n_]
                key_ = (l, sq_, t_)
                DMA("sp", hTp[n_ % 2].rearrange("p k t -> p (k t)"), hT_scr[sq_, t_], [R_hs[key_]], [R_hTp[n_ % 2]])
                DMA("sp", yaTs[n_ % 2], ya_scr[sq_, t_].rearrange("(k p) t -> p k t", p=128), [R_ya[key_]], [R_yaTs[n_ % 2]])
                DMA("sp", ybT3s[n_ % 2], yb_scr[sq_, t_].rearrange("(k p) t -> p k t", p=128), [R_yb[key_]], [R_ybT3s[n_ % 2]])
            ld3(0)
            if True:
                for n3, (seq, tt) in enumerate(tiles3):
                    if n3 + 1 < len(tiles3):
                        ld3(n3 + 1)
                    per = -(-len(w1th) // len(tiles3)) if w1th else 0
                    for th_ in w1th[n3 * per:(n3 + 1) * per]:
                        th_()
                    hTc, R_hTc = hTp[n3 % 2], R_hTp[n3 % 2]
                    yaT, R_yaT = yaTs[n3 % 2], R_yaTs[n3 % 2]
                    ybT3, R_ybT3 = ybT3s[n3 % 2], R_ybT3s[n3 % 2]
                    for ec in range(8):
                        es = slice(ec * 128, (ec + 1) * 128)
                        pa, _, rpa = pA.next()
                        for k in range(4):
                            MM(pa, wbr[:, 0, k, es], yaT[:, k, :], k == 0, k == 3, [R_W3, R_yaT], [rpa])
                        pb, _, rpb = pA.next()
                        for k in range(4):
                            MM(pb, wbr[:, 1, k, es], ybT3[:, k, :], k == 0, k == 3, [R_W3, R_ybT3], [rpb])
                        pc, _, rpc = pA.next()
                        for k in range(8):
                            MM(pc, Wm[:, k, es], hTc[:, k, :], k == 0, k == 7, [R_W3, R_hTc], [rpc])
                        pd, _, rpd = pA.next()
                        for k in range(8):
                            MM(pd, Wm[:, k, 1024 + ec * 128:1024 + (ec + 1) * 128], hTc[:, k, :], k == 0, k == 7,
                               [R_W3, R_hTc], [rpd])
                        ta, R_ta = nscr3()
                        ACT(ta, pc, AF.Tanh, [rpc], [R_ta], scale=0.5)
                        tb_, R_tb = nscr3()
                        ACT(tb_, pd, AF.Tanh, [rpd], [R_tb], scale=0.5)
                        STT(ta, ta, 1.0, pa, ALU.add, ALU.mult, [R_ta, rpa], [R_ta])
                        STT(tb_, tb_, 1.0, pb, ALU.add, ALU.mult, [R_tb, rpb], [R_tb])
                        TT("pool", mg[:, ec, :], ta, tb_, ALU.add, [R_ta, R_tb], [R_mg])
                    for blk in range(4):
                        r0 = tt * 512 + blk * 128
                        xr, R_x = xres[blk % 2], R_xr[blk % 2]
                        DMA("sp", xr, src_t[seq, r0:r0 + 128, :], [R_src], [R_x])
                        for ch in range(2):
                            po, _, rpo = pS.next()
                            for ec in range(8):
                                MM(po, mg[:, ec, blk * 128:(blk + 1) * 128], wo[:, ec, ch * 512:(ch + 1) * 512], ec == 0, ec == 7,
                                   [R_mg, R_W3], [rpo])
                            STT(xr[:, ch * 512:(ch + 1) * 512], po, 0.5, xr[:, ch * 512:(ch + 1) * 512], ALU.mult, ALU.add,
                                [rpo, R_x], [R_x])
                        R_dst = [R_xmid] if l < DEPTH - 1 else []
                        tok = DMA("sp", dst_t[seq, r0:r0 + 128, :], xr, [R_x], R_dst)
                        if l == DEPTH - 1:
                            out_toks.append(tok)
            S_.barrier()
        S_.emit()
    global LAST_SCHED
    LAST_SCHED = S_
    print('arena peak', AR.peak)
    return nc


_CACHE = {}
LAST_SCHED = None


def kernel(**inputs):
    NC = 8
    x = np.ascontiguousarray(inputs["x"], dtype=np.float32)
    B, S, _ = x.shape
    NSEQ = B // NC
    DEPTH = inputs["w_in"].shape[0]
    key = (S, NSEQ, DEPTH)
    if key not in _CACHE:
        _CACHE[key] = build(S, NSEQ, DEPTH)
    nc = _CACHE[key]
    consts = host_consts()
    shared = {k: np.ascontiguousarray(v, dtype=np.float32) for k, v in inputs.items() if k != "x"}
    shared.update(consts)
    in_maps = []
    for c in range(NC):
        m = dict(shared)
        m["x"] = np.ascontiguousarray(x[c * NSEQ:(c + 1) * NSEQ])
        in_maps.append(m)
    res = run_bass_kernel_spmd(nc, in_maps, core_ids=list(range(NC)))
    return np.concatenate([np.asarray(r["out"], dtype=np.float32) for r in res.results], axis=0)
```

```python
import contextlib
import math
import numpy as np
import concourse.bass as bass
import concourse.mybir as mybir
from concourse.bass_utils import run_bass_kernel_spmd

F32 = mybir.dt.float32
BF16 = mybir.dt.bfloat16
ALU = mybir.AluOpType
AF = mybir.ActivationFunctionType

D = 1024
NIN = 5320
NEG = -30000.0
EPS = 1e-6
import os
INTERLEAVE = int(os.environ.get('K_IL', '1'))
POOL_MASK = int(os.environ.get('K_PM', '1'))
POOL_DG = int(os.environ.get('K_PD', '1'))
PE_CAUSAL = int(os.environ.get('K_PC', '1'))
K_E1 = int(os.environ.get('K_E1', '1'))
K_STOP = int(os.environ.get('K_STOP', '0'))
MASK_MOD = int(os.environ.get('K_MM', '3'))
BCH = int(os.environ.get('K_BCH', '1'))
RELU_DVE = [int(c) for c in os.environ.get('K_RD', '')]
NIT = int(os.environ.get('K_NIT', '16'))

C_Q, C_K, C_V, C_GA, C_QI, C_KI, C_WI, C_U, C_VB, C_GB, C_MA, C_MB = (
    0, 512, 576, 640, 1152, 1664, 1728, 1736, 2248, 2760, 3272, 4296)


class Res:
    __slots__ = ("name", "w", "r")

    def __init__(self, name):
        self.name = name
        self.w = None
        self.r = []


class Sched:
    ENG = ("pe", "act", "dve", "pool", "sp")
    NDS = 56

    def __init__(self, nc, stack):
        self.nc = nc
        self.sems = {}
        for e in self.ENG:
            self.sems[e] = stack.enter_context(nc.semaphore("s_" + e))
        for i in range(self.NDS):
            self.sems[("d", i)] = stack.enter_context(nc.semaphore("d%d" % i))
        self.dval = [0] * self.NDS
        self.dnext = {"sp": 0, "pool": 0, "act": 0}
        self.drange = {"sp": (0, 16), "act": (0, 16), "pool": (16, self.NDS)}
        self.cnt = {e: 0 for e in self.ENG}
        self.ops = {e: [] for e in self.ENG}
        self.waited = {e: {} for e in self.ENG}

    def _deps(self, eng, reads, writes):
        deps = {}

        def add(tok, kind):
            if tok is None:
                return
            k, v = tok
            if k == eng and (eng == "pe" or kind != "raw"):
                return
            if deps.get(k, 0) < v:
                deps[k] = v
        def addw(wt, kind):
            if isinstance(wt, list):
                for t_ in wt:
                    add(t_, kind)
            else:
                add(wt, kind)
        for r in reads:
            addw(r.w, "raw")
        for w in writes:
            addw(w.w, "waw")
            for t in w.r:
                add(t, "war")
        need = []
        wd = self.waited[eng]
        for k, v in deps.items():
            if wd.get(k, 0) < v:
                wd[k] = v
                need.append((k, v))
        return need

    def _mark(self, tok, reads, writes, append=False):
        for r in reads:
            r.r.append(tok)
            if len(r.r) > 64:
                best = {}
                for k, v in r.r:
                    if best.get(k, 0) < v:
                        best[k] = v
                r.r = list(best.items())
        for w in writes:
            if append and isinstance(w.w, list):
                w.w.append(tok)
            elif append:
                w.w = [tok]
            else:
                w.w = tok
            w.r = []

    def op(self, eng, fn, reads=(), writes=()):
        need = self._deps(eng, reads, writes)
        self.cnt[eng] += 1
        tok = (eng, self.cnt[eng])
        self.ops[eng].append((need, fn, (eng, 1)))
        self._mark(tok, reads, writes)
        return tok

    def dma(self, eng, fn, reads=(), writes=(), append=False):
        lo_, hi_ = self.drange[eng]
        j = lo_ + self.dnext[eng] % (hi_ - lo_)
        self.dnext[eng] += 1
        need = self._deps(eng, reads, writes)
        if self.dval[j] > 0:
            wd = self.waited[eng]
            if wd.get(("d", j), 0) < self.dval[j]:
                wd[("d", j)] = self.dval[j]
                need.append((("d", j), self.dval[j]))
        self.dval[j] += 16
        tok = (("d", j), self.dval[j])
        self.ops[eng].append((need, fn, (("d", j), 16)))
        self._mark(tok, reads, writes, append)
        return tok

    def barrier(self):
        for e in self.ENG:
            need = []
            wd = self.waited[e]
            for k in self.ENG:
                if k != e and self.cnt[k] > wd.get(k, 0):
                    wd[k] = self.cnt[k]
                    need.append((k, self.cnt[k]))
            for j in range(self.NDS):
                if self.dval[j] > wd.get(("d", j), 0):
                    wd[("d", j)] = self.dval[j]
                    need.append((("d", j), self.dval[j]))
            self.ops[e].append((need, None, None))

    def emit(self):
        nc = self.nc
        sems = self.sems
        with nc.Block() as block:
            def run(e_name):
                def body(engine):
                    for need, fn, inc in self.ops[e_name]:
                        for k, v in need:
                            engine.wait_ge(sems[k], v)
                        if fn is not None:
                            fn(engine).then_inc(sems[inc[0]], inc[1])
                return body
            block.tensor(run("pe"))
            block.scalar(run("act"))
            block.vector(run("dve"))
            block.gpsimd(run("pool"))
            block.sync(run("sp"))


class Arena:
    def __init__(self, t16):
        self.t16 = t16
        self.t32 = t16.bitcast(F32)
        self.off = 0
        self.cap = t16.shape[1] * 2
        self.peak = 0

    def reset(self):
        self.off = 0

    def alloc_at(self, off, free_shape, dt):
        save = self.off
        self.off = off
        ap = self.alloc(free_shape, dt)
        self.off = save
        return ap

    def alloc_dual(self, n32):
        self.off = (self.off + 63) // 64 * 64
        o = self.off
        self.off += n32 * 4
        self.peak = max(self.peak, self.off)
        assert self.off <= self.cap, ("arena overflow", self.off, self.cap)
        return self.t32[:, o // 4:o // 4 + n32], self.t16[:, o // 2:o // 2 + 2 * n32]

    def alloc(self, free_shape, dt, parts=128):
        es = 4 if dt == F32 else 2
        n = int(np.prod(free_shape))
        self.off = (self.off + 63) // 64 * 64
        o = self.off
        self.off += n * es
        self.peak = max(self.peak, self.off)
        assert self.off <= self.cap, ("arena overflow", self.off, self.cap)
        base = self.t32 if dt == F32 else self.t16
        ap = base[0:parts, o // es:o // es + n]
        if len(free_shape) == 2:
            ap = ap.rearrange("p (a b) -> p a b", a=free_shape[0])
        elif len(free_shape) == 3:
            ap = ap.rearrange("p (a b c) -> p a b c", a=free_shape[0], b=free_shape[1])
        return ap


def _bucket_table():
    oh = np.zeros((33, 384), np.float32)
    for m in range(384):
        rel = m - 127
        if rel < 0:
            oh[32, m] = 1.0
            continue
        if rel < 16:
            b = rel
        else:
            nf = np.float32(max(rel, 1))
            v = np.log(nf / np.float32(16.0)) / np.float32(math.log(128 / 16)) * np.float32(16.0)
            b = min(16 + int(np.float32(v).astype(np.int32)), 31)
        oh[b, m] = 1.0
    return oh


def host_consts():
    t = np.arange(128)
    return {
        "c_ident": np.eye(128, dtype=np.float32),
        "c_tril": (t[:, None] >= t[None, :]).astype(np.float32),
        "c_cneg": np.where(t[None, :] <= t[:, None], 0.0, -1e30).astype(np.float32),
        "c_oh": _bucket_table(),
    }


def build(S, NSEQ, DEPTH, dbg=None):
    nc = bass.Bass("TRN2", target_bir_lowering=False)
    NTT = S // 512
    KTOP = min(256, S // 4)
    dt_in = lambda name, shape: nc.dram_tensor(name, shape, F32, kind="ExternalInput")
    x_in = dt_in("x", [NSEQ, S, D])
    norm_g = dt_in("norm_g", [DEPTH, D])
    w_in = dt_in("w_in", [DEPTH, D, NIN])
    q_norm_g = dt_in("q_norm_g", [DEPTH, 64])
    k_norm_g = dt_in("k_norm_g", [DEPTH, 64])
    rel_bias = dt_in("rel_bias", [32, 8])
    sgu_ln_g = dt_in("sgu_ln_g", [DEPTH, 512])
    sgu_ln_b = dt_in("sgu_ln_b", [DEPTH, 512])
    w_spatial = dt_in("w_spatial", [DEPTH, 4, 128, 128])
    b_spatial = dt_in("b_spatial", [DEPTH, 4, 128])
    w_branch = dt_in("w_branch", [DEPTH, 2, 512, D])
    w_out = dt_in("w_out", [DEPTH, D, D])
    c_ident = dt_in("c_ident", [128, 128])
    c_tril = dt_in("c_tril", [128, 128])
    c_cneg = dt_in("c_cneg", [128, 128])
    c_oh = dt_in("c_oh", [33, 384])
    out = nc.dram_tensor("out", [NSEQ, S, D], F32, kind="ExternalOutput")
    xmid = nc.dram_tensor("xmid", [NSEQ, S, D], F32, kind="Internal")
    hT_scr = nc.dram_tensor("hT_scr", [NSEQ, NTT, 128, 8 * 512], BF16, kind="Internal")
    ya_scr = nc.dram_tensor("ya_scr", [NSEQ, NTT, 512, 512], BF16, kind="Internal")
    yb_scr = nc.dram_tensor("yb_scr", [NSEQ, NTT, 512, 512], BF16, kind="Internal")
    bias_scr = nc.dram_tensor("bias_scr", [128, 8 * 384], BF16, kind="Internal")
    R_xmid, R_hT_scr, R_ya_scr, R_yb_scr, R_bias_scr = (Res(n) for n in ("xmid", "hTs", "yas", "ybs", "bs"))
    R_ya = {}
    R_yb = {}
    R_hs = {}

    with contextlib.ExitStack() as st:
        S_ = Sched(nc, st)
        sb = lambda name, shape, dt: st.enter_context(nc.sbuf_tensor(name, shape, dt))
        def MM(out_, lhsT, rhs, start, stop, R, W):
            S_.op("pe", lambda e: e.matmul(out_, lhsT=lhsT, rhs=rhs, start=start, stop=stop), R, W)

        def TR(out_, in_, R, W):
            S_.op("pe", lambda e: e.transpose(out=out_, in_=in_, identity=idb[:]), list(R) + [R_const], W)

        def ACT(out_, in_, func, R, W, bias=None, scale=None, accum=None):
            kw = {}
            if bias is not None:
                kw["bias"] = bias
            if scale is not None:
                kw["scale"] = scale
            if accum is not None:
                kw["accum_out"] = accum
            S_.op("act", lambda e: e.activation(out=out_, in_=in_, func=func, **kw), R, W)

        def TS(eng, out_, in0, s1, s2, op0, op1, R, W, accum=None):
            kw = {}
            if accum is not None:
                kw["accum_out"] = accum
            if op1 is None:
                S_.op(eng, lambda e: e.tensor_scalar(out=out_, in0=in0, scalar1=s1, scalar2=None, op0=op0, **kw), R, W)
            else:
                S_.op(eng, lambda e: e.tensor_scalar(out=out_, in0=in0, scalar1=s1, scalar2=s2, op0=op0, op1=op1, **kw), R, W)

        def TT(eng, out_, in0, in1, op, R, W):
            S_.op(eng, lambda e: e.tensor_tensor(out=out_, in0=in0, in1=in1, op=op), R, W)

        def STT(out_, in0, scalar, in1, op0, op1, R, W):
            S_.op("dve", lambda e: e.scalar_tensor_tensor(out=out_, in0=in0, scalar=scalar, in1=in1, op0=op0, op1=op1), R, W)

        def CP(eng, out_, in_, R, W):
            if eng == "act":
                S_.op("act", lambda e: e.copy(out=out_, in_=in_), R, W)
            else:
                S_.op(eng, lambda e: e.tensor_copy(out=out_, in_=in_), R, W)

        def MS(eng, ap, val, W):
            S_.op(eng, lambda e: e.memset(ap, val), (), W)

        def DMA(q, out_, in_, R, W, append=False):
            return S_.dma(q, lambda e: e.dma_start(out=out_, in_=in_), R, W, append)

        R_const = Res("const")
        idb = sb("idb", [128, 128], BF16)
        tril = sb("tril", [128, 128], F32)
        idf = sb("idf", [128, 128], F32)
        cneg = sb("cneg", [128, 128], F32)
        ones_bf = sb("ones_bf", [128, 128], BF16)
        o64 = sb("o64", [128, 128], BF16)
        smallc = sb("smallc", [128, 8], F32)
        biasT = sb("biasT", [128, 8, 2, 128], BF16)
        cfar = sb("cfar", [128, 8], F32)
        gbc = sb("gbc", [128, D], F32)
        lngb = sb("lngb", [128, 512], F32)
        lnbb = sb("lnbb", [128, 512], F32)
        gqk = sb("gqk", [64, 4], F32)
        wT_sp = sb("wT_sp", [128, 4, 128], BF16)
        bsp = sb("bsp", [1, 4, 128], BF16)
        hT = sb("hT", [128, 8, 512], BF16)
        R_hT = Res("hT")
        R_lay = Res("layerconst")
        psum = [st.enter_context(nc.psum_tensor("ps%d" % i, [128, 512], F32)) for i in range(8)]
        psum16 = [p.bitcast(BF16) for p in psum]
        R_ps = [Res("ps%d" % i) for i in range(8)]

        class PPool:
            def __init__(self, idxs):
                self.idxs = idxs
                self.n = 0

            def next(self):
                i = self.idxs[self.n % len(self.idxs)]
                self.n += 1
                return psum[i][:, :], psum16[i][:, :], R_ps[i]

        ARENA_BYTES = int(os.environ.get("K_AR", "176")) * 1024
        arena_t = sb("arena", [128, ARENA_BYTES // 2], BF16)
        AR = Arena(arena_t)

        DMA("pool", idb[:], c_ident[:, :], (), [R_const])
        DMA("sp", tril[:], c_tril[:, :], (), [R_const])
        DMA("sp", idf[:], c_ident[:, :], (), [R_const])
        DMA("sp", cneg[:], c_cneg[:, :], (), [R_const])
        MS("dve", ones_bf[:], 1.0, [R_const])
        MS("dve", o64[:], 1.0 / 64, [R_const])
        MS("dve", smallc[:, 0:1], -0.5, [R_const])
        MS("dve", smallc[:, 1:2], EPS, [R_const])
        MS("dve", smallc[:, 2:3], 4 * EPS, [R_const])
        MS("dve", smallc[:, 3:4], 1.0, [R_const])
        one_t = smallc[:, 3:4]
        mhalf = smallc[:, 0:1]
        eps_t = smallc[:, 1:2]
        eps4_t = smallc[:, 2:3]

        AR.reset()
        oh_f = AR.alloc([384], F32)
        rb_ext = AR.alloc([8], F32)
        ones33 = AR.alloc([128], F32)
        lh = AR.alloc([8, 128], F32)
        Bsb = AR.alloc([8, 384], BF16)
        R_su = Res("setup")
        DMA("sp", oh_f[0:33, :], c_oh[:, :], (), [R_su])
        MS("dve", rb_ext[32:33, :], NEG, [R_su])
        DMA("sp", rb_ext[0:32, :], rel_bias[:, :], (), [R_su])
        MS("dve", ones33[0:33, :], 1.0, [R_su])
        for h in range(8):
            TS("dve", lh[0:33, h, :], ones33[0:33, :], rb_ext[0:33, h:h + 1], None, ALU.mult, None, [R_su], [R_su])
        for h in range(8):
            pB, _, rB = psum[h % 2], None, R_ps[h % 2]
            MM(pB[:, 0:384], lh[0:33, h, :], oh_f[0:33, :], True, True, [R_su], [rB])
            CP("dve", cfar[:, h:h + 1], pB[:, 300:301], [rB], [R_const])
            TS("dve", Bsb[:, h, :], pB[:, 0:384], cfar[:, h:h + 1], None, ALU.subtract, None, [rB, R_const], [R_su])
        DMA("sp", bias_scr[:, :], Bsb[:].rearrange("p h m -> p (h m)"), [R_su], [R_bias_scr])
        for h in range(8):
            for dl in range(2):
                src = bass.AP(bias_scr, h * 384 + 127 + 128 * dl, [[8 * 384 - 1, 128], [1, 128]])
                DMA("sp", biasT[:, h, dl, :], src, [R_bias_scr], [R_const])
        S_.barrier()

        W1_OFF = ARENA_BYTES - 8 * 1736 * 2 - 64
        W1_OFF -= W1_OFF % 64
        W1 = AR.alloc_at(W1_OFF, [8, 1736], BF16)
        R_W1 = Res("W1")
        lay = {}

        def load_W1(l_, first):
            for k in range(8):
                DMA("pool", W1[:, k, :], w_in[l_, k * 128:(k + 1) * 128, 0:1736], (), [R_W1], append=(k > 0))

        def load_W2(l_, W2_, R_W2_, extraW):
            for k in range(8):
                DMA("pool", W2_[:, k, :], w_in[l_, k * 128:(k + 1) * 128, C_U:C_U + 1536], (),
                    [R_W2_] + (list(extraW) if k == 0 else []), append=(k > 0))

        def load_W3(l_, Wm_, wbr_, wo_, R_W3_):
            th = []
            for k in range(8):
                th.append(lambda k=k: DMA("pool", Wm_[:, k, :], w_in[l_, k * 128:(k + 1) * 128, C_MA:C_MA + 2048], (),
                                          [R_W3_], append=(k > 0)))
            for n_ in range(2):
                for k in range(4):
                    th.append(lambda n_=n_, k=k: DMA("pool", wbr_[:, n_, k, :], w_branch[l_, n_, k * 128:(k + 1) * 128, :], (),
                                                     [R_W3_], append=True))
            for k in range(8):
                th.append(lambda k=k: DMA("pool", wo_[:, k, :], w_out[l_, k * 128:(k + 1) * 128, :], (), [R_W3_], append=True))
            return th

        def load_W1_th(l_):
            return [lambda k=k: DMA("pool", W1[:, k, :], w_in[l_, k * 128:(k + 1) * 128, 0:1736], (), [R_W1], append=(k > 0))
                    for k in range(8)]

        load_W1(0, True)
        out_toks = []
        for l in range(DEPTH):
            src_t = x_in if l == 0 else xmid
            dst_t = out if l == DEPTH - 1 else xmid
            R_src = Res("src") if l == 0 else R_xmid
            assert DEPTH <= 2
            AR.reset()
            DMA("sp", gbc[:], bass.AP(norm_g, l * D, [[0, 128], [1, D]]), (), [R_lay])
            DMA("sp", lngb[:], bass.AP(sgu_ln_g, l * 512, [[0, 128], [1, 512]]), (), [R_lay])
            DMA("sp", lnbb[:], bass.AP(sgu_ln_b, l * 512, [[0, 128], [1, 512]]), (), [R_lay])
            DMA("sp", gqk[:, 0:1], bass.AP(q_norm_g, l * 64, [[1, 64], [1, 1]]), (), [R_lay])
            DMA("sp", gqk[:, 1:2], bass.AP(k_norm_g, l * 64, [[1, 64], [1, 1]]), (), [R_lay])
            TS("dve", gqk[:, 2:3], gqk[:, 1:2], 0.125, None, ALU.mult, None, [R_lay], [R_lay])
            DMA("pool", bsp[:].rearrange("o g t -> o (g t)"), bass.AP(b_spatial, l * 512, [[0, 1], [1, 512]]), (), [R_lay])
            wsp_f = AR.alloc([4, 128], F32)
            wsp_m = AR.alloc([4, 128], BF16)
            R_w = Res("wsp")
            DMA("sp", wsp_f, w_spatial[l].rearrange("g t s -> t g s"), (), [R_w])
            for g in range(4):
                TT("dve", wsp_m[:, g, :], wsp_f[:, g, :], tril[:], ALU.mult, [R_w, R_const], [R_w])
            pT, pT16, rT = psum[0][:, :], psum16[0][:, :], R_ps[0]
            for g in range(4):
                TR(pT16[:, g * 128:(g + 1) * 128], wsp_m[:, g, :], [R_w], [rT])
            CP("dve", wT_sp[:].rearrange("p g t -> p (g t)"), pT16[:, 0:512], [rT], [R_lay])
            S_.barrier()

            AR.reset()
            hT2 = AR.alloc([8, 512], BF16)
            hTs = [hT[:, :, :], hT2]
            R_hTs = [R_hT, Res("hT2")]
            kT = [AR.alloc([S], BF16) for _ in range(2)]
            kidxT = [AR.alloc([S], BF16) for _ in range(2)]
            vaug = [AR.alloc([S // 128, 66], BF16) for _ in range(2)]
            R_kv = {}
            qT = [AR.alloc([512], BF16) for _ in range(2)]
            R_qT = [Res("qT0"), Res("qT1")]
            sga = [AR.alloc([512], BF16) for _ in range(3)]
            R_sga = [Res("sga0"), Res("sga1"), Res("sga2")]
            rdb = [AR.alloc([512], BF16) for _ in range(2)]
            R_rd = [Res("rd0"), Res("rd1")]
            yah = [AR.alloc([512], BF16) for _ in range(2)]
            R_yah = [Res("yah0"), Res("yah1")]
            maskT0 = AR.alloc([max(4, S // 128 - 4), 512], BF16)
            AR.off = max((AR.off + 63) // 64 * 64, 48 * 1024)
            X0 = AR.off
            xt = [AR.alloc([D], F32) for _ in range(2)]
            R_xt = [Res("xt0"), Res("xt1")]
            hb = [AR.alloc([D], BF16) for _ in range(2)]
            R_hb = [Res("hb0"), Res("hb1")]
            qidxT = AR.alloc([4, 512], BF16)
            Wki2 = AR.alloc([8, 128], BF16)
            pad_ = AR.alloc([1024], BF16)
            R_Wki2 = Res('Wki2')
            R_qi = Res("qidxT")
            Dg = AR.alloc([4, 8, 128], BF16)
            R_Dg = [Res("Dg%d" % i) for i in range(4)]
            Rb = [AR.alloc([512], BF16) for _ in range(4)]
            R_Rb = [Res("Rb%d" % i) for i in range(4)]
            Isb = [AR.alloc([S], F32) for _ in range(3)]
            R_I = [Res("I0"), Res("I1"), Res("I2")]
            junkI = AR.alloc([S], BF16)
            R_junk = Res("junkI")
            X1 = AR.off
            assert X1 - X0 >= 8 * 1536 * 2, (X0, X1)
            xonly_res = R_xt + R_hb + [R_qi] + R_Dg + R_Rb + R_I + [R_junk, R_Wki2]
            st_ = AR.alloc([16], F32)
            R_st = Res("st")
            widx = AR.alloc([4, 8], F32)
            R_wi = Res("widx")
            bis = [AR.alloc([8], F32) for _ in range(4)]
            R_bis = [Res("bis%d" % i) for i in range(4)]
            maskT = [maskT0, AR.alloc([S // 128, 512], BF16)]
            R_mT = [Res("maskT0"), Res("maskT1")]
            cnegb = AR.alloc([128], BF16)
            scr = [AR.alloc([512], F32) for _ in range(4)]
            R_scr = [Res("scr%d" % i) for i in range(4)]
            junkA = AR.alloc([S], BF16)
            R_junkA = Res("junkA")
            scrn = [0]
            Eb = [AR.alloc([512], BF16) for _ in range(4)]
            R_E = [Res("E%d" % i) for i in range(4)]
            sqb = [AR.alloc([512], BF16) for _ in range(2)]
            R_sqb = [Res("sqb0"), Res("sqb1")]
            sqn = [0]
            assert AR.off <= W1_OFF, (AR.off, W1_OFF)
            W2_OFF = X0
            W3_OFF = (X0 + 8 * 1536 * 2 + 63) // 64 * 64
            W2 = AR.alloc_at(W2_OFF, [8, 1536], BF16)
            R_W2 = Res("W2")
            assert W3_OFF + (8 * 2048 + 8 * D + 8 * D) * 2 + 256 <= W1_OFF
            pA = PPool([0, 1, 2])
            pS = PPool([3, 4])
            pI = PPool([5])
            pO = PPool([6, 7])
            mcnt = [0]

            def next_scr():
                i = scrn[0] % 4
                scrn[0] += 1
                return scr[i], R_scr[i]

            for i in range(2):
                MS("dve", vaug[i][:, :, 64:65], 1.0, [R_const])
            CP("dve", cnegb, cneg[:], [R_const], [R_const])

            def proj_fm(hTb, R_hTb, Wt, R_Wt, c0, M):
                p, _, rp = pA.next()
                for k in range(8):
                    MM(p[0:M, :], Wt[:, k, c0:c0 + M], hTb[:, k, :], k == 0, k == 7, [R_Wt, R_hTb], [rp])
                return p, rp

            def rms64(p, rp, gcol, out_ap, R_out):
                sq, R_sq = next_scr()
                s1, R_s1 = next_scr()
                sb_, R_sb_ = sqb[sqn[0] % 2], R_sqb[sqn[0] % 2]
                sqn[0] += 1
                ACT(sb_[0:64, :], p[0:64, :], AF.Square, [rp], [R_sb_])
                pm, _, rpm = pA.next()
                MM(pm[0:64, :], o64[0:64, 0:64], sb_[0:64, :], True, True, [R_sb_, R_const], [rpm])
                ACT(sq[0:64, :], pm[0:64, :], AF.Ln, [rpm, R_const], [R_sq], bias=eps_t[0:64, :])
                ACT(s1[0:64, :], sq[0:64, :], AF.Exp, [R_sq], [R_s1], scale=-0.5)
                STT(out_ap, p[0:64, :], gcol, s1[0:64, :], ALU.mult, ALU.mult, [rp, R_s1, R_lay], [R_out])

            def X_chunks(seq, tt, par):
                ch = []
                T0 = tt * 512
                hTb, R_hTb = hTs[par], R_hTs[par]
                kTs, kis, vas = kT[seq % 2], kidxT[seq % 2], vaug[seq % 2]
                rkv = Res("kv")
                R_kv[(seq, tt)] = rkv
                mT, R_mTb = maskT[par], R_mT[par]

                def c_norm(blk):
                    r0 = T0 + blk * 128
                    xb, R_xb = xt[blk % 2], R_xt[blk % 2]
                    hbl, R_hbl = hb[blk % 2], R_hb[blk % 2]
                    DMA("sp", xb, src_t[seq, r0:r0 + 128, :], [R_src], [R_xb])
                    c = blk * 3
                    ACT(hbl, xb, AF.Square, [R_xb], [R_hbl, R_st], accum=st_[:, c:c + 1])
                    TS("dve", st_[:, c + 1:c + 2], st_[:, c:c + 1], 1.0 / D, EPS, ALU.mult, ALU.add, [R_st], [R_st])
                    TT("pool", st_[:, c + 2:c + 3], st_[:, c + 1:c + 2], mhalf, ALU.pow, [R_st, R_const], [R_st])
                    STT(hbl, xb, st_[:, c + 2:c + 3], gbc[:], ALU.mult, ALU.mult, [R_xb, R_st, R_lay], [R_hbl])
                    _, p16, rp = pA.next()
                    for k in range(8):
                        TR(p16[:, k * 128:(k + 1) * 128], hbl[:, k * 128:(k + 1) * 128], [R_hbl], [rp])
                    CP("act", hTb[:, :, blk * 128:(blk + 1) * 128],
                       p16[:, 0:1024].rearrange("p (k t) -> p k t", k=8), [rp], [R_hTb])
                for blk in range(4):
                    ch.append(lambda blk=blk: c_norm(blk))

                def c_store():
                    R_hs[(l, seq, tt)] = Res("hs")
                    DMA("sp", hT_scr[seq, tt], hTb.rearrange("p k t -> p (k t)"), [R_hTb], [R_hs[(l, seq, tt)]])
                ch.append(c_store)

                def c_k():
                    p, rp = proj_fm(hTb, R_hTb, W1, R_W1, C_K, 64)
                    rms64(p, rp, gqk[:, 2:3], kTs[0:64, T0:T0 + 512], rkv)
                    p, _, rp = pA.next()
                    for k in range(8):
                        MM(p[:, :], Wki2[:, k, :], hTb[:, k, :], k == 0, k == 7, [R_Wki2, R_hTb], [rp])
                    CP("dve" if K_E1 else "act", kis[:, T0:T0 + 512], p[:, :], [rp], [rkv])
                ch.append(c_k)

                def c_v(blk):
                    p, _, rp = pA.next()
                    for k in range(8):
                        MM(p[:, 0:64], hTb[:, k, blk * 128:(blk + 1) * 128], W1[:, k, C_V:C_V + 64], k == 0, k == 7,
                           [R_W1, R_hTb], [rp])
                    for k in range(8):
                        MM(p[:, 64:72], hTb[:, k, blk * 128:(blk + 1) * 128], W1[:, k, C_WI:C_WI + 8], k == 0, k == 7,
                           [R_W1, R_hTb], [rp])
                    CP("dve" if K_E1 else "act", vas[:, tt * 4 + blk, 0:64], p[:, 0:64], [rp], [rkv])
                    TS("dve", widx[:, blk, :], p[:, 64:72], (8 ** -0.5) * (64 ** -0.5), None, ALU.mult, None, [rp], [R_wi])
                    for h in range(8):
                        if POOL_DG:
                            TS("pool", Dg[:, blk, h, :], idb[:], widx[:, blk, h:h + 1], 1.0, ALU.mult, ALU.mult,
                               [R_const, R_wi], [R_Dg[blk]])
                        else:
                            TS("dve", Dg[:, blk, h, :], idb[:], widx[:, blk, h:h + 1], None, ALU.mult, None,
                               [R_const, R_wi], [R_Dg[blk]])
                for blk in range(4):
                    ch.append(lambda blk=blk: c_v(blk))

                def c_qi(j):
                    p, rp = proj_fm(hTb, R_hTb, W1, R_W1, C_QI + 128 * j, 128)
                    CP("act", qidxT[:, j, :], p[:, :], [rp], [R_qi])
                for j in range(4):
                    ch.append(lambda j=j: c_qi(j))

                def c_idx(blk, j):
                    tb = tt * 4 + blk
                    Ib, R_Ib = Isb[tb % 3], R_I[tb % 3]
                    wdt = 512 if j < tt else (blk + 1) * 128
                    s0 = j * 512
                    pi, _, rpi = pI.next()
                    pend = []
                    for jp in range(5):
                        cur = []
                        if jp < 4:
                            for h in (2 * jp, 2 * jp + 1):
                                lo_ = (h % 2) * 64
                                psc, _, rps = pS.next()
                                MM(psc[:, 0:wdt], qidxT[lo_:lo_ + 64, h // 2, blk * 128:(blk + 1) * 128],
                                   kis[lo_:lo_ + 64, s0:s0 + wdt], True, True, [R_qi, R_kv[(seq, j)]], [rps])
                                cur.append((h, psc, rps))
                            for h, psc, rps in cur:
                                rb_, R_rb_ = Rb[h % 4], R_Rb[h % 4]
                                if h in RELU_DVE:
                                    TS("dve", rb_[:, 0:wdt], psc[:, 0:wdt], 0.0, None, ALU.max, None, [rps], [R_rb_])
                                else:
                                    ACT(rb_[:, 0:wdt], psc[:, 0:wdt], AF.Relu, [rps], [R_rb_])
                        for hh in pend:
                            last = (hh == 7) and (j < tt or not PE_CAUSAL)
                            MM(pi[:, 0:wdt], Dg[:, blk, hh, :], Rb[hh % 4][:, 0:wdt], hh == 0, last, [R_Dg[blk], R_Rb[hh % 4]], [rpi])
                        pend = [h for h, _, _ in cur]
                    if j == tt and PE_CAUSAL:
                        MM(pi[:, blk * 128:wdt], idb[:], cnegb, False, True, [R_const], [rpi])
                    if j == tt and not PE_CAUSAL:
                        if blk > 0:
                            CP("act", Ib[:, s0:s0 + blk * 128], pi[:, 0:blk * 128], [rpi], [R_Ib])
                        TT("dve", Ib[:, s0 + blk * 128:s0 + wdt], pi[:, blk * 128:wdt], cneg[:], ALU.add,
                           [rpi, R_const], [R_Ib])
                    else:
                        CP("act", Ib[:, s0:s0 + wdt], pi[:, 0:wdt], [rpi], [R_Ib])

                def c_bis0(blk):
                    tb = tt * 4 + blk
                    Wd = (tb + 1) * 128
                    Ib, R_Ib = Isb[tb % 3], R_I[tb % 3]
                    bs, R_bs = bis[tb % 4], R_bis[tb % 4]
                    lo, W0, mid, cnt, stp, hi = (bs[:, i:i + 1] for i in range(6))
                    if Wd <= KTOP:
                        MS("dve", lo, -1e29, [R_bs])
                    else:
                        S_.op("dve", lambda e, o=lo, i_=Ib[:, 0:tb * 128]: e.tensor_reduce(
                            out=o, in_=i_, axis=mybir.AxisListType.X, op=ALU.min), [R_Ib], [R_bs])
                        S_.op("dve", lambda e, o=hi, i_=Ib[:, 0:Wd]: e.tensor_reduce(
                            out=o, in_=i_, axis=mybir.AxisListType.X, op=ALU.max), [R_Ib], [R_bs])
                        STT(W0, hi, 1.0, lo, ALU.add, ALU.subtract, [R_bs], [R_bs])

                def c_bis(blk, it0, it1):
                    tb = tt * 4 + blk
                    Wd = (tb + 1) * 128
                    if Wd <= KTOP:
                        return
                    Ib, R_Ib = Isb[tb % 3], R_I[tb % 3]
                    bs, R_bs = bis[tb % 4], R_bis[tb % 4]
                    lo, W0, mid, cnt, stp, hi = (bs[:, i:i + 1] for i in range(6))
                    for it in range(it0, it1):
                        c = 2.0 ** -(it + 1)
                        STT(mid, W0, c, lo, ALU.mult, ALU.add, [R_bs], [R_bs])
                        TS("dve", junkI[:, 0:Wd], Ib[:, 0:Wd], mid, None, ALU.is_ge, ALU.add, [R_Ib, R_bs],
                           [R_junk, R_bs], accum=cnt)
                        STT(stp, cnt, KTOP - 0.5, W0, ALU.is_ge, ALU.mult, [R_bs], [R_bs])
                        STT(lo, stp, c, lo, ALU.mult, ALU.add, [R_bs], [R_bs])

                def c_maskD(blk):
                    tb = tt * 4 + blk
                    Wd = (tb + 1) * 128
                    Ib, R_Ib = Isb[tb % 3], R_I[tb % 3]
                    bs, R_bs = bis[tb % 4], R_bis[tb % 4]
                    lo = bs[:, 0:1]
                    TS("dve", Ib[:, 0:Wd], Ib[:, 0:Wd], lo, None, ALU.is_ge, None, [R_Ib, R_bs], [R_Ib])

                def c_maskP(blk):
                    tb = tt * 4 + blk
                    mkb, R_mkb = Isb[tb % 3], R_I[tb % 3]
                    sb0 = 0
                    while sb0 <= tb:
                        n = min(4, tb + 1 - sb0)
                        p32, _, rp = pA.next()
                        for i in range(n):
                            S_.op("pe", lambda e, o=p32[:, i * 128:(i + 1) * 128], i_=mkb[:, (sb0 + i) * 128:(sb0 + i + 1) * 128]:
                                  e.transpose(out=o, in_=i_, identity=idf[:]), [R_mkb, R_const], [rp])
                        CP("act", mT[:, sb0:sb0 + n, blk * 128:(blk + 1) * 128],
                           p32[:, 0:n * 128].rearrange("p (k t) -> p k t", k=n), [rp], [R_mTb])
                        sb0 += n

                def add_I(blk):
                    for j in range(tt + 1):
                        ch.append(lambda blk=blk, j=j: c_idx(blk, j))

                def c_bis_pair(bA, bB, it):
                    c = 2.0 ** -(it + 1)
                    info = []
                    for b_ in (bA, bB):
                        tb = tt * 4 + b_
                        Wd = (tb + 1) * 128
                        act = Wd > KTOP
                        Ib, R_Ib = Isb[tb % 3], R_I[tb % 3]
                        bs, R_bs = bis[tb % 4], R_bis[tb % 4]
                        info.append((act, Wd, Ib, R_Ib, bs, R_bs))
                    actA, WdA, IA, R_IA, bsA, R_bsA = info[0]
                    actB, WdB, IB, R_IB, bsB, R_bsB = info[1]
                    if actB:
                        loB, W0B, midB, cntB, stpB = (bsB[:, i:i + 1] for i in range(5))
                        STT(midB, W0B, -c, loB, ALU.mult, ALU.subtract, [R_bsB], [R_bsB])
                        ACT(junkA[:, 0:WdB], IB[:, 0:WdB], AF.Sign, [R_IB, R_bsB], [R_junkA, R_bsB], bias=midB, accum=cntB)
                    if actA:
                        loA, W0A, midA, cntA, stpA = (bsA[:, i:i + 1] for i in range(5))
                        STT(midA, W0A, c, loA, ALU.mult, ALU.add, [R_bsA], [R_bsA])
                        TS("dve", junkI[:, 0:WdA], IA[:, 0:WdA], midA, None, ALU.is_ge, ALU.add, [R_IA, R_bsA],
                           [R_junk, R_bsA], accum=cntA)
                        STT(stpA, cntA, KTOP - 0.5, W0A, ALU.is_ge, ALU.mult, [R_bsA], [R_bsA])
                        STT(loA, stpA, c, loA, ALU.mult, ALU.add, [R_bsA], [R_bsA])
                    if actB:
                        STT(stpB, cntB, 2 * KTOP - WdB - 0.5, W0B, ALU.is_ge, ALU.mult, [R_bsB], [R_bsB])
                        STT(loB, stpB, c, loB, ALU.mult, ALU.add, [R_bsB], [R_bsB])

                def add_Bpair(bA, bB):
                    ch.append(lambda: c_bis0(bA))
                    ch.append(lambda: c_bis0(bB))
                    if (tt * 4 + bB + 1) * 128 > KTOP:
                        for it in range(NIT):
                            ch.append(lambda it=it: c_bis_pair(bA, bB, it))
                    ch.append(lambda: c_maskD(bA))
                    ch.append(lambda: c_maskD(bB))

                add_I(0); add_I(1); add_I(2)
                add_Bpair(0, 1)
                ch.append(lambda: c_maskP(0))
                add_I(3)
                ch.append(lambda: c_maskP(1))
                add_Bpair(2, 3)
                ch.append(lambda: c_maskP(2))
                ch.append(lambda: c_maskP(3))
                return ch

            def Y_stage(seq, tt, par, xch):
                hTb, R_hTb = hTs[par], R_hTs[par]
                kTs, vas = kT[seq % 2], vaug[seq % 2]
                mT, R_mTb = maskT[par], R_mT[par]
                NSB = 4 * (tt + 1)
                nsteps = 8 * (NSB + 2)
                state = {"done": 0, "step": 0}

                def pump():
                    state["step"] += 1
                    target = (len(xch) * state["step"]) // nsteps
                    while state["done"] < min(target, len(xch)):
                        xch[state["done"]]()
                        state["done"] += 1

                def proj_head(h):
                    p, rp = proj_fm(hTb, R_hTb, W1, R_W1, C_Q + 64 * h, 64)
                    pg, rpg = proj_fm(hTb, R_hTb, W1, R_W1, C_GA + 64 * h, 64)
                    rms64(p, rp, gqk[:, 0:1], qT[h % 2][0:64, :], R_qT[h % 2])
                    th, R_th = next_scr()
                    ACT(th[0:64, :], pg[0:64, :], AF.Exp, [rpg], [R_th], scale=-1.0)
                    ACT(th[0:64, :], th[0:64, :], AF.Ln, [R_th, R_const], [R_th], bias=one_t[0:64, :])
                    ACT(th[0:64, :], th[0:64, :], AF.Exp, [R_th], [R_th], scale=-1.0)
                    TT("dve", sga[h % 3][0:64, :], th[0:64, :], pg[0:64, :], ALU.mult, [R_th, rpg], [R_sga[h % 3]])

                def attn_head(h, fin_prev):
                    q_, R_q_ = qT[h % 2], R_qT[h % 2]
                    po, _, rpo = pO.next()
                    pend = []
                    for sbi in range(NSB + 2):
                        if sbi < NSB:
                            j = sbi - 4 * tt
                            c0 = max(j, 0) * 128
                            rk = R_kv[(seq, sbi // 4)]
                            plt, _, rpl = pS.next()
                            near = []
                            for blk in range(4):
                                tb = 4 * tt + blk
                                if sbi == tb:
                                    near.append((blk, 0))
                                elif sbi == tb - 1:
                                    near.append((blk, 1))
                            MM(plt[:, c0:512], kTs[0:64, sbi * 128:(sbi + 1) * 128], q_[0:64, c0:512], True, len(near) == 0,
                               [rk, R_q_], [rpl])
                            for ni, (blk, dl) in enumerate(near):
                                MM(plt[:, blk * 128:(blk + 1) * 128], idb[:], biasT[:, h, dl, :], False,
                                   ni == len(near) - 1, [R_const], [rpl])
                            e_, R_e_ = Eb[sbi % 4], R_E[sbi % 4]
                            ACT(e_[:, c0:512], plt[:, c0:512], AF.Exp, [rpl, R_const], [R_e_], bias=cfar[:, h:h + 1])
                            mcnt[0] += 1
                            TT("pool" if (POOL_MASK and mcnt[0] % MASK_MOD != 0) else "dve", e_[:, c0:512], e_[:, c0:512],
                               mT[:, sbi, c0:512], ALU.mult, [R_e_, R_mTb], [R_e_])
                            pend.append((sbi, c0, e_, R_e_, rk))
                        if sbi >= 2:
                            ps_, pc0, pe_, R_pe_, prk = pend.pop(0)
                            MM(po[0:65, pc0:512], vas[:, ps_, 0:65], pe_[:, pc0:512], ps_ == 0, ps_ == NSB - 1,
                               [prk, R_pe_, R_const], [rpo])
                        if sbi == 0 and fin_prev is not None:
                            fin_prev(0)
                        if sbi == 3 and fin_prev is not None:
                            fin_prev(1)
                        pump()
                    def finish(part, h=h, po=po, rpo=rpo):
                        rd, R_rd_ = rdb[h % 2], R_rd[h % 2]
                        if part == 0:
                            ln_, R_ln = next_scr()
                            ACT(ln_[64:65, :], po[64:65, :], AF.Ln, [rpo], [R_ln])
                            ACT(rd[64:65, :], ln_[64:65, :], AF.Exp, [R_ln], [R_rd_], scale=-1.0)
                            return
                        pb, _, rpb = pA.next()
                        MM(pb[0:64, :], ones_bf[64:65, 0:64], rd[64:65, :], True, True, [R_const, R_rd_], [rpb])
                        tmp, R_tmp = next_scr()
                        TT("dve", tmp[0:64, :], sga[h % 3][0:64, :], po[0:64, :], ALU.mult, [R_sga[h % 3], rpo], [R_tmp])
                        y_, R_y_ = yah[h % 2], R_yah[h % 2]
                        TT("dve", y_[0:64, :], tmp[0:64, :], pb[0:64, :], ALU.mult, [R_tmp, rpb], [R_y_])
                        key = (l, seq, tt)
                        if key not in R_ya:
                            R_ya[key] = Res("ya")
                        DMA("sp", ya_scr[seq, tt, h * 64:(h + 1) * 64, :], y_[0:64, :], [R_y_], [R_ya[key]])
                    return finish

                proj_head(0)
                fin = None
                for h in range(8):
                    fin_prev = fin
                    if h + 1 < 8:
                        proj_head(h + 1)
                    fin = attn_head(h, fin_prev)
                fin(0)
                fin(1)
                while state["done"] < len(xch):
                    xch[state["done"]]()
                    state["done"] += 1

            for half in range(2):
                CP("dve", Wki2[:, :, half * 64:(half + 1) * 64], W1[:, :, C_KI:C_KI + 64], [R_W1], [R_Wki2])
            tiles = [(seq, tt) for seq in range(NSEQ) for tt in range(NTT)]
            for c_ in X_chunks(tiles[0][0], tiles[0][1], 0):
                c_()
            for n_, (seq, tt) in enumerate(tiles):
                if n_ == len(tiles) - 1:
                    load_W2(l, W2, R_W2, xonly_res)
                if K_STOP == 1:
                    R_ya[(l, seq, tt)] = Res("ya")
                    continue
                nxt = X_chunks(tiles[n_ + 1][0], tiles[n_ + 1][1], (n_ + 1) % 2) if n_ + 1 < len(tiles) else []
                if INTERLEAVE:
                    Y_stage(seq, tt, n_ % 2, nxt)
                else:
                    Y_stage(seq, tt, n_ % 2, [])
                    for c_ in nxt:
                        c_()
            S_.barrier()

            AR.reset()
            Wm = AR.alloc_at(W3_OFF, [8, 2048], BF16)
            wbr = AR.alloc_at(W3_OFF + 8 * 2048 * 2, [2, 4, D], BF16)
            wo = AR.alloc_at(W3_OFF + 8 * 2048 * 2 + 8 * D * 2, [8, D], BF16)
            R_W3 = Res("W3")
            w3th = load_W3(l, Wm, wbr, wo, R_W3)
            hT2b = AR.alloc([8, 512], BF16)
            hTp = [hT[:, :, :], hT2b]
            R_hTp = [R_hT, Res("hT2b")]
            vln = AR.alloc([4, 512], BF16)
            R_vln = Res("vln")
            gu2 = AR.alloc([4, 512], BF16)
            R_gu = Res("gu2")
            sgb = AR.alloc([4, 512], BF16)
            R_sgb = Res("sgb")
            ybT = AR.alloc([4, 512], BF16)
            R_ybT = Res("ybT")
            scr2 = [AR.alloc([512], F32) for _ in range(8)]
            R_scr2 = [Res("s2_%d" % i) for i in range(8)]
            sc2n = [0]
            bnst = AR.alloc([4, 8], F32)
            R_bn = Res("bn")
            assert AR.off <= W2_OFF, (AR.off, W2_OFF)
            pA = PPool([0, 1, 2, 3, 4, 5])
            pS = PPool([6, 7])

            def nscr2():
                i = sc2n[0] % 8
                sc2n[0] += 1
                return scr2[i], R_scr2[i]

            def gelu2(p, rp, out_ap, R_out):
                a, R_a = nscr2()
                ACT(a, p, AF.Square, [rp], [R_a])
                TS("dve", a, a, 0.044715, 1.0, ALU.mult, ALU.add, [R_a], [R_a])
                b_, R_b = nscr2()
                TT("dve", b_, a, p, ALU.mult, [R_a, rp], [R_b])
                ACT(b_, b_, AF.Tanh, [R_b], [R_b], scale=0.7978845608028654)
                STT(out_ap, b_, 1.0, p, ALU.add, ALU.mult, [R_b, rp], [R_out])

            tiles2 = [(seq, tt) for seq in range(NSEQ) for tt in range(NTT)]

            def ld2(n_):
                sq_, t_ = tiles2[n_]
                DMA("sp", hTp[n_ % 2].rearrange("p k t -> p (k t)"), hT_scr[sq_, t_], [R_hs[(l, sq_, t_)]], [R_hTp[n_ % 2]])
            ld2(0)
            if True:
                for n2, (seq, tt) in enumerate(tiles2):
                    if n2 + 1 < len(tiles2):
                        ld2(n2 + 1)
                    per = -(-len(w3th) // len(tiles2))
                    for th_ in w3th[n2 * per:(n2 + 1) * per]:
                        th_()
                    hTc, R_hTc = hTp[n2 % 2], R_hTp[n2 % 2]
                    for blk in range(4):
                        p, _, rp = pA.next()
                        for k in range(8):
                            MM(p[:, :], hTc[:, k, blk * 128:(blk + 1) * 128], W2[:, k, 512:1024], k == 0, k == 7,
                               [R_W2, R_hTc], [rp])
                        g2, R_g2 = nscr2()
                        gelu2(p, rp, g2, R_g2)
                        S_.op("dve", lambda e, o=bnst[:, blk, 0:6], i_=g2: e.bn_stats(out=o, in_=i_), [R_g2], [R_bn])
                        S_.op("dve", lambda e, o=bnst[:, blk, 6:8], i_=bnst[:, blk, 0:6]: e.bn_aggr(out=o, in_=i_), [R_bn], [R_bn])
                        TS("dve", bnst[:, blk, 7:8], bnst[:, blk, 7:8], 4 * EPS, None, ALU.add, None, [R_bn], [R_bn])
                        TT("pool", bnst[:, blk, 7:8], bnst[:, blk, 7:8], mhalf, ALU.pow, [R_bn, R_const], [R_bn])
                        TS("dve", g2, g2, bnst[:, blk, 6:7], bnst[:, blk, 7:8], ALU.subtract, ALU.mult, [R_g2, R_bn], [R_g2])
                        TT("dve", g2, g2, lngb[:], ALU.mult, [R_g2, R_lay], [R_g2])
                        TT("dve", vln[:, blk, :], g2, lnbb[:], ALU.add, [R_g2, R_lay], [R_vln])
                    for c in range(4):
                        p, rp = proj_fm(hTc, R_hTc, W2, R_W2, c * 128, 128)
                        gelu2(p, rp, gu2[:, c, :], R_gu)
                        p, rp = proj_fm(hTc, R_hTc, W2, R_W2, 1024 + c * 128, 128)
                        th, R_th = nscr2()
                        ACT(th, p, AF.Tanh, [rp], [R_th], scale=0.5)
                        STT(sgb[:, c, :], th, 1.0, p, ALU.add, ALU.mult, [R_th, rp], [R_sgb])
                    for g in range(4):
                        p, _, rp = pS.next()
                        for blk in range(4):
                            MM(p[:, blk * 128:(blk + 1) * 128], vln[:, blk, g * 128:(g + 1) * 128], wT_sp[:, g, :], True, False,
                               [R_vln, R_lay], [rp])
                            MM(p[:, blk * 128:(blk + 1) * 128], ones_bf[0:1, :], bsp[0:1, g, :], False, True,
                               [R_const, R_lay], [rp])
                        t_, R_t = nscr2()
                        STT(t_, gu2[:, g, :], 0.25, p, ALU.mult, ALU.mult, [R_gu, rp], [R_t])
                        TT("dve", ybT[:, g, :], t_, sgb[:, g, :], ALU.mult, [R_t, R_sgb], [R_ybT])
                    key = (l, seq, tt)
                    R_yb[key] = Res("yb")
                    DMA("sp", yb_scr[seq, tt].rearrange("(g p) t -> p g t", p=128), ybT, [R_ybT], [R_yb[key]])
            S_.barrier()

            AR.reset()
            w1th = load_W1_th(l + 1) if l + 1 < DEPTH else []
            hT2c = AR.alloc([8, 512], BF16)
            hTp = [hT[:, :, :], hT2c]
            R_hTp = [R_hT, Res("hT2c")]
            yaTs = [AR.alloc([4, 512], BF16) for _ in range(2)]
            R_yaTs = [Res("yaT0"), Res("yaT1")]
            ybT3s = [AR.alloc([4, 512], BF16) for _ in range(2)]
            R_ybT3s = [Res("ybT30"), Res("ybT31")]
            mg = AR.alloc([8, 512], BF16)
            R_mg = Res("mg")
            scr3 = [AR.alloc([512], F32) for _ in range(8)]
            R_scr3 = [Res("s3_%d" % i) for i in range(8)]
            sc3n = [0]
            xres = [AR.alloc([D], F32) for _ in range(2)]
            R_xr = [Res("xr0"), Res("xr1")]
            assert AR.off <= W3_OFF, (AR.off, W3_OFF)
            pA = PPool([0, 1, 2, 3, 4, 5])
            pS = PPool([6, 7])

            def nscr3():
                i = sc3n[0] % 8
                sc3n[0] += 1
                return scr3[i], R_scr3[i]

            tiles3 = [(seq, tt) for seq in range(NSEQ) for tt in range(NTT)]

            def ld3(n_):
                sq_, t_ = tiles3[n_]
                key_ = (l, sq_, t_)
                DMA("sp", hTp[n_ % 2].rearrange("p k t -> p (k t)"), hT_scr[sq_, t_], [R_hs[key_]], [R_hTp[n_ % 2]])
                DMA("sp", yaTs[n_ % 2], ya_scr[sq_, t_].rearrange("(k p) t -> p k t", p=128), [R_ya[key_]], [R_yaTs[n_ % 2]])
                DMA("sp", ybT3s[n_ % 2], yb_scr[sq_, t_].rearrange("(k p) t -> p k t", p=128), [R_yb[key_]], [R_ybT3s[n_ % 2]])
            ld3(0)
            if True:
                for n3, (seq, tt) in enumerate(tiles3):
                    if n3 + 1 < len(tiles3):
                        ld3(n3 + 1)
                    per = -(-len(w1th) // len(tiles3)) if w1th else 0
                    for th_ in w1th[n3 * per:(n3 + 1) * per]:
                        th_()
                    hTc, R_hTc = hTp[n3 % 2], R_hTp[n3 % 2]
                    yaT, R_yaT = yaTs[n3 % 2], R_yaTs[n3 % 2]
                    ybT3, R_ybT3 = ybT3s[n3 % 2], R_ybT3s[n3 % 2]
                    for ec in range(8):
                        es = slice(ec * 128, (ec + 1) * 128)
                        pa, _, rpa = pA.next()
                        for k in range(4):
                            MM(pa, wbr[:, 0, k, es], yaT[:, k, :], k == 0, k == 3, [R_W3, R_yaT], [rpa])
                        pb, _, rpb = pA.next()
                        for k in range(4):
                            MM(pb, wbr[:, 1, k, es], ybT3[:, k, :], k == 0, k == 3, [R_W3, R_ybT3], [rpb])
                        pc, _, rpc = pA.next()
                        for k in range(8):
                            MM(pc, Wm[:, k, es], hTc[:, k, :], k == 0, k == 7, [R_W3, R_hTc], [rpc])
                        pd, _, rpd = pA.next()
                        for k in range(8):
                            MM(pd, Wm[:, k, 1024 + ec * 128:1024 + (ec + 1) * 128], hTc[:, k, :], k == 0, k == 7,
                               [R_W3, R_hTc], [rpd])
                        ta, R_ta = nscr3()
                        ACT(ta, pc, AF.Tanh, [rpc], [R_ta], scale=0.5)
                        tb_, R_tb = nscr3()
                        ACT(tb_, pd, AF.Tanh, [rpd], [R_tb], scale=0.5)
                        STT(ta, ta, 1.0, pa, ALU.add, ALU.mult, [R_ta, rpa], [R_ta])
                        STT(tb_, tb_, 1.0, pb, ALU.add, ALU.mult, [R_tb, rpb], [R_tb])
                        TT("pool", mg[:, ec, :], ta, tb_, ALU.add, [R_ta, R_tb], [R_mg])
                    for blk in range(4):
                        r0 = tt * 512 + blk * 128
                        xr, R_x = xres[blk % 2], R_xr[blk % 2]
                        DMA("sp", xr, src_t[seq, r0:r0 + 128, :], [R_src], [R_x])
                        for ch in range(2):
                            po, _, rpo = pS.next()
                            for ec in range(8):
                                MM(po, mg[:, ec, blk * 128:(blk + 1) * 128], wo[:, ec, ch * 512:(ch + 1) * 512], ec == 0, ec == 7,
                                   [R_mg, R_W3], [rpo])
                            STT(xr[:, ch * 512:(ch + 1) * 512], po, 0.5, xr[:, ch * 512:(ch + 1) * 512], ALU.mult, ALU.add,
                                [rpo, R_x], [R_x])
                        R_dst = [R_xmid] if l < DEPTH - 1 else []
                        tok = DMA("sp", dst_t[seq, r0:r0 + 128, :], xr, [R_x], R_dst)
                        if l == DEPTH - 1:
                            out_toks.append(tok)
            S_.barrier()
        S_.emit()
    global LAST_SCHED
    LAST_SCHED = S_
    print('arena peak', AR.peak)
    return nc


_CACHE = {}
LAST_SCHED = None


def kernel(**inputs):
    NC = 8
    x = np.ascontiguousarray(inputs["x"], dtype=np.float32)
    B, S, _ = x.shape
    NSEQ = B // NC
    DEPTH = inputs["w_in"].shape[0]
    key = (S, NSEQ, DEPTH)
    if key not in _CACHE:
        _CACHE[key] = build(S, NSEQ, DEPTH)
    nc = _CACHE[key]
    consts = host_consts()
    shared = {k: np.ascontiguousarray(v, dtype=np.float32) for k, v in inputs.items() if k != "x"}
    shared.update(consts)
    in_maps = []
    for c in range(NC):
        m = dict(shared)
        m["x"] = np.ascontiguousarray(x[c * NSEQ:(c + 1) * NSEQ])
        in_maps.append(m)
    res = run_bass_kernel_spmd(nc, in_maps, core_ids=list(range(NC)))
    return np.concatenate([np.asarray(r["out"], dtype=np.float32) for r in res.results], axis=0)
```

```python
import contextlib
import math
import numpy as np
import concourse.bass as bass
import concourse.mybir as mybir
from concourse.bass_utils import run_bass_kernel_spmd

F32 = mybir.dt.float32
BF16 = mybir.dt.bfloat16
ALU = mybir.AluOpType
AF = mybir.ActivationFunctionType

D = 1024
NIN = 5320
NEG = -30000.0
EPS = 1e-6
import os
INTERLEAVE = int(os.environ.get('K_IL', '1'))
POOL_MASK = int(os.environ.get('K_PM', '1'))
POOL_DG = int(os.environ.get('K_PD', '1'))
PE_CAUSAL = int(os.environ.get('K_PC', '1'))
K_E1 = int(os.environ.get('K_E1', '1'))
K_STOP = int(os.environ.get('K_STOP', '0'))
MASK_MOD = int(os.environ.get('K_MM', '3'))
BCH = int(os.environ.get('K_BCH', '1'))
RELU_DVE = [int(c) for c in os.environ.get('K_RD', '')]
SC_PA = int(os.environ.get('K_SCPA', '1'))
PVLAG = int(os.environ.get('K_LAG', '3'))
NIT = int(os.environ.get('K_NIT', '16'))

C_Q, C_K, C_V, C_GA, C_QI, C_KI, C_WI, C_U, C_VB, C_GB, C_MA, C_MB = (
    0, 512, 576, 640, 1152, 1664, 1728, 1736, 2248, 2760, 3272, 4296)


class Res:
    __slots__ = ("name", "w", "r")

    def __init__(self, name):
        self.name = name
        self.w = None
        self.r = []


class Sched:
    ENG = ("pe", "act", "dve", "pool", "sp")
    NDS = 56

    def __init__(self, nc, stack):
        self.nc = nc
        self.sems = {}
        for e in self.ENG:
            self.sems[e] = stack.enter_context(nc.semaphore("s_" + e))
        for i in range(self.NDS):
            self.sems[("d", i)] = stack.enter_context(nc.semaphore("d%d" % i))
        self.dval = [0] * self.NDS
        self.dnext = {"sp": 0, "pool": 0, "act": 0}
        self.drange = {"sp": (0, 16), "act": (0, 16), "pool": (16, self.NDS)}
        self.cnt = {e: 0 for e in self.ENG}
        self.ops = {e: [] for e in self.ENG}
        self.waited = {e: {} for e in self.ENG}

    def _deps(self, eng, reads, writes):
        deps = {}

        def add(tok, kind):
            if tok is None:
                return
            k, v = tok
            if k == eng and (eng == "pe" or kind != "raw"):
                return
            if deps.get(k, 0) < v:
                deps[k] = v
        def addw(wt, kind):
            if isinstance(wt, list):
                for t_ in wt:
                    add(t_, kind)
            else:
                add(wt, kind)
        for r in reads:
            addw(r.w, "raw")
        for w in writes:
            addw(w.w, "waw")
            for t in w.r:
                add(t, "war")
        need = []
        wd = self.waited[eng]
        for k, v in deps.items():
            if wd.get(k, 0) < v:
                wd[k] = v
                need.append((k, v))
        return need

    def _mark(self, tok, reads, writes, append=False):
        for r in reads:
            r.r.append(tok)
            if len(r.r) > 64:
                best = {}
                for k, v in r.r:
                    if best.get(k, 0) < v:
                        best[k] = v
                r.r = list(best.items())
        for w in writes:
            if append and isinstance(w.w, list):
                w.w.append(tok)
            elif append:
                w.w = [tok]
            else:
                w.w = tok
            w.r = []

    def op(self, eng, fn, reads=(), writes=()):
        need = self._deps(eng, reads, writes)
        self.cnt[eng] += 1
        tok = (eng, self.cnt[eng])
        self.ops[eng].append((need, fn, (eng, 1)))
        self._mark(tok, reads, writes)
        return tok

    def dma(self, eng, fn, reads=(), writes=(), append=False):
        lo_, hi_ = self.drange[eng]
        j = lo_ + self.dnext[eng] % (hi_ - lo_)
        self.dnext[eng] += 1
        need = self._deps(eng, reads, writes)
        if self.dval[j] > 0:
            wd = self.waited[eng]
            if wd.get(("d", j), 0) < self.dval[j]:
                wd[("d", j)] = self.dval[j]
                need.append((("d", j), self.dval[j]))
        self.dval[j] += 16
        tok = (("d", j), self.dval[j])
        self.ops[eng].append((need, fn, (("d", j), 16)))
        self._mark(tok, reads, writes, append)
        return tok

    def barrier(self):
        for e in self.ENG:
            need = []
            wd = self.waited[e]
            for k in self.ENG:
                if k != e and self.cnt[k] > wd.get(k, 0):
                    wd[k] = self.cnt[k]
                    need.append((k, self.cnt[k]))
            for j in range(self.NDS):
                if self.dval[j] > wd.get(("d", j), 0):
                    wd[("d", j)] = self.dval[j]
                    need.append((("d", j), self.dval[j]))
            self.ops[e].append((need, None, None))

    def emit(self):
        nc = self.nc
        sems = self.sems
        with nc.Block() as block:
            def run(e_name):
                def body(engine):
                    for need, fn, inc in self.ops[e_name]:
                        for k, v in need:
                            engine.wait_ge(sems[k], v)
                        if fn is not None:
                            fn(engine).then_inc(sems[inc[0]], inc[1])
                return body
            block.tensor(run("pe"))
            block.scalar(run("act"))
            block.vector(run("dve"))
            block.gpsimd(run("pool"))
            block.sync(run("sp"))


class Arena:
    def __init__(self, t16):
        self.t16 = t16
        self.t32 = t16.bitcast(F32)
        self.off = 0
        self.cap = t16.shape[1] * 2
        self.peak = 0

    def reset(self):
        self.off = 0

    def alloc_at(self, off, free_shape, dt):
        save = self.off
        self.off = off
        ap = self.alloc(free_shape, dt)
        self.off = save
        return ap

    def alloc_dual(self, n32):
        self.off = (self.off + 63) // 64 * 64
        o = self.off
        self.off += n32 * 4
        self.peak = max(self.peak, self.off)
        assert self.off <= self.cap, ("arena overflow", self.off, self.cap)
        return self.t32[:, o // 4:o // 4 + n32], self.t16[:, o // 2:o // 2 + 2 * n32]

    def alloc(self, free_shape, dt, parts=128):
        es = 4 if dt == F32 else 2
        n = int(np.prod(free_shape))
        self.off = (self.off + 63) // 64 * 64
        o = self.off
        self.off += n * es
        self.peak = max(self.peak, self.off)
        assert self.off <= self.cap, ("arena overflow", self.off, self.cap)
        base = self.t32 if dt == F32 else self.t16
        ap = base[0:parts, o // es:o // es + n]
        if len(free_shape) == 2:
            ap = ap.rearrange("p (a b) -> p a b", a=free_shape[0])
        elif len(free_shape) == 3:
            ap = ap.rearrange("p (a b c) -> p a b c", a=free_shape[0], b=free_shape[1])
        return ap


def _bucket_table():
    oh = np.zeros((33, 384), np.float32)
    for m in range(384):
        rel = m - 127
        if rel < 0:
            oh[32, m] = 1.0
            continue
        if rel < 16:
            b = rel
        else:
            nf = np.float32(max(rel, 1))
            v = np.log(nf / np.float32(16.0)) / np.float32(math.log(128 / 16)) * np.float32(16.0)
            b = min(16 + int(np.float32(v).astype(np.int32)), 31)
        oh[b, m] = 1.0
    return oh


def host_consts():
    t = np.arange(128)
    return {
        "c_ident": np.eye(128, dtype=np.float32),
        "c_tril": (t[:, None] >= t[None, :]).astype(np.float32),
        "c_cneg": np.where(t[None, :] <= t[:, None], 0.0, -1e30).astype(np.float32),
        "c_oh": _bucket_table(),
    }


def build(S, NSEQ, DEPTH, dbg=None):
    nc = bass.Bass("TRN2", target_bir_lowering=False)
    NTT = S // 512
    KTOP = min(256, S // 4)
    dt_in = lambda name, shape: nc.dram_tensor(name, shape, F32, kind="ExternalInput")
    x_in = dt_in("x", [NSEQ, S, D])
    norm_g = dt_in("norm_g", [DEPTH, D])
    w_in = dt_in("w_in", [DEPTH, D, NIN])
    q_norm_g = dt_in("q_norm_g", [DEPTH, 64])
    k_norm_g = dt_in("k_norm_g", [DEPTH, 64])
    rel_bias = dt_in("rel_bias", [32, 8])
    sgu_ln_g = dt_in("sgu_ln_g", [DEPTH, 512])
    sgu_ln_b = dt_in("sgu_ln_b", [DEPTH, 512])
    w_spatial = dt_in("w_spatial", [DEPTH, 4, 128, 128])
    b_spatial = dt_in("b_spatial", [DEPTH, 4, 128])
    w_branch = dt_in("w_branch", [DEPTH, 2, 512, D])
    w_out = dt_in("w_out", [DEPTH, D, D])
    c_ident = dt_in("c_ident", [128, 128])
    c_tril = dt_in("c_tril", [128, 128])
    c_cneg = dt_in("c_cneg", [128, 128])
    c_oh = dt_in("c_oh", [33, 384])
    out = nc.dram_tensor("out", [NSEQ, S, D], F32, kind="ExternalOutput")
    xmid = nc.dram_tensor("xmid", [NSEQ, S, D], F32, kind="Internal")
    hT_scr = nc.dram_tensor("hT_scr", [NSEQ, NTT, 128, 8 * 512], BF16, kind="Internal")
    ya_scr = nc.dram_tensor("ya_scr", [NSEQ, NTT, 512, 512], BF16, kind="Internal")
    yb_scr = nc.dram_tensor("yb_scr", [NSEQ, NTT, 512, 512], BF16, kind="Internal")
    bias_scr = nc.dram_tensor("bias_scr", [128, 8 * 384], BF16, kind="Internal")
    R_xmid, R_hT_scr, R_ya_scr, R_yb_scr, R_bias_scr = (Res(n) for n in ("xmid", "hTs", "yas", "ybs", "bs"))
    R_ya = {}
    R_yb = {}
    R_hs = {}

    with contextlib.ExitStack() as st:
        S_ = Sched(nc, st)
        sb = lambda name, shape, dt: st.enter_context(nc.sbuf_tensor(name, shape, dt))
        def MM(out_, lhsT, rhs, start, stop, R, W):
            S_.op("pe", lambda e: e.matmul(out_, lhsT=lhsT, rhs=rhs, start=start, stop=stop), R, W)

        def TR(out_, in_, R, W):
            S_.op("pe", lambda e: e.transpose(out=out_, in_=in_, identity=idb[:]), list(R) + [R_const], W)

        def ACT(out_, in_, func, R, W, bias=None, scale=None, accum=None):
            kw = {}
            if bias is not None:
                kw["bias"] = bias
            if scale is not None:
                kw["scale"] = scale
            if accum is not None:
                kw["accum_out"] = accum
            S_.op("act", lambda e: e.activation(out=out_, in_=in_, func=func, **kw), R, W)

        def TS(eng, out_, in0, s1, s2, op0, op1, R, W, accum=None):
            kw = {}
            if accum is not None:
                kw["accum_out"] = accum
            if op1 is None:
                S_.op(eng, lambda e: e.tensor_scalar(out=out_, in0=in0, scalar1=s1, scalar2=None, op0=op0, **kw), R, W)
            else:
                S_.op(eng, lambda e: e.tensor_scalar(out=out_, in0=in0, scalar1=s1, scalar2=s2, op0=op0, op1=op1, **kw), R, W)

        def TT(eng, out_, in0, in1, op, R, W):
            S_.op(eng, lambda e: e.tensor_tensor(out=out_, in0=in0, in1=in1, op=op), R, W)

        def STT(out_, in0, scalar, in1, op0, op1, R, W):
            S_.op("dve", lambda e: e.scalar_tensor_tensor(out=out_, in0=in0, scalar=scalar, in1=in1, op0=op0, op1=op1), R, W)

        def CP(eng, out_, in_, R, W):
            if eng == "act":
                S_.op("act", lambda e: e.copy(out=out_, in_=in_), R, W)
            else:
                S_.op(eng, lambda e: e.tensor_copy(out=out_, in_=in_), R, W)

        def MS(eng, ap, val, W):
            S_.op(eng, lambda e: e.memset(ap, val), (), W)

        def DMA(q, out_, in_, R, W, append=False):
            return S_.dma(q, lambda e: e.dma_start(out=out_, in_=in_), R, W, append)

        R_const = Res("const")
        idb = sb("idb", [128, 128], BF16)
        tril = sb("tril", [128, 128], F32)
        idf = sb("idf", [128, 128], F32)
        cneg = sb("cneg", [128, 128], F32)
        ones_bf = sb("ones_bf", [128, 128], BF16)
        o64 = sb("o64", [128, 128], BF16)
        smallc = sb("smallc", [128, 8], F32)
        biasT = sb("biasT", [128, 8, 2, 128], BF16)
        cfar = sb("cfar", [128, 8], F32)
        gbc = sb("gbc", [128, D], F32)
        lngb = sb("lngb", [128, 512], F32)
        lnbb = sb("lnbb", [128, 512], F32)
        gqk = sb("gqk", [64, 4], F32)
        wT_sp = sb("wT_sp", [128, 4, 128], BF16)
        bsp = sb("bsp", [1, 4, 128], BF16)
        hT = sb("hT", [128, 8, 512], BF16)
        R_hT = Res("hT")
        R_lay = Res("layerconst")
        psum = [st.enter_context(nc.psum_tensor("ps%d" % i, [128, 512], F32)) for i in range(8)]
        psum16 = [p.bitcast(BF16) for p in psum]
        R_ps = [Res("ps%d" % i) for i in range(8)]

        class PPool:
            def __init__(self, idxs):
                self.idxs = idxs
                self.n = 0

            def next(self):
                i = self.idxs[self.n % len(self.idxs)]
                self.n += 1
                return psum[i][:, :], psum16[i][:, :], R_ps[i]

        ARENA_BYTES = int(os.environ.get("K_AR", "176")) * 1024
        arena_t = sb("arena", [128, ARENA_BYTES // 2], BF16)
        AR = Arena(arena_t)

        DMA("pool", idb[:], c_ident[:, :], (), [R_const])
        DMA("sp", tril[:], c_tril[:, :], (), [R_const])
        DMA("sp", idf[:], c_ident[:, :], (), [R_const])
        DMA("sp", cneg[:], c_cneg[:, :], (), [R_const])
        MS("dve", ones_bf[:], 1.0, [R_const])
        MS("dve", o64[:], 1.0 / 64, [R_const])
        MS("dve", smallc[:, 0:1], -0.5, [R_const])
        MS("dve", smallc[:, 1:2], EPS, [R_const])
        MS("dve", smallc[:, 2:3], 4 * EPS, [R_const])
        MS("dve", smallc[:, 3:4], 1.0, [R_const])
        one_t = smallc[:, 3:4]
        mhalf = smallc[:, 0:1]
        eps_t = smallc[:, 1:2]
        eps4_t = smallc[:, 2:3]

        AR.reset()
        oh_f = AR.alloc([384], F32)
        rb_ext = AR.alloc([8], F32)
        ones33 = AR.alloc([128], F32)
        lh = AR.alloc([8, 128], F32)
        Bsb = AR.alloc([8, 384], BF16)
        R_su = Res("setup")
        DMA("sp", oh_f[0:33, :], c_oh[:, :], (), [R_su])
        MS("dve", rb_ext[32:33, :], NEG, [R_su])
        DMA("sp", rb_ext[0:32, :], rel_bias[:, :], (), [R_su])
        MS("dve", ones33[0:33, :], 1.0, [R_su])
        for h in range(8):
            TS("dve", lh[0:33, h, :], ones33[0:33, :], rb_ext[0:33, h:h + 1], None, ALU.mult, None, [R_su], [R_su])
        for h in range(8):
            pB, _, rB = psum[h % 2], None, R_ps[h % 2]
            MM(pB[:, 0:384], lh[0:33, h, :], oh_f[0:33, :], True, True, [R_su], [rB])
            CP("dve", cfar[:, h:h + 1], pB[:, 300:301], [rB], [R_const])
            TS("dve", Bsb[:, h, :], pB[:, 0:384], cfar[:, h:h + 1], None, ALU.subtract, None, [rB, R_const], [R_su])
        DMA("sp", bias_scr[:, :], Bsb[:].rearrange("p h m -> p (h m)"), [R_su], [R_bias_scr])
        for h in range(8):
            for dl in range(2):
                src = bass.AP(bias_scr, h * 384 + 127 + 128 * dl, [[8 * 384 - 1, 128], [1, 128]])
                DMA("sp", biasT[:, h, dl, :], src, [R_bias_scr], [R_const])
        S_.barrier()

        W1_OFF = ARENA_BYTES - 8 * 1736 * 2 - 64
        W1_OFF -= W1_OFF % 64
        W1 = AR.alloc_at(W1_OFF, [8, 1736], BF16)
        R_W1 = Res("W1")
        lay = {}

        def load_W1(l_, first):
            for k in range(8):
                DMA("pool", W1[:, k, :], w_in[l_, k * 128:(k + 1) * 128, 0:1736], (), [R_W1], append=(k > 0))

        def load_W2(l_, W2_, R_W2_, extraW):
            for k in range(8):
                DMA("pool", W2_[:, k, :], w_in[l_, k * 128:(k + 1) * 128, C_U:C_U + 1536], (),
                    [R_W2_] + (list(extraW) if k == 0 else []), append=(k > 0))

        def load_W3(l_, Wm_, wbr_, wo_, R_W3_):
            th = []
            for k in range(8):
                th.append(lambda k=k: DMA("pool", Wm_[:, k, :], w_in[l_, k * 128:(k + 1) * 128, C_MA:C_MA + 2048], (),
                                          [R_W3_], append=(k > 0)))
            for n_ in range(2):
                for k in range(4):
                    th.append(lambda n_=n_, k=k: DMA("pool", wbr_[:, n_, k, :], w_branch[l_, n_, k * 128:(k + 1) * 128, :], (),
                                                     [R_W3_], append=True))
            for k in range(8):
                th.append(lambda k=k: DMA("pool", wo_[:, k, :], w_out[l_, k * 128:(k + 1) * 128, :], (), [R_W3_], append=True))
            return th

        def load_W1_th(l_):
            return [lambda k=k: DMA("pool", W1[:, k, :], w_in[l_, k * 128:(k + 1) * 128, 0:1736], (), [R_W1], append=(k > 0))
                    for k in range(8)]

        load_W1(0, True)
        out_toks = []
        for l in range(DEPTH):
            src_t = x_in if l == 0 else xmid
            dst_t = out if l == DEPTH - 1 else xmid
            R_src = Res("src") if l == 0 else R_xmid
            assert DEPTH <= 2
            AR.reset()
            DMA("sp", gbc[:], bass.AP(norm_g, l * D, [[0, 128], [1, D]]), (), [R_lay])
            DMA("sp", lngb[:], bass.AP(sgu_ln_g, l * 512, [[0, 128], [1, 512]]), (), [R_lay])
            DMA("sp", lnbb[:], bass.AP(sgu_ln_b, l * 512, [[0, 128], [1, 512]]), (), [R_lay])
            DMA("sp", gqk[:, 0:1], bass.AP(q_norm_g, l * 64, [[1, 64], [1, 1]]), (), [R_lay])
            DMA("sp", gqk[:, 1:2], bass.AP(k_norm_g, l * 64, [[1, 64], [1, 1]]), (), [R_lay])
            TS("dve", gqk[:, 2:3], gqk[:, 1:2], 0.125, None, ALU.mult, None, [R_lay], [R_lay])
            DMA("pool", bsp[:].rearrange("o g t -> o (g t)"), bass.AP(b_spatial, l * 512, [[0, 1], [1, 512]]), (), [R_lay])
            wsp_f = AR.alloc([4, 128], F32)
            wsp_m = AR.alloc([4, 128], BF16)
            R_w = Res("wsp")
            DMA("sp", wsp_f, w_spatial[l].rearrange("g t s -> t g s"), (), [R_w])
            for g in range(4):
                TT("dve", wsp_m[:, g, :], wsp_f[:, g, :], tril[:], ALU.mult, [R_w, R_const], [R_w])
            pT, pT16, rT = psum[0][:, :], psum16[0][:, :], R_ps[0]
            for g in range(4):
                TR(pT16[:, g * 128:(g + 1) * 128], wsp_m[:, g, :], [R_w], [rT])
            CP("dve", wT_sp[:].rearrange("p g t -> p (g t)"), pT16[:, 0:512], [rT], [R_lay])
            S_.barrier()

            AR.reset()
            hT2 = AR.alloc([8, 512], BF16)
            hTs = [hT[:, :, :], hT2]
            R_hTs = [R_hT, Res("hT2")]
            kT = [AR.alloc([S], BF16) for _ in range(2)]
            kidxT = [AR.alloc([S], BF16) for _ in range(2)]
            vaug = [AR.alloc([S // 128, 66], BF16) for _ in range(2)]
            R_kv = {}
            qT = [AR.alloc([512], BF16) for _ in range(2)]
            R_qT = [Res("qT0"), Res("qT1")]
            sga = [AR.alloc([512], BF16) for _ in range(3)]
            R_sga = [Res("sga0"), Res("sga1"), Res("sga2")]
            EbX = [AR.alloc([512], BF16) for _ in range(2)]
            R_rd = [Res("rd0"), Res("rd1")]
            yah = [AR.alloc([512], BF16) for _ in range(2)]
            R_yah = [Res("yah0"), Res("yah1")]
            rdb = yah
            maskT0 = AR.alloc([max(4, S // 128 - 4), 512], BF16)
            AR.off = max((AR.off + 63) // 64 * 64, 48 * 1024)
            X0 = AR.off
            xt = [AR.alloc([D], F32) for _ in range(2)]
            R_xt = [Res("xt0"), Res("xt1")]
            hb = [AR.alloc([D], BF16) for _ in range(2)]
            R_hb = [Res("hb0"), Res("hb1")]
            qidxT = AR.alloc([4, 512], BF16)
            Wki2 = AR.alloc([8, 128], BF16)
            pad_ = AR.alloc([1024], BF16)
            R_Wki2 = Res('Wki2')
            R_qi = Res("qidxT")
            Dg = AR.alloc([4, 8, 128], BF16)
            R_Dg = [Res("Dg%d" % i) for i in range(4)]
            Rb = [AR.alloc([512], BF16) for _ in range(4)]
            R_Rb = [Res("Rb%d" % i) for i in range(4)]
            Isb = [AR.alloc([S], F32) for _ in range(3)]
            R_I = [Res("I0"), Res("I1"), Res("I2")]
            junkI = AR.alloc([S], BF16)
            R_junk = Res("junkI")
            X1 = AR.off
            assert X1 - X0 >= 8 * 1536 * 2, (X0, X1)
            xonly_res = R_xt + R_hb + [R_qi] + R_Dg + R_Rb + R_I + [R_junk, R_Wki2]
            st_ = AR.alloc([16], F32)
            R_st = Res("st")
            widx = AR.alloc([4, 8], F32)
            R_wi = Res("widx")
            bis = [AR.alloc([8], F32) for _ in range(4)]
            R_bis = [Res("bis%d" % i) for i in range(4)]
            maskT = [maskT0, AR.alloc([S // 128, 512], BF16)]
            R_mT = [Res("maskT0"), Res("maskT1")]
            cnegb = AR.alloc([128], BF16)
            scr = [AR.alloc([512], F32) for _ in range(4)]
            R_scr = [Res("scr%d" % i) for i in range(4)]
            junkA = AR.alloc([S], BF16)
            R_junkA = Res("junkA")
            scrn = [0]
            Eb = [AR.alloc([512], BF16) for _ in range(4)] + EbX
            R_E = [Res("E%d" % i) for i in range(6)]
            sqb = [AR.alloc([512], BF16) for _ in range(2)]
            R_sqb = [Res("sqb0"), Res("sqb1")]
            sqn = [0]
            assert AR.off <= W1_OFF, (AR.off, W1_OFF)
            W2_OFF = X0
            W3_OFF = (X0 + 8 * 1536 * 2 + 63) // 64 * 64
            W2 = AR.alloc_at(W2_OFF, [8, 1536], BF16)
            R_W2 = Res("W2")
            assert W3_OFF + (8 * 2048 + 8 * D + 8 * D) * 2 + 256 <= W1_OFF
            pA = PPool([0, 1, 2])
            pS = PPool([3, 4])
            pI = PPool([5])
            pO = PPool([6, 7])
            mcnt = [0]

            def next_scr():
                i = scrn[0] % 4
                scrn[0] += 1
                return scr[i], R_scr[i]

            for i in range(2):
                MS("dve", vaug[i][:, :, 64:65], 1.0, [R_const])
            CP("dve", cnegb, cneg[:], [R_const], [R_const])

            def proj_fm(hTb, R_hTb, Wt, R_Wt, c0, M):
                p, _, rp = pA.next()
                for k in range(8):
                    MM(p[0:M, :], Wt[:, k, c0:c0 + M], hTb[:, k, :], k == 0, k == 7, [R_Wt, R_hTb], [rp])
                return p, rp

            def rms64(p, rp, gcol, out_ap, R_out):
                sq, R_sq = next_scr()
                s1, R_s1 = next_scr()
                sb_, R_sb_ = sqb[sqn[0] % 2], R_sqb[sqn[0] % 2]
                sqn[0] += 1
                ACT(sb_[0:64, :], p[0:64, :], AF.Square, [rp], [R_sb_])
                pm, _, rpm = pA.next()
                MM(pm[0:64, :], o64[0:64, 0:64], sb_[0:64, :], True, True, [R_sb_, R_const], [rpm])
                ACT(sq[0:64, :], pm[0:64, :], AF.Ln, [rpm, R_const], [R_sq], bias=eps_t[0:64, :])
                ACT(s1[0:64, :], sq[0:64, :], AF.Exp, [R_sq], [R_s1], scale=-0.5)
                STT(out_ap, p[0:64, :], gcol, s1[0:64, :], ALU.mult, ALU.mult, [rp, R_s1, R_lay], [R_out])

            def X_chunks(seq, tt, par):
                ch = []
                T0 = tt * 512
                hTb, R_hTb = hTs[par], R_hTs[par]
                kTs, kis, vas = kT[seq % 2], kidxT[seq % 2], vaug[seq % 2]
                rkv = Res("kv")
                R_kv[(seq, tt)] = rkv
                mT, R_mTb = maskT[par], R_mT[par]

                def c_norm(blk):
                    r0 = T0 + blk * 128
                    xb, R_xb = xt[blk % 2], R_xt[blk % 2]
                    hbl, R_hbl = hb[blk % 2], R_hb[blk % 2]
                    DMA("sp", xb, src_t[seq, r0:r0 + 128, :], [R_src], [R_xb])
                    c = blk * 3
                    ACT(hbl, xb, AF.Square, [R_xb], [R_hbl, R_st], accum=st_[:, c:c + 1])
                    TS("dve", st_[:, c + 1:c + 2], st_[:, c:c + 1], 1.0 / D, EPS, ALU.mult, ALU.add, [R_st], [R_st])
                    TT("pool", st_[:, c + 2:c + 3], st_[:, c + 1:c + 2], mhalf, ALU.pow, [R_st, R_const], [R_st])
                    STT(hbl, xb, st_[:, c + 2:c + 3], gbc[:], ALU.mult, ALU.mult, [R_xb, R_st, R_lay], [R_hbl])
                    _, p16, rp = pA.next()
                    for k in range(8):
                        TR(p16[:, k * 128:(k + 1) * 128], hbl[:, k * 128:(k + 1) * 128], [R_hbl], [rp])
                    CP("act", hTb[:, :, blk * 128:(blk + 1) * 128],
                       p16[:, 0:1024].rearrange("p (k t) -> p k t", k=8), [rp], [R_hTb])
                for blk in range(4):
                    ch.append(lambda blk=blk: c_norm(blk))

                def c_store():
                    R_hs[(l, seq, tt)] = Res("hs")
                    DMA("sp", hT_scr[seq, tt], hTb.rearrange("p k t -> p (k t)"), [R_hTb], [R_hs[(l, seq, tt)]])
                ch.append(c_store)

                def c_k():
                    p, rp = proj_fm(hTb, R_hTb, W1, R_W1, C_K, 64)
                    rms64(p, rp, gqk[:, 2:3], kTs[0:64, T0:T0 + 512], rkv)
                    p, _, rp = pA.next()
                    for k in range(8):
                        MM(p[:, :], Wki2[:, k, :], hTb[:, k, :], k == 0, k == 7, [R_Wki2, R_hTb], [rp])
                    CP("dve" if K_E1 else "act", kis[:, T0:T0 + 512], p[:, :], [rp], [rkv])
                ch.append(c_k)

                def c_v(blk):
                    p, _, rp = pA.next()
                    for k in range(8):
                        MM(p[:, 0:64], hTb[:, k, blk * 128:(blk + 1) * 128], W1[:, k, C_V:C_V + 64], k == 0, k == 7,
                           [R_W1, R_hTb], [rp])
                    for k in range(8):
                        MM(p[:, 64:72], hTb[:, k, blk * 128:(blk + 1) * 128], W1[:, k, C_WI:C_WI + 8], k == 0, k == 7,
                           [R_W1, R_hTb], [rp])
                    CP("dve" if K_E1 else "act", vas[:, tt * 4 + blk, 0:64], p[:, 0:64], [rp], [rkv])
                    TS("dve", widx[:, blk, :], p[:, 64:72], (8 ** -0.5) * (64 ** -0.5), None, ALU.mult, None, [rp], [R_wi])
                    for h in range(8):
                        if POOL_DG:
                            TS("pool", Dg[:, blk, h, :], idb[:], widx[:, blk, h:h + 1], 1.0, ALU.mult, ALU.mult,
                               [R_const, R_wi], [R_Dg[blk]])
                        else:
                            TS("dve", Dg[:, blk, h, :], idb[:], widx[:, blk, h:h + 1], None, ALU.mult, None,
                               [R_const, R_wi], [R_Dg[blk]])
                for blk in range(4):
                    ch.append(lambda blk=blk: c_v(blk))

                def c_qi(j):
                    p, rp = proj_fm(hTb, R_hTb, W1, R_W1, C_QI + 128 * j, 128)
                    CP("act", qidxT[:, j, :], p[:, :], [rp], [R_qi])
                for j in range(4):
                    ch.append(lambda j=j: c_qi(j))

                def c_idx(blk, j):
                    tb = tt * 4 + blk
                    Ib, R_Ib = Isb[tb % 3], R_I[tb % 3]
                    wdt = 512 if j < tt else (blk + 1) * 128
                    s0 = j * 512
                    pi, _, rpi = pI.next()
                    pend = []
                    for jp in range(5):
                        cur = []
                        if jp < 4:
                            for h in (2 * jp, 2 * jp + 1):
                                lo_ = (h % 2) * 64
                                psc, _, rps = (pA if SC_PA else pS).next()
                                MM(psc[:, 0:wdt], qidxT[lo_:lo_ + 64, h // 2, blk * 128:(blk + 1) * 128],
                                   kis[lo_:lo_ + 64, s0:s0 + wdt], True, True, [R_qi, R_kv[(seq, j)]], [rps])
                                cur.append((h, psc, rps))
                            for h, psc, rps in cur:
                                rb_, R_rb_ = Rb[h % 4], R_Rb[h % 4]
                                if h in RELU_DVE:
                                    TS("dve", rb_[:, 0:wdt], psc[:, 0:wdt], 0.0, None, ALU.max, None, [rps], [R_rb_])
                                else:
                                    ACT(rb_[:, 0:wdt], psc[:, 0:wdt], AF.Relu, [rps], [R_rb_])
                        for hh in pend:
                            last = (hh == 7) and (j < tt or not PE_CAUSAL)
                            MM(pi[:, 0:wdt], Dg[:, blk, hh, :], Rb[hh % 4][:, 0:wdt], hh == 0, last, [R_Dg[blk], R_Rb[hh % 4]], [rpi])
                        pend = [h for h, _, _ in cur]
                    if j == tt and PE_CAUSAL:
                        MM(pi[:, blk * 128:wdt], idb[:], cnegb, False, True, [R_const], [rpi])
                    if j == tt and not PE_CAUSAL:
                        if blk > 0:
                            CP("act", Ib[:, s0:s0 + blk * 128], pi[:, 0:blk * 128], [rpi], [R_Ib])
                        TT("dve", Ib[:, s0 + blk * 128:s0 + wdt], pi[:, blk * 128:wdt], cneg[:], ALU.add,
                           [rpi, R_const], [R_Ib])
                    else:
                        CP("act", Ib[:, s0:s0 + wdt], pi[:, 0:wdt], [rpi], [R_Ib])

                def c_bis0(blk):
                    tb = tt * 4 + blk
                    Wd = (tb + 1) * 128
                    Ib, R_Ib = Isb[tb % 3], R_I[tb % 3]
                    bs, R_bs = bis[tb % 4], R_bis[tb % 4]
                    lo, W0, mid, cnt, stp, hi = (bs[:, i:i + 1] for i in range(6))
                    if Wd <= KTOP:
                        MS("dve", lo, -1e29, [R_bs])
                    else:
                        S_.op("dve", lambda e, o=lo, i_=Ib[:, 0:tb * 128]: e.tensor_reduce(
                            out=o, in_=i_, axis=mybir.AxisListType.X, op=ALU.min), [R_Ib], [R_bs])
                        S_.op("dve", lambda e, o=hi, i_=Ib[:, 0:Wd]: e.tensor_reduce(
                            out=o, in_=i_, axis=mybir.AxisListType.X, op=ALU.max), [R_Ib], [R_bs])
                        STT(W0, hi, 1.0, lo, ALU.add, ALU.subtract, [R_bs], [R_bs])

                def c_bis(blk, it0, it1):
                    tb = tt * 4 + blk
                    Wd = (tb + 1) * 128
                    if Wd <= KTOP:
                        return
                    Ib, R_Ib = Isb[tb % 3], R_I[tb % 3]
                    bs, R_bs = bis[tb % 4], R_bis[tb % 4]
                    lo, W0, mid, cnt, stp, hi = (bs[:, i:i + 1] for i in range(6))
                    for it in range(it0, it1):
                        c = 2.0 ** -(it + 1)
                        STT(mid, W0, c, lo, ALU.mult, ALU.add, [R_bs], [R_bs])
                        TS("dve", junkI[:, 0:Wd], Ib[:, 0:Wd], mid, None, ALU.is_ge, ALU.add, [R_Ib, R_bs],
                           [R_junk, R_bs], accum=cnt)
                        STT(stp, cnt, KTOP - 0.5, W0, ALU.is_ge, ALU.mult, [R_bs], [R_bs])
                        STT(lo, stp, c, lo, ALU.mult, ALU.add, [R_bs], [R_bs])

                def c_maskD(blk):
                    tb = tt * 4 + blk
                    Wd = (tb + 1) * 128
                    Ib, R_Ib = Isb[tb % 3], R_I[tb % 3]
                    bs, R_bs = bis[tb % 4], R_bis[tb % 4]
                    lo = bs[:, 0:1]
                    TS("dve", Ib[:, 0:Wd], Ib[:, 0:Wd], lo, None, ALU.is_ge, None, [R_Ib, R_bs], [R_Ib])

                def c_maskP(blk):
                    tb = tt * 4 + blk
                    mkb, R_mkb = Isb[tb % 3], R_I[tb % 3]
                    sb0 = 0
                    while sb0 <= tb:
                        n = min(4, tb + 1 - sb0)
                        p32, _, rp = pA.next()
                        for i in range(n):
                            S_.op("pe", lambda e, o=p32[:, i * 128:(i + 1) * 128], i_=mkb[:, (sb0 + i) * 128:(sb0 + i + 1) * 128]:
                                  e.transpose(out=o, in_=i_, identity=idf[:]), [R_mkb, R_const], [rp])
                        CP("act", mT[:, sb0:sb0 + n, blk * 128:(blk + 1) * 128],
                           p32[:, 0:n * 128].rearrange("p (k t) -> p k t", k=n), [rp], [R_mTb])
                        sb0 += n

                def add_I(blk):
                    for j in range(tt + 1):
                        ch.append(lambda blk=blk, j=j: c_idx(blk, j))

                def c_bis_pair(bA, bB, it):
                    c = 2.0 ** -(it + 1)
                    info = []
                    for b_ in (bA, bB):
                        tb = tt * 4 + b_
                        Wd = (tb + 1) * 128
                        act = Wd > KTOP
                        Ib, R_Ib = Isb[tb % 3], R_I[tb % 3]
                        bs, R_bs = bis[tb % 4], R_bis[tb % 4]
                        info.append((act, Wd, Ib, R_Ib, bs, R_bs))
                    actA, WdA, IA, R_IA, bsA, R_bsA = info[0]
                    actB, WdB, IB, R_IB, bsB, R_bsB = info[1]
                    if actB:
                        loB, W0B, midB, cntB, stpB = (bsB[:, i:i + 1] for i in range(5))
                        STT(midB, W0B, -c, loB, ALU.mult, ALU.subtract, [R_bsB], [R_bsB])
                        ACT(junkA[:, 0:WdB], IB[:, 0:WdB], AF.Sign, [R_IB, R_bsB], [R_junkA, R_bsB], bias=midB, accum=cntB)
                    if actA:
                        loA, W0A, midA, cntA, stpA = (bsA[:, i:i + 1] for i in range(5))
                        STT(midA, W0A, c, loA, ALU.mult, ALU.add, [R_bsA], [R_bsA])
                        TS("dve", junkI[:, 0:WdA], IA[:, 0:WdA], midA, None, ALU.is_ge, ALU.add, [R_IA, R_bsA],
                           [R_junk, R_bsA], accum=cntA)
                        STT(stpA, cntA, KTOP - 0.5, W0A, ALU.is_ge, ALU.mult, [R_bsA], [R_bsA])
                        STT(loA, stpA, c, loA, ALU.mult, ALU.add, [R_bsA], [R_bsA])
                    if actB:
                        STT(stpB, cntB, 2 * KTOP - WdB - 0.5, W0B, ALU.is_ge, ALU.mult, [R_bsB], [R_bsB])
                        STT(loB, stpB, c, loB, ALU.mult, ALU.add, [R_bsB], [R_bsB])

                def add_Bpair(bA, bB):
                    ch.append(lambda: c_bis0(bA))
                    ch.append(lambda: c_bis0(bB))
                    if (tt * 4 + bB + 1) * 128 > KTOP:
                        for it in range(NIT):
                            ch.append(lambda it=it: c_bis_pair(bA, bB, it))
                    ch.append(lambda: c_maskD(bA))
                    ch.append(lambda: c_maskD(bB))

                add_I(0); add_I(1); add_I(2)
                add_Bpair(0, 1)
                ch.append(lambda: c_maskP(0))
                add_I(3)
                ch.append(lambda: c_maskP(1))
                add_Bpair(2, 3)
                ch.append(lambda: c_maskP(2))
                ch.append(lambda: c_maskP(3))
                return ch

            def Y_stage(seq, tt, par, xch):
                hTb, R_hTb = hTs[par], R_hTs[par]
                kTs, vas = kT[seq % 2], vaug[seq % 2]
                mT, R_mTb = maskT[par], R_mT[par]
                NSB = 4 * (tt + 1)
                nsteps = 8 * (NSB + PVLAG)
                state = {"done": 0, "step": 0}

                def pump():
                    state["step"] += 1
                    target = (len(xch) * state["step"]) // nsteps
                    while state["done"] < min(target, len(xch)):
                        xch[state["done"]]()
                        state["done"] += 1

                def proj_head(h):
                    p, rp = proj_fm(hTb, R_hTb, W1, R_W1, C_Q + 64 * h, 64)
                    pg, rpg = proj_fm(hTb, R_hTb, W1, R_W1, C_GA + 64 * h, 64)
                    rms64(p, rp, gqk[:, 0:1], qT[h % 2][0:64, :], R_qT[h % 2])
                    th, R_th = next_scr()
                    ACT(th[0:64, :], pg[0:64, :], AF.Exp, [rpg], [R_th], scale=-1.0)
                    ACT(th[0:64, :], th[0:64, :], AF.Ln, [R_th, R_const], [R_th], bias=one_t[0:64, :])
                    ACT(th[0:64, :], th[0:64, :], AF.Exp, [R_th], [R_th], scale=-1.0)
                    TT("dve", sga[h % 3][0:64, :], th[0:64, :], pg[0:64, :], ALU.mult, [R_th, rpg], [R_sga[h % 3]])

                def attn_head(h, fin_prev):
                    q_, R_q_ = qT[h % 2], R_qT[h % 2]
                    po, _, rpo = pO.next()
                    pend = []
                    for sbi in range(NSB + PVLAG):
                        if sbi < NSB:
                            j = sbi - 4 * tt
                            c0 = max(j, 0) * 128
                            rk = R_kv[(seq, sbi // 4)]
                            plt, _, rpl = pS.next()
                            near = []
                            for blk in range(4):
                                tb = 4 * tt + blk
                                if sbi == tb:
                                    near.append((blk, 0))
                                elif sbi == tb - 1:
                                    near.append((blk, 1))
                            MM(plt[:, c0:512], kTs[0:64, sbi * 128:(sbi + 1) * 128], q_[0:64, c0:512], True, len(near) == 0,
                               [rk, R_q_], [rpl])
                            for ni, (blk, dl) in enumerate(near):
                                MM(plt[:, blk * 128:(blk + 1) * 128], idb[:], biasT[:, h, dl, :], False,
                                   ni == len(near) - 1, [R_const], [rpl])
                            e_, R_e_ = Eb[sbi % 6], R_E[sbi % 6]
                            ACT(e_[:, c0:512], plt[:, c0:512], AF.Exp, [rpl, R_const], [R_e_], bias=cfar[:, h:h + 1])
                            mcnt[0] += 1
                            TT("pool" if (POOL_MASK and mcnt[0] % MASK_MOD != 0) else "dve", e_[:, c0:512], e_[:, c0:512],
                               mT[:, sbi, c0:512], ALU.mult, [R_e_, R_mTb], [R_e_])
                            pend.append((sbi, c0, e_, R_e_, rk))
                        if sbi >= PVLAG:
                            ps_, pc0, pe_, R_pe_, prk = pend.pop(0)
                            MM(po[0:65, pc0:512], vas[:, ps_, 0:65], pe_[:, pc0:512], ps_ == 0, ps_ == NSB - 1,
                               [prk, R_pe_, R_const], [rpo])
                        if sbi == 0 and fin_prev is not None:
                            fin_prev(0)
                        if sbi == 3 and fin_prev is not None:
                            fin_prev(1)
                        pump()
                    def finish(part, h=h, po=po, rpo=rpo):
                        rd, R_rd_ = rdb[h % 2], R_rd[h % 2]
                        if part == 0:
                            ln_, R_ln = next_scr()
                            ACT(ln_[64:65, :], po[64:65, :], AF.Ln, [rpo], [R_ln])
                            ACT(rd[64:65, :], ln_[64:65, :], AF.Exp, [R_ln], [R_rd_], scale=-1.0)
                            return
                        pb, _, rpb = pA.next()
                        MM(pb[0:64, :], ones_bf[64:65, 0:64], rd[64:65, :], True, True, [R_const, R_rd_], [rpb])
                        tmp, R_tmp = next_scr()
                        TT("dve", tmp[0:64, :], sga[h % 3][0:64, :], po[0:64, :], ALU.mult, [R_sga[h % 3], rpo], [R_tmp])
                        y_, R_y_ = yah[h % 2], R_yah[h % 2]
                        TT("dve", y_[0:64, :], tmp[0:64, :], pb[0:64, :], ALU.mult, [R_tmp, rpb], [R_y_])
                        key = (l, seq, tt)
                        if key not in R_ya:
                            R_ya[key] = Res("ya")
                        DMA("sp", ya_scr[seq, tt, h * 64:(h + 1) * 64, :], y_[0:64, :], [R_y_], [R_ya[key]])
                    return finish

                proj_head(0)
                fin = None
                for h in range(8):
                    fin_prev = fin
                    if h + 1 < 8:
                        proj_head(h + 1)
                    fin = attn_head(h, fin_prev)
                fin(0)
                fin(1)
                while state["done"] < len(xch):
                    xch[state["done"]]()
                    state["done"] += 1

            for half in range(2):
                CP("dve", Wki2[:, :, half * 64:(half + 1) * 64], W1[:, :, C_KI:C_KI + 64], [R_W1], [R_Wki2])
            tiles = [(seq, tt) for seq in range(NSEQ) for tt in range(NTT)]
            for c_ in X_chunks(tiles[0][0], tiles[0][1], 0):
                c_()
            for n_, (seq, tt) in enumerate(tiles):
                if n_ == len(tiles) - 1:
                    load_W2(l, W2, R_W2, xonly_res)
                if K_STOP == 1:
                    R_ya[(l, seq, tt)] = Res("ya")
                    continue
                nxt = X_chunks(tiles[n_ + 1][0], tiles[n_ + 1][1], (n_ + 1) % 2) if n_ + 1 < len(tiles) else []
                if INTERLEAVE:
                    Y_stage(seq, tt, n_ % 2, nxt)
                else:
                    Y_stage(seq, tt, n_ % 2, [])
                    for c_ in nxt:
                        c_()
            S_.barrier()

            AR.reset()
            Wm = AR.alloc_at(W3_OFF, [8, 2048], BF16)
            wbr = AR.alloc_at(W3_OFF + 8 * 2048 * 2, [2, 4, D], BF16)
            wo = AR.alloc_at(W3_OFF + 8 * 2048 * 2 + 8 * D * 2, [8, D], BF16)
            R_W3 = Res("W3")
            w3th = load_W3(l, Wm, wbr, wo, R_W3)
            hT2b = AR.alloc([8, 512], BF16)
            hTp = [hT[:, :, :], hT2b]
            R_hTp = [R_hT, Res("hT2b")]
            vln = AR.alloc([4, 512], BF16)
            R_vln = Res("vln")
            gu2 = AR.alloc([4, 512], BF16)
            R_gu = Res("gu2")
            sgb = AR.alloc([4, 512], BF16)
            R_sgb = Res("sgb")
            ybT = AR.alloc([4, 512], BF16)
            R_ybT = Res("ybT")
            scr2 = [AR.alloc([512], F32) for _ in range(8)]
            R_scr2 = [Res("s2_%d" % i) for i in range(8)]
            sc2n = [0]
            bnst = AR.alloc([4, 8], F32)
            R_bn = Res("bn")
            assert AR.off <= W2_OFF, (AR.off, W2_OFF)
            pA = PPool([0, 1, 2, 3, 4, 5])
            pS = PPool([6, 7])

            def nscr2():
                i = sc2n[0] % 8
                sc2n[0] += 1
                return scr2[i], R_scr2[i]

            def gelu2(p, rp, out_ap, R_out):
                a, R_a = nscr2()
                ACT(a, p, AF.Square, [rp], [R_a])
                TS("dve", a, a, 0.044715, 1.0, ALU.mult, ALU.add, [R_a], [R_a])
                b_, R_b = nscr2()
                TT("dve", b_, a, p, ALU.mult, [R_a, rp], [R_b])
                ACT(b_, b_, AF.Tanh, [R_b], [R_b], scale=0.7978845608028654)
                STT(out_ap, b_, 1.0, p, ALU.add, ALU.mult, [R_b, rp], [R_out])

            tiles2 = [(seq, tt) for seq in range(NSEQ) for tt in range(NTT)]

            def ld2(n_):
                sq_, t_ = tiles2[n_]
                DMA("sp", hTp[n_ % 2].rearrange("p k t -> p (k t)"), hT_scr[sq_, t_], [R_hs[(l, sq_, t_)]], [R_hTp[n_ % 2]])
            ld2(0)
            if True:
                for n2, (seq, tt) in enumerate(tiles2):
                    if n2 + 1 < len(tiles2):
                        ld2(n2 + 1)
                    per = -(-len(w3th) // len(tiles2))
                    for th_ in w3th[n2 * per:(n2 + 1) * per]:
                        th_()
                    hTc, R_hTc = hTp[n2 % 2], R_hTp[n2 % 2]
                    for blk in range(4):
                        p, _, rp = pA.next()
                        for k in range(8):
                            MM(p[:, :], hTc[:, k, blk * 128:(blk + 1) * 128], W2[:, k, 512:1024], k == 0, k == 7,
                               [R_W2, R_hTc], [rp])
                        g2, R_g2 = nscr2()
                        gelu2(p, rp, g2, R_g2)
                        S_.op("dve", lambda e, o=bnst[:, blk, 0:6], i_=g2: e.bn_stats(out=o, in_=i_), [R_g2], [R_bn])
                        S_.op("dve", lambda e, o=bnst[:, blk, 6:8], i_=bnst[:, blk, 0:6]: e.bn_aggr(out=o, in_=i_), [R_bn], [R_bn])
                        TS("dve", bnst[:, blk, 7:8], bnst[:, blk, 7:8], 4 * EPS, None, ALU.add, None, [R_bn], [R_bn])
                        TT("pool", bnst[:, blk, 7:8], bnst[:, blk, 7:8], mhalf, ALU.pow, [R_bn, R_const], [R_bn])
                        TS("dve", g2, g2, bnst[:, blk, 6:7], bnst[:, blk, 7:8], ALU.subtract, ALU.mult, [R_g2, R_bn], [R_g2])
                        TT("dve", g2, g2, lngb[:], ALU.mult, [R_g2, R_lay], [R_g2])
                        TT("dve", vln[:, blk, :], g2, lnbb[:], ALU.add, [R_g2, R_lay], [R_vln])
                    for c in range(4):
                        p, rp = proj_fm(hTc, R_hTc, W2, R_W2, c * 128, 128)
                        gelu2(p, rp, gu2[:, c, :], R_gu)
                        p, rp = proj_fm(hTc, R_hTc, W2, R_W2, 1024 + c * 128, 128)
                        th, R_th = nscr2()
                        ACT(th, p, AF.Tanh, [rp], [R_th], scale=0.5)
                        STT(sgb[:, c, :], th, 1.0, p, ALU.add, ALU.mult, [R_th, rp], [R_sgb])
                    for g in range(4):
                        p, _, rp = pS.next()
                        for blk in range(4):
                            MM(p[:, blk * 128:(blk + 1) * 128], vln[:, blk, g * 128:(g + 1) * 128], wT_sp[:, g, :], True, False,
                               [R_vln, R_lay], [rp])
                            MM(p[:, blk * 128:(blk + 1) * 128], ones_bf[0:1, :], bsp[0:1, g, :], False, True,
                               [R_const, R_lay], [rp])
                        t_, R_t = nscr2()
                        STT(t_, gu2[:, g, :], 0.25, p, ALU.mult, ALU.mult, [R_gu, rp], [R_t])
                        TT("dve", ybT[:, g, :], t_, sgb[:, g, :], ALU.mult, [R_t, R_sgb], [R_ybT])
                    key = (l, seq, tt)
                    R_yb[key] = Res("yb")
                    DMA("sp", yb_scr[seq, tt].rearrange("(g p) t -> p g t", p=128), ybT, [R_ybT], [R_yb[key]])
            S_.barrier()

            AR.reset()
            w1th = load_W1_th(l + 1) if l + 1 < DEPTH else []
            hT2c = AR.alloc([8, 512], BF16)
            hTp = [hT[:, :, :], hT2c]
            R_hTp = [R_hT, Res("hT2c")]
            yaTs = [AR.alloc([4, 512], BF16) for _ in range(2)]
            R_yaTs = [Res("yaT0"), Res("yaT1")]
            ybT3s = [AR.alloc([4, 512], BF16) for _ in range(2)]
            R_ybT3s = [Res("ybT30"), Res("ybT31")]
            mg = AR.alloc([8, 512], BF16)
            R_mg = Res("mg")
            scr3 = [AR.alloc([512], F32) for _ in range(8)]
            R_scr3 = [Res("s3_%d" % i) for i in range(8)]
            sc3n = [0]
            xres = [AR.alloc([D], F32) for _ in range(2)]
            R_xr = [Res("xr0"), Res("xr1")]
            assert AR.off <= W3_OFF, (AR.off, W3_OFF)
            pA = PPool([0, 1, 2, 3, 4, 5])
            pS = PPool([6, 7])

            def nscr3():
                i = sc3n[0] % 8
                sc3n[0] += 1
                return scr3[i], R_scr3[i]

            tiles3 = [(seq, tt) for seq in range(NSEQ) for tt in range(NTT)]

            def ld3(n_):
                sq_, t_ = tiles3[n_]
                key_ = (l, sq_, t_)
                DMA("sp", hTp[n_ % 2].rearrange("p k t -> p (k t)"), hT_scr[sq_, t_], [R_hs[key_]], [R_hTp[n_ % 2]])
                DMA("sp", yaTs[n_ % 2], ya_scr[sq_, t_].rearrange("(k p) t -> p k t", p=128), [R_ya[key_]], [R_yaTs[n_ % 2]])
                DMA("sp", ybT3s[n_ % 2], yb_scr[sq_, t_].rearrange("(k p) t -> p k t", p=128), [R_yb[key_]], [R_ybT3s[n_ % 2]])
            ld3(0)
            if True:
                for n3, (seq, tt) in enumerate(tiles3):
                    if n3 + 1 < len(tiles3):
                        ld3(n3 + 1)
                    per = -(-len(w1th) // len(tiles3)) if w1th else 0
                    for th_ in w1th[n3 * per:(n3 + 1) * per]:
                        th_()
                    hTc, R_hTc = hTp[n3 % 2], R_hTp[n3 % 2]
                    yaT, R_yaT = yaTs[n3 % 2], R_yaTs[n3 % 2]
                    ybT3, R_ybT3 = ybT3s[n3 % 2], R_ybT3s[n3 % 2]
                    for ec in range(8):
                        es = slice(ec * 128, (ec + 1) * 128)
                        pa, _, rpa = pA.next()
                        for k in range(4):
                            MM(pa, wbr[:, 0, k, es], yaT[:, k, :], k == 0, k == 3, [R_W3, R_yaT], [rpa])
                        pb, _, rpb = pA.next()
                        for k in range(4):
                            MM(pb, wbr[:, 1, k, es], ybT3[:, k, :], k == 0, k == 3, [R_W3, R_ybT3], [rpb])
                        pc, _, rpc = pA.next()
                        for k in range(8):
                            MM(pc, Wm[:, k, es], hTc[:, k, :], k == 0, k == 7, [R_W3, R_hTc], [rpc])
                        pd, _, rpd = pA.next()
                        for k in range(8):
                            MM(pd, Wm[:, k, 1024 + ec * 128:1024 + (ec + 1) * 128], hTc[:, k, :], k == 0, k == 7,
                               [R_W3, R_hTc], [rpd])
                        ta, R_ta = nscr3()
                        ACT(ta, pc, AF.Tanh, [rpc], [R_ta], scale=0.5)
                        tb_, R_tb = nscr3()
                        ACT(tb_, pd, AF.Tanh, [rpd], [R_tb], scale=0.5)
                        STT(ta, ta, 1.0, pa, ALU.add, ALU.mult, [R_ta, rpa], [R_ta])
                        STT(tb_, tb_, 1.0, pb, ALU.add, ALU.mult, [R_tb, rpb], [R_tb])
                        TT("pool", mg[:, ec, :], ta, tb_, ALU.add, [R_ta, R_tb], [R_mg])
                    for blk in range(4):
                        r0 = tt * 512 + blk * 128
                        xr, R_x = xres[blk % 2], R_xr[blk % 2]
                        DMA("sp", xr, src_t[seq, r0:r0 + 128, :], [R_src], [R_x])
                        for ch in range(2):
                            po, _, rpo = pS.next()
                            for ec in range(8):
                                MM(po, mg[:, ec, blk * 128:(blk + 1) * 128], wo[:, ec, ch * 512:(ch + 1) * 512], ec == 0, ec == 7,
                                   [R_mg, R_W3], [rpo])
                            STT(xr[:, ch * 512:(ch + 1) * 512], po, 0.5, xr[:, ch * 512:(ch + 1) * 512], ALU.mult, ALU.add,
                                [rpo, R_x], [R_x])
                        R_dst = [R_xmid] if l < DEPTH - 1 else []
                        tok = DMA("sp", dst_t[seq, r0:r0 + 128, :], xr, [R_x], R_dst)
                        if l == DEPTH - 1:
                            out_toks.append(tok)
            S_.barrier()
        S_.emit()
    global LAST_SCHED
    LAST_SCHED = S_
    print('arena peak', AR.peak)
    return nc


_CACHE = {}
LAST_SCHED = None


def kernel(**inputs):
    NC = 8
    x = np.ascontiguousarray(inputs["x"], dtype=np.float32)
    B, S, _ = x.shape
    NSEQ = B // NC
    DEPTH = inputs["w_in"].shape[0]
    key = (S, NSEQ, DEPTH)
    if key not in _CACHE:
        _CACHE[key] = build(S, NSEQ, DEPTH)
    nc = _CACHE[key]
    consts = host_consts()
    shared = {k: np.ascontiguousarray(v, dtype=np.float32) for k, v in inputs.items() if k != "x"}
    shared.update(consts)
    in_maps = []
    for c in range(NC):
        m = dict(shared)
        m["x"] = np.ascontiguousarray(x[c * NSEQ:(c + 1) * NSEQ])
        in_maps.append(m)
    res = run_bass_kernel_spmd(nc, in_maps, core_ids=list(range(NC)))
    return np.concatenate([np.asarray(r["out"], dtype=np.float32) for r in res.results], axis=0)
```

```python
import contextlib
import math
import numpy as np
import concourse.bass as bass
import concourse.mybir as mybir
from concourse.bass_utils import run_bass_kernel_spmd

F32 = mybir.dt.float32
BF16 = mybir.dt.bfloat16
ALU = mybir.AluOpType
AF = mybir.ActivationFunctionType

D = 1024
NIN = 5320
NEG = -30000.0
EPS = 1e-6
import os
INTERLEAVE = int(os.environ.get('K_IL', '1'))
POOL_MASK = int(os.environ.get('K_PM', '1'))
POOL_DG = int(os.environ.get('K_PD', '1'))
PE_CAUSAL = int(os.environ.get('K_PC', '1'))
K_E1 = int(os.environ.get('K_E1', '1'))
K_STOP = int(os.environ.get('K_STOP', '0'))
MASK_MOD = int(os.environ.get('K_MM', '3'))
BCH = int(os.environ.get('K_BCH', '1'))
RELU_DVE = [int(c) for c in os.environ.get('K_RD', '')]
SC_PA = int(os.environ.get('K_SCPA', '2'))
PVLAG = int(os.environ.get('K_LAG', '3'))
NIT = int(os.environ.get('K_NIT', '16'))

C_Q, C_K, C_V, C_GA, C_QI, C_KI, C_WI, C_U, C_VB, C_GB, C_MA, C_MB = (
    0, 512, 576, 640, 1152, 1664, 1728, 1736, 2248, 2760, 3272, 4296)


class Res:
    __slots__ = ("name", "w", "r")

    def __init__(self, name):
        self.name = name
        self.w = None
        self.r = []


class Sched:
    ENG = ("pe", "act", "dve", "pool", "sp")
    NDS = 56

    def __init__(self, nc, stack):
        self.nc = nc
        self.sems = {}
        for e in self.ENG:
            self.sems[e] = stack.enter_context(nc.semaphore("s_" + e))
        for i in range(self.NDS):
            self.sems[("d", i)] = stack.enter_context(nc.semaphore("d%d" % i))
        self.dval = [0] * self.NDS
        self.dnext = {"sp": 0, "pool": 0, "act": 0}
        self.drange = {"sp": (0, 16), "act": (0, 16), "pool": (16, self.NDS)}
        self.cnt = {e: 0 for e in self.ENG}
        self.ops = {e: [] for e in self.ENG}
        self.waited = {e: {} for e in self.ENG}

    def _deps(self, eng, reads, writes):
        deps = {}

        def add(tok, kind):
            if tok is None:
                return
            k, v = tok
            if k == eng and (eng == "pe" or kind != "raw"):
                return
            if deps.get(k, 0) < v:
                deps[k] = v
        def addw(wt, kind):
            if isinstance(wt, list):
                for t_ in wt:
                    add(t_, kind)
            else:
                add(wt, kind)
        for r in reads:
            addw(r.w, "raw")
        for w in writes:
            addw(w.w, "waw")
            for t in w.r:
                add(t, "war")
        need = []
        wd = self.waited[eng]
        for k, v in deps.items():
            if wd.get(k, 0) < v:
                wd[k] = v
                need.append((k, v))
        return need

    def _mark(self, tok, reads, writes, append=False):
        for r in reads:
            r.r.append(tok)
            if len(r.r) > 64:
                best = {}
                for k, v in r.r:
                    if best.get(k, 0) < v:
                        best[k] = v
                r.r = list(best.items())
        for w in writes:
            if append and isinstance(w.w, list):
                w.w.append(tok)
            elif append:
                w.w = [tok]
            else:
                w.w = tok
            w.r = []

    def op(self, eng, fn, reads=(), writes=()):
        need = self._deps(eng, reads, writes)
        self.cnt[eng] += 1
        tok = (eng, self.cnt[eng])
        self.ops[eng].append((need, fn, (eng, 1)))
        self._mark(tok, reads, writes)
        return tok

    def dma(self, eng, fn, reads=(), writes=(), append=False):
        lo_, hi_ = self.drange[eng]
        j = lo_ + self.dnext[eng] % (hi_ - lo_)
        self.dnext[eng] += 1
        need = self._deps(eng, reads, writes)
        if self.dval[j] > 0:
            wd = self.waited[eng]
            if wd.get(("d", j), 0) < self.dval[j]:
                wd[("d", j)] = self.dval[j]
                need.append((("d", j), self.dval[j]))
        self.dval[j] += 16
        tok = (("d", j), self.dval[j])
        self.ops[eng].append((need, fn, (("d", j), 16)))
        self._mark(tok, reads, writes, append)
        return tok

    def barrier(self):
        for e in self.ENG:
            need = []
            wd = self.waited[e]
            for k in self.ENG:
                if k != e and self.cnt[k] > wd.get(k, 0):
                    wd[k] = self.cnt[k]
                    need.append((k, self.cnt[k]))
            for j in range(self.NDS):
                if self.dval[j] > wd.get(("d", j), 0):
                    wd[("d", j)] = self.dval[j]
                    need.append((("d", j), self.dval[j]))
            self.ops[e].append((need, None, None))

    def emit(self):
        nc = self.nc
        sems = self.sems
        with nc.Block() as block:
            def run(e_name):
                def body(engine):
                    for need, fn, inc in self.ops[e_name]:
                        for k, v in need:
                            engine.wait_ge(sems[k], v)
                        if fn is not None:
                            fn(engine).then_inc(sems[inc[0]], inc[1])
                return body
            block.tensor(run("pe"))
            block.scalar(run("act"))
            block.vector(run("dve"))
            block.gpsimd(run("pool"))
            block.sync(run("sp"))


class Arena:
    def __init__(self, t16):
        self.t16 = t16
        self.t32 = t16.bitcast(F32)
        self.off = 0
        self.cap = t16.shape[1] * 2
        self.peak = 0

    def reset(self):
        self.off = 0

    def alloc_at(self, off, free_shape, dt):
        save = self.off
        self.off = off
        ap = self.alloc(free_shape, dt)
        self.off = save
        return ap

    def alloc_dual(self, n32):
        self.off = (self.off + 63) // 64 * 64
        o = self.off
        self.off += n32 * 4
        self.peak = max(self.peak, self.off)
        assert self.off <= self.cap, ("arena overflow", self.off, self.cap)
        return self.t32[:, o // 4:o // 4 + n32], self.t16[:, o // 2:o // 2 + 2 * n32]

    def alloc(self, free_shape, dt, parts=128):
        es = 4 if dt == F32 else 2
        n = int(np.prod(free_shape))
        self.off = (self.off + 63) // 64 * 64
        o = self.off
        self.off += n * es
        self.peak = max(self.peak, self.off)
        assert self.off <= self.cap, ("arena overflow", self.off, self.cap)
        base = self.t32 if dt == F32 else self.t16
        ap = base[0:parts, o // es:o // es + n]
        if len(free_shape) == 2:
            ap = ap.rearrange("p (a b) -> p a b", a=free_shape[0])
        elif len(free_shape) == 3:
            ap = ap.rearrange("p (a b c) -> p a b c", a=free_shape[0], b=free_shape[1])
        return ap


def _bucket_table():
    oh = np.zeros((33, 384), np.float32)
    for m in range(384):
        rel = m - 127
        if rel < 0:
            oh[32, m] = 1.0
            continue
        if rel < 16:
            b = rel
        else:
            nf = np.float32(max(rel, 1))
            v = np.log(nf / np.float32(16.0)) / np.float32(math.log(128 / 16)) * np.float32(16.0)
            b = min(16 + int(np.float32(v).astype(np.int32)), 31)
        oh[b, m] = 1.0
    return oh


def host_consts():
    t = np.arange(128)
    return {
        "c_ident": np.eye(128, dtype=np.float32),
        "c_tril": (t[:, None] >= t[None, :]).astype(np.float32),
        "c_cneg": np.where(t[None, :] <= t[:, None], 0.0, -1e30).astype(np.float32),
        "c_oh": _bucket_table(),
    }


def build(S, NSEQ, DEPTH, dbg=None):
    nc = bass.Bass("TRN2", target_bir_lowering=False)
    NTT = S // 512
    KTOP = min(256, S // 4)
    dt_in = lambda name, shape: nc.dram_tensor(name, shape, F32, kind="ExternalInput")
    x_in = dt_in("x", [NSEQ, S, D])
    norm_g = dt_in("norm_g", [DEPTH, D])
    w_in = dt_in("w_in", [DEPTH, D, NIN])
    q_norm_g = dt_in("q_norm_g", [DEPTH, 64])
    k_norm_g = dt_in("k_norm_g", [DEPTH, 64])
    rel_bias = dt_in("rel_bias", [32, 8])
    sgu_ln_g = dt_in("sgu_ln_g", [DEPTH, 512])
    sgu_ln_b = dt_in("sgu_ln_b", [DEPTH, 512])
    w_spatial = dt_in("w_spatial", [DEPTH, 4, 128, 128])
    b_spatial = dt_in("b_spatial", [DEPTH, 4, 128])
    w_branch = dt_in("w_branch", [DEPTH, 2, 512, D])
    w_out = dt_in("w_out", [DEPTH, D, D])
    c_ident = dt_in("c_ident", [128, 128])
    c_tril = dt_in("c_tril", [128, 128])
    c_cneg = dt_in("c_cneg", [128, 128])
    c_oh = dt_in("c_oh", [33, 384])
    out = nc.dram_tensor("out", [NSEQ, S, D], F32, kind="ExternalOutput")
    xmid = nc.dram_tensor("xmid", [NSEQ, S, D], F32, kind="Internal")
    hT_scr = nc.dram_tensor("hT_scr", [NSEQ, NTT, 128, 8 * 512], BF16, kind="Internal")
    ya_scr = nc.dram_tensor("ya_scr", [NSEQ, NTT, 512, 512], BF16, kind="Internal")
    yb_scr = nc.dram_tensor("yb_scr", [NSEQ, NTT, 512, 512], BF16, kind="Internal")
    bias_scr = nc.dram_tensor("bias_scr", [128, 8 * 384], BF16, kind="Internal")
    R_xmid, R_hT_scr, R_ya_scr, R_yb_scr, R_bias_scr = (Res(n) for n in ("xmid", "hTs", "yas", "ybs", "bs"))
    R_ya = {}
    R_yb = {}
    R_hs = {}

    with contextlib.ExitStack() as st:
        S_ = Sched(nc, st)
        sb = lambda name, shape, dt: st.enter_context(nc.sbuf_tensor(name, shape, dt))
        def MM(out_, lhsT, rhs, start, stop, R, W):
            S_.op("pe", lambda e: e.matmul(out_, lhsT=lhsT, rhs=rhs, start=start, stop=stop), R, W)

        def TR(out_, in_, R, W):
            S_.op("pe", lambda e: e.transpose(out=out_, in_=in_, identity=idb[:]), list(R) + [R_const], W)

        def ACT(out_, in_, func, R, W, bias=None, scale=None, accum=None):
            kw = {}
            if bias is not None:
                kw["bias"] = bias
            if scale is not None:
                kw["scale"] = scale
            if accum is not None:
                kw["accum_out"] = accum
            S_.op("act", lambda e: e.activation(out=out_, in_=in_, func=func, **kw), R, W)

        def TS(eng, out_, in0, s1, s2, op0, op1, R, W, accum=None):
            kw = {}
            if accum is not None:
                kw["accum_out"] = accum
            if op1 is None:
                S_.op(eng, lambda e: e.tensor_scalar(out=out_, in0=in0, scalar1=s1, scalar2=None, op0=op0, **kw), R, W)
            else:
                S_.op(eng, lambda e: e.tensor_scalar(out=out_, in0=in0, scalar1=s1, scalar2=s2, op0=op0, op1=op1, **kw), R, W)

        def TT(eng, out_, in0, in1, op, R, W):
            S_.op(eng, lambda e: e.tensor_tensor(out=out_, in0=in0, in1=in1, op=op), R, W)

        def STT(out_, in0, scalar, in1, op0, op1, R, W):
            S_.op("dve", lambda e: e.scalar_tensor_tensor(out=out_, in0=in0, scalar=scalar, in1=in1, op0=op0, op1=op1), R, W)

        def CP(eng, out_, in_, R, W):
            if eng == "act":
                S_.op("act", lambda e: e.copy(out=out_, in_=in_), R, W)
            else:
                S_.op(eng, lambda e: e.tensor_copy(out=out_, in_=in_), R, W)

        def MS(eng, ap, val, W):
            S_.op(eng, lambda e: e.memset(ap, val), (), W)

        def DMA(q, out_, in_, R, W, append=False):
            return S_.dma(q, lambda e: e.dma_start(out=out_, in_=in_), R, W, append)

        R_const = Res("const")
        idb = sb("idb", [128, 128], BF16)
        tril = sb("tril", [128, 128], F32)
        idf = sb("idf", [128, 128], F32)
        cneg = sb("cneg", [128, 128], F32)
        ones_bf = sb("ones_bf", [128, 128], BF16)
        o64 = sb("o64", [128, 128], BF16)
        smallc = sb("smallc", [128, 8], F32)
        biasT = sb("biasT", [128, 8, 2, 128], BF16)
        cfar = sb("cfar", [128, 8], F32)
        gbc = sb("gbc", [128, D], F32)
        lngb = sb("lngb", [128, 512], F32)
        lnbb = sb("lnbb", [128, 512], F32)
        gqk = sb("gqk", [64, 4], F32)
        wT_sp = sb("wT_sp", [128, 4, 128], BF16)
        bsp = sb("bsp", [1, 4, 128], BF16)
        hT = sb("hT", [128, 8, 512], BF16)
        R_hT = Res("hT")
        R_lay = Res("layerconst")
        psum = [st.enter_context(nc.psum_tensor("ps%d" % i, [128, 512], F32)) for i in range(8)]
        psum16 = [p.bitcast(BF16) for p in psum]
        R_ps = [Res("ps%d" % i) for i in range(8)]

        class PPool:
            def __init__(self, idxs):
                self.idxs = idxs
                self.n = 0

            def next(self):
                i = self.idxs[self.n % len(self.idxs)]
                self.n += 1
                return psum[i][:, :], psum16[i][:, :], R_ps[i]

        ARENA_BYTES = int(os.environ.get("K_AR", "176")) * 1024
        arena_t = sb("arena", [128, ARENA_BYTES // 2], BF16)
        AR = Arena(arena_t)

        DMA("pool", idb[:], c_ident[:, :], (), [R_const])
        DMA("sp", tril[:], c_tril[:, :], (), [R_const])
        DMA("sp", idf[:], c_ident[:, :], (), [R_const])
        DMA("sp", cneg[:], c_cneg[:, :], (), [R_const])
        MS("dve", ones_bf[:], 1.0, [R_const])
        MS("dve", o64[:], 1.0 / 64, [R_const])
        MS("dve", smallc[:, 0:1], -0.5, [R_const])
        MS("dve", smallc[:, 1:2], EPS, [R_const])
        MS("dve", smallc[:, 2:3], 4 * EPS, [R_const])
        MS("dve", smallc[:, 3:4], 1.0, [R_const])
        one_t = smallc[:, 3:4]
        mhalf = smallc[:, 0:1]
        eps_t = smallc[:, 1:2]
        eps4_t = smallc[:, 2:3]

        AR.reset()
        oh_f = AR.alloc([384], F32)
        rb_ext = AR.alloc([8], F32)
        ones33 = AR.alloc([128], F32)
        lh = AR.alloc([8, 128], F32)
        Bsb = AR.alloc([8, 384], BF16)
        R_su = Res("setup")
        DMA("sp", oh_f[0:33, :], c_oh[:, :], (), [R_su])
        MS("dve", rb_ext[32:33, :], NEG, [R_su])
        DMA("sp", rb_ext[0:32, :], rel_bias[:, :], (), [R_su])
        MS("dve", ones33[0:33, :], 1.0, [R_su])
        for h in range(8):
            TS("dve", lh[0:33, h, :], ones33[0:33, :], rb_ext[0:33, h:h + 1], None, ALU.mult, None, [R_su], [R_su])
        for h in range(8):
            pB, _, rB = psum[h % 2], None, R_ps[h % 2]
            MM(pB[:, 0:384], lh[0:33, h, :], oh_f[0:33, :], True, True, [R_su], [rB])
            CP("dve", cfar[:, h:h + 1], pB[:, 300:301], [rB], [R_const])
            TS("dve", Bsb[:, h, :], pB[:, 0:384], cfar[:, h:h + 1], None, ALU.subtract, None, [rB, R_const], [R_su])
        DMA("sp", bias_scr[:, :], Bsb[:].rearrange("p h m -> p (h m)"), [R_su], [R_bias_scr])
        for h in range(8):
            for dl in range(2):
                src = bass.AP(bias_scr, h * 384 + 127 + 128 * dl, [[8 * 384 - 1, 128], [1, 128]])
                DMA("sp", biasT[:, h, dl, :], src, [R_bias_scr], [R_const])
        S_.barrier()

        W1_OFF = ARENA_BYTES - 8 * 1736 * 2 - 64
        W1_OFF -= W1_OFF % 64
        W1 = AR.alloc_at(W1_OFF, [8, 1736], BF16)
        R_W1 = Res("W1")
        lay = {}

        def load_W1(l_, first):
            for k in range(8):
                DMA("pool", W1[:, k, :], w_in[l_, k * 128:(k + 1) * 128, 0:1736], (), [R_W1], append=(k > 0))

        def load_W2(l_, W2_, R_W2_, extraW):
            for k in range(8):
                DMA("pool", W2_[:, k, :], w_in[l_, k * 128:(k + 1) * 128, C_U:C_U + 1536], (),
                    [R_W2_] + (list(extraW) if k == 0 else []), append=(k > 0))

        def load_W3(l_, Wm_, wbr_, wo_, R_W3_):
            th = []
            for k in range(8):
                th.append(lambda k=k: DMA("pool", Wm_[:, k, :], w_in[l_, k * 128:(k + 1) * 128, C_MA:C_MA + 2048], (),
                                          [R_W3_], append=(k > 0)))
            for n_ in range(2):
                for k in range(4):
                    th.append(lambda n_=n_, k=k: DMA("pool", wbr_[:, n_, k, :], w_branch[l_, n_, k * 128:(k + 1) * 128, :], (),
                                                     [R_W3_], append=True))
            for k in range(8):
                th.append(lambda k=k: DMA("pool", wo_[:, k, :], w_out[l_, k * 128:(k + 1) * 128, :], (), [R_W3_], append=True))
            return th

        def load_W1_th(l_):
            return [lambda k=k: DMA("pool", W1[:, k, :], w_in[l_, k * 128:(k + 1) * 128, 0:1736], (), [R_W1], append=(k > 0))
                    for k in range(8)]

        load_W1(0, True)
        out_toks = []
        for l in range(DEPTH):
            src_t = x_in if l == 0 else xmid
            dst_t = out if l == DEPTH - 1 else xmid
            R_src = Res("src") if l == 0 else R_xmid
            assert DEPTH <= 2
            AR.reset()
            DMA("sp", gbc[:], bass.AP(norm_g, l * D, [[0, 128], [1, D]]), (), [R_lay])
            DMA("sp", lngb[:], bass.AP(sgu_ln_g, l * 512, [[0, 128], [1, 512]]), (), [R_lay])
            DMA("sp", lnbb[:], bass.AP(sgu_ln_b, l * 512, [[0, 128], [1, 512]]), (), [R_lay])
            DMA("sp", gqk[:, 0:1], bass.AP(q_norm_g, l * 64, [[1, 64], [1, 1]]), (), [R_lay])
            DMA("sp", gqk[:, 1:2], bass.AP(k_norm_g, l * 64, [[1, 64], [1, 1]]), (), [R_lay])
            TS("dve", gqk[:, 2:3], gqk[:, 1:2], 0.125, None, ALU.mult, None, [R_lay], [R_lay])
            DMA("pool", bsp[:].rearrange("o g t -> o (g t)"), bass.AP(b_spatial, l * 512, [[0, 1], [1, 512]]), (), [R_lay])
            wsp_f = AR.alloc([4, 128], F32)
            wsp_m = AR.alloc([4, 128], BF16)
            R_w = Res("wsp")
            DMA("sp", wsp_f, w_spatial[l].rearrange("g t s -> t g s"), (), [R_w])
            for g in range(4):
                TT("dve", wsp_m[:, g, :], wsp_f[:, g, :], tril[:], ALU.mult, [R_w, R_const], [R_w])
            pT, pT16, rT = psum[0][:, :], psum16[0][:, :], R_ps[0]
            for g in range(4):
                TR(pT16[:, g * 128:(g + 1) * 128], wsp_m[:, g, :], [R_w], [rT])
            CP("dve", wT_sp[:].rearrange("p g t -> p (g t)"), pT16[:, 0:512], [rT], [R_lay])
            S_.barrier()

            AR.reset()
            hT2 = AR.alloc([8, 512], BF16)
            hTs = [hT[:, :, :], hT2]
            R_hTs = [R_hT, Res("hT2")]
            kT = [AR.alloc([S], BF16) for _ in range(2)]
            kidxT = [AR.alloc([S], BF16) for _ in range(2)]
            vaug = [AR.alloc([S // 128, 66], BF16) for _ in range(2)]
            R_kv = {}
            qT = [AR.alloc([512], BF16) for _ in range(2)]
            R_qT = [Res("qT0"), Res("qT1")]
            sga = [AR.alloc([512], BF16) for _ in range(3)]
            R_sga = [Res("sga0"), Res("sga1"), Res("sga2")]
            EbX = [AR.alloc([512], BF16) for _ in range(2)]
            R_rd = [Res("rd0"), Res("rd1")]
            yah = [AR.alloc([512], BF16) for _ in range(2)]
            R_yah = [Res("yah0"), Res("yah1")]
            rdb = yah
            maskT0 = AR.alloc([max(4, S // 128 - 4), 512], BF16)
            AR.off = max((AR.off + 63) // 64 * 64, 48 * 1024)
            X0 = AR.off
            xt = [AR.alloc([D], F32) for _ in range(2)]
            R_xt = [Res("xt0"), Res("xt1")]
            hb = [AR.alloc([D], BF16) for _ in range(2)]
            R_hb = [Res("hb0"), Res("hb1")]
            qidxT = AR.alloc([4, 512], BF16)
            Wki2 = AR.alloc([8, 128], BF16)
            pad_ = AR.alloc([1024], BF16)
            R_Wki2 = Res('Wki2')
            R_qi = Res("qidxT")
            Dg = AR.alloc([4, 8, 128], BF16)
            R_Dg = [Res("Dg%d" % i) for i in range(4)]
            Rb = [AR.alloc([512], BF16) for _ in range(4)]
            R_Rb = [Res("Rb%d" % i) for i in range(4)]
            Isb = [AR.alloc([S], F32) for _ in range(3)]
            R_I = [Res("I0"), Res("I1"), Res("I2")]
            junkI = AR.alloc([S], BF16)
            R_junk = Res("junkI")
            X1 = AR.off
            assert X1 - X0 >= 8 * 1536 * 2, (X0, X1)
            xonly_res = R_xt + R_hb + [R_qi] + R_Dg + R_Rb + R_I + [R_junk, R_Wki2]
            st_ = AR.alloc([16], F32)
            R_st = Res("st")
            widx = AR.alloc([4, 8], F32)
            R_wi = Res("widx")
            bis = [AR.alloc([8], F32) for _ in range(4)]
            R_bis = [Res("bis%d" % i) for i in range(4)]
            maskT = [maskT0, AR.alloc([S // 128, 512], BF16)]
            R_mT = [Res("maskT0"), Res("maskT1")]
            cnegb = AR.alloc([128], BF16)
            scr = [AR.alloc([512], F32) for _ in range(4)]
            R_scr = [Res("scr%d" % i) for i in range(4)]
            junkA = AR.alloc([S], BF16)
            R_junkA = Res("junkA")
            scrn = [0]
            Eb = [AR.alloc([512], BF16) for _ in range(4)] + EbX
            R_E = [Res("E%d" % i) for i in range(6)]
            sqb = [AR.alloc([512], BF16) for _ in range(2)]
            R_sqb = [Res("sqb0"), Res("sqb1")]
            sqn = [0]
            assert AR.off <= W1_OFF, (AR.off, W1_OFF)
            W2_OFF = X0
            W3_OFF = (X0 + 8 * 1536 * 2 + 63) // 64 * 64
            W2 = AR.alloc_at(W2_OFF, [8, 1536], BF16)
            R_W2 = Res("W2")
            assert W3_OFF + (8 * 2048 + 8 * D + 8 * D) * 2 + 256 <= W1_OFF
            pA = PPool([0, 1, 2])
            pS = PPool([3, 4])
            pI = PPool([5])
            pO = PPool([6, 7])
            mcnt = [0]

            def next_scr():
                i = scrn[0] % 4
                scrn[0] += 1
                return scr[i], R_scr[i]

            for i in range(2):
                MS("dve", vaug[i][:, :, 64:65], 1.0, [R_const])
            CP("dve", cnegb, cneg[:], [R_const], [R_const])

            def proj_fm(hTb, R_hTb, Wt, R_Wt, c0, M):
                p, _, rp = pA.next()
                for k in range(8):
                    MM(p[0:M, :], Wt[:, k, c0:c0 + M], hTb[:, k, :], k == 0, k == 7, [R_Wt, R_hTb], [rp])
                return p, rp

            def rms64(p, rp, gcol, out_ap, R_out):
                sq, R_sq = next_scr()
                s1, R_s1 = next_scr()
                sb_, R_sb_ = sqb[sqn[0] % 2], R_sqb[sqn[0] % 2]
                sqn[0] += 1
                ACT(sb_[0:64, :], p[0:64, :], AF.Square, [rp], [R_sb_])
                pm, _, rpm = pA.next()
                MM(pm[0:64, :], o64[0:64, 0:64], sb_[0:64, :], True, True, [R_sb_, R_const], [rpm])
                ACT(sq[0:64, :], pm[0:64, :], AF.Ln, [rpm, R_const], [R_sq], bias=eps_t[0:64, :])
                ACT(s1[0:64, :], sq[0:64, :], AF.Exp, [R_sq], [R_s1], scale=-0.5)
                STT(out_ap, p[0:64, :], gcol, s1[0:64, :], ALU.mult, ALU.mult, [rp, R_s1, R_lay], [R_out])

            def X_chunks(seq, tt, par):
                ch = []
                T0 = tt * 512
                hTb, R_hTb = hTs[par], R_hTs[par]
                kTs, kis, vas = kT[seq % 2], kidxT[seq % 2], vaug[seq % 2]
                rkv = Res("kv")
                R_kv[(seq, tt)] = rkv
                mT, R_mTb = maskT[par], R_mT[par]

                def c_norm(blk):
                    r0 = T0 + blk * 128
                    xb, R_xb = xt[blk % 2], R_xt[blk % 2]
                    hbl, R_hbl = hb[blk % 2], R_hb[blk % 2]
                    DMA("sp", xb, src_t[seq, r0:r0 + 128, :], [R_src], [R_xb])
                    c = blk * 3
                    ACT(hbl, xb, AF.Square, [R_xb], [R_hbl, R_st], accum=st_[:, c:c + 1])
                    ACT(st_[:, c + 1:c + 2], st_[:, c:c + 1], AF.Ln, [R_st, R_const], [R_st], bias=eps_t, scale=1.0 / D)
                    ACT(st_[:, c + 2:c + 3], st_[:, c + 1:c + 2], AF.Exp, [R_st], [R_st], scale=-0.5)
                    STT(hbl, xb, st_[:, c + 2:c + 3], gbc[:], ALU.mult, ALU.mult, [R_xb, R_st, R_lay], [R_hbl])
                    _, p16, rp = pA.next()
                    for k in range(8):
                        TR(p16[:, k * 128:(k + 1) * 128], hbl[:, k * 128:(k + 1) * 128], [R_hbl], [rp])
                    CP("act", hTb[:, :, blk * 128:(blk + 1) * 128],
                       p16[:, 0:1024].rearrange("p (k t) -> p k t", k=8), [rp], [R_hTb])
                for blk in range(4):
                    ch.append(lambda blk=blk: c_norm(blk))

                def c_store():
                    R_hs[(l, seq, tt)] = Res("hs")
                    DMA("sp", hT_scr[seq, tt], hTb.rearrange("p k t -> p (k t)"), [R_hTb], [R_hs[(l, seq, tt)]])
                ch.append(c_store)

                def c_k():
                    p, rp = proj_fm(hTb, R_hTb, W1, R_W1, C_K, 64)
                    rms64(p, rp, gqk[:, 2:3], kTs[0:64, T0:T0 + 512], rkv)
                    p, _, rp = pA.next()
                    for k in range(8):
                        MM(p[:, :], Wki2[:, k, :], hTb[:, k, :], k == 0, k == 7, [R_Wki2, R_hTb], [rp])
                    CP("dve" if K_E1 else "act", kis[:, T0:T0 + 512], p[:, :], [rp], [rkv])
                ch.append(c_k)

                def c_v(blk):
                    p, _, rp = pA.next()
                    for k in range(8):
                        MM(p[:, 0:64], hTb[:, k, blk * 128:(blk + 1) * 128], W1[:, k, C_V:C_V + 64], k == 0, k == 7,
                           [R_W1, R_hTb], [rp])
                    for k in range(8):
                        MM(p[:, 64:72], hTb[:, k, blk * 128:(blk + 1) * 128], W1[:, k, C_WI:C_WI + 8], k == 0, k == 7,
                           [R_W1, R_hTb], [rp])
                    CP("dve" if K_E1 else "act", vas[:, tt * 4 + blk, 0:64], p[:, 0:64], [rp], [rkv])
                    TS("dve", widx[:, blk, :], p[:, 64:72], (8 ** -0.5) * (64 ** -0.5), None, ALU.mult, None, [rp], [R_wi])
                    for h in range(8):
                        if POOL_DG:
                            TS("pool", Dg[:, blk, h, :], idb[:], widx[:, blk, h:h + 1], 1.0, ALU.mult, ALU.mult,
                               [R_const, R_wi], [R_Dg[blk]])
                        else:
                            TS("dve", Dg[:, blk, h, :], idb[:], widx[:, blk, h:h + 1], None, ALU.mult, None,
                               [R_const, R_wi], [R_Dg[blk]])
                for blk in range(4):
                    ch.append(lambda blk=blk: c_v(blk))

                def c_qi(j):
                    p, rp = proj_fm(hTb, R_hTb, W1, R_W1, C_QI + 128 * j, 128)
                    CP("act", qidxT[:, j, :], p[:, :], [rp], [R_qi])
                for j in range(4):
                    ch.append(lambda j=j: c_qi(j))

                def c_idx(blk, j):
                    tb = tt * 4 + blk
                    Ib, R_Ib = Isb[tb % 3], R_I[tb % 3]
                    wdt = 512 if j < tt else (blk + 1) * 128
                    s0 = j * 512
                    pi, _, rpi = pI.next()
                    pend = []
                    for jp in range(5):
                        cur = []
                        if jp < 4:
                            for h in (2 * jp, 2 * jp + 1):
                                lo_ = (h % 2) * 64
                                psc, _, rps = (pA if (SC_PA == 1 or (SC_PA == 2 and jp % 2 == 0)) else pS).next()
                                MM(psc[:, 0:wdt], qidxT[lo_:lo_ + 64, h // 2, blk * 128:(blk + 1) * 128],
                                   kis[lo_:lo_ + 64, s0:s0 + wdt], True, True, [R_qi, R_kv[(seq, j)]], [rps])
                                cur.append((h, psc, rps))
                            for h, psc, rps in cur:
                                rb_, R_rb_ = Rb[h % 4], R_Rb[h % 4]
                                if h in RELU_DVE:
                                    TS("dve", rb_[:, 0:wdt], psc[:, 0:wdt], 0.0, None, ALU.max, None, [rps], [R_rb_])
                                else:
                                    ACT(rb_[:, 0:wdt], psc[:, 0:wdt], AF.Relu, [rps], [R_rb_])
                        for hh in pend:
                            last = (hh == 7) and (j < tt or not PE_CAUSAL)
                            MM(pi[:, 0:wdt], Dg[:, blk, hh, :], Rb[hh % 4][:, 0:wdt], hh == 0, last, [R_Dg[blk], R_Rb[hh % 4]], [rpi])
                        pend = [h for h, _, _ in cur]
                    if j == tt and PE_CAUSAL:
                        MM(pi[:, blk * 128:wdt], idb[:], cnegb, False, True, [R_const], [rpi])
                    if j == tt and not PE_CAUSAL:
                        if blk > 0:
                            CP("act", Ib[:, s0:s0 + blk * 128], pi[:, 0:blk * 128], [rpi], [R_Ib])
                        TT("dve", Ib[:, s0 + blk * 128:s0 + wdt], pi[:, blk * 128:wdt], cneg[:], ALU.add,
                           [rpi, R_const], [R_Ib])
                    else:
                        CP("act", Ib[:, s0:s0 + wdt], pi[:, 0:wdt], [rpi], [R_Ib])

                def c_bis0(blk):
                    tb = tt * 4 + blk
                    Wd = (tb + 1) * 128
                    Ib, R_Ib = Isb[tb % 3], R_I[tb % 3]
                    bs, R_bs = bis[tb % 4], R_bis[tb % 4]
                    lo, W0, mid, cnt, stp, hi = (bs[:, i:i + 1] for i in range(6))
                    if Wd <= KTOP:
                        MS("dve", lo, -1e29, [R_bs])
                    else:
                        S_.op("dve", lambda e, o=lo, i_=Ib[:, 0:tb * 128]: e.tensor_reduce(
                            out=o, in_=i_, axis=mybir.AxisListType.X, op=ALU.min), [R_Ib], [R_bs])
                        S_.op("dve", lambda e, o=hi, i_=Ib[:, 0:Wd]: e.tensor_reduce(
                            out=o, in_=i_, axis=mybir.AxisListType.X, op=ALU.max), [R_Ib], [R_bs])
                        STT(W0, hi, 1.0, lo, ALU.add, ALU.subtract, [R_bs], [R_bs])

                def c_bis(blk, it0, it1):
                    tb = tt * 4 + blk
                    Wd = (tb + 1) * 128
                    if Wd <= KTOP:
                        return
                    Ib, R_Ib = Isb[tb % 3], R_I[tb % 3]
                    bs, R_bs = bis[tb % 4], R_bis[tb % 4]
                    lo, W0, mid, cnt, stp, hi = (bs[:, i:i + 1] for i in range(6))
                    for it in range(it0, it1):
                        c = 2.0 ** -(it + 1)
                        STT(mid, W0, c, lo, ALU.mult, ALU.add, [R_bs], [R_bs])
                        TS("dve", junkI[:, 0:Wd], Ib[:, 0:Wd], mid, None, ALU.is_ge, ALU.add, [R_Ib, R_bs],
                           [R_junk, R_bs], accum=cnt)
                        STT(stp, cnt, KTOP - 0.5, W0, ALU.is_ge, ALU.mult, [R_bs], [R_bs])
                        STT(lo, stp, c, lo, ALU.mult, ALU.add, [R_bs], [R_bs])

                def c_maskD(blk):
                    tb = tt * 4 + blk
                    Wd = (tb + 1) * 128
                    Ib, R_Ib = Isb[tb % 3], R_I[tb % 3]
                    bs, R_bs = bis[tb % 4], R_bis[tb % 4]
                    lo = bs[:, 0:1]
                    TS("dve", Ib[:, 0:Wd], Ib[:, 0:Wd], lo, None, ALU.is_ge, None, [R_Ib, R_bs], [R_Ib])

                def c_maskP(blk):
                    tb = tt * 4 + blk
                    mkb, R_mkb = Isb[tb % 3], R_I[tb % 3]
                    sb0 = 0
                    while sb0 <= tb:
                        n = min(4, tb + 1 - sb0)
                        p32, _, rp = pA.next()
                        for i in range(n):
                            S_.op("pe", lambda e, o=p32[:, i * 128:(i + 1) * 128], i_=mkb[:, (sb0 + i) * 128:(sb0 + i + 1) * 128]:
                                  e.transpose(out=o, in_=i_, identity=idf[:]), [R_mkb, R_const], [rp])
                        CP("act", mT[:, sb0:sb0 + n, blk * 128:(blk + 1) * 128],
                           p32[:, 0:n * 128].rearrange("p (k t) -> p k t", k=n), [rp], [R_mTb])
                        sb0 += n

                def add_I(blk):
                    for j in range(tt + 1):
                        ch.append(lambda blk=blk, j=j: c_idx(blk, j))

                def c_bis_pair(bA, bB, it):
                    c = 2.0 ** -(it + 1)
                    info = []
                    for b_ in (bA, bB):
                        tb = tt * 4 + b_
                        Wd = (tb + 1) * 128
                        act = Wd > KTOP
                        Ib, R_Ib = Isb[tb % 3], R_I[tb % 3]
                        bs, R_bs = bis[tb % 4], R_bis[tb % 4]
                        info.append((act, Wd, Ib, R_Ib, bs, R_bs))
                    actA, WdA, IA, R_IA, bsA, R_bsA = info[0]
                    actB, WdB, IB, R_IB, bsB, R_bsB = info[1]
                    if actB:
                        loB, W0B, midB, cntB, stpB = (bsB[:, i:i + 1] for i in range(5))
                        STT(midB, W0B, -c, loB, ALU.mult, ALU.subtract, [R_bsB], [R_bsB])
                        ACT(junkA[:, 0:WdB], IB[:, 0:WdB], AF.Sign, [R_IB, R_bsB], [R_junkA, R_bsB], bias=midB, accum=cntB)
                    if actA:
                        loA, W0A, midA, cntA, stpA = (bsA[:, i:i + 1] for i in range(5))
                        STT(midA, W0A, c, loA, ALU.mult, ALU.add, [R_bsA], [R_bsA])
                        TS("dve", junkI[:, 0:WdA], IA[:, 0:WdA], midA, None, ALU.is_ge, ALU.add, [R_IA, R_bsA],
                           [R_junk, R_bsA], accum=cntA)
                        STT(stpA, cntA, KTOP - 0.5, W0A, ALU.is_ge, ALU.mult, [R_bsA], [R_bsA])
                        STT(loA, stpA, c, loA, ALU.mult, ALU.add, [R_bsA], [R_bsA])
                    if actB:
                        STT(stpB, cntB, 2 * KTOP - WdB - 0.5, W0B, ALU.is_ge, ALU.mult, [R_bsB], [R_bsB])
                        STT(loB, stpB, c, loB, ALU.mult, ALU.add, [R_bsB], [R_bsB])

                def add_Bpair(bA, bB):
                    ch.append(lambda: c_bis0(bA))
                    ch.append(lambda: c_bis0(bB))
                    if (tt * 4 + bB + 1) * 128 > KTOP:
                        for it in range(NIT):
                            ch.append(lambda it=it: c_bis_pair(bA, bB, it))
                    ch.append(lambda: c_maskD(bA))
                    ch.append(lambda: c_maskD(bB))

                add_I(0); add_I(1); add_I(2)
                add_Bpair(0, 1)
                ch.append(lambda: c_maskP(0))
                add_I(3)
                ch.append(lambda: c_maskP(1))
                add_Bpair(2, 3)
                ch.append(lambda: c_maskP(2))
                ch.append(lambda: c_maskP(3))
                return ch

            def Y_stage(seq, tt, par, xch):
                hTb, R_hTb = hTs[par], R_hTs[par]
                kTs, vas = kT[seq % 2], vaug[seq % 2]
                mT, R_mTb = maskT[par], R_mT[par]
                NSB = 4 * (tt + 1)
                nsteps = 8 * (NSB + PVLAG)
                state = {"done": 0, "step": 0}

                def pump():
                    state["step"] += 1
                    target = (len(xch) * state["step"]) // nsteps
                    while state["done"] < min(target, len(xch)):
                        xch[state["done"]]()
                        state["done"] += 1

                def proj_head(h):
                    p, rp = proj_fm(hTb, R_hTb, W1, R_W1, C_Q + 64 * h, 64)
                    pg, rpg = proj_fm(hTb, R_hTb, W1, R_W1, C_GA + 64 * h, 64)
                    rms64(p, rp, gqk[:, 0:1], qT[h % 2][0:64, :], R_qT[h % 2])
                    th, R_th = next_scr()
                    ACT(th[0:64, :], pg[0:64, :], AF.Exp, [rpg], [R_th], scale=-1.0)
                    ACT(th[0:64, :], th[0:64, :], AF.Ln, [R_th, R_const], [R_th], bias=one_t[0:64, :])
                    ACT(th[0:64, :], th[0:64, :], AF.Exp, [R_th], [R_th], scale=-1.0)
                    TT("dve", sga[h % 3][0:64, :], th[0:64, :], pg[0:64, :], ALU.mult, [R_th, rpg], [R_sga[h % 3]])

                def attn_head(h, fin_prev):
                    q_, R_q_ = qT[h % 2], R_qT[h % 2]
                    po, _, rpo = pO.next()
                    pend = []
                    for sbi in range(NSB + PVLAG):
                        if sbi < NSB:
                            j = sbi - 4 * tt
                            c0 = max(j, 0) * 128
                            rk = R_kv[(seq, sbi // 4)]
                            plt, _, rpl = pS.next()
                            near = []
                            for blk in range(4):
                                tb = 4 * tt + blk
                                if sbi == tb:
                                    near.append((blk, 0))
                                elif sbi == tb - 1:
                                    near.append((blk, 1))
                            MM(plt[:, c0:512], kTs[0:64, sbi * 128:(sbi + 1) * 128], q_[0:64, c0:512], True, len(near) == 0,
                               [rk, R_q_], [rpl])
                            for ni, (blk, dl) in enumerate(near):
                                MM(plt[:, blk * 128:(blk + 1) * 128], idb[:], biasT[:, h, dl, :], False,
                                   ni == len(near) - 1, [R_const], [rpl])
                            e_, R_e_ = Eb[sbi % 6], R_E[sbi % 6]
                            ACT(e_[:, c0:512], plt[:, c0:512], AF.Exp, [rpl, R_const], [R_e_], bias=cfar[:, h:h + 1])
                            mcnt[0] += 1
                            TT("pool" if (POOL_MASK and mcnt[0] % MASK_MOD != 0) else "dve", e_[:, c0:512], e_[:, c0:512],
                               mT[:, sbi, c0:512], ALU.mult, [R_e_, R_mTb], [R_e_])
                            pend.append((sbi, c0, e_, R_e_, rk))
                        if sbi >= PVLAG:
                            ps_, pc0, pe_, R_pe_, prk = pend.pop(0)
                            MM(po[0:65, pc0:512], vas[:, ps_, 0:65], pe_[:, pc0:512], ps_ == 0, ps_ == NSB - 1,
                               [prk, R_pe_, R_const], [rpo])
                        if sbi == 0 and fin_prev is not None:
                            fin_prev(0)
                        if sbi == 3 and fin_prev is not None:
                            fin_prev(1)
                        pump()
                    def finish(part, h=h, po=po, rpo=rpo):
                        rd, R_rd_ = rdb[h % 2], R_rd[h % 2]
                        if part == 0:
                            ln_, R_ln = next_scr()
                            ACT(ln_[64:65, :], po[64:65, :], AF.Ln, [rpo], [R_ln])
                            ACT(rd[64:65, :], ln_[64:65, :], AF.Exp, [R_ln], [R_rd_], scale=-1.0)
                            return
                        pb, _, rpb = pA.next()
                        MM(pb[0:64, :], ones_bf[64:65, 0:64], rd[64:65, :], True, True, [R_const, R_rd_], [rpb])
                        tmp, R_tmp = next_scr()
                        TT("dve", tmp[0:64, :], sga[h % 3][0:64, :], po[0:64, :], ALU.mult, [R_sga[h % 3], rpo], [R_tmp])
                        y_, R_y_ = yah[h % 2], R_yah[h % 2]
                        TT("dve", y_[0:64, :], tmp[0:64, :], pb[0:64, :], ALU.mult, [R_tmp, rpb], [R_y_])
                        key = (l, seq, tt)
                        if key not in R_ya:
                            R_ya[key] = Res("ya")
                        DMA("sp", ya_scr[seq, tt, h * 64:(h + 1) * 64, :], y_[0:64, :], [R_y_], [R_ya[key]])
                    return finish

                proj_head(0)
                fin = None
                for h in range(8):
                    fin_prev = fin
                    if h + 1 < 8:
                        proj_head(h + 1)
                    fin = attn_head(h, fin_prev)
                fin(0)
                fin(1)
                while state["done"] < len(xch):
                    xch[state["done"]]()
                    state["done"] += 1

            for half in range(2):
                CP("dve", Wki2[:, :, half * 64:(half + 1) * 64], W1[:, :, C_KI:C_KI + 64], [R_W1], [R_Wki2])
            tiles = [(seq, tt) for seq in range(NSEQ) for tt in range(NTT)]
            for c_ in X_chunks(tiles[0][0], tiles[0][1], 0):
                c_()
            for n_, (seq, tt) in enumerate(tiles):
                if n_ == len(tiles) - 1:
                    load_W2(l, W2, R_W2, xonly_res)
                if K_STOP == 1:
                    R_ya[(l, seq, tt)] = Res("ya")
                    continue
                nxt = X_chunks(tiles[n_ + 1][0], tiles[n_ + 1][1], (n_ + 1) % 2) if n_ + 1 < len(tiles) else []
                if INTERLEAVE:
                    Y_stage(seq, tt, n_ % 2, nxt)
                else:
                    Y_stage(seq, tt, n_ % 2, [])
                    for c_ in nxt:
                        c_()
            S_.barrier()

            AR.reset()
            Wm = AR.alloc_at(W3_OFF, [8, 2048], BF16)
            wbr = AR.alloc_at(W3_OFF + 8 * 2048 * 2, [2, 4, D], BF16)
            wo = AR.alloc_at(W3_OFF + 8 * 2048 * 2 + 8 * D * 2, [8, D], BF16)
            R_W3 = Res("W3")
            w3th = load_W3(l, Wm, wbr, wo, R_W3)
            hT2b = AR.alloc([8, 512], BF16)
            hTp = [hT[:, :, :], hT2b]
            R_hTp = [R_hT, Res("hT2b")]
            vln = AR.alloc([4, 512], BF16)
            R_vln = Res("vln")
            gu2 = AR.alloc([4, 512], BF16)
            R_gu = Res("gu2")
            sgb = AR.alloc([4, 512], BF16)
            R_sgb = Res("sgb")
            ybT = AR.alloc([4, 512], BF16)
            R_ybT = Res("ybT")
            scr2 = [AR.alloc([512], F32) for _ in range(8)]
            R_scr2 = [Res("s2_%d" % i) for i in range(8)]
            sc2n = [0]
            bnst = AR.alloc([4, 8], F32)
            R_bn = Res("bn")
            assert AR.off <= W2_OFF, (AR.off, W2_OFF)
            pA = PPool([0, 1, 2, 3, 4, 5])
            pS = PPool([6, 7])

            def nscr2():
                i = sc2n[0] % 8
                sc2n[0] += 1
                return scr2[i], R_scr2[i]

            def gelu2(p, rp, out_ap, R_out):
                a, R_a = nscr2()
                ACT(a, p, AF.Square, [rp], [R_a])
                TS("dve", a, a, 0.044715, 1.0, ALU.mult, ALU.add, [R_a], [R_a])
                b_, R_b = nscr2()
                TT("dve", b_, a, p, ALU.mult, [R_a, rp], [R_b])
                ACT(b_, b_, AF.Tanh, [R_b], [R_b], scale=0.7978845608028654)
                STT(out_ap, b_, 1.0, p, ALU.add, ALU.mult, [R_b, rp], [R_out])

            tiles2 = [(seq, tt) for seq in range(NSEQ) for tt in range(NTT)]

            def ld2(n_):
                sq_, t_ = tiles2[n_]
                DMA("sp", hTp[n_ % 2].rearrange("p k t -> p (k t)"), hT_scr[sq_, t_], [R_hs[(l, sq_, t_)]], [R_hTp[n_ % 2]])
            ld2(0)
            if True:
                for n2, (seq, tt) in enumerate(tiles2):
                    if n2 + 1 < len(tiles2):
                        ld2(n2 + 1)
                    per = -(-len(w3th) // len(tiles2))
                    for th_ in w3th[n2 * per:(n2 + 1) * per]:
                        th_()
                    hTc, R_hTc = hTp[n2 % 2], R_hTp[n2 % 2]
                    for blk in range(4):
                        p, _, rp = pA.next()
                        for k in range(8):
                            MM(p[:, :], hTc[:, k, blk * 128:(blk + 1) * 128], W2[:, k, 512:1024], k == 0, k == 7,
                               [R_W2, R_hTc], [rp])
                        g2, R_g2 = nscr2()
                        gelu2(p, rp, g2, R_g2)
                        S_.op("dve", lambda e, o=bnst[:, blk, 0:6], i_=g2: e.bn_stats(out=o, in_=i_), [R_g2], [R_bn])
                        S_.op("dve", lambda e, o=bnst[:, blk, 6:8], i_=bnst[:, blk, 0:6]: e.bn_aggr(out=o, in_=i_), [R_bn], [R_bn])
                        TS("dve", bnst[:, blk, 7:8], bnst[:, blk, 7:8], 4 * EPS, None, ALU.add, None, [R_bn], [R_bn])
                        TT("pool", bnst[:, blk, 7:8], bnst[:, blk, 7:8], mhalf, ALU.pow, [R_bn, R_const], [R_bn])
                        TS("dve", g2, g2, bnst[:, blk, 6:7], bnst[:, blk, 7:8], ALU.subtract, ALU.mult, [R_g2, R_bn], [R_g2])
                        TT("dve", g2, g2, lngb[:], ALU.mult, [R_g2, R_lay], [R_g2])
                        TT("dve", vln[:, blk, :], g2, lnbb[:], ALU.add, [R_g2, R_lay], [R_vln])
                    for c in range(4):
                        p, rp = proj_fm(hTc, R_hTc, W2, R_W2, c * 128, 128)
                        gelu2(p, rp, gu2[:, c, :], R_gu)
                        p, rp = proj_fm(hTc, R_hTc, W2, R_W2, 1024 + c * 128, 128)
                        th, R_th = nscr2()
                        ACT(th, p, AF.Tanh, [rp], [R_th], scale=0.5)
                        STT(sgb[:, c, :], th, 1.0, p, ALU.add, ALU.mult, [R_th, rp], [R_sgb])
                    for g in range(4):
                        p, _, rp = pS.next()
                        for blk in range(4):
                            MM(p[:, blk * 128:(blk + 1) * 128], vln[:, blk, g * 128:(g + 1) * 128], wT_sp[:, g, :], True, False,
                               [R_vln, R_lay], [rp])
                            MM(p[:, blk * 128:(blk + 1) * 128], ones_bf[0:1, :], bsp[0:1, g, :], False, True,
                               [R_const, R_lay], [rp])
                        t_, R_t = nscr2()
                        STT(t_, gu2[:, g, :], 0.25, p, ALU.mult, ALU.mult, [R_gu, rp], [R_t])
                        TT("dve", ybT[:, g, :], t_, sgb[:, g, :], ALU.mult, [R_t, R_sgb], [R_ybT])
                    key = (l, seq, tt)
                    R_yb[key] = Res("yb")
                    DMA("sp", yb_scr[seq, tt].rearrange("(g p) t -> p g t", p=128), ybT, [R_ybT], [R_yb[key]])
            S_.barrier()

            AR.reset()
            w1th = load_W1_th(l + 1) if l + 1 < DEPTH else []
            hT2c = AR.alloc([8, 512], BF16)
            hTp = [hT[:, :, :], hT2c]
            R_hTp = [R_hT, Res("hT2c")]
            yaTs = [AR.alloc([4, 512], BF16) for _ in range(2)]
            R_yaTs = [Res("yaT0"), Res("yaT1")]
            ybT3s = [AR.alloc([4, 512], BF16) for _ in range(2)]
            R_ybT3s = [Res("ybT30"), Res("ybT31")]
            mg = AR.alloc([8, 512], BF16)
            R_mg = Res("mg")
            scr3 = [AR.alloc([512], F32) for _ in range(8)]
            R_scr3 = [Res("s3_%d" % i) for i in range(8)]
            sc3n = [0]
            xres = [AR.alloc([D], F32) for _ in range(2)]
            R_xr = [Res("xr0"), Res("xr1")]
            assert AR.off <= W3_OFF, (AR.off, W3_OFF)
            pA = PPool([0, 1, 2, 3, 4, 5])
            pS = PPool([6, 7])

            def nscr3():
                i = sc3n[0] % 8
                sc3n[0] += 1
                return scr3[i], R_scr3[i]

            tiles3 = [(seq, tt) for seq in range(NSEQ) for tt in range(NTT)]

            def ld3(n_):
                sq_, t_ = tiles3[n_]
                key_ = (l, sq_, t_)
                DMA("sp", hTp[n_ % 2].rearrange("p k t -> p (k t)"), hT_scr[sq_, t_], [R_hs[key_]], [R_hTp[n_ % 2]])
                DMA("sp", yaTs[n_ % 2], ya_scr[sq_, t_].rearrange("(k p) t -> p k t", p=128), [R_ya[key_]], [R_yaTs[n_ % 2]])
                DMA("sp", ybT3s[n_ % 2], yb_scr[sq_, t_].rearrange("(k p) t -> p k t", p=128), [R_yb[key_]], [R_ybT3s[n_ % 2]])
            ld3(0)
            if True:
                for n3, (seq, tt) in enumerate(tiles3):
                    if n3 + 1 < len(tiles3):
                        ld3(n3 + 1)
                    per = -(-len(w1th) // len(tiles3)) if w1th else 0
                    for th_ in w1th[n3 * per:(n3 + 1) * per]:
                        th_()
                    hTc, R_hTc = hTp[n3 % 2], R_hTp[n3 % 2]
                    yaT, R_yaT = yaTs[n3 % 2], R_yaTs[n3 % 2]
                    ybT3, R_ybT3 = ybT3s[n3 % 2], R_ybT3s[n3 % 2]
                    for ec in range(8):
                        es = slice(ec * 128, (ec + 1) * 128)
                        pa, _, rpa = pA.next()
                        for k in range(4):
                            MM(pa, wbr[:, 0, k, es], yaT[:, k, :], k == 0, k == 3, [R_W3, R_yaT], [rpa])
                        pb, _, rpb = pA.next()
                        for k in range(4):
                            MM(pb, wbr[:, 1, k, es], ybT3[:, k, :], k == 0, k == 3, [R_W3, R_ybT3], [rpb])
                        pc, _, rpc = pA.next()
                        for k in range(8):
                            MM(pc, Wm[:, k, es], hTc[:, k, :], k == 0, k == 7, [R_W3, R_hTc], [rpc])
                        pd, _, rpd = pA.next()
                        for k in range(8):
                            MM(pd, Wm[:, k, 1024 + ec * 128:1024 + (ec + 1) * 128], hTc[:, k, :], k == 0, k == 7,
                               [R_W3, R_hTc], [rpd])
                        ta, R_ta = nscr3()
                        ACT(ta, pc, AF.Tanh, [rpc], [R_ta], scale=0.5)
                        tb_, R_tb = nscr3()
                        ACT(tb_, pd, AF.Tanh, [rpd], [R_tb], scale=0.5)
                        STT(ta, ta, 1.0, pa, ALU.add, ALU.mult, [R_ta, rpa], [R_ta])
                        STT(tb_, tb_, 1.0, pb, ALU.add, ALU.mult, [R_tb, rpb], [R_tb])
                        TT("pool", mg[:, ec, :], ta, tb_, ALU.add, [R_ta, R_tb], [R_mg])
                    for blk in range(4):
                        r0 = tt * 512 + blk * 128
                        xr, R_x = xres[blk % 2], R_xr[blk % 2]
                        DMA("sp", xr, src_t[seq, r0:r0 + 128, :], [R_src], [R_x])
                        for ch in range(2):
                            po, _, rpo = pS.next()
                            for ec in range(8):
                                MM(po, mg[:, ec, blk * 128:(blk + 1) * 128], wo[:, ec, ch * 512:(ch + 1) * 512], ec == 0, ec == 7,
                                   [R_mg, R_W3], [rpo])
                            STT(xr[:, ch * 512:(ch + 1) * 512], po, 0.5, xr[:, ch * 512:(ch + 1) * 512], ALU.mult, ALU.add,
                                [rpo, R_x], [R_x])
                        R_dst = [R_xmid] if l < DEPTH - 1 else []
                        tok = DMA("sp", dst_t[seq, r0:r0 + 128, :], xr, [R_x], R_dst)
                        if l == DEPTH - 1:
                            out_toks.append(tok)
            S_.barrier()
        S_.emit()
    global LAST_SCHED
    LAST_SCHED = S_
    print('arena peak', AR.peak)
    return nc


_CACHE = {}
LAST_SCHED = None


def kernel(**inputs):
    NC = 8
    x = np.ascontiguousarray(inputs["x"], dtype=np.float32)
    B, S, _ = x.shape
    NSEQ = B // NC
    DEPTH = inputs["w_in"].shape[0]
    key = (S, NSEQ, DEPTH)
    if key not in _CACHE:
        _CACHE[key] = build(S, NSEQ, DEPTH)
    nc = _CACHE[key]
    consts = host_consts()
    shared = {k: np.ascontiguousarray(v, dtype=np.float32) for k, v in inputs.items() if k != "x"}
    shared.update(consts)
    in_maps = []
    for c in range(NC):
        m = dict(shared)
        m["x"] = np.ascontiguousarray(x[c * NSEQ:(c + 1) * NSEQ])
        in_maps.append(m)
    res = run_bass_kernel_spmd(nc, in_maps, core_ids=list(range(NC)))
    return np.concatenate([np.asarray(r["out"], dtype=np.float32) for r in res.results], axis=0)
```
